# Optimizing a Trainium2 kernel written in Bass

```python
import math
import jax, jax.numpy as jnp
from jax import lax
import numpy as np


D_MODEL = 1024
BATCH = 2
SEQ = 8192
DEPTH = 2

GRID_W = 64
CTX_LEN = 256
N_MOD = 6
FFN_HIDDEN = -(-8 * D_MODEL // (3 * 256)) * 256
NORM_EPS = 1e-6

FOURIER_WIDTH = D_MODEL // 2
FOURIER_GROUPS = 4
FOURIER_GROUP_DIM = FOURIER_WIDTH // FOURIER_GROUPS

SSD_HEAD_DIM = 64
SSD_INNER = 3 * D_MODEL // 2
SSD_HEADS = SSD_INNER // SSD_HEAD_DIM
SSD_GROUPS = 4
SSD_HEADS_PER_GROUP = SSD_HEADS // SSD_GROUPS
SSD_STATE = 128
SSD_CONV = 5
SSD_CHUNK = 128
SSD_XBC = SSD_INNER + 2 * SSD_GROUPS * SSD_STATE
DT_MIN = 0.001
DT_MAX = 0.1
EVEN_IN = FOURIER_WIDTH + SSD_INNER + SSD_XBC + 2 * SSD_HEADS
EVEN_SPLIT = [FOURIER_WIDTH, FOURIER_WIDTH + SSD_INNER, FOURIER_WIDTH + SSD_INNER + SSD_XBC]
EVEN_OUT = FOURIER_WIDTH + SSD_INNER

ATTN_HEADS = D_MODEL // 128
ATTN_QK_DIM = 64
ATTN_V_DIM = 2 * ATTN_QK_DIM
QK_WIDTH = ATTN_HEADS * 2 * ATTN_QK_DIM
ATTN_WIDTH = ATTN_HEADS * ATTN_V_DIM
ATTN_SCALE = ATTN_QK_DIM ** -0.5
Q_BLOCK = 128
ROPE_BASE = 10000.0
ROPE_AXIS_DIM = ATTN_QK_DIM // 2
ROPE_FREQS = ROPE_AXIS_DIM // 2

CONF_CHANNELS = D_MODEL // 2
CONF_WIDTH = 31
ODD_IN = 2 * QK_WIDTH + ATTN_WIDTH + 2 * CONF_CHANNELS
ODD_SPLIT = [QK_WIDTH, 2 * QK_WIDTH, 2 * QK_WIDTH + ATTN_WIDTH]
ODD_OUT = ATTN_WIDTH + CONF_CHANNELS

kernel_name = 'hybrid_fourier_ssd_diffattn_conformer_dit'


def rmsnorm(x, g):
    xf = x.astype(jnp.float32)
    y = xf * lax.rsqrt(jnp.mean(xf * xf, axis=-1, keepdims=True) + NORM_EPS)
    return (y * g.astype(jnp.float32)).astype(x.dtype)


def layernorm(x, g, b):
    xf = x.astype(jnp.float32)
    mu = jnp.mean(xf, axis=-1, keepdims=True)
    var = jnp.mean(jnp.square(xf - mu), axis=-1, keepdims=True)
    y = (xf - mu) * lax.rsqrt(var + NORM_EPS)
    return (y * g.astype(jnp.float32) + b.astype(jnp.float32)).astype(x.dtype)


def modulate(x, shift, scale):
    return x * (1 + scale) + shift


def depthwise_conv(u, w, b):
    k = w.shape[0]
    out = lax.conv_general_dilated(
        u, w[:, None, :].astype(u.dtype), window_strides=(1,),
        padding=[(k // 2, k // 2)], dimension_numbers=('NWC', 'WIO', 'NWC'),
        feature_group_count=u.shape[-1])
    return out + b.astype(u.dtype)


def swiglu(h, wg, wu, wd):
    return (jax.nn.silu(h @ wg) * (h @ wu)) @ wd


def fourier_mix(u):
    b, t, _ = u.shape
    g = u.reshape(b, t, FOURIER_GROUPS, FOURIER_GROUP_DIM).astype(jnp.float32)
    f = jnp.fft.fft2(g, axes=(1, 3), norm='ortho').real
    return f.reshape(b, t, FOURIER_WIDTH).astype(u.dtype)


def ssd_scan(xs, dt, a, bm, cm, state0):
    bsz, t = xs.shape[:2]
    nc = t // SSD_CHUNK

    def chunks(u):
        return u.reshape((bsz, nc, SSD_CHUNK) + u.shape[2:])

    xdt = chunks(xs * dt[..., None])
    bm_c, cm_c = chunks(bm), chunks(cm)
    acs = jnp.cumsum(chunks(dt * a), axis=2)
    mask = jnp.tril(jnp.ones((SSD_CHUNK, SSD_CHUNK), bool))[None, None, :, :, None, None]
    seg = acs[:, :, :, None] - acs[:, :, None, :]
    decay = jnp.exp(jnp.where(mask, seg, -jnp.inf))
    cb = jnp.einsum('bclgn,bcsgn->bclsg', cm_c, bm_c)
    y_diag = jnp.einsum('bclsgh,bcsghp->bclghp', cb[..., None] * decay, xdt)
    to_end = jnp.exp(acs[:, :, -1:] - acs)
    states = jnp.einsum('bcsgn,bcsgh,bcsghp->bcghpn', bm_c, to_end, xdt)
    chunk_decay = jnp.exp(acs[:, :, -1])

    def step(s, inp):
        st, dec = inp
        return s * dec[..., None, None] + st, s

    final, s_in = lax.scan(step, state0, (jnp.moveaxis(states, 1, 0), jnp.moveaxis(chunk_decay, 1, 0)))
    s_in = jnp.moveaxis(s_in, 0, 1)
    y_off = jnp.einsum('bclgn,bcghpn->bclghp', cm_c, s_in) * jnp.exp(acs)[..., None]
    return (y_diag + y_off).reshape(xs.shape), final


def _flip(u, reverse):
    return jnp.flip(u, axis=1) if reverse else u


def ssd_bidirectional(lat, ctx, dt_bias, a_log, d_skip):
    xs_l, bm_l, cm_l, dt_l = lat
    xs_c, bm_c, cm_c, dt_c = ctx
    bsz = xs_l.shape[0]
    gh = (SSD_GROUPS, SSD_HEADS_PER_GROUP)
    ys_l, ys_c = [], []
    for d in range(2):
        rev = d == 1
        a = -jnp.exp(a_log[d].astype(jnp.float32)).reshape(gh)
        bias = dt_bias[d].astype(jnp.float32).reshape(gh)
        dskip = d_skip[d].astype(jnp.float32).reshape(gh)[..., None]
        dl = jax.nn.softplus(dt_l[:, :, d] + bias)
        dc = jax.nn.softplus(dt_c[:, :, d] + bias)
        state0 = jnp.zeros((bsz,) + gh + (SSD_HEAD_DIM, SSD_STATE), jnp.float32)
        yc, sc = ssd_scan(_flip(xs_c, rev), _flip(dc, rev), a, _flip(bm_c, rev), _flip(cm_c, rev), state0)
        yl, _ = ssd_scan(_flip(xs_l, rev), _flip(dl, rev), a, _flip(bm_l, rev), _flip(cm_l, rev), sc)
        ys_c.append(_flip(yc, rev) + dskip * xs_c)
        ys_l.append(_flip(yl, rev) + dskip * xs_l)
    return ys_l[0] + ys_l[1], ys_c[0] + ys_c[1]


def even_mixer(h_lat, h_ctx, w_in, conv_w, conv_b, dt_bias, a_log, d_skip, gnorm_g, w_out, need_ctx):
    def project(h):
        b, t = h.shape[:2]
        f, z, xbc, dt = jnp.split(h @ w_in, EVEN_SPLIT, axis=-1)
        xbc = jax.nn.silu(depthwise_conv(xbc, conv_w, conv_b)).astype(jnp.float32)
        xs, bm, cm = jnp.split(xbc, [SSD_INNER, SSD_INNER + SSD_GROUPS * SSD_STATE], axis=-1)
        ssd_in = (xs.reshape(b, t, SSD_GROUPS, SSD_HEADS_PER_GROUP, SSD_HEAD_DIM),
                  bm.reshape(b, t, SSD_GROUPS, SSD_STATE),
                  cm.reshape(b, t, SSD_GROUPS, SSD_STATE),
                  dt.astype(jnp.float32).reshape(b, t, 2, SSD_GROUPS, SSD_HEADS_PER_GROUP))
        return f, z, ssd_in

    f_l, z_l, ssd_l = project(h_lat)
    f_c, z_c, ssd_c = project(h_ctx)
    y_l, y_c = ssd_bidirectional(ssd_l, ssd_c, dt_bias, a_log, d_skip)

    def finish(f, z, y):
        b, t = z.shape[:2]
        zg = jax.nn.silu(z.astype(jnp.float32)).reshape(y.shape)
        gated = (y * zg).reshape(b, t, SSD_GROUPS, SSD_HEADS_PER_GROUP * SSD_HEAD_DIM)
        yn = rmsnorm(gated, gnorm_g.reshape(SSD_GROUPS, -1)).reshape(b, t, SSD_INNER).astype(z.dtype)
        return jnp.concatenate([fourier_mix(f), yn], axis=-1) @ w_out

    out_c = finish(f_c, z_c, y_c) if need_ctx else None
    return finish(f_l, z_l, y_l), out_c


def apply_axial_rope(u, cos, sin):
    s = u.shape
    u = u.reshape(s[:-1] + (2, 2, ROPE_FREQS))
    cs = cos[None, :, None, None].astype(u.dtype)
    sn = sin[None, :, None, None].astype(u.dtype)
    u1, u2 = u[..., 0, :], u[..., 1, :]
    out = jnp.stack([u1 * cs - u2 * sn, u2 * cs + u1 * sn], axis=-2)
    return out.reshape(s)


def diff_attention_core(q, k, v, lam):
    s = jnp.einsum('bqhcd,bkhcd->bhcqk', q, k).astype(jnp.float32) * ATTN_SCALE
    p = jax.nn.softmax(s, axis=-1)
    w = p[:, :, 0] - lam * p[:, :, 1]
    return jnp.einsum('bhqk,bkhd->bqhd', w.astype(v.dtype), v)


def diff_attention_blocked(q, k, v, lam):
    b, t = q.shape[:2]
    nb = t // Q_BLOCK
    qb = jnp.moveaxis(q.reshape((b, nb, Q_BLOCK) + q.shape[2:]), 1, 0)
    out = lax.map(lambda qi: diff_attention_core(qi, k, v, lam), qb)
    return jnp.moveaxis(out, 0, 1).reshape(b, t, ATTN_HEADS, ATTN_V_DIM)


def conformer_conv(u, conv_w, conv_b, cn_g, cn_b):
    a, gate = jnp.split(u, 2, axis=-1)
    v = a * jax.nn.sigmoid(gate)
    v = depthwise_conv(v, conv_w, conv_b)
    return jax.nn.silu(layernorm(v, cn_g, cn_b))


def odd_mixer(h_lat, h_ctx, cos, sin, w_in, lam, subln_g, conv_w, conv_b, cn_g, cn_b, w_out,
              lambda_init, need_ctx):
    def split_heads(q, k, v):
        b, t = q.shape[:2]
        return (q.reshape(b, t, ATTN_HEADS, 2, ATTN_QK_DIM),
                k.reshape(b, t, ATTN_HEADS, 2, ATTN_QK_DIM),
                v.reshape(b, t, ATTN_HEADS, ATTN_V_DIM))

    q_l, k_l, v_l, u_l = jnp.split(h_lat @ w_in, ODD_SPLIT, axis=-1)
    q_l, k_l, v_l = split_heads(q_l, k_l, v_l)
    q_l = apply_axial_rope(q_l, cos, sin)
    k_l = apply_axial_rope(k_l, cos, sin)
    if need_ctx:
        q_c, k_c, v_c, u_c = jnp.split(h_ctx @ w_in, ODD_SPLIT, axis=-1)
    else:
        q_c = u_c = None
        k_c, v_c = jnp.split(h_ctx @ w_in[:, QK_WIDTH:2 * QK_WIDTH + ATTN_WIDTH], [QK_WIDTH], axis=-1)
        k_c = jnp.concatenate([k_c, k_c[..., :0]], axis=-1)
    b, tc = h_ctx.shape[:2]
    k_c = k_c.reshape(b, tc, ATTN_HEADS, 2, ATTN_QK_DIM)
    v_c = v_c.reshape(b, tc, ATTN_HEADS, ATTN_V_DIM)

    lf = lam.astype(jnp.float32)
    lam_full = jnp.exp(jnp.sum(lf[0] * lf[1])) - jnp.exp(jnp.sum(lf[2] * lf[3])) + lambda_init

    def finish(o, u):
        bb, t = u.shape[:2]
        o = rmsnorm(o, subln_g) * (1 - lambda_init)
        cv = conformer_conv(u, conv_w, conv_b, cn_g, cn_b)
        return jnp.concatenate([o.reshape(bb, t, ATTN_WIDTH), cv], axis=-1) @ w_out

    k_all = jnp.concatenate([k_c, k_l], axis=1)
    v_all = jnp.concatenate([v_c, v_l], axis=1)
    out_l = finish(diff_attention_blocked(q_l, k_all, v_all, lam_full), u_l)
    out_c = None
    if need_ctx:
        q_c = q_c.reshape(b, tc, ATTN_HEADS, 2, ATTN_QK_DIM)
        out_c = finish(diff_attention_core(q_c, k_c, v_c, lam_full), u_c)
    return out_l, out_c


def setup_inputs(seed: int = 0) -> dict:
    key = jax.random.key(seed)
    k = jax.random.split(key, 26)
    f32 = jnp.float32

    def normal(i, shape, scale):
        return jax.random.normal(k[i], shape, f32) * scale

    n_even = (DEPTH + 1) // 2
    n_odd = DEPTH // 2
    dt = jnp.exp(jax.random.uniform(k[13], (n_even, 2, SSD_HEADS), f32, math.log(DT_MIN), math.log(DT_MAX)))
    dt_bias = dt + jnp.log(-jnp.expm1(-dt))
    a_log = jnp.log(jax.random.uniform(k[14], (n_even, 2, SSD_HEADS), f32, 1.0, 16.0))
    return {
        'x': normal(0, (BATCH, SEQ, D_MODEL), 1.0),
        'c': normal(1, (BATCH, D_MODEL), 1.0),
        'ctx': normal(2, (BATCH, CTX_LEN, D_MODEL), 1.0),
        'c_ctx': normal(3, (D_MODEL,), 1.0),
        'mod_w': normal(4, (DEPTH, D_MODEL, N_MOD * D_MODEL), 0.5 * D_MODEL ** -0.5),
        'mod_b': normal(5, (DEPTH, N_MOD * D_MODEL), 0.02),
        'norm_g': 1.0 + normal(6, (DEPTH, 4, D_MODEL), 0.05),
        'ffn_w_gate': normal(7, (DEPTH, D_MODEL, FFN_HIDDEN), D_MODEL ** -0.5),
        'ffn_w_up': normal(8, (DEPTH, D_MODEL, FFN_HIDDEN), D_MODEL ** -0.5),
        'ffn_w_down': normal(9, (DEPTH, FFN_HIDDEN, D_MODEL), FFN_HIDDEN ** -0.5),
        'ev_w_in': normal(10, (n_even, D_MODEL, EVEN_IN), D_MODEL ** -0.5),
        'ev_conv_w': normal(11, (n_even, SSD_CONV, SSD_XBC), SSD_CONV ** -0.5),
        'ev_conv_b': normal(12, (n_even, SSD_XBC), 0.02),
        'ev_dt_bias': dt_bias,
        'ev_a_log': a_log,
        'ev_d_skip': 1.0 + normal(15, (n_even, 2, SSD_HEADS), 0.05),
        'ev_gnorm_g': 1.0 + normal(16, (n_even, SSD_INNER), 0.05),
        'ev_w_out': normal(17, (n_even, EVEN_OUT, D_MODEL), EVEN_OUT ** -0.5),
        'od_w_in': normal(18, (n_odd, D_MODEL, ODD_IN), D_MODEL ** -0.5),
        'od_lambda': normal(19, (n_odd, 4, ATTN_QK_DIM), 0.1),
        'od_subln_g': 1.0 + normal(20, (n_odd, ATTN_V_DIM), 0.05),
        'od_conv_w': normal(21, (n_odd, CONF_WIDTH, CONF_CHANNELS), CONF_WIDTH ** -0.5),
        'od_conv_b': normal(22, (n_odd, CONF_CHANNELS), 0.02),
        'od_cnorm_g': 1.0 + normal(23, (n_odd, CONF_CHANNELS), 0.05),
        'od_cnorm_b': normal(24, (n_odd, CONF_CHANNELS), 0.02),
        'od_w_out': normal(25, (n_odd, ODD_OUT, D_MODEL), ODD_OUT ** -0.5),
    }


def reference(x, c, ctx, c_ctx, mod_w, mod_b, norm_g, ffn_w_gate, ffn_w_up, ffn_w_down,
              ev_w_in, ev_conv_w, ev_conv_b, ev_dt_bias, ev_a_log, ev_d_skip, ev_gnorm_g, ev_w_out,
              od_w_in, od_lambda, od_subln_g, od_conv_w, od_conv_b, od_cnorm_g, od_cnorm_b, od_w_out):
    t = x.shape[1]
    rows = t // GRID_W
    row = jnp.repeat(jnp.arange(rows, dtype=jnp.float32), GRID_W)
    col = jnp.tile(jnp.arange(GRID_W, dtype=jnp.float32), rows)
    inv_freq = ROPE_BASE ** (-jnp.arange(ROPE_FREQS, dtype=jnp.float32) * 2.0 / ROPE_AXIS_DIM)
    ang = jnp.stack([row, col], axis=-1)[:, :, None] * inv_freq
    cos, sin = jnp.cos(ang), jnp.sin(ang)

    h, s = x, ctx
    for i in range(DEPTH):
        last = i == DEPTH - 1
        j = i // 2
        g = norm_g[i]
        m = (jax.nn.silu(c) @ mod_w[i] + mod_b[i]).reshape(-1, N_MOD, 1, D_MODEL)
        mc = (jax.nn.silu(c_ctx) @ mod_w[i] + mod_b[i]).reshape(N_MOD, D_MODEL)
        a_h = modulate(rmsnorm(h, g[0]), m[:, 0], m[:, 1])
        a_s = modulate(rmsnorm(s, g[0]), mc[0], mc[1])
        if i % 2 == 0:
            o_h, o_s = even_mixer(a_h, a_s, ev_w_in[j], ev_conv_w[j], ev_conv_b[j], ev_dt_bias[j],
                                  ev_a_log[j], ev_d_skip[j], ev_gnorm_g[j], ev_w_out[j], not last)
        else:
            lambda_init = 0.8 - 0.6 * math.exp(-0.3 * i)
            o_h, o_s = odd_mixer(a_h, a_s, cos, sin, od_w_in[j], od_lambda[j], od_subln_g[j],
                                 od_conv_w[j], od_conv_b[j], od_cnorm_g[j], od_cnorm_b[j], od_w_out[j],
                                 lambda_init, not last)
        h = h + m[:, 2] * rmsnorm(o_h, g[1])
        f_h = swiglu(modulate(rmsnorm(h, g[2]), m[:, 3], m[:, 4]), ffn_w_gate[i], ffn_w_up[i], ffn_w_down[i])
        h = h + m[:, 5] * rmsnorm(f_h, g[3])
        if not last:
            s = s + mc[2] * rmsnorm(o_s, g[1])
            f_s = swiglu(modulate(rmsnorm(s, g[2]), mc[3], mc[4]), ffn_w_gate[i], ffn_w_up[i], ffn_w_down[i])
            s = s + mc[5] * rmsnorm(f_s, g[3])
    return h
```

```python
import numpy as np
from contextlib import ExitStack
import concourse.bass as bass
import concourse.mybir as mybir
from concourse.bass_utils import run_bass_kernel_spmd

F32 = mybir.dt.float32
BF16 = mybir.dt.bfloat16
AF = mybir.ActivationFunctionType
ALU = mybir.AluOpType
AX = mybir.AxisListType


class Buf:
    __slots__ = ("name", "w", "r")

    def __init__(self, name=""):
        self.name = name
        self.w = None
        self.r = []


class Prog:
    ENGS = ("pe", "act", "dve", "pool", "sp")
    RING = 8

    def __init__(self, nc, stack):
        self.nc = nc
        self.stack = stack
        self.ops = {e: [] for e in self.ENGS}
        self.ccount = {e: 0 for e in self.ENGS}
        self.dcount = {e: 0 for e in self.ENGS}
        self.seen = {e: {} for e in self.ENGS}
        self.sems = {}
        self.nbuf = 0
        for e in self.ENGS:
            self.sems[("c", e)] = stack.enter_context(nc.semaphore("c_" + e))
        for e in ("sp", "pool", "act"):
            for i in range(self.RING):
                self.sems[("d", e, i)] = stack.enter_context(nc.semaphore(f"d_{e}{i}"))

    def sb(self, name, shape, dtype):
        return self.stack.enter_context(self.nc.sbuf_tensor("sb_" + name, list(shape), dtype))

    def ps(self, name, shape, dtype):
        return self.stack.enter_context(self.nc.psum_tensor("ps_" + name, list(shape), dtype))

    def buf(self, name=""):
        self.nbuf += 1
        return Buf(name or f"b{self.nbuf}")

    def alias(self, old):
        b = self.buf()
        b.r = list(old.r) + ([old.w] if old.w is not None else [])
        return b

    def _deps(self, eng, reads, writes):
        deps = {}

        def add(t):
            k, v, e2 = t
            if eng == "pe" and e2 == "pe" and k[0] == "c":
                return
            if v > deps.get(k, 0):
                deps[k] = v
        for b in reads:
            if b.w is not None:
                add(b.w)
        for b in writes:
            if b.w is not None:
                add(b.w)
            for t in b.r:
                add(t)
        seen = self.seen[eng]
        out = []
        for k, v in deps.items():
            if seen.get(k, 0) >= v:
                continue
            seen[k] = v
            out.append((k, v))
        return out

    def op(self, eng, fn, reads=(), writes=()):
        waits = self._deps(eng, reads, writes)
        self.ccount[eng] += 1
        tok = (("c", eng), self.ccount[eng], eng)
        self.ops[eng].append(("c", fn, waits, tok))
        for b in reads:
            b.r.append(tok)
        for b in writes:
            b.w = tok
            b.r = []
        return tok

    def dma(self, eng, out_ap, in_ap, reads=(), writes=(), **kw):
        waits = self._deps(eng, reads, writes)
        j = self.dcount[eng]
        self.dcount[eng] += 1
        slot = j % self.RING
        k = ("d", eng, slot)
        need = 16 * (j // self.RING)
        if need > 0 and self.seen[eng].get(k, 0) < need:
            self.seen[eng][k] = need
            waits.append((k, need))
        tok = (k, 16 * (j // self.RING + 1), eng)

        def fn(e, out_ap=out_ap, in_ap=in_ap, kw=kw):
            return e.dma_start(out=out_ap, in_=in_ap, **kw)
        self.ops[eng].append(("d", fn, waits, tok))
        for b in reads:
            b.r.append(tok)
        for b in writes:
            b.w = tok
            b.r = []
        return tok

    def finish_wait(self, eng, toks):
        waits = []
        for (k, v, _e) in toks:
            if self.seen[eng].get(k, 0) < v:
                self.seen[eng][k] = v
                waits.append((k, v))
        self.ops[eng].append(("w", None, waits, None))

    def wait_all_dma(self, eng="sp"):
        toks = []
        for e in ("sp", "pool", "act"):
            n = self.dcount[e]
            for slot in range(self.RING):
                if n == 0:
                    continue
                last = ((n - 1 - slot) // self.RING) * self.RING + slot if n - 1 >= slot else -1
                if last >= 0:
                    toks.append((("d", e, slot), 16 * (last // self.RING + 1), e))
        self.finish_wait(eng, toks)

    def emit(self):
        nc = self.nc
        prog = self
        with nc.Block() as block:
            def run(engname, e):
                for kind, fn, waits, tok in prog.ops[engname]:
                    for (k, v) in waits:
                        e.wait_ge(prog.sems[k], v)
                    if kind == "w":
                        continue
                    ins = fn(e)
                    if kind == "c":
                        ins.then_inc(prog.sems[tok[0]], 1)
                    else:
                        ins.then_inc(prog.sems[tok[0]], 16)

            @block.tensor
            def _(e):
                run("pe", e)

            @block.scalar
            def _(e):
                run("act", e)

            @block.vector
            def _(e):
                run("dve", e)

            @block.gpsimd
            def _(e):
                run("pool", e)

            @block.sync
            def _(e):
                run("sp", e)


EPS = 1e-6


def new_nc():
    return bass.Bass("TRN2", target_bir_lowering=False)


def load_weight_bf16(P, w_dram, w_sb, b_w, nk, ncol, stage, b_stage, cast_engs=("pool",)):
    for k in range(nk):
        s = k % len(stage)
        P.dma("sp", stage[s][:, 0:ncol], w_dram[k * 128:(k + 1) * 128, :], writes=[b_stage[s]])
        eng = cast_engs[k % len(cast_engs)]
        P.op(eng, lambda e, s=s, k=k: e.tensor_copy(out=w_sb[:, k, :], in_=stage[s][:, 0:ncol]),
             reads=[b_stage[s]], writes=[b_w])


class NormT:
    def __init__(self, P, pfx, ident, b_ident, nt):
        self.P = P
        self.ident, self.b_ident = ident, b_ident
        self.junk = P.sb(pfx + "junk", [128, 1024], BF16)
        self.b_junk = P.buf()
        self.ss = P.sb(pfx + "ss", [128, nt], F32)
        self.b_ss = P.buf()
        self.rs = P.sb(pfx + "rs", [128, nt], F32)
        self.b_rs = [P.buf() for _ in range(nt)]
        self.xn = [P.sb(pfx + f"xn{i}", [128, 1024], BF16) for i in range(2)]
        self.b_xn = [P.buf() for _ in range(2)]
        self.psT = [P.ps(pfx + f"psT{i}", [128, 8, 128], BF16) for i in range(2)]
        self.b_psT = [P.buf() for _ in range(2)]
        P.op("pool", lambda e: e.memset(self.ss[:], 0.0), writes=[self.b_ss])
        self.n = 0

    def run(self, x_ap, b_x, t, aT_ap, b_aT, Gs, Sh, b_gs, cls):
        P = self.P
        i = self.n % 2
        self.n += 1
        ss, rs, xn, psT = self.ss, self.rs, self.xn[i], self.psT[i]
        P.op("act", lambda e: e.activation(out=self.junk[:], in_=x_ap, func=AF.Square, accum_out=ss[:, t:t + 1]),
             reads=[b_x, self.b_ss], writes=[self.b_junk, self.b_rs[t]])
        P.op("dve", lambda e: e.tensor_scalar(out=rs[:, t:t + 1], in0=ss[:, t:t + 1], scalar1=1.0 / 1024, scalar2=EPS,
                                              op0=ALU.mult, op1=ALU.add), reads=[self.b_rs[t]], writes=[self.b_rs[t]])
        P.op("act", lambda e: e.activation(out=rs[:, t:t + 1], in_=rs[:, t:t + 1], func=AF.Sqrt),
             reads=[self.b_rs[t]], writes=[self.b_rs[t]])
        P.op("dve", lambda e: e.reciprocal(out=rs[:, t:t + 1], in_=rs[:, t:t + 1]),
             reads=[self.b_rs[t]], writes=[self.b_rs[t]])
        P.op("dve", lambda e: e.tensor_scalar_mul(out=xn[:], in0=x_ap, scalar1=rs[:, t:t + 1]),
             reads=[b_x, self.b_rs[t]], writes=[self.b_xn[i]])
        for k in range(8):
            P.op("pe", lambda e, k=k: e.transpose(out=psT[:, k, :], in_=xn[:, k * 128:(k + 1) * 128], identity=self.ident[:]),
                 reads=[self.b_xn[i], self.b_ident], writes=[self.b_psT[i]])
        for k in range(8):
            P.op("act", lambda e, k=k: e.activation(out=aT_ap[:, k, :], in_=psT[:, k, :], func=AF.Identity,
                                                    scale=Gs[:, cls, k:k + 1], bias=Sh[:, cls, k:k + 1]),
                 reads=[self.b_psT[i], b_gs], writes=[b_aT])


def build_k1(NT, NOUT, ctx_tiles=(16,)):
    nc = new_nc()
    h = nc.dram_tensor("h", [NT * 128, 1024], F32, kind="ExternalInput").ap()
    w = nc.dram_tensor("w", [1024, NOUT], F32, kind="ExternalInput").ap()
    modT = nc.dram_tensor("modT", [128, 2, 2, 8], F32, kind="ExternalInput").ap()
    gT = nc.dram_tensor("gT", [128, 8], F32, kind="ExternalInput").ap()
    identd = nc.dram_tensor("ident", [128, 128], BF16, kind="ExternalInput").ap()
    out = nc.dram_tensor("out", [NT * 128, NOUT], F32, kind="ExternalOutput").ap()
    ncb = (NOUT + 511) // 512
    with ExitStack() as st:
        P = Prog(nc, st)
        ident = P.sb("ident", [128, 128], BF16); b_ident = P.buf()
        P.dma("sp", ident[:], identd, writes=[b_ident])
        modsb = P.sb("modsb", [128, 2, 2, 8], F32); b_mod = P.buf()
        P.dma("sp", modsb[:], modT, writes=[b_mod])
        gsb = P.sb("gsb", [128, 8], F32); b_g = P.buf()
        P.dma("sp", gsb[:], gT, writes=[b_g])
        Gs = P.sb("Gs", [128, 2, 8], F32); Sh = P.sb("Sh", [128, 2, 8], F32); b_gs = P.buf()
        for cls in range(2):
            P.op("dve", lambda e, cls=cls: e.scalar_tensor_tensor(out=Gs[:, cls, :], in0=modsb[:, cls, 1, :], scalar=1.0,
                                                                   in1=gsb[:], op0=ALU.add, op1=ALU.mult),
                 reads=[b_mod, b_g], writes=[b_gs])
            P.op("dve", lambda e, cls=cls: e.tensor_copy(out=Sh[:, cls, :], in_=modsb[:, cls, 0, :]),
                 reads=[b_mod], writes=[b_gs])
        w_sb = P.sb("w_sb", [128, 8, NOUT], BF16); b_w = P.buf()
        stage = [P.sb(f"stage{i}", [128, NOUT], F32) for i in range(2)]
        b_stage = [P.buf() for _ in range(2)]
        load_weight_bf16(P, w, w_sb, b_w, 8, NOUT, stage, b_stage)
        nt = NormT(P, "n_", ident, b_ident, NT)
        xt = [P.sb(f"xt{i}", [128, 1024], F32) for i in range(2)]; b_xt = [P.buf() for _ in range(2)]
        aT = [P.sb(f"aT{i}", [128, 8, 128], BF16) for i in range(2)]; b_aT = [P.buf() for _ in range(2)]
        ot = stage; b_ot = b_stage
        pso = [P.ps(f"pso{i}", [128, 512], F32) for i in range(4)]; b_pso = [P.buf() for _ in range(4)]
        outs = []
        nps = 0
        for t in range(NT):
            i = t % 2
            cls = 1 if t in ctx_tiles else 0
            P.dma("sp", xt[i][:], h[t * 128:(t + 1) * 128, :], writes=[b_xt[i]])
            nt.run(xt[i][:], b_xt[i], t, aT[i], b_aT[i], Gs, Sh, b_gs, cls)
            for cb in range(ncb):
                c0 = cb * 512; cn = min(512, NOUT - c0)
                pi = nps % 4; nps += 1
                for k in range(8):
                    P.op("pe", lambda e, pi=pi, k=k, c0=c0, cn=cn, i=i: e.matmul(
                        pso[pi][:, 0:cn], lhsT=aT[i][:, k, :], rhs=w_sb[:, k, c0:c0 + cn], start=(k == 0), stop=(k == 7)),
                        reads=[b_aT[i], b_w], writes=[b_pso[pi]])
                if cb % 2 == 0:
                    P.op("dve", lambda e, pi=pi, c0=c0, cn=cn, i=i: e.tensor_copy(out=ot[i][:, c0:c0 + cn], in_=pso[pi][:, 0:cn]),
                         reads=[b_pso[pi]], writes=[b_ot[i]])
                else:
                    P.op("act", lambda e, pi=pi, c0=c0, cn=cn, i=i: e.copy(out=ot[i][:, c0:c0 + cn], in_=pso[pi][:, 0:cn]),
                         reads=[b_pso[pi]], writes=[b_ot[i]])
            outs.append(P.dma("sp", out[t * 128:(t + 1) * 128, :], ot[i][:], reads=[b_ot[i]]))
        P.finish_wait("sp", outs)
        P.emit()
    return nc


class ResNorm:
    def __init__(self, P, pfx, nt):
        self.P = P
        self.junk = P.sb(pfx + "junk", [128, 1024], BF16); self.b_junk = P.buf()
        self.ss = P.sb(pfx + "ss", [128, nt], F32); self.b_ss = P.buf()
        self.rs = P.sb(pfx + "rs", [128, nt], F32); self.b_rs = [P.buf() for _ in range(nt)]
        self.tmp = [P.sb(pfx + f"tmp{i}", [128, 1024], F32) for i in range(2)]; self.b_tmp = [P.buf() for _ in range(2)]
        P.op("pool", lambda e: e.memset(self.ss[:], 0.0), writes=[self.b_ss])
        self.n = 0

    def run(self, po, b_po, t, hres, b_hres, GG_ap, b_gg, out_ap=None, b_out=None):
        P = self.P
        i = self.n % 2
        self.n += 1
        ss, rs, tmp = self.ss, self.rs, self.tmp[i]
        P.op("act", lambda e: e.activation(out=self.junk[:], in_=po, func=AF.Square, accum_out=ss[:, t:t + 1]),
             reads=[b_po, self.b_ss], writes=[self.b_junk, self.b_rs[t]])
        P.op("dve", lambda e: e.tensor_scalar(out=rs[:, t:t + 1], in0=ss[:, t:t + 1], scalar1=1.0 / 1024, scalar2=EPS,
                                              op0=ALU.mult, op1=ALU.add), reads=[self.b_rs[t]], writes=[self.b_rs[t]])
        P.op("act", lambda e: e.activation(out=rs[:, t:t + 1], in_=rs[:, t:t + 1], func=AF.Sqrt),
             reads=[self.b_rs[t]], writes=[self.b_rs[t]])
        P.op("dve", lambda e: e.reciprocal(out=rs[:, t:t + 1], in_=rs[:, t:t + 1]),
             reads=[self.b_rs[t]], writes=[self.b_rs[t]])
        P.op("dve", lambda e: e.scalar_tensor_tensor(out=tmp[:], in0=po, scalar=rs[:, t:t + 1], in1=GG_ap,
                                                     op0=ALU.mult, op1=ALU.mult),
             reads=[b_po, self.b_rs[t], b_gg], writes=[self.b_tmp[i]])
        if out_ap is None:
            P.op("pool", lambda e: e.tensor_add(out=tmp[:], in0=tmp[:], in1=hres),
                 reads=[self.b_tmp[i], b_hres], writes=[self.b_tmp[i]])
            return tmp, self.b_tmp[i]
        P.op("pool", lambda e: e.tensor_add(out=out_ap, in0=tmp[:], in1=hres),
             reads=[self.b_tmp[i], b_hres], writes=[b_out])
        return None


def build_k3a(NT, CM, ctx_tiles=(16,)):
    nc = new_nc()
    nk = CM // 128
    mixT = nc.dram_tensor("mixT", [CM, NT * 128], F32, kind="ExternalInput").ap()
    h = nc.dram_tensor("h", [NT * 128, 1024], F32, kind="ExternalInput").ap()
    w = nc.dram_tensor("w", [CM, 1024], F32, kind="ExternalInput").ap()
    gR = nc.dram_tensor("gR", [128, 1024], F32, kind="ExternalInput").ap()
    gateR = nc.dram_tensor("gateR", [128, 2, 1024], F32, kind="ExternalInput").ap()
    out = nc.dram_tensor("out", [NT * 128, 1024], F32, kind="ExternalOutput").ap()
    with ExitStack() as st:
        P = Prog(nc, st)
        g_sb = P.sb("g_sb", [128, 1024], F32); b_g = P.buf()
        P.dma("sp", g_sb[:], gR, writes=[b_g])
        GG = P.sb("GG", [128, 2, 1024], F32); b_gg = P.buf()
        P.dma("sp", GG[:], gateR, writes=[b_gg])
        for cls in range(2):
            P.op("dve", lambda e, cls=cls: e.tensor_mul(out=GG[:, cls, :], in0=GG[:, cls, :], in1=g_sb[:]),
                 reads=[b_g, b_gg], writes=[b_gg])
        w_sb = P.sb("w_sb", [128, nk, 1024], BF16); b_w = P.buf()
        stage = [P.sb(f"stage{i}", [128, 1024], F32) for i in range(2)]; b_stage = [P.buf() for _ in range(2)]
        load_weight_bf16(P, w, w_sb, b_w, nk, 1024, stage, b_stage)
        rn = ResNorm(P, "r_", NT)
        mst = [P.sb(f"mst{i}", [128, nk, 128], F32) for i in range(2)]; b_mst = [P.buf() for _ in range(2)]
        mT = [P.sb(f"mT{i}", [128, nk, 128], BF16) for i in range(2)]; b_mT = [P.buf() for _ in range(2)]
        xt = [P.sb(f"xt{i}", [128, 1024], F32) for i in range(2)]; b_xt = [P.buf() for _ in range(2)]
        ot = [P.sb(f"ot{i}", [128, 1024], F32) for i in range(2)]; b_ot = [P.buf() for _ in range(2)]
        po = [P.ps(f"po{i}", [128, 1024], F32) for i in range(2)]; b_po = [P.buf() for _ in range(2)]
        mv = mixT.rearrange("(k p) t -> p k t", p=128)
        outs = []
        for t in range(NT):
            i = t % 2
            cls = 1 if t in ctx_tiles else 0
            P.dma("sp", mst[i][:], mv[:, :, t * 128:(t + 1) * 128], writes=[b_mst[i]])
            P.dma("sp", xt[i][:], h[t * 128:(t + 1) * 128, :], writes=[b_xt[i]])
            P.op("pool", lambda e, i=i: e.tensor_copy(out=mT[i][:], in_=mst[i][:]), reads=[b_mst[i]], writes=[b_mT[i]])
            for cb in range(2):
                for k in range(nk):
                    P.op("pe", lambda e, i=i, k=k, cb=cb: e.matmul(
                        po[i][:, cb * 512:(cb + 1) * 512], lhsT=mT[i][:, k, :], rhs=w_sb[:, k, cb * 512:(cb + 1) * 512],
                        start=(k == 0), stop=(k == nk - 1)), reads=[b_mT[i], b_w], writes=[b_po[i]])
            rn.run(po[i][:], b_po[i], t, xt[i][:], b_xt[i], GG[:, cls, :], b_gg, ot[i][:], b_ot[i])
            outs.append(P.dma("sp", out[t * 128:(t + 1) * 128, :], ot[i][:], reads=[b_ot[i]]))
        P.finish_wait("sp", outs)
        P.emit()
    return nc


def build_k3b(NT, ctx_tiles=(16,)):
    nc = new_nc()
    FH = 2816
    NJ = FH // 128
    h = nc.dram_tensor("h", [NT * 128, 1024], F32, kind="ExternalInput").ap()
    wg = nc.dram_tensor("wg", [1024, FH], F32, kind="ExternalInput").ap()
    wu = nc.dram_tensor("wu", [1024, FH], F32, kind="ExternalInput").ap()
    wd = nc.dram_tensor("wd", [FH, 1024], F32, kind="ExternalInput").ap()
    modT = nc.dram_tensor("modT", [128, 2, 2, 8], F32, kind="ExternalInput").ap()
    gT = nc.dram_tensor("gT", [128, 8], F32, kind="ExternalInput").ap()
    gR = nc.dram_tensor("gR", [128, 1024], F32, kind="ExternalInput").ap()
    gateR = nc.dram_tensor("gateR", [128, 2, 1024], F32, kind="ExternalInput").ap()
    identd = nc.dram_tensor("ident", [128, 128], BF16, kind="ExternalInput").ap()
    out = nc.dram_tensor("out", [NT * 128, 1024], F32, kind="ExternalOutput").ap()
    with ExitStack() as st:
        P = Prog(nc, st)
        ident = P.sb("ident", [128, 128], BF16); b_ident = P.buf()
        P.dma("sp", ident[:], identd, writes=[b_ident])
        modsb = P.sb("modsb", [128, 2, 2, 8], F32); b_mod = P.buf()
        P.dma("sp", modsb[:], modT, writes=[b_mod])
        gsb = P.sb("gsb", [128, 8], F32); b_g = P.buf()
        P.dma("sp", gsb[:], gT, writes=[b_g])
        Gs = P.sb("Gs", [128, 2, 8], F32); Sh = P.sb("Sh", [128, 2, 8], F32); b_gs = P.buf()
        for cls in range(2):
            P.op("dve", lambda e, cls=cls: e.scalar_tensor_tensor(out=Gs[:, cls, :], in0=modsb[:, cls, 1, :], scalar=1.0,
                                                                   in1=gsb[:], op0=ALU.add, op1=ALU.mult),
                 reads=[b_mod, b_g], writes=[b_gs])
            P.op("dve", lambda e, cls=cls: e.tensor_copy(out=Sh[:, cls, :], in_=modsb[:, cls, 0, :]),
                 reads=[b_mod], writes=[b_gs])
        g_sb = P.sb("g_sb", [128, 1024], F32); b_g3 = P.buf()
        P.dma("sp", g_sb[:], gR, writes=[b_g3])
        GG = P.sb("GG", [128, 2, 1024], F32); b_gg = P.buf()
        P.dma("sp", GG[:], gateR, writes=[b_gg])
        for cls in range(2):
            P.op("dve", lambda e, cls=cls: e.tensor_mul(out=GG[:, cls, :], in0=GG[:, cls, :], in1=g_sb[:]),
                 reads=[b_g3, b_gg], writes=[b_gg])
        wg_sb = P.sb("wg_sb", [128, 8, FH], BF16); b_wg = P.buf()
        wu_sb = P.sb("wu_sb", [128, 8, FH], BF16); b_wu = P.buf()
        wd_sb = P.sb("wd_sb", [128, NJ, 1024], BF16); b_wd = P.buf()
        stage = [P.sb(f"stage{i}", [128, FH], F32) for i in range(2)]; b_stage = [P.buf() for _ in range(2)]
        load_weight_bf16(P, wg, wg_sb, b_wg, 8, FH, stage, b_stage, cast_engs=("pool", "dve"))
        load_weight_bf16(P, wu, wu_sb, b_wu, 8, FH, stage, b_stage, cast_engs=("pool", "dve"))
        load_weight_bf16(P, wd, wd_sb, b_wd, NJ, 1024, stage, b_stage, cast_engs=("pool", "dve"))
        nt = NormT(P, "n_", ident, b_ident, NT)
        rn = ResNorm(P, "r_", NT)
        ST = 2
        xt = [stage[0][:, i * 1024:(i + 1) * 1024] for i in range(2)]; b_xt = [P.alias(b_stage[0]) for _ in range(2)]
        xr = [stage[1][:, i * 1024:(i + 1) * 1024] for i in range(2)]; b_xr = [P.alias(b_stage[1]) for _ in range(2)]
        aT = P.sb("aT", [128, 8, ST * 128], BF16); b_aT = P.buf()
        hidT = P.sb("hidT", [128, NJ, ST * 128], BF16); b_hid = P.buf()
        sg = [P.sb(f"sg{i}", [128, ST * 128], F32) for i in range(2)]; b_sg = [P.buf() for _ in range(2)]
        psg = [P.ps(f"psg{i}", [128, 512], F32) for i in range(2)]; b_psg = [P.buf() for _ in range(2)]
        psu = [P.ps(f"psu{i}", [128, 512], F32) for i in range(2)]; b_psu = [P.buf() for _ in range(2)]
        po = P.ps("po", [128, 1024], F32); b_po = P.buf()
        outs = []
        nx = 0
        nr = 0
        for s0 in range(0, NT, ST):
            tiles = list(range(s0, min(NT, s0 + ST)))
            N = len(tiles) * 128
            for ti, t in enumerate(tiles):
                i = nx % 2; nx += 1
                cls = 1 if t in ctx_tiles else 0
                P.dma("sp", xt[i], h[t * 128:(t + 1) * 128, :], writes=[b_xt[i]])
                nt.run(xt[i], b_xt[i], t, aT[:, :, ti * 128:(ti + 1) * 128], b_aT, Gs, Sh, b_gs, cls)
            for j in range(NJ):
                pi = j % 2
                for k in range(8):
                    P.op("pe", lambda e, pi=pi, j=j, k=k, N=N: e.matmul(
                        psg[pi][:, 0:N], lhsT=wg_sb[:, k, j * 128:(j + 1) * 128], rhs=aT[:, k, 0:N],
                        start=(k == 0), stop=(k == 7)), reads=[b_wg, b_aT], writes=[b_psg[pi]])
                for k in range(8):
                    P.op("pe", lambda e, pi=pi, j=j, k=k, N=N: e.matmul(
                        psu[pi][:, 0:N], lhsT=wu_sb[:, k, j * 128:(j + 1) * 128], rhs=aT[:, k, 0:N],
                        start=(k == 0), stop=(k == 7)), reads=[b_wu, b_aT], writes=[b_psu[pi]])
                P.op("act", lambda e, pi=pi, N=N: e.activation(out=sg[pi][:, 0:N], in_=psg[pi][:, 0:N], func=AF.Silu),
                     reads=[b_psg[pi]], writes=[b_sg[pi]])
                P.op("dve", lambda e, pi=pi, j=j, N=N: e.tensor_mul(out=hidT[:, j, 0:N], in0=sg[pi][:, 0:N], in1=psu[pi][:, 0:N]),
                     reads=[b_sg[pi], b_psu[pi]], writes=[b_hid])
            for ti, t in enumerate(tiles):
                i = nr % 2; nr += 1
                cls = 1 if t in ctx_tiles else 0
                P.dma("sp", xr[i], h[t * 128:(t + 1) * 128, :], writes=[b_xr[i]])
                for cb in range(2):
                    for j in range(NJ):
                        P.op("pe", lambda e, j=j, cb=cb, ti=ti: e.matmul(
                            po[:, cb * 512:(cb + 1) * 512], lhsT=hidT[:, j, ti * 128:(ti + 1) * 128],
                            rhs=wd_sb[:, j, cb * 512:(cb + 1) * 512], start=(j == 0), stop=(j == NJ - 1)),
                            reads=[b_hid, b_wd], writes=[b_po])
                o_t, b_o = rn.run(po[:], b_po, t, xr[i], b_xr[i], GG[:, cls, :], b_gg)
                outs.append(P.dma("sp", out[t * 128:(t + 1) * 128, :], o_t[:], reads=[b_o]))
        P.finish_wait("sp", outs)
        P.emit()
    return nc


def conv_fm(P, eng, vin, b_in, wsb, j, bias_ap, b_w, acc, b_acc, K, T, t0=0):
    P.op(eng, lambda e: e.tensor_scalar(out=acc, in0=vin[:, t0:t0 + T], scalar1=wsb[:, j, 0:1], scalar2=bias_ap,
                                        op0=ALU.mult, op1=ALU.add), reads=[b_in, b_w], writes=[b_acc])
    for k in range(1, K):
        P.op(eng, lambda e, k=k: e.scalar_tensor_tensor(out=acc, in0=vin[:, t0 + k:t0 + k + T], scalar=wsb[:, j, k:k + 1],
                                                        in1=acc, op0=ALU.mult, op1=ALU.add),
             reads=[b_in, b_w, b_acc], writes=[b_acc])


def build_k2a(TL=8192, TC=256, NCH=5, K=5):
    nc = new_nc()
    H = K - 1
    TT = TL + TC + 2 * H
    pre = nc.dram_tensor("pre", [NCH * 128, TT], F32, kind="ExternalInput").ap()
    cw = nc.dram_tensor("cw", [128, NCH, K], F32, kind="ExternalInput").ap()
    cb = nc.dram_tensor("cb", [128, NCH], F32, kind="ExternalInput").ap()
    out = nc.dram_tensor("out", [NCH * 128, TL + TC], F32, kind="ExternalOutput").ap()
    BL = 2048
    with ExitStack() as st:
        P = Prog(nc, st)
        wsb = P.sb("wsb", [128, NCH, K], F32); bsb = P.sb("bsb", [128, NCH], F32); b_w = P.buf()
        P.dma("sp", wsb[:], cw, writes=[b_w])
        P.dma("sp", bsb[:], cb, writes=[b_w])
        vin = [P.sb(f"vin{i}", [128, TT], F32) for i in range(2)]; b_vin = [P.buf() for _ in range(2)]
        acc = [P.sb(f"acc{i}", [128, BL], F32) for i in range(2)]; b_acc = [P.buf() for _ in range(2)]
        res = [P.sb(f"res{i}", [128, BL], F32) for i in range(2)]; b_res = [P.buf() for _ in range(2)]
        outs = []
        n = 0
        for j in range(NCH):
            vi = j % 2
            P.dma("sp", vin[vi][:], pre[j * 128:(j + 1) * 128, :], writes=[b_vin[vi]])
            blocks = [(t0, BL, t0) for t0 in range(0, TL, BL)] + [(TL + H, TC, TL)]
            for (i0, T, o0) in blocks:
                i = n % 2; n += 1
                eng = "dve" if i == 0 else "pool"
                conv_fm(P, "dve", vin[vi], b_vin[vi], wsb, j, bsb[:, j:j + 1], b_w, acc[i][:, 0:T], b_acc[i], K, T, t0=i0)
                P.op("act", lambda e, i=i, T=T: e.activation(out=res[i][:, 0:T], in_=acc[i][:, 0:T], func=AF.Silu),
                     reads=[b_acc[i]], writes=[b_res[i]])
                outs.append(P.dma("sp", out[j * 128:(j + 1) * 128, o0:o0 + T], res[i][:, 0:T], reads=[b_res[i]]))
        P.finish_wait("sp", outs)
        P.emit()
    return nc


def build_k5(T=2048, K=31):
    nc = new_nc()
    H = K - 1
    TT = T + H
    uT = nc.dram_tensor("uT", [1024, TT], F32, kind="ExternalInput").ap()
    cw = nc.dram_tensor("cw", [128, 4, K], F32, kind="ExternalInput").ap()
    cb = nc.dram_tensor("cb", [128, 4], F32, kind="ExternalInput").ap()
    lng = nc.dram_tensor("lng", [128, 4], F32, kind="ExternalInput").ap()
    lnb = nc.dram_tensor("lnb", [128, 4], F32, kind="ExternalInput").ap()
    out = nc.dram_tensor("out", [512, T], F32, kind="ExternalOutput").ap()
    with ExitStack() as st:
        P = Prog(nc, st)
        wsb = P.sb("wsb", [128, 4, K], F32); bsb = P.sb("bsb", [128, 4], F32); b_w = P.buf()
        gsb = P.sb("gsb", [128, 4], F32); lbsb = P.sb("lbsb", [128, 4], F32)
        P.dma("sp", wsb[:], cw, writes=[b_w]); P.dma("sp", bsb[:], cb, writes=[b_w])
        P.dma("sp", gsb[:], lng, writes=[b_w]); P.dma("sp", lbsb[:], lnb, writes=[b_w])
        ones = P.sb("ones", [128, 128], F32); b_ones = P.buf()
        P.op("pool", lambda e: e.memset(ones[:], 1.0 / 512), writes=[b_ones])
        a_sb = [P.sb(f"a{i}", [128, TT], F32) for i in range(2)]; b_a = [P.buf() for _ in range(2)]
        g_sb = [P.sb(f"g{i}", [128, TT], F32) for i in range(2)]; b_g = [P.buf() for _ in range(2)]
        cv = P.sb("cv", [128, 4, T], F32); b_cv = [P.buf() for _ in range(4)]
        for j in range(4):
            i = j % 2
            P.dma("sp", a_sb[i][:], uT[j * 128:(j + 1) * 128, :], writes=[b_a[i]])
            P.dma("sp", g_sb[i][:], uT[512 + j * 128:512 + (j + 1) * 128, :], writes=[b_g[i]])
            P.op("act", lambda e, i=i: e.activation(out=g_sb[i][:], in_=g_sb[i][:], func=AF.Sigmoid), reads=[b_g[i]], writes=[b_g[i]])
            eng = "dve" if i == 0 else "pool"
            P.op(eng, lambda e, i=i: e.tensor_mul(out=a_sb[i][:], in0=a_sb[i][:], in1=g_sb[i][:]), reads=[b_a[i], b_g[i]], writes=[b_a[i]])
            conv_fm(P, "dve", a_sb[i], b_a[i], wsb, j, bsb[:, j:j + 1], b_w, cv[:, j, :], b_cv[j], K, T)
        sq = P.sb("sq", [128, 4, 512], F32); b_sq = P.buf()
        pm = [P.ps(f"pm{i}", [128, 512], F32) for i in range(2)]; b_pm = [P.buf() for _ in range(2)]
        pq = [P.ps(f"pq{i}", [128, 512], F32) for i in range(2)]; b_pq = [P.buf() for _ in range(2)]
        rstd = P.sb("rstd", [128, 512], F32); b_rstd = P.buf()
        msq = P.sb("msq", [128, 512], F32); b_msq = P.buf()
        xc = [P.sb(f"xc{i}", [128, 512], F32) for i in range(2)]; b_xc = [P.buf() for _ in range(2)]
        outs = []
        n = 0
        for tb in range(T // 512):
            sl = slice(tb * 512, (tb + 1) * 512)
            pi = tb % 2
            P.op("act", lambda e, sl=sl: e.activation(out=sq[:], in_=cv[:, :, sl], func=AF.Square), reads=b_cv, writes=[b_sq])
            for j in range(4):
                P.op("pe", lambda e, j=j, sl=sl, pi=pi: e.matmul(pm[pi][:], lhsT=ones[:], rhs=cv[:, j, sl], start=(j == 0), stop=(j == 3)),
                     reads=[b_ones, b_cv[j]], writes=[b_pm[pi]])
            for j in range(4):
                P.op("pe", lambda e, j=j, pi=pi: e.matmul(pq[pi][:], lhsT=ones[:], rhs=sq[:, j, :], start=(j == 0), stop=(j == 3)),
                     reads=[b_ones, b_sq], writes=[b_pq[pi]])
            P.op("act", lambda e, pi=pi: e.activation(out=msq[:], in_=pm[pi][:], func=AF.Square), reads=[b_pm[pi]], writes=[b_msq])
            P.op("dve", lambda e, pi=pi: e.scalar_tensor_tensor(out=rstd[:], in0=pq[pi][:], scalar=EPS, in1=msq[:], op0=ALU.add, op1=ALU.subtract),
                 reads=[b_pq[pi], b_msq], writes=[b_rstd])
            P.op("act", lambda e: e.activation(out=rstd[:], in_=rstd[:], func=AF.Sqrt), reads=[b_rstd], writes=[b_rstd])
            P.op("dve", lambda e: e.reciprocal(out=rstd[:], in_=rstd[:]), reads=[b_rstd], writes=[b_rstd])
            for j in range(4):
                i = n % 2; n += 1
                P.op("dve", lambda e, i=i, j=j, sl=sl, pi=pi: e.tensor_sub(out=xc[i][:], in0=cv[:, j, sl], in1=pm[pi][:]),
                     reads=[b_cv[j], b_pm[pi]], writes=[b_xc[i]])
                P.op("pool", lambda e, i=i: e.tensor_mul(out=xc[i][:], in0=xc[i][:], in1=rstd[:]), reads=[b_xc[i], b_rstd], writes=[b_xc[i]])
                P.op("act", lambda e, i=i, j=j: e.activation(out=xc[i][:], in_=xc[i][:], func=AF.Silu, scale=gsb[:, j:j + 1], bias=lbsb[:, j:j + 1]),
                     reads=[b_xc[i], b_w], writes=[b_xc[i]])
                outs.append(P.dma("sp", out[j * 128:(j + 1) * 128, sl], xc[i][:], reads=[b_xc[i]]))
        P.finish_wait("sp", outs)
        P.emit()
    return nc


def build_k4(lambda_init, NU=2, TL=8192, TC=256, NQB=None):
    nc = new_nc()
    NKT = (TL + TC) // 128
    NCT = TC // 128
    NLT = TL // 128
    if NQB is None:
        NQB = TL // 512
    qk = nc.dram_tensor("qk", [NU, 2, TL, 128], F32, kind="ExternalInput").ap()
    v = nc.dram_tensor("v", [NU, TL, 128], F32, kind="ExternalInput").ap()
    kc = nc.dram_tensor("kc", [NU, TC, 128], F32, kind="ExternalInput").ap()
    vc = nc.dram_tensor("vc", [NU, TC, 128], F32, kind="ExternalInput").ap()
    csd = nc.dram_tensor("cs", [TL, 128], F32, kind="ExternalInput").ap()
    snd = nc.dram_tensor("sn", [TL, 128], F32, kind="ExternalInput").ap()
    lamRd = nc.dram_tensor("lamR", [128, 4, 64], F32, kind="ExternalInput").ap()
    subRd = nc.dram_tensor("subR", [128, 128], F32, kind="ExternalInput").ap()
    identd = nc.dram_tensor("ident", [128, 128], BF16, kind="ExternalInput").ap()
    out = nc.dram_tensor("out", [NU, TL, 128], F32, kind="ExternalOutput").ap()
    with ExitStack() as st:
        P = Prog(nc, st)
        ident = P.sb("ident", [128, 128], BF16); b_ident = P.buf()
        P.dma("sp", ident[:], identd, writes=[b_ident])
        lam = P.sb("lam", [128, 4, 64], F32); b_lam = P.buf()
        P.dma("sp", lam[:], lamRd, writes=[b_lam])
        GS = P.sb("GS", [128, 128], F32); b_gs = P.buf()
        P.dma("sp", GS[:], subRd, writes=[b_gs])
        P.op("dve", lambda e: e.tensor_scalar_mul(out=GS[:], in0=GS[:], scalar1=float(1.0 - lambda_init)), reads=[b_gs], writes=[b_gs])
        lp = P.sb("lp", [128, 2, 64], F32); ls = P.sb("ls", [128, 4], F32); b_ls = P.buf()
        P.op("dve", lambda e: e.tensor_mul(out=lp[:, 0, :], in0=lam[:, 0, :], in1=lam[:, 1, :]), reads=[b_lam], writes=[b_ls])
        P.op("dve", lambda e: e.tensor_mul(out=lp[:, 1, :], in0=lam[:, 2, :], in1=lam[:, 3, :]), reads=[b_lam, b_ls], writes=[b_ls])
        P.op("dve", lambda e: e.reduce_sum(out=ls[:, 0:2], in_=lp[:], axis=AX.X), reads=[b_ls], writes=[b_ls])
        P.op("act", lambda e: e.activation(out=ls[:, 0:2], in_=ls[:, 0:2], func=AF.Exp), reads=[b_ls], writes=[b_ls])
        P.op("dve", lambda e: e.tensor_sub(out=ls[:, 2:3], in0=ls[:, 1:2], in1=ls[:, 0:1]), reads=[b_ls], writes=[b_ls])
        P.op("dve", lambda e: e.tensor_scalar_add(out=ls[:, 3:4], in0=ls[:, 2:3], scalar1=float(-lambda_init)), reads=[b_ls], writes=[b_ls])
        neglam = ls[:, 3:4]

        kT = P.sb("kT", [128, TL + TC], BF16); b_kT = P.buf()
        qT = P.sb("qT", [128, TL], BF16); b_qT = P.buf()
        vaug = P.sb("vaug", [128, NKT, 129], BF16); b_va = P.buf()
        P.op("pool", lambda e: e.memset(vaug[:, :, 128:129], 1.0), writes=[b_va])
        ld = {n: [P.sb(f"ld_{n}{i}", [128, 128], F32) for i in range(2)] for n in ("q", "k", "v", "cs", "sn")}
        b_ld = {n: [P.buf() for _ in range(2)] for n in ld}
        t1 = {n: [P.sb(f"t1_{n}{i}", [128, 128], F32) for i in range(2)] for n in ("q", "k")}
        t2 = {n: [P.sb(f"t2_{n}{i}", [128, 128], F32) for i in range(2)] for n in ("q", "k")}
        b_t1 = {n: [P.buf() for _ in range(2)] for n in t1}
        b_t2 = {n: [P.buf() for _ in range(2)] for n in t1}
        rb = {n: [P.sb(f"rb_{n}{i}", [128, 128], BF16) for i in range(2)] for n in ("q", "k")}
        b_rb = {n: [P.buf() for _ in range(2)] for n in rb}
        psT = [P.ps(f"psT{i}", [128, 128], BF16) for i in range(2)]; b_psT = [P.buf() for _ in range(2)]
        ps_s = [P.ps(f"ps_s{i}", [128, 512], F32) for i in range(2)]; b_ps_s = [P.buf() for _ in range(2)]
        acc = [P.ps(f"acc{i}", [128, 129], F32) for i in range(4)]; b_acc = [P.buf() for _ in range(4)]
        E = [P.sb(f"E{i}", [128, 512], BF16) for i in range(2)]; b_E = [P.buf() for _ in range(2)]
        att0 = [P.sb(f"att0_{i}", [128, 128], F32) for i in range(4)]; b_att0 = [P.buf() for _ in range(4)]
        att1 = [P.sb(f"att1_{i}", [128, 128], F32) for i in range(2)]; b_att1 = [P.buf() for _ in range(2)]
        rec = P.sb("rec", [128, 8], F32); b_rec = [P.buf() for _ in range(8)]
        junk = P.sb("junk", [128, 128], F32); b_junk = P.buf()
        sst = P.sb("sst", [128, 2], F32); b_sst = [P.buf() for _ in range(2)]
        v5 = lambda ap: ap.rearrange("p (c a h f) -> p c a h f", c=2, a=2, h=2, f=16)
        outs = []
        npt = [0]

        def transpose_to(src_bf, b_src, dst_ap, b_dst):
            pi = npt[0] % 2; npt[0] += 1
            P.op("pe", lambda e: e.transpose(out=psT[pi][:], in_=src_bf, identity=ident[:]), reads=[b_src, b_ident], writes=[b_psT[pi]])
            P.op("act", lambda e: e.copy(out=dst_ap, in_=psT[pi][:]), reads=[b_psT[pi]], writes=[b_dst])

        for u in range(NU):
            for j in range(NCT):
                i = j % 2
                P.dma("sp", ld["k"][i][:], kc[u, j * 128:(j + 1) * 128, :], writes=[b_ld["k"][i]])
                P.dma("sp", ld["v"][i][:], vc[u, j * 128:(j + 1) * 128, :], writes=[b_ld["v"][i]])
                P.op("dve", lambda e, i=i: e.tensor_copy(out=rb["k"][i][:], in_=ld["k"][i][:]), reads=[b_ld["k"][i]], writes=[b_rb["k"][i]])
                transpose_to(rb["k"][i][:], b_rb["k"][i], kT[:, j * 128:(j + 1) * 128], b_kT)
                P.op("pool", lambda e, i=i, j=j: e.tensor_copy(out=vaug[:, j, 0:128], in_=ld["v"][i][:]), reads=[b_ld["v"][i]], writes=[b_va])
            for j in range(NLT):
                i = j % 2
                sl = slice(j * 128, (j + 1) * 128)
                P.dma("sp", ld["q"][i][:], qk[u, 0, sl, :], writes=[b_ld["q"][i]])
                P.dma("sp", ld["k"][i][:], qk[u, 1, sl, :], writes=[b_ld["k"][i]])
                P.dma("sp", ld["v"][i][:], v[u, sl, :], writes=[b_ld["v"][i]])
                P.dma("sp", ld["cs"][i][:], csd[sl, :], writes=[b_ld["cs"][i]])
                P.dma("sp", ld["sn"][i][:], snd[sl, :], writes=[b_ld["sn"][i]])
                for n in ("q", "k"):
                    x = ld[n][i]; a1 = t1[n][i]; a2 = t2[n][i]
                    P.op("dve", lambda e, x=x, a1=a1, i=i: e.tensor_mul(out=a1[:], in0=x[:], in1=ld["cs"][i][:]),
                         reads=[b_ld[n][i], b_ld["cs"][i]], writes=[b_t1[n][i]])
                    P.op("pool", lambda e, x=x, a2=a2, i=i: e.tensor_mul(out=v5(a2[:])[:, :, :, 0, :], in0=v5(x[:])[:, :, :, 1, :],
                                                                         in1=v5(ld["sn"][i][:])[:, :, :, 0, :]),
                         reads=[b_ld[n][i], b_ld["sn"][i]], writes=[b_t2[n][i]])
                    P.op("pool", lambda e, x=x, a2=a2, i=i: e.tensor_mul(out=v5(a2[:])[:, :, :, 1, :], in0=v5(x[:])[:, :, :, 0, :],
                                                                         in1=v5(ld["sn"][i][:])[:, :, :, 1, :]),
                         reads=[b_ld[n][i], b_ld["sn"][i]], writes=[b_t2[n][i]])
                    P.op("dve", lambda e, a1=a1, a2=a2, n=n, i=i: e.tensor_add(out=rb[n][i][:], in0=a1[:], in1=a2[:]),
                         reads=[b_t1[n][i], b_t2[n][i]], writes=[b_rb[n][i]])
                transpose_to(rb["q"][i][:], b_rb["q"][i], qT[:, sl], b_qT)
                transpose_to(rb["k"][i][:], b_rb["k"][i], kT[:, TC + j * 128:TC + (j + 1) * 128], b_kT)
                P.op("pool", lambda e, i=i, j=j: e.tensor_copy(out=vaug[:, NCT + j, 0:128], in_=ld["v"][i][:]), reads=[b_ld["v"][i]], writes=[b_va])
            for qb in range(NQB):
                qsl = slice(qb * 512, (qb + 1) * 512)
                for c in range(2):
                    cp = slice(c * 64, (c + 1) * 64)

                    def score(kt, cp=cp, qsl=qsl):
                        pi = kt % 2
                        P.op("pe", lambda e, kt=kt, pi=pi, cp=cp, qsl=qsl: e.matmul(ps_s[pi][:], lhsT=kT[cp, kt * 128:(kt + 1) * 128], rhs=qT[cp, qsl],
                                                                     start=True, stop=True), reads=[b_kT, b_qT], writes=[b_ps_s[pi]])
                    score(0)
                    for kt in range(NKT):
                        pi = kt % 2
                        if kt + 1 < NKT:
                            score(kt + 1)
                        P.op("act", lambda e, pi=pi: e.activation(out=E[pi][:], in_=ps_s[pi][:], func=AF.Exp, scale=0.125),
                             reads=[b_ps_s[pi]], writes=[b_E[pi]])
                        for qs in range(4):
                            P.op("pe", lambda e, pi=pi, qs=qs, kt=kt: e.matmul(acc[qs][:], lhsT=E[pi][:, qs * 128:(qs + 1) * 128], rhs=vaug[:, kt, :],
                                                                              start=(kt == 0), stop=(kt == NKT - 1)),
                                 reads=[b_E[pi], b_va], writes=[b_acc[qs]])
                    for qs in range(4):
                        r = c * 4 + qs
                        P.op("dve", lambda e, qs=qs, r=r: e.reciprocal(out=rec[:, r:r + 1], in_=acc[qs][:, 128:129]), reads=[b_acc[qs]], writes=[b_rec[r]])
                        if c == 0:
                            P.op("dve", lambda e, qs=qs, r=r: e.tensor_scalar_mul(out=att0[qs][:], in0=acc[qs][:, 0:128], scalar1=rec[:, r:r + 1]),
                                 reads=[b_acc[qs], b_rec[r]], writes=[b_att0[qs]])
                        else:
                            ai = qs % 2
                            P.op("dve", lambda e, qs=qs, r=r, ai=ai: e.tensor_scalar_mul(out=att1[ai][:], in0=acc[qs][:, 0:128], scalar1=rec[:, r:r + 1]),
                                 reads=[b_acc[qs], b_rec[r]], writes=[b_att1[ai]])
                            P.op("dve", lambda e, qs=qs, ai=ai: e.scalar_tensor_tensor(out=att1[ai][:], in0=att1[ai][:], scalar=neglam, in1=att0[qs][:],
                                                                                       op0=ALU.mult, op1=ALU.add),
                                 reads=[b_att1[ai], b_att0[qs], b_ls], writes=[b_att1[ai]])
                            P.op("pool", lambda e, ai=ai: e.memset(sst[:, ai:ai + 1], 0.0), writes=[b_sst[ai]])
                            P.op("act", lambda e, ai=ai: e.activation(out=junk[:], in_=att1[ai][:], func=AF.Square, accum_out=sst[:, ai:ai + 1]),
                                 reads=[b_att1[ai], b_sst[ai]], writes=[b_junk, b_sst[ai]])
                            P.op("dve", lambda e, ai=ai: e.tensor_scalar(out=sst[:, ai:ai + 1], in0=sst[:, ai:ai + 1], scalar1=1.0 / 128, scalar2=EPS,
                                                                         op0=ALU.mult, op1=ALU.add), reads=[b_sst[ai]], writes=[b_sst[ai]])
                            P.op("act", lambda e, ai=ai: e.activation(out=sst[:, ai:ai + 1], in_=sst[:, ai:ai + 1], func=AF.Sqrt), reads=[b_sst[ai]], writes=[b_sst[ai]])
                            P.op("dve", lambda e, ai=ai: e.reciprocal(out=sst[:, ai:ai + 1], in_=sst[:, ai:ai + 1]), reads=[b_sst[ai]], writes=[b_sst[ai]])
                            P.op("dve", lambda e, ai=ai: e.scalar_tensor_tensor(out=att1[ai][:], in0=att1[ai][:], scalar=sst[:, ai:ai + 1], in1=GS[:],
                                                                                op0=ALU.mult, op1=ALU.mult),
                                 reads=[b_att1[ai], b_sst[ai], b_gs], writes=[b_att1[ai]])
                            r0 = qb * 512 + qs * 128
                            outs.append(P.dma("sp", out[u, r0:r0 + 128, :], att1[ai][:], reads=[b_att1[ai]]))
        P.finish_wait("sp", outs)
        P.emit()
    return nc


def build_kf(TL=8192, TC=256):
    nc = new_nc()
    NL = TL // 128
    NCt = TC // 128
    GT = nc.dram_tensor("GT", [128, TL + TC], F32, kind="ExternalInput").ap()
    ccsc = nc.dram_tensor("ccsc", [128, 256], F32, kind="ExternalInput").ap()
    CTd = nc.dram_tensor("CT", [TL, TL], BF16, kind="ExternalInput").ap()
    STd = nc.dram_tensor("ST", [TL, TL], BF16, kind="ExternalInput").ap()
    CTc = nc.dram_tensor("CTc", [TC, TC], BF16, kind="ExternalInput").ap()
    STc = nc.dram_tensor("STc", [TC, TC], BF16, kind="ExternalInput").ap()
    out = nc.dram_tensor("out", [128, TL + TC], F32, kind="ExternalOutput").ap()
    with ExitStack() as st:
        P = Prog(nc, st)
        g32 = P.sb("g32", [128, TL + TC], F32); b_g32 = P.buf()
        P.dma("sp", g32[:], GT, writes=[b_g32])
        gbf = P.sb("gbf", [128, TL + TC], BF16); b_gbf = P.buf()
        P.op("dve", lambda e: e.tensor_copy(out=gbf[:], in_=g32[:]), reads=[b_g32], writes=[b_gbf])
        cc32 = P.sb("cc32", [128, 256], F32); ccb = P.sb("ccb", [128, 256], BF16); b_cc = P.buf()
        P.dma("sp", cc32[:], ccsc, writes=[b_cc])
        P.op("dve", lambda e: e.tensor_copy(out=ccb[:], in_=cc32[:]), reads=[b_cc], writes=[b_cc])
        H = P.sb("H", [128, NL + NCt, 256], BF16); b_H = P.buf()
        ph = [P.ps(f"ph{i}", [128, 256], F32) for i in range(2)]; b_ph = [P.buf() for _ in range(2)]
        for j in range(NL + NCt):
            pi = j % 2
            P.op("pe", lambda e, j=j, pi=pi: e.matmul(ph[pi][:], lhsT=gbf[:, j * 128:(j + 1) * 128], rhs=ccb[:], start=True, stop=True),
                 reads=[b_gbf, b_cc], writes=[b_ph[pi]])
            if pi == 0:
                P.op("act", lambda e, j=j, pi=pi: e.copy(out=H[:, j, :], in_=ph[pi][:]), reads=[b_ph[pi]], writes=[b_H])
            else:
                P.op("dve", lambda e, j=j, pi=pi: e.tensor_copy(out=H[:, j, :], in_=ph[pi][:]), reads=[b_ph[pi]], writes=[b_H])
        JB = 16
        cbuf = [P.sb(f"cbuf{i}", [128, JB, 512], BF16) for i in range(2)]; b_cbuf = [P.buf() for _ in range(2)]
        sbuf_ = [P.sb(f"sbuf{i}", [128, JB, 512], BF16) for i in range(2)]; b_sbuf = [P.buf() for _ in range(2)]
        po = [P.ps(f"po{i}", [128, 512], F32) for i in range(2)]; b_po = [P.buf() for _ in range(2)]
        ot = [P.sb(f"ot{i}", [128, 512], F32) for i in range(2)]; b_ot = [P.buf() for _ in range(2)]
        CTv = CTd.rearrange("(j p) f -> p j f", p=128)
        STv = STd.rearrange("(j p) f -> p j f", p=128)
        outs = []
        nb = 0
        for fb in range(TL // 512):
            fsl = slice(fb * 512, (fb + 1) * 512)
            pi = fb % 2
            for j0 in range(0, NL, JB):
                bi = nb % 2; nb += 1
                P.dma("sp", cbuf[bi][:], CTv[:, j0:j0 + JB, fsl], writes=[b_cbuf[bi]])
                P.dma("pool", sbuf_[bi][:], STv[:, j0:j0 + JB, fsl], writes=[b_sbuf[bi]])
                for jj in range(JB):
                    j = j0 + jj
                    P.op("pe", lambda e, j=j, jj=jj, bi=bi, pi=pi: e.matmul(po[pi][:], lhsT=H[:, j, 0:128], rhs=cbuf[bi][:, jj, :],
                                                                         start=(j == 0), stop=False), reads=[b_H, b_cbuf[bi]], writes=[b_po[pi]])
                    P.op("pe", lambda e, j=j, jj=jj, bi=bi, pi=pi: e.matmul(po[pi][:], lhsT=H[:, j, 128:256], rhs=sbuf_[bi][:, jj, :],
                                                                         start=False, stop=(j == NL - 1)), reads=[b_H, b_sbuf[bi]], writes=[b_po[pi]])
            P.op("act", lambda e, pi=pi: e.copy(out=ot[pi][:], in_=po[pi][:]), reads=[b_po[pi]], writes=[b_ot[pi]])
            outs.append(P.dma("sp", out[:, fsl], ot[pi][:], reads=[b_ot[pi]]))
        cc_ = P.sb("cc_", [128, NCt, TC], BF16); sc_ = P.sb("sc_", [128, NCt, TC], BF16); b_c2 = P.buf()
        P.dma("sp", cc_[:], CTc.rearrange("(j p) f -> p j f", p=128), writes=[b_c2])
        P.dma("sp", sc_[:], STc.rearrange("(j p) f -> p j f", p=128), writes=[b_c2])
        pi = 0
        for j in range(NCt):
            P.op("pe", lambda e, j=j: e.matmul(po[pi][:, 0:TC], lhsT=H[:, NL + j, 0:128], rhs=cc_[:, j, :], start=(j == 0), stop=False),
                 reads=[b_H, b_c2], writes=[b_po[pi]])
            P.op("pe", lambda e, j=j: e.matmul(po[pi][:, 0:TC], lhsT=H[:, NL + j, 128:256], rhs=sc_[:, j, :], start=False, stop=(j == NCt - 1)),
                 reads=[b_H, b_c2], writes=[b_po[pi]])
        P.op("act", lambda e: e.copy(out=ot[pi][:, 0:TC], in_=po[pi][:, 0:TC]), reads=[b_po[pi]], writes=[b_ot[pi]])
        outs.append(P.dma("sp", out[:, TL:TL + TC], ot[pi][:, 0:TC], reads=[b_ot[pi]]))
        P.finish_wait("sp", outs)
        P.emit()
    return nc


def build_ssd(NCH=66, NCTX=2):
    nc = new_nc()
    T = NCH * 128
    NC6 = NCH * 6
    xs = nc.dram_tensor("xs", [T, 384], F32, kind="ExternalInput").ap()
    Btm = nc.dram_tensor("Btm", [T, 128], F32, kind="ExternalInput").ap()
    BfT = nc.dram_tensor("BfT", [128, T], F32, kind="ExternalInput").ap()
    CfT = nc.dram_tensor("CfT", [128, T], F32, kind="ExternalInput").ap()
    dtd = nc.dram_tensor("dt", [128, 2, NCH, 6], F32, kind="ExternalInput").ap()
    prm = nc.dram_tensor("prm", [128, 3, 2, NCH, 6], F32, kind="ExternalInput").ap()
    trid = nc.dram_tensor("tri", [128, 3, 128], F32, kind="ExternalInput").ap()
    y = nc.dram_tensor("y", [2, T, 384], F32, kind="ExternalOutput").ap()
    with ExitStack() as st:
        P = Prog(nc, st)
        tri = P.sb("tri", [128, 3, 128], F32); b_tri = P.buf()
        P.dma("sp", tri[:], trid, writes=[b_tri])
        prs = P.sb("prs", [128, 3, 2, NCH, 6], F32); b_prs = P.buf()
        P.dma("sp", prs[:], prm, writes=[b_prs])
        dts = P.sb("dts", [128, 2, NCH, 6], F32); b_dts = P.buf()
        P.dma("sp", dts[:], dtd, writes=[b_dts])
        stg = P.sb("stg", [128, T], F32); b_stg = P.buf()
        Bf = P.sb("Bf", [128, T], BF16); Cf = P.sb("Cf", [128, T], BF16); Bt = P.sb("Bt", [128, NCH, 128], BF16)
        b_Bf, b_Cf, b_Bt = P.buf(), P.buf(), P.buf()
        P.dma("sp", stg[:], BfT, writes=[b_stg])
        P.op("dve", lambda e: e.tensor_copy(out=Bf[:], in_=stg[:]), reads=[b_stg], writes=[b_Bf])
        P.dma("sp", stg[:], CfT, writes=[b_stg])
        P.op("pool", lambda e: e.tensor_copy(out=Cf[:], in_=stg[:]), reads=[b_stg], writes=[b_Cf])
        P.dma("sp", stg[:].rearrange("p (c n) -> p c n", n=128), Btm.rearrange("(c p) n -> p c n", p=128), writes=[b_stg])
        P.op("dve", lambda e: e.tensor_copy(out=Bt[:], in_=stg[:].rearrange("p (c n) -> p c n", n=128)), reads=[b_stg], writes=[b_Bt])
        dtsp = P.sb("dtsp", [128, 2, NCH, 6], F32); dta = P.sb("dta", [128, 2, NCH, 6], F32); b_dt = P.buf()
        P.op("dve", lambda e: e.tensor_add(out=dtsp[:], in0=dts[:], in1=prs[:, 0]), reads=[b_dts, b_prs], writes=[b_dt])
        P.op("act", lambda e: e.activation(out=dtsp[:], in_=dtsp[:], func=AF.Exp), reads=[b_dt], writes=[b_dt])
        P.op("dve", lambda e: e.tensor_scalar_add(out=dtsp[:], in0=dtsp[:], scalar1=1.0), reads=[b_dt], writes=[b_dt])
        P.op("act", lambda e: e.activation(out=dtsp[:], in_=dtsp[:], func=AF.Ln), reads=[b_dt], writes=[b_dt])
        P.op("act", lambda e: e.activation(out=dta[:], in_=prs[:, 1], func=AF.Exp), reads=[b_prs, b_dt], writes=[b_dt])
        P.op("dve", lambda e: e.scalar_tensor_tensor(out=dta[:], in0=dta[:], scalar=-1.0, in1=dtsp[:], op0=ALU.mult, op1=ALU.mult),
             reads=[b_dt], writes=[b_dt])
        acs = P.sb("acs", [128, 2, NCH, 6], F32); eacs = P.sb("eacs", [128, 2, NCH, 6], F32)
        tend = P.sb("tend", [128, 2, NCH, 6], F32); cdec = P.sb("cdec", [128, 2, NCH, 6], F32); b_ac = P.buf()
        pb = [P.ps(f"pb{i}", [128, NC6], F32) for i in range(2)]; b_pb = [P.buf() for _ in range(2)]
        for d in range(2):
            P.op("pe", lambda e, d=d: e.matmul(pb[0][:], lhsT=tri[:, d, :], rhs=dta[:, d].rearrange("p c h -> p (c h)"), start=True, stop=True),
                 reads=[b_tri, b_dt], writes=[b_pb[0]])
            P.op("pe", lambda e, d=d: e.matmul(pb[1][:], lhsT=tri[:, 2, :], rhs=dta[:, d].rearrange("p c h -> p (c h)"), start=True, stop=True),
                 reads=[b_tri, b_dt], writes=[b_pb[1]])
            av = lambda t, d=d: t[:, d].rearrange("p c h -> p (c h)")
            P.op("dve", lambda e, d=d, av=av: e.tensor_copy(out=av(acs), in_=pb[0][:]), reads=[b_pb[0]], writes=[b_ac])
            P.op("act", lambda e, d=d, av=av: e.activation(out=av(eacs), in_=pb[0][:], func=AF.Exp), reads=[b_pb[0]], writes=[b_ac])
            P.op("act", lambda e, d=d, av=av: e.activation(out=av(cdec), in_=pb[1][:], func=AF.Exp), reads=[b_pb[1]], writes=[b_ac])
            P.op("dve", lambda e, d=d, av=av: e.tensor_sub(out=av(tend), in0=pb[1][:], in1=av(acs)), reads=[b_pb[1], b_ac], writes=[b_ac])
            P.op("act", lambda e, d=d, av=av: e.activation(out=av(tend), in_=av(tend), func=AF.Exp), reads=[b_ac], writes=[b_ac])
        xt = [P.sb(f"xt{i}", [128, 6, 64], F32) for i in range(2)]; b_xt = [P.buf() for _ in range(2)]
        Dm = P.sb("Dm", [128, 6, 128], F32); b_Dm = P.buf()
        pR = [P.ps(f"pR{i}", [128, 3, 128], F32) for i in range(2)]; b_pR = [P.buf() for _ in range(2)]
        pcb = P.ps("pcb", [128, 128], F32); b_pcb = P.buf()
        cbU = P.sb("cbU", [128, 128], F32); b_cbU = P.buf()
        arg = [P.sb(f"arg{i}", [128, 128], F32) for i in range(2)]; b_arg = [P.buf() for _ in range(2)]
        Wh = P.sb("Wh", [128, 6, 128], BF16); b_Wh = P.buf()
        xdt = P.sb("xdt", [128, 6, 64], BF16); b_xdt = P.buf()
        xdtE = P.sb("xdtE", [128, 6, 64], BF16); b_xdtE = P.buf()
        py = P.ps("py", [128, 6, 64], F32); b_py = P.buf()
        pyo = P.ps("pyo", [128, 6, 64], F32); b_pyo = P.buf()
        pst = P.ps("pst", [128, 6, 64], F32); b_pst = P.buf()
        t1 = P.sb("t1", [128, 6, 64], F32); b_t1 = P.buf()
        yo = [P.sb(f"yo{i}", [128, 6, 64], F32) for i in range(2)]; b_yo = [P.buf() for _ in range(2)]
        S = P.sb("S", [128, 6, 64], F32); Sb = P.sb("Sb", [128, 6, 64], BF16); b_S = P.buf(); b_Sb = P.buf()
        outs = []
        n = 0
        for d in range(2):
            ctx_order = list(range(NCTX)) if d == 0 else list(range(NCTX - 1, -1, -1))
            lat_order = list(range(NCTX, NCH)) if d == 0 else list(range(NCH - 1, NCTX - 1, -1))
            P.op("pool", lambda e: e.memset(S[:], 0.0), writes=[b_S])
            P.op("pool", lambda e: e.memset(Sb[:], 0.0), writes=[b_Sb])
            for c in ctx_order + lat_order:
                i = n % 2; n += 1
                csl = slice(c * 128, (c + 1) * 128)
                P.dma("sp", xt[i][:].rearrange("p h e -> p (h e)"), xs[csl, :], writes=[b_xt[i]])
                for h in range(6):
                    eng = "dve" if h % 2 == 0 else "pool"
                    P.op(eng, lambda e, h=h, c=c, d=d: e.tensor_scalar_mul(out=Dm[:, h, :], in0=tri[:, d, :], scalar1=dta[:, d, c, h:h + 1]),
                         reads=[b_tri, b_dt], writes=[b_Dm])
                for hh in range(2):
                    P.op("pe", lambda e, hh=hh: e.matmul(pR[hh][:].rearrange("p h l -> p (h l)"), lhsT=tri[:, 2, :],
                                                         rhs=Dm[:, hh * 3:(hh + 1) * 3, :].rearrange("p h l -> p (h l)"), start=True, stop=True),
                         reads=[b_tri, b_Dm], writes=[b_pR[hh]])
                P.op("pe", lambda e, csl=csl: e.matmul(pcb[:], lhsT=Bf[:, csl], rhs=Cf[:, csl], start=True, stop=True),
                     reads=[b_Bf, b_Cf], writes=[b_pcb])
                P.op("dve", lambda e, d=d: e.tensor_mul(out=cbU[:], in0=pcb[:], in1=tri[:, d, :]), reads=[b_pcb, b_tri], writes=[b_cbU])
                for h in range(6):
                    ai = h % 2
                    P.op("dve", lambda e, h=h, c=c, d=d, ai=ai: e.tensor_scalar(out=arg[ai][:], in0=pR[h // 3][:, h % 3, :], scalar1=acs[:, d, c, h:h + 1],
                                                                                scalar2=0.0, op0=ALU.subtract, op1=ALU.min),
                         reads=[b_pR[h // 3], b_ac], writes=[b_arg[ai]])
                    P.op("act", lambda e, ai=ai: e.activation(out=arg[ai][:], in_=arg[ai][:], func=AF.Exp), reads=[b_arg[ai]], writes=[b_arg[ai]])
                    P.op("pool", lambda e, h=h, ai=ai: e.tensor_mul(out=Wh[:, h, :], in0=arg[ai][:], in1=cbU[:]), reads=[b_arg[ai], b_cbU], writes=[b_Wh])
                    P.op("pool", lambda e, h=h, c=c, d=d, i=i: e.tensor_scalar_mul(out=xdt[:, h, :], in0=xt[i][:, h, :], scalar1=dtsp[:, d, c, h:h + 1]),
                         reads=[b_xt[i], b_dt], writes=[b_xdt])
                    P.op("pool", lambda e, h=h, c=c, d=d: e.tensor_scalar_mul(out=xdtE[:, h, :], in0=xdt[:, h, :], scalar1=tend[:, d, c, h:h + 1]),
                         reads=[b_xdt, b_ac], writes=[b_xdtE])
                for h in range(6):
                    P.op("pe", lambda e, h=h: e.matmul(py[:, h, :], lhsT=Wh[:, h, :], rhs=xdt[:, h, :], start=True, stop=True),
                         reads=[b_Wh, b_xdt], writes=[b_py])
                P.op("pe", lambda e, csl=csl: e.matmul(pyo[:].rearrange("p h e -> p (h e)"), lhsT=Cf[:, csl], rhs=Sb[:].rearrange("p h e -> p (h e)"),
                                                       start=True, stop=True), reads=[b_Cf, b_Sb], writes=[b_pyo])
                for h in range(6):
                    P.op("dve", lambda e, h=h, c=c, d=d: e.tensor_scalar_mul(out=t1[:, h, :], in0=pyo[:, h, :], scalar1=eacs[:, d, c, h:h + 1]),
                         reads=[b_pyo, b_ac], writes=[b_t1])
                    P.op("dve", lambda e, h=h, c=c, d=d, i=i: e.scalar_tensor_tensor(out=t1[:, h, :], in0=xt[i][:, h, :], scalar=prs[:, 2, d, c, h:h + 1],
                                                                                     in1=t1[:, h, :], op0=ALU.mult, op1=ALU.add),
                         reads=[b_xt[i], b_prs, b_t1], writes=[b_t1])
                P.op("dve", lambda e, i=i: e.tensor_add(out=yo[i][:], in0=t1[:], in1=py[:]), reads=[b_t1, b_py], writes=[b_yo[i]])
                outs.append(P.dma("sp", y[d, csl, :], yo[i][:].rearrange("p h e -> p (h e)"), reads=[b_yo[i]]))
                P.op("pe", lambda e, c=c: e.matmul(pst[:].rearrange("p h e -> p (h e)"), lhsT=Bt[:, c, :], rhs=xdtE[:].rearrange("p h e -> p (h e)"),
                                                   start=True, stop=True), reads=[b_Bt, b_xdtE], writes=[b_pst])
                for h in range(6):
                    P.op("dve", lambda e, h=h, c=c, d=d: e.scalar_tensor_tensor(out=S[:, h, :], in0=S[:, h, :], scalar=cdec[:, d, c, h:h + 1],
                                                                                in1=pst[:, h, :], op0=ALU.mult, op1=ALU.add),
                         reads=[b_S, b_ac, b_pst], writes=[b_S])
                P.op("act", lambda e: e.copy(out=Sb[:], in_=S[:]), reads=[b_S], writes=[b_Sb])
        P.finish_wait("sp", outs)
        P.emit()
    return nc


def build_k2c(NT):
    nc = new_nc()
    W = 1536
    y2 = nc.dram_tensor("y2", [2, NT * 128, W], F32, kind="ExternalInput").ap()
    z = nc.dram_tensor("z", [NT * 128, W], F32, kind="ExternalInput").ap()
    gnR = nc.dram_tensor("gnR", [128, W], F32, kind="ExternalInput").ap()
    out = nc.dram_tensor("out", [NT * 128, W], F32, kind="ExternalOutput").ap()
    with ExitStack() as st:
        P = Prog(nc, st)
        gn = P.sb("gn", [128, W], F32); b_gn = P.buf()
        P.dma("sp", gn[:], gnR, writes=[b_gn])
        ss = P.sb("ss", [128, NT, 4], F32); b_ss0 = P.buf(); b_ss = [P.buf() for _ in range(NT)]
        P.op("pool", lambda e: e.memset(ss[:], 0.0), writes=[b_ss0])
        junk = P.sb("junk", [128, 384], F32); b_junk = P.buf()
        y0 = [P.sb(f"y0_{i}", [128, W], F32) for i in range(2)]; b_y0 = [P.buf() for _ in range(2)]
        y1 = [P.sb(f"y1_{i}", [128, W], F32) for i in range(2)]; b_y1 = [P.buf() for _ in range(2)]
        zt = [P.sb(f"zt{i}", [128, W], F32) for i in range(2)]; b_zt = [P.buf() for _ in range(2)]
        ot = [P.sb(f"ot{i}", [128, W], F32) for i in range(2)]; b_ot = [P.buf() for _ in range(2)]
        outs = []
        for t in range(NT):
            i = t % 2
            rsl = slice(t * 128, (t + 1) * 128)
            P.dma("sp", y0[i][:], y2[0, rsl, :], writes=[b_y0[i]])
            P.dma("sp", y1[i][:], y2[1, rsl, :], writes=[b_y1[i]])
            P.dma("sp", zt[i][:], z[rsl, :], writes=[b_zt[i]])
            P.op("act", lambda e, i=i: e.activation(out=zt[i][:], in_=zt[i][:], func=AF.Silu), reads=[b_zt[i]], writes=[b_zt[i]])
            P.op("pool", lambda e, i=i: e.tensor_add(out=y0[i][:], in0=y0[i][:], in1=y1[i][:]), reads=[b_y0[i], b_y1[i]], writes=[b_y0[i]])
            P.op("dve", lambda e, i=i: e.tensor_mul(out=y0[i][:], in0=y0[i][:], in1=zt[i][:]), reads=[b_y0[i], b_zt[i]], writes=[b_y0[i]])
            for g in range(4):
                P.op("act", lambda e, i=i, g=g, t=t: e.activation(out=junk[:], in_=y0[i][:, g * 384:(g + 1) * 384], func=AF.Square,
                                                                  accum_out=ss[:, t, g:g + 1]),
                     reads=[b_y0[i], b_ss0], writes=[b_junk, b_ss[t]])
            P.op("dve", lambda e, t=t: e.tensor_scalar(out=ss[:, t, :], in0=ss[:, t, :], scalar1=1.0 / 384, scalar2=EPS, op0=ALU.mult, op1=ALU.add),
                 reads=[b_ss[t]], writes=[b_ss[t]])
            P.op("act", lambda e, t=t: e.activation(out=ss[:, t, :], in_=ss[:, t, :], func=AF.Sqrt), reads=[b_ss[t]], writes=[b_ss[t]])
            P.op("dve", lambda e, t=t: e.reciprocal(out=ss[:, t, :], in_=ss[:, t, :]), reads=[b_ss[t]], writes=[b_ss[t]])
            for g in range(4):
                gs = slice(g * 384, (g + 1) * 384)
                P.op("dve", lambda e, i=i, g=g, gs=gs, t=t: e.scalar_tensor_tensor(out=ot[i][:, gs], in0=y0[i][:, gs], scalar=ss[:, t, g:g + 1], in1=gn[:, gs],
                                                                                   op0=ALU.mult, op1=ALU.mult),
                     reads=[b_y0[i], b_ss[t], b_gn], writes=[b_ot[i]])
            outs.append(P.dma("sp", out[rsl, :], ot[i][:], reads=[b_ot[i]]))
        P.finish_wait("sp", outs)
        P.emit()
    return nc


def build_k0():
    nc = new_nc()
    NCOL = 768
    cT = nc.dram_tensor("cT", [128, 8, 3], F32, kind="ExternalInput").ap()
    mw = nc.dram_tensor("mw", [2, 1024, NCOL], F32, kind="ExternalInput").ap()
    mb = nc.dram_tensor("mb", [1, 2, NCOL], F32, kind="ExternalInput").ap()
    out = nc.dram_tensor("out", [3, 2, NCOL], F32, kind="ExternalOutput").ap()
    with ExitStack() as st:
        P = Prog(nc, st)
        ct_sb = P.sb("ct_sb", [128, 8, 3], F32)
        sc_sb = P.sb("sc_sb", [128, 8, 3], F32)
        w_sb = P.sb("w_sb", [128, 2, 8, NCOL], F32)
        b_sb = P.sb("b_sb", [1, 2, NCOL], F32)
        ones = P.sb("ones", [1, 4], F32)
        o_sb = P.sb("o_sb", [3, 2, NCOL], F32)
        ps = [P.ps(f"ps{i}", [3, 512], F32) for i in range(4)]
        b_ct, b_sc, b_w, b_b, b_ones, b_o = [P.buf() for _ in range(6)]
        b_ps = [P.buf() for _ in range(4)]

        P.dma("sp", ct_sb[:], cT, writes=[b_ct])
        P.dma("sp", w_sb[:], mw.rearrange("l (k p) n -> p l k n", p=128), writes=[b_w])
        P.dma("sp", b_sb[:], mb, writes=[b_b])
        P.op("dve", lambda e: e.memset(ones[:], 1.0), writes=[b_ones])
        P.op("act", lambda e: e.activation(out=sc_sb[:], in_=ct_sb[:], func=AF.Silu), reads=[b_ct], writes=[b_sc])
        pi = 0
        for l in range(2):
            for (c0, cn) in ((0, 512), (512, 256)):
                pt = ps[pi]; bp = b_ps[pi]; pi += 1
                for k in range(8):
                    P.op("pe", lambda e, pt=pt, k=k, l=l, c0=c0, cn=cn: e.matmul(
                        pt[:, 0:cn], lhsT=sc_sb[:, k, :], rhs=w_sb[:, l, k, c0:c0 + cn], start=(k == 0), stop=False),
                        reads=[b_sc, b_w], writes=[bp])
                P.op("pe", lambda e, pt=pt, l=l, c0=c0, cn=cn: e.matmul(
                    pt[:, 0:cn], lhsT=ones[:, 0:3], rhs=b_sb[:, l, c0:c0 + cn], start=False, stop=True),
                    reads=[b_ones, b_b], writes=[bp])
                P.op("dve", lambda e, pt=pt, l=l, c0=c0, cn=cn: e.tensor_copy(out=o_sb[:, l, c0:c0 + cn], in_=pt[:, 0:cn]),
                     reads=[bp], writes=[b_o])
        t = P.dma("sp", out, o_sb[:], reads=[b_o])
        P.finish_wait("sp", [t])
        P.emit()
    return nc


import math
import ml_dtypes

NCORES = 8
CORES = list(range(NCORES))
_NC_CACHE = {}


def _get(name, fn, *a, **k):
    key = (name, a, tuple(sorted(k.items())))
    if key not in _NC_CACHE:
        _NC_CACHE[key] = fn(*a, **k)
    return _NC_CACHE[key]


def _run(nc, in_maps):
    res = run_bass_kernel_spmd(nc, in_maps, core_ids=CORES)
    return res.results


def featT(v):
    n = v.shape[0] // 128
    return np.ascontiguousarray(v.reshape(n, 128).T)


def rep(v):
    return np.ascontiguousarray(np.broadcast_to(v, (128,) + v.shape))


def tok_shard(lat, ctx, core, nt):
    b, q = core // 4, core % 4
    w = lat.shape[-1]
    o = np.zeros((nt * 128, w), np.float32)
    o[:2048] = lat[b, q * 2048:(q + 1) * 2048]
    if ctx is not None and nt > 16:
        o[2048:2112] = ctx[b, q * 64:(q + 1) * 64]
    return o


def tok_gather(results, key, w, with_ctx):
    lat = np.zeros((2, 8192, w), np.float32)
    ctx = np.zeros((2, 256, w), np.float32) if with_ctx else None
    for core in CORES:
        b, q = core // 4, core % 4
        o = results[core][key]
        lat[b, q * 2048:(q + 1) * 2048] = o[:2048]
        if with_ctx:
            ctx[b, q * 64:(q + 1) * 64] = o[2048:2112]
    return lat, ctx


def dft_tabs(n):
    tab = np.arange(n, dtype=np.float64) * (2 * np.pi / n)
    idx = (np.arange(n, dtype=np.int64)[:, None] * np.arange(n, dtype=np.int64)[None, :]) % n
    c = (np.cos(tab) / math.sqrt(n)).astype(np.float32).astype(ml_dtypes.bfloat16)
    s = (np.sin(tab) / math.sqrt(n)).astype(np.float32).astype(ml_dtypes.bfloat16)
    return c[idx], s[idx]


def rope_tables():
    t = 8192
    row = np.repeat(np.arange(t // 64, dtype=np.float32), 64)
    col = np.tile(np.arange(64, dtype=np.float32), t // 64)
    inv = (10000.0 ** (-np.arange(16, dtype=np.float32) * 2.0 / 32)).astype(np.float32)
    ang = np.stack([row, col], -1)[:, :, None] * inv
    cos, sin = np.cos(ang).astype(np.float32), np.sin(ang).astype(np.float32)
    cs = np.broadcast_to(cos[:, None, :, None, :], (t, 2, 2, 2, 16)).reshape(t, 128)
    sg = np.array([-1.0, 1.0], np.float32)[None, None, None, :, None]
    sn = (np.broadcast_to(sin[:, None, :, None, :], (t, 2, 2, 2, 16)) * sg).reshape(t, 128)
    return np.ascontiguousarray(cs), np.ascontiguousarray(sn)


def mod_maps(m, mc, j0, j1, b):
    o = np.zeros((128, 2, 2, 8), np.float32)
    o[:, 0, 0] = featT(m[b, j0]); o[:, 0, 1] = featT(m[b, j1])
    o[:, 1, 0] = featT(mc[j0]); o[:, 1, 1] = featT(mc[j1])
    return o


def kernel(x, c, ctx, c_ctx, mod_w, mod_b, norm_g, ffn_w_gate, ffn_w_up, ffn_w_down,
           ev_w_in, ev_conv_w, ev_conv_b, ev_dt_bias, ev_a_log, ev_d_skip, ev_gnorm_g, ev_w_out,
           od_w_in, od_lambda, od_subln_g, od_conv_w, od_conv_b, od_cnorm_g, od_cnorm_b, od_w_out):
    f32 = lambda a: np.ascontiguousarray(np.asarray(a, dtype=np.float32))
    x, c, ctx, c_ctx, mod_w, mod_b, norm_g = map(f32, (x, c, ctx, c_ctx, mod_w, mod_b, norm_g))
    ffn_w_gate, ffn_w_up, ffn_w_down = map(f32, (ffn_w_gate, ffn_w_up, ffn_w_down))
    ev_w_in, ev_conv_w, ev_conv_b, ev_dt_bias, ev_a_log, ev_d_skip, ev_gnorm_g, ev_w_out = map(
        f32, (ev_w_in, ev_conv_w, ev_conv_b, ev_dt_bias, ev_a_log, ev_d_skip, ev_gnorm_g, ev_w_out))
    od_w_in, od_lambda, od_subln_g, od_conv_w, od_conv_b, od_cnorm_g, od_cnorm_b, od_w_out = map(
        f32, (od_w_in, od_lambda, od_subln_g, od_conv_w, od_conv_b, od_cnorm_g, od_cnorm_b, od_w_out))
    ident = np.eye(128, dtype=np.float32).astype(ml_dtypes.bfloat16)

    cvec = np.concatenate([c, c_ctx[None]], 0)
    cT = np.ascontiguousarray(cvec.reshape(3, 8, 128).transpose(2, 1, 0))
    r = _run(_get("k0", build_k0), [{"cT": cT, "mw": np.ascontiguousarray(mod_w[:, :, i * 768:(i + 1) * 768]),
                                      "mb": np.ascontiguousarray(mod_b[None, :, i * 768:(i + 1) * 768])} for i in CORES])
    mall = np.concatenate([r[i]["out"] for i in CORES], axis=2)
    M = [mall[0:2, l].reshape(2, 6, 1024) for l in range(2)]
    MC = [mall[2, l].reshape(6, 1024) for l in range(2)]

    def ffn_block(l, hmid_cores, nt, ctx_tiles):
        m, mc, g = M[l], MC[l], norm_g[l]
        maps = []
        for core in CORES:
            b = core // 4
            maps.append({"h": hmid_cores[core], "wg": ffn_w_gate[l], "wu": ffn_w_up[l], "wd": ffn_w_down[l],
                         "modT": mod_maps(m, mc, 3, 4, b), "gT": featT(g[2]), "gR": rep(g[3]),
                         "gateR": rep(np.stack([m[b, 5], mc[5]])), "ident": ident})
        return _run(_get("k3b", build_k3b, nt, ctx_tiles=ctx_tiles), maps)

    def outproj_block(l, mix_lat, mix_ctx, h_lat, h_ctx, w_out, nt, ctx_tiles):
        m, mc, g = M[l], MC[l], norm_g[l]
        maps = []
        for core in CORES:
            b = core // 4
            mx = tok_shard(mix_lat, mix_ctx, core, nt)
            maps.append({"mixT": np.ascontiguousarray(mx.T), "h": tok_shard(h_lat, h_ctx, core, nt), "w": w_out,
                         "gR": rep(g[1]), "gateR": rep(np.stack([m[b, 2], mc[2]]))})
        r = _run(_get("k3a", build_k3a, nt, w_out.shape[0], ctx_tiles=ctx_tiles), maps)
        return [r[core]["out"] for core in CORES]

    def inproj_block(l, h_lat, h_ctx, w_in):
        m, mc, g = M[l], MC[l], norm_g[l]
        maps = []
        for core in CORES:
            b = core // 4
            maps.append({"h": tok_shard(h_lat, h_ctx, core, 17), "w": w_in, "modT": mod_maps(m, mc, 0, 1, b),
                         "gT": featT(g[0]), "ident": ident})
        r = _run(_get("k1", build_k1, 17, w_in.shape[1]), maps)
        return tok_gather(r, "out", w_in.shape[1], True)

    pj_h, pj_s = inproj_block(0, x, ctx, ev_w_in[0])
    maps = []
    for core in CORES:
        b, q = core // 4, core % 4
        ch = slice(2048 + q * 640, 2048 + (q + 1) * 640)
        pre = np.zeros((640, 8192 + 256 + 8), np.float32)
        pre[:, 2:2 + 8192] = pj_h[b, :, ch].T
        pre[:, 8192 + 6:8192 + 6 + 256] = pj_s[b, :, ch].T
        cq = slice(q * 640, (q + 1) * 640)
        maps.append({"pre": pre, "cw": np.ascontiguousarray(ev_conv_w[0][:, cq].T.reshape(5, 128, 5).transpose(1, 0, 2)),
                     "cb": np.ascontiguousarray(ev_conv_b[0][cq].reshape(5, 128).T)})
    r = _run(_get("k2a", build_k2a), maps)
    xbc = np.zeros((2, 8448, 2560), np.float32)
    for core in CORES:
        b, q = core // 4, core % 4
        o = r[core]["out"]
        xbc[b, 256:, q * 640:(q + 1) * 640] = o[:, :8192].T
        xbc[b, :256, q * 640:(q + 1) * 640] = o[:, 8192:].T
    CT, ST = dft_tabs(8192)
    CTc, STc = dft_tabs(256)
    kk = np.arange(128)
    ang = 2 * np.pi * np.outer(kk, kk) / 128
    ccsc = (np.concatenate([np.cos(ang), -np.sin(ang)], 1) / math.sqrt(128)).astype(np.float32)
    maps = []
    for core in CORES:
        b, g = core // 4, core % 4
        gs = slice(g * 128, (g + 1) * 128)
        maps.append({"GT": np.ascontiguousarray(np.concatenate([pj_h[b, :, gs].T, pj_s[b, :, gs].T], 1)),
                     "ccsc": ccsc, "CT": CT, "ST": ST, "CTc": CTc, "STc": STc})
    r = _run(_get("kf", build_kf), maps)
    del CT, ST
    mix_h = np.zeros((2, 8192, 2048), np.float32)
    mix_s = np.zeros((2, 256, 2048), np.float32)
    for core in CORES:
        b, g = core // 4, core % 4
        o = r[core]["out"]
        mix_h[b, :, g * 128:(g + 1) * 128] = o[:, :8192].T
        mix_s[b, :, g * 128:(g + 1) * 128] = o[:, 8192:].T
    tri = np.zeros((128, 3, 128), np.float32)
    s_, l_ = np.meshgrid(np.arange(128), np.arange(128), indexing="ij")
    tri[:, 0] = (s_ <= l_); tri[:, 1] = (s_ >= l_); tri[:, 2] = 1.0
    maps = []
    for core in CORES:
        b, g = core // 4, core % 4
        dt = np.concatenate([pj_s[b, :, 4608:4656], pj_h[b, :, 4608:4656]], 0).reshape(8448, 2, 4, 6)[:, :, g, :]
        prm = np.zeros((128, 3, 2, 66, 6), np.float32)
        for k_, a_ in enumerate((ev_dt_bias, ev_a_log, ev_d_skip)):
            prm[:, k_] = a_[0].reshape(2, 4, 6)[:, g, :][None, :, None, :]
        bs = slice(1536 + g * 128, 1536 + (g + 1) * 128)
        cs_ = slice(2048 + g * 128, 2048 + (g + 1) * 128)
        maps.append({"xs": np.ascontiguousarray(xbc[b, :, g * 384:(g + 1) * 384]),
                     "Btm": np.ascontiguousarray(xbc[b, :, bs]), "BfT": np.ascontiguousarray(xbc[b, :, bs].T),
                     "CfT": np.ascontiguousarray(xbc[b, :, cs_].T),
                     "dt": np.ascontiguousarray(dt.reshape(66, 128, 2, 6).transpose(1, 2, 0, 3)), "prm": prm, "tri": tri})
    r = _run(_get("ssd", build_ssd), maps)
    yall = np.zeros((2, 2, 8448, 1536), np.float32)
    for core in CORES:
        b, g = core // 4, core % 4
        yall[b, :, :, g * 384:(g + 1) * 384] = r[core]["y"]
    maps = []
    for core in CORES:
        b = core // 4
        y2 = np.stack([tok_shard(yall[:, d, 256:], yall[:, d, :256], core, 17) for d in range(2)])
        maps.append({"y2": y2, "z": tok_shard(pj_h[:, :, 512:2048], pj_s[:, :, 512:2048], core, 17), "gnR": rep(ev_gnorm_g[0])})
    r = _run(_get("k2c", build_k2c, 17), maps)
    yn_h, yn_s = tok_gather(r, "out", 1536, True)
    mix_h[:, :, 512:] = yn_h
    mix_s[:, :, 512:] = yn_s
    del pj_h, pj_s, xbc, yall
    hmid = outproj_block(0, mix_h, mix_s, x, ctx, ev_w_out[0], 17, (16,))
    r = ffn_block(0, hmid, 17, (16,))
    h1, s1 = tok_gather(r, "out", 1024, True)

    lambda_init = 0.8 - 0.6 * math.exp(-0.3 * 1)
    pj_h, pj_s = inproj_block(1, h1, s1, od_w_in[0])
    CS, SN = rope_tables()
    maps = []
    for core in CORES:
        b = core // 4
        qk = np.zeros((2, 2, 8192, 128), np.float32); v = np.zeros((2, 8192, 128), np.float32)
        kc = np.zeros((2, 256, 128), np.float32); vc = np.zeros((2, 256, 128), np.float32)
        for u in range(2):
            hd = 2 * (core % 4) + u
            hs = slice(hd * 128, (hd + 1) * 128)
            qk[u, 0] = pj_h[b, :, hs]; qk[u, 1] = pj_h[b, :, 1024 + hd * 128:1024 + (hd + 1) * 128]
            v[u] = pj_h[b, :, 2048 + hd * 128:2048 + (hd + 1) * 128]
            kc[u] = pj_s[b, :, 1024 + hd * 128:1024 + (hd + 1) * 128]
            vc[u] = pj_s[b, :, 2048 + hd * 128:2048 + (hd + 1) * 128]
        maps.append({"qk": qk, "v": v, "kc": kc, "vc": vc, "cs": CS, "sn": SN, "lamR": rep(od_lambda[0]),
                     "subR": rep(od_subln_g[0]), "ident": ident})
    r = _run(_get("k4", build_k4, lambda_init), maps)
    mix = np.zeros((2, 8192, 1536), np.float32)
    for core in CORES:
        b = core // 4
        for u in range(2):
            hd = 2 * (core % 4) + u
            mix[b, :, hd * 128:(hd + 1) * 128] = r[core]["out"][u]
    tr4 = lambda a: np.ascontiguousarray(a.reshape(4, 128).T)
    cw = np.ascontiguousarray(od_conv_w[0].T.reshape(4, 128, 31).transpose(1, 0, 2))
    maps = []
    for core in CORES:
        b, q = core // 4, core % 4
        up = np.zeros((8192 + 30, 1024), np.float32)
        up[15:15 + 8192] = pj_h[b, :, 3072:4096]
        maps.append({"uT": np.ascontiguousarray(up[q * 2048:q * 2048 + 2048 + 30].T), "cw": cw, "cb": tr4(od_conv_b[0]),
                     "lng": tr4(od_cnorm_g[0]), "lnb": tr4(od_cnorm_b[0])})
    r = _run(_get("k5", build_k5), maps)
    for core in CORES:
        b, q = core // 4, core % 4
        mix[b, q * 2048:(q + 1) * 2048, 1024:] = r[core]["out"].T
    del pj_h, pj_s
    hmid = outproj_block(1, mix, None, h1, None, od_w_out[0], 16, ())
    r = ffn_block(1, hmid, 16, ())
    out, _ = tok_gather(r, "out", 1024, False)
    return out
```

```python
import numpy as np
from contextlib import ExitStack
import concourse.bass as bass
import concourse.mybir as mybir
from concourse.bass_utils import run_bass_kernel_spmd

F32 = mybir.dt.float32
BF16 = mybir.dt.bfloat16
AF = mybir.ActivationFunctionType
ALU = mybir.AluOpType
AX = mybir.AxisListType


class Buf:
    __slots__ = ("name", "w", "r")

    def __init__(self, name=""):
        self.name = name
        self.w = {}
        self.r = {}


def _merge(dst, tok):
    k, v, e = tok
    if k not in dst or dst[k][0] < v:
        dst[k] = (v, e)


class Prog:
    ENGS = ("pe", "act", "dve", "pool", "sp")
    RING = 8

    def __init__(self, nc, stack):
        self.nc = nc
        self.stack = stack
        self.ops = {e: [] for e in self.ENGS}
        self.ccount = {e: 0 for e in self.ENGS}
        self.dcount = {e: 0 for e in self.ENGS}
        self.seen = {e: {} for e in self.ENGS}
        self.sems = {}
        self.nbuf = 0
        self.ncc = 0
        self._root_stack = stack
        self.scope_id = 0
        self._nscope = 0
        for e in self.ENGS:
            self.sems[("c", e)] = stack.enter_context(nc.semaphore("c_" + e))
        for e in ("sp", "pool", "act"):
            for i in range(self.RING):
                self.sems[("d", e, i)] = stack.enter_context(nc.semaphore(f"d_{e}{i}"))

    def sb(self, name, shape, dtype):
        return self.stack.enter_context(self.nc.sbuf_tensor(f"sb{self.scope_id}_" + name, list(shape), dtype))

    def ps(self, name, shape, dtype):
        return self.stack.enter_context(self.nc.psum_tensor(f"ps{self.scope_id}_" + name, list(shape), dtype))

    def buf(self, name=""):
        self.nbuf += 1
        return Buf(name or f"b{self.nbuf}")

    def alias(self, old):
        b = self.buf()
        for k, (v, e) in list(old.r.items()) + list(old.w.items()):
            _merge(b.r, (k, v, e))
        return b

    def _deps(self, eng, reads, writes, djw=()):
        deps = {}

        def add(k, v, e2):
            if eng == "pe" and e2 == "pe" and k[0] == "c":
                return
            if v > deps.get(k, 0):
                deps[k] = v
        for b in reads:
            for k, (v, e2) in b.w.items():
                add(k, v, e2)
        for b in writes:
            for k, (v, e2) in b.w.items():
                add(k, v, e2)
            for k, (v, e2) in b.r.items():
                add(k, v, e2)
        for b in djw:
            for k, (v, e2) in b.r.items():
                add(k, v, e2)
        seen = self.seen[eng]
        out = []
        for k, v in deps.items():
            if seen.get(k, 0) >= v:
                continue
            seen[k] = v
            out.append((k, v))
        return out

    def _record(self, tok, reads, writes, djw=()):
        for b in reads:
            _merge(b.r, tok)
        for b in writes:
            b.w = {tok[0]: (tok[1], tok[2])}
            b.r = {}
        for b in djw:
            _merge(b.w, tok)

    def op(self, eng, fn, reads=(), writes=(), djw=()):
        waits = self._deps(eng, reads, writes, djw)
        self.ccount[eng] += 1
        tok = (("c", eng), self.ccount[eng], eng)
        self.ops[eng].append(("c", fn, waits, tok))
        self._record(tok, reads, writes, djw)
        return tok

    def dma(self, eng, out_ap, in_ap, reads=(), writes=(), djw=(), **kw):
        waits = self._deps(eng, reads, writes, djw)
        j = self.dcount[eng]
        self.dcount[eng] += 1
        slot = j % self.RING
        k = ("d", eng, slot)
        need = 16 * (j // self.RING)
        if need > 0 and self.seen[eng].get(k, 0) < need:
            self.seen[eng][k] = need
            waits.append((k, need))
        tok = (k, 16 * (j // self.RING + 1), eng)

        def fn(e, out_ap=out_ap, in_ap=in_ap, kw=kw):
            o = out_ap() if callable(out_ap) else out_ap
            i = in_ap() if callable(in_ap) else in_ap
            return e.dma_start(out=o, in_=i, **kw)
        self.ops[eng].append(("d", fn, waits, tok))
        self._record(tok, reads, writes, djw)
        return tok

    def finish_wait(self, eng, toks):
        waits = []
        for (k, v, _e) in toks:
            if self.seen[eng].get(k, 0) < v:
                self.seen[eng][k] = v
                waits.append((k, v))
        self.ops[eng].append(("w", None, waits, None))

    def wait_all_dma(self, eng="sp"):
        toks = []
        for e in ("sp", "pool", "act"):
            n = self.dcount[e]
            for slot in range(self.RING):
                if n == 0:
                    continue
                last = ((n - 1 - slot) // self.RING) * self.RING + slot if n - 1 >= slot else -1
                if last >= 0:
                    toks.append((("d", e, slot), 16 * (last // self.RING + 1), e))
        self.finish_wait(eng, toks)

    def push_scope(self):
        self._outer = getattr(self, "_outer", [])
        self._outer.append(self.stack)
        self.stack = ExitStack()
        self.stack.__enter__()
        self._nscope += 1
        self.scope_id = self._nscope

    def pop_scope(self):
        self.stack.__exit__(None, None, None)
        self.stack = self._outer.pop()

    def all_tokens(self):
        toks = []
        for e in self.ENGS:
            if self.ccount[e] > 0:
                toks.append((("c", e), self.ccount[e], e))
        for e in ("sp", "pool", "act"):
            n = self.dcount[e]
            for slot in range(self.RING):
                if n - 1 >= slot:
                    last = ((n - 1 - slot) // self.RING) * self.RING + slot
                    toks.append((("d", e, slot), 16 * (last // self.RING + 1), e))
        for i in range(self.ncc):
            toks.append((("cc", i), 1, "pool"))
        return toks

    def barrier(self):
        toks = self.all_tokens()
        for e in self.ENGS:
            self.finish_wait(e, toks)

    def cc(self, kind, in_ap, out_ap, groups, reads=(), writes=()):
        waits = self._deps("pool", reads, writes)
        i = self.ncc
        self.ncc += 1
        k = ("cc", i)
        self.sems[k] = self._root_stack.enter_context(self.nc.semaphore(f"cc{i}"))
        tok = (k, 1, "pool")

        def fn(e):
            return e.collective_compute(kind, ALU.bypass, replica_groups=groups, ins=[in_ap], outs=[out_ap])
        self.ops["pool"].append(("x", fn, waits, tok))
        self._record(tok, reads, writes)
        return tok

    def raw(self, eng, fn):
        self.ops[eng].append(("r", fn, [], None))

    def emit(self):
        nc = self.nc
        prog = self
        with nc.Block() as block:
            def run(engname, e):
                for kind, fn, waits, tok in prog.ops[engname]:
                    for (k, v) in waits:
                        e.wait_ge(prog.sems[k], v)
                    if kind == "w":
                        continue
                    if kind == "r":
                        fn(e)
                        continue
                    ins = fn(e)
                    if kind == "c":
                        ins.then_inc(prog.sems[tok[0]], 1)
                    elif kind == "x":
                        ins.then_inc(prog.sems[tok[0]])
                    else:
                        ins.then_inc(prog.sems[tok[0]], 16)

            @block.tensor
            def _(e):
                run("pe", e)

            @block.scalar
            def _(e):
                run("act", e)

            @block.vector
            def _(e):
                run("dve", e)

            @block.gpsimd
            def _(e):
                run("pool", e)

            @block.sync
            def _(e):
                run("sp", e)


EPS = 1e-6


def new_nc():
    return bass.Bass("TRN2", target_bir_lowering=False)


def load_weight_bf16(P, w_dram, w_sb, b_w, nk, ncol, stage, b_stage, cast_engs=("pool",)):
    for k in range(nk):
        s = k % len(stage)
        P.dma("sp", stage[s][:, 0:ncol], w_dram[k * 128:(k + 1) * 128, :], writes=[b_stage[s]])
        eng = cast_engs[k % len(cast_engs)]
        P.op(eng, lambda e, s=s, k=k: e.tensor_copy(out=w_sb[:, k, :], in_=stage[s][:, 0:ncol]),
             reads=[b_stage[s]], writes=[b_w])


class NormT:
    def __init__(self, P, pfx, ident, b_ident, nt):
        self.P = P
        self.ident, self.b_ident = ident, b_ident
        self.junk = P.sb(pfx + "junk", [128, 1024], BF16)
        self.b_junk = P.buf()
        self.ss = P.sb(pfx + "ss", [128, nt], F32)
        self.b_ss = P.buf()
        self.rs = P.sb(pfx + "rs", [128, nt], F32)
        self.b_rs = [P.buf() for _ in range(nt)]
        self.xn = [P.sb(pfx + f"xn{i}", [128, 1024], BF16) for i in range(2)]
        self.b_xn = [P.buf() for _ in range(2)]
        self.psT = [P.ps(pfx + f"psT{i}", [128, 8, 128], BF16) for i in range(2)]
        self.b_psT = [P.buf() for _ in range(2)]
        P.op("pool", lambda e: e.memset(self.ss[:], 0.0), writes=[self.b_ss])
        self.n = 0

    def run(self, x_ap, b_x, t, aT_ap, b_aT, Gs, Sh, b_gs, cls):
        P = self.P
        i = self.n % 2
        self.n += 1
        ss, rs, xn, psT = self.ss, self.rs, self.xn[i], self.psT[i]
        P.op("act", lambda e: e.activation(out=self.junk[:], in_=x_ap, func=AF.Square, accum_out=ss[:, t:t + 1]),
             reads=[b_x, self.b_ss], writes=[self.b_junk, self.b_rs[t]])
        P.op("dve", lambda e: e.tensor_scalar(out=rs[:, t:t + 1], in0=ss[:, t:t + 1], scalar1=1.0 / 1024, scalar2=EPS,
                                              op0=ALU.mult, op1=ALU.add), reads=[self.b_rs[t]], writes=[self.b_rs[t]])
        P.op("act", lambda e: e.activation(out=rs[:, t:t + 1], in_=rs[:, t:t + 1], func=AF.Sqrt),
             reads=[self.b_rs[t]], writes=[self.b_rs[t]])
        P.op("dve", lambda e: e.reciprocal(out=rs[:, t:t + 1], in_=rs[:, t:t + 1]),
             reads=[self.b_rs[t]], writes=[self.b_rs[t]])
        P.op("dve", lambda e: e.tensor_scalar_mul(out=xn[:], in0=x_ap, scalar1=rs[:, t:t + 1]),
             reads=[b_x, self.b_rs[t]], writes=[self.b_xn[i]])
        for k in range(8):
            P.op("pe", lambda e, k=k: e.transpose(out=psT[:, k, :], in_=xn[:, k * 128:(k + 1) * 128], identity=self.ident[:]),
                 reads=[self.b_xn[i], self.b_ident], writes=[self.b_psT[i]])
        for k in range(8):
            P.op("act", lambda e, k=k: e.activation(out=aT_ap[:, k, :], in_=psT[:, k, :], func=AF.Identity,
                                                    scale=Gs[:, cls, k:k + 1], bias=Sh[:, cls, k:k + 1]),
                 reads=[self.b_psT[i], b_gs], djw=[b_aT])


def build_k1(NT, NOUT, ctx_tiles=(16,)):
    nc = new_nc()
    h = nc.dram_tensor("h", [NT * 128, 1024], F32, kind="ExternalInput").ap()
    w = nc.dram_tensor("w", [1024, NOUT], F32, kind="ExternalInput").ap()
    modT = nc.dram_tensor("modT", [128, 2, 2, 8], F32, kind="ExternalInput").ap()
    gT = nc.dram_tensor("gT", [128, 8], F32, kind="ExternalInput").ap()
    identd = nc.dram_tensor("ident", [128, 128], BF16, kind="ExternalInput").ap()
    out = nc.dram_tensor("out", [NT * 128, NOUT], F32, kind="ExternalOutput").ap()
    ncb = (NOUT + 511) // 512
    with ExitStack() as st:
        P = Prog(nc, st)
        ident = P.sb("ident", [128, 128], BF16); b_ident = P.buf()
        P.dma("sp", ident[:], identd, writes=[b_ident])
        modsb = P.sb("modsb", [128, 2, 2, 8], F32); b_mod = P.buf()
        P.dma("sp", modsb[:], modT, writes=[b_mod])
        gsb = P.sb("gsb", [128, 8], F32); b_g = P.buf()
        P.dma("sp", gsb[:], gT, writes=[b_g])
        Gs = P.sb("Gs", [128, 2, 8], F32); Sh = P.sb("Sh", [128, 2, 8], F32); b_gs = P.buf()
        for cls in range(2):
            P.op("dve", lambda e, cls=cls: e.scalar_tensor_tensor(out=Gs[:, cls, :], in0=modsb[:, cls, 1, :], scalar=1.0,
                                                                   in1=gsb[:], op0=ALU.add, op1=ALU.mult),
                 reads=[b_mod, b_g], writes=[b_gs])
            P.op("dve", lambda e, cls=cls: e.tensor_copy(out=Sh[:, cls, :], in_=modsb[:, cls, 0, :]),
                 reads=[b_mod], writes=[b_gs])
        w_sb = P.sb("w_sb", [128, 8, NOUT], BF16); b_w = P.buf()
        stage = [P.sb(f"stage{i}", [128, NOUT], F32) for i in range(2)]
        b_stage = [P.buf() for _ in range(2)]
        load_weight_bf16(P, w, w_sb, b_w, 8, NOUT, stage, b_stage)
        nt = NormT(P, "n_", ident, b_ident, NT)
        xt = [P.sb(f"xt{i}", [128, 1024], F32) for i in range(2)]; b_xt = [P.buf() for _ in range(2)]
        aT = [P.sb(f"aT{i}", [128, 8, 128], BF16) for i in range(2)]; b_aT = [P.buf() for _ in range(2)]
        ot = stage; b_ot = b_stage
        pso = [P.ps(f"pso{i}", [128, 512], F32) for i in range(4)]; b_pso = [P.buf() for _ in range(4)]
        outs = []
        nps = 0
        for t in range(NT):
            i = t % 2
            cls = 1 if t in ctx_tiles else 0
            P.dma("sp", xt[i][:], h[t * 128:(t + 1) * 128, :], writes=[b_xt[i]])
            nt.run(xt[i][:], b_xt[i], t, aT[i], b_aT[i], Gs, Sh, b_gs, cls)
            for cb in range(ncb):
                c0 = cb * 512; cn = min(512, NOUT - c0)
                pi = nps % 4; nps += 1
                for k in range(8):
                    P.op("pe", lambda e, pi=pi, k=k, c0=c0, cn=cn, i=i: e.matmul(
                        pso[pi][:, 0:cn], lhsT=aT[i][:, k, :], rhs=w_sb[:, k, c0:c0 + cn], start=(k == 0), stop=(k == 7)),
                        reads=[b_aT[i], b_w], writes=[b_pso[pi]])
                if cb % 2 == 0:
                    P.op("dve", lambda e, pi=pi, c0=c0, cn=cn, i=i: e.tensor_copy(out=ot[i][:, c0:c0 + cn], in_=pso[pi][:, 0:cn]),
                         reads=[b_pso[pi]], writes=[b_ot[i]])
                else:
                    P.op("act", lambda e, pi=pi, c0=c0, cn=cn, i=i: e.copy(out=ot[i][:, c0:c0 + cn], in_=pso[pi][:, 0:cn]),
                         reads=[b_pso[pi]], writes=[b_ot[i]])
            outs.append(P.dma("sp", out[t * 128:(t + 1) * 128, :], ot[i][:], reads=[b_ot[i]]))
        P.finish_wait("sp", outs)
        P.emit()
    return nc


class ResNorm:
    def __init__(self, P, pfx, nt):
        self.P = P
        self.junk = P.sb(pfx + "junk", [128, 1024], BF16); self.b_junk = P.buf()
        self.ss = P.sb(pfx + "ss", [128, nt], F32); self.b_ss = P.buf()
        self.rs = P.sb(pfx + "rs", [128, nt], F32); self.b_rs = [P.buf() for _ in range(nt)]
        self.tmp = [P.sb(pfx + f"tmp{i}", [128, 1024], F32) for i in range(2)]; self.b_tmp = [P.buf() for _ in range(2)]
        P.op("pool", lambda e: e.memset(self.ss[:], 0.0), writes=[self.b_ss])
        self.n = 0

    def run(self, po, b_po, t, hres, b_hres, GG_ap, b_gg, out_ap=None, b_out=None):
        P = self.P
        i = self.n % 2
        self.n += 1
        ss, rs, tmp = self.ss, self.rs, self.tmp[i]
        P.op("act", lambda e: e.activation(out=self.junk[:], in_=po, func=AF.Square, accum_out=ss[:, t:t + 1]),
             reads=[b_po, self.b_ss], writes=[self.b_junk, self.b_rs[t]])
        P.op("dve", lambda e: e.tensor_scalar(out=rs[:, t:t + 1], in0=ss[:, t:t + 1], scalar1=1.0 / 1024, scalar2=EPS,
                                              op0=ALU.mult, op1=ALU.add), reads=[self.b_rs[t]], writes=[self.b_rs[t]])
        P.op("act", lambda e: e.activation(out=rs[:, t:t + 1], in_=rs[:, t:t + 1], func=AF.Sqrt),
             reads=[self.b_rs[t]], writes=[self.b_rs[t]])
        P.op("dve", lambda e: e.reciprocal(out=rs[:, t:t + 1], in_=rs[:, t:t + 1]),
             reads=[self.b_rs[t]], writes=[self.b_rs[t]])
        P.op("dve", lambda e: e.scalar_tensor_tensor(out=tmp[:], in0=po, scalar=rs[:, t:t + 1], in1=GG_ap,
                                                     op0=ALU.mult, op1=ALU.mult),
             reads=[b_po, self.b_rs[t], b_gg], writes=[self.b_tmp[i]])
        if out_ap is None:
            P.op("pool", lambda e: e.tensor_add(out=tmp[:], in0=tmp[:], in1=hres),
                 reads=[self.b_tmp[i], b_hres], writes=[self.b_tmp[i]])
            return tmp, self.b_tmp[i]
        P.op("pool", lambda e: e.tensor_add(out=out_ap, in0=tmp[:], in1=hres),
             reads=[self.b_tmp[i], b_hres], writes=[b_out])
        return None


def build_k3a(NT, CM, ctx_tiles=(16,)):
    nc = new_nc()
    nk = CM // 128
    mixT = nc.dram_tensor("mixT", [CM, NT * 128], F32, kind="ExternalInput").ap()
    h = nc.dram_tensor("h", [NT * 128, 1024], F32, kind="ExternalInput").ap()
    w = nc.dram_tensor("w", [CM, 1024], F32, kind="ExternalInput").ap()
    gR = nc.dram_tensor("gR", [128, 1024], F32, kind="ExternalInput").ap()
    gateR = nc.dram_tensor("gateR", [128, 2, 1024], F32, kind="ExternalInput").ap()
    out = nc.dram_tensor("out", [NT * 128, 1024], F32, kind="ExternalOutput").ap()
    with ExitStack() as st:
        P = Prog(nc, st)
        g_sb = P.sb("g_sb", [128, 1024], F32); b_g = P.buf()
        P.dma("sp", g_sb[:], gR, writes=[b_g])
        GG = P.sb("GG", [128, 2, 1024], F32); b_gg = P.buf()
        P.dma("sp", GG[:], gateR, writes=[b_gg])
        for cls in range(2):
            P.op("dve", lambda e, cls=cls: e.tensor_mul(out=GG[:, cls, :], in0=GG[:, cls, :], in1=g_sb[:]),
                 reads=[b_g, b_gg], writes=[b_gg])
        w_sb = P.sb("w_sb", [128, nk, 1024], BF16); b_w = P.buf()
        stage = [P.sb(f"stage{i}", [128, 1024], F32) for i in range(2)]; b_stage = [P.buf() for _ in range(2)]
        load_weight_bf16(P, w, w_sb, b_w, nk, 1024, stage, b_stage)
        rn = ResNorm(P, "r_", NT)
        mst = [P.sb(f"mst{i}", [128, nk, 128], F32) for i in range(2)]; b_mst = [P.buf() for _ in range(2)]
        mT = [P.sb(f"mT{i}", [128, nk, 128], BF16) for i in range(2)]; b_mT = [P.buf() for _ in range(2)]
        xt = [P.sb(f"xt{i}", [128, 1024], F32) for i in range(2)]; b_xt = [P.buf() for _ in range(2)]
        ot = [P.sb(f"ot{i}", [128, 1024], F32) for i in range(2)]; b_ot = [P.buf() for _ in range(2)]
        po = [P.ps(f"po{i}", [128, 1024], F32) for i in range(2)]; b_po = [P.buf() for _ in range(2)]
        mv = mixT.rearrange("(k p) t -> p k t", p=128)
        outs = []
        for t in range(NT):
            i = t % 2
            cls = 1 if t in ctx_tiles else 0
            P.dma("sp", mst[i][:], mv[:, :, t * 128:(t + 1) * 128], writes=[b_mst[i]])
            P.dma("sp", xt[i][:], h[t * 128:(t + 1) * 128, :], writes=[b_xt[i]])
            P.op("pool", lambda e, i=i: e.tensor_copy(out=mT[i][:], in_=mst[i][:]), reads=[b_mst[i]], writes=[b_mT[i]])
            for cb in range(2):
                for k in range(nk):
                    P.op("pe", lambda e, i=i, k=k, cb=cb: e.matmul(
                        po[i][:, cb * 512:(cb + 1) * 512], lhsT=mT[i][:, k, :], rhs=w_sb[:, k, cb * 512:(cb + 1) * 512],
                        start=(k == 0), stop=(k == nk - 1)), reads=[b_mT[i], b_w], writes=[b_po[i]])
            rn.run(po[i][:], b_po[i], t, xt[i][:], b_xt[i], GG[:, cls, :], b_gg, ot[i][:], b_ot[i])
            outs.append(P.dma("sp", out[t * 128:(t + 1) * 128, :], ot[i][:], reads=[b_ot[i]]))
        P.finish_wait("sp", outs)
        P.emit()
    return nc


def build_k3b(NT, ctx_tiles=(16,)):
    nc = new_nc()
    FH = 2816
    NJ = FH // 128
    h = nc.dram_tensor("h", [NT * 128, 1024], F32, kind="ExternalInput").ap()
    wg = nc.dram_tensor("wg", [1024, FH], F32, kind="ExternalInput").ap()
    wu = nc.dram_tensor("wu", [1024, FH], F32, kind="ExternalInput").ap()
    wd = nc.dram_tensor("wd", [FH, 1024], F32, kind="ExternalInput").ap()
    modT = nc.dram_tensor("modT", [128, 2, 2, 8], F32, kind="ExternalInput").ap()
    gT = nc.dram_tensor("gT", [128, 8], F32, kind="ExternalInput").ap()
    gR = nc.dram_tensor("gR", [128, 1024], F32, kind="ExternalInput").ap()
    gateR = nc.dram_tensor("gateR", [128, 2, 1024], F32, kind="ExternalInput").ap()
    identd = nc.dram_tensor("ident", [128, 128], BF16, kind="ExternalInput").ap()
    out = nc.dram_tensor("out", [NT * 128, 1024], F32, kind="ExternalOutput").ap()
    with ExitStack() as st:
        P = Prog(nc, st)
        ident = P.sb("ident", [128, 128], BF16); b_ident = P.buf()
        P.dma("sp", ident[:], identd, writes=[b_ident])
        modsb = P.sb("modsb", [128, 2, 2, 8], F32); b_mod = P.buf()
        P.dma("sp", modsb[:], modT, writes=[b_mod])
        gsb = P.sb("gsb", [128, 8], F32); b_g = P.buf()
        P.dma("sp", gsb[:], gT, writes=[b_g])
        Gs = P.sb("Gs", [128, 2, 8], F32); Sh = P.sb("Sh", [128, 2, 8], F32); b_gs = P.buf()
        for cls in range(2):
            P.op("dve", lambda e, cls=cls: e.scalar_tensor_tensor(out=Gs[:, cls, :], in0=modsb[:, cls, 1, :], scalar=1.0,
                                                                   in1=gsb[:], op0=ALU.add, op1=ALU.mult),
                 reads=[b_mod, b_g], writes=[b_gs])
            P.op("dve", lambda e, cls=cls: e.tensor_copy(out=Sh[:, cls, :], in_=modsb[:, cls, 0, :]),
                 reads=[b_mod], writes=[b_gs])
        g_sb = P.sb("g_sb", [128, 1024], F32); b_g3 = P.buf()
        P.dma("sp", g_sb[:], gR, writes=[b_g3])
        GG = P.sb("GG", [128, 2, 1024], F32); b_gg = P.buf()
        P.dma("sp", GG[:], gateR, writes=[b_gg])
        for cls in range(2):
            P.op("dve", lambda e, cls=cls: e.tensor_mul(out=GG[:, cls, :], in0=GG[:, cls, :], in1=g_sb[:]),
                 reads=[b_g3, b_gg], writes=[b_gg])
        wg_sb = P.sb("wg_sb", [128, 8, FH], BF16); b_wg = P.buf()
        wu_sb = P.sb("wu_sb", [128, 8, FH], BF16); b_wu = P.buf()
        wd_sb = P.sb("wd_sb", [128, NJ, 1024], BF16); b_wd = P.buf()
        stage = [P.sb(f"stage{i}", [128, FH], F32) for i in range(2)]; b_stage = [P.buf() for _ in range(2)]
        load_weight_bf16(P, wg, wg_sb, b_wg, 8, FH, stage, b_stage, cast_engs=("pool", "dve"))
        load_weight_bf16(P, wu, wu_sb, b_wu, 8, FH, stage, b_stage, cast_engs=("pool", "dve"))
        load_weight_bf16(P, wd, wd_sb, b_wd, NJ, 1024, stage, b_stage, cast_engs=("pool", "dve"))
        nt = NormT(P, "n_", ident, b_ident, NT)
        rn = ResNorm(P, "r_", NT)
        ST = 2
        xt = [stage[0][:, i * 1024:(i + 1) * 1024] for i in range(2)]; b_xt = [P.alias(b_stage[0]) for _ in range(2)]
        xr = [stage[1][:, i * 1024:(i + 1) * 1024] for i in range(2)]; b_xr = [P.alias(b_stage[1]) for _ in range(2)]
        aT = P.sb("aT", [128, 8, ST * 128], BF16); b_aT = P.buf()
        hidT = P.sb("hidT", [128, NJ, ST * 128], BF16); b_hid = P.buf()
        sg = [P.sb(f"sg{i}", [128, ST * 128], F32) for i in range(2)]; b_sg = [P.buf() for _ in range(2)]
        psg = [P.ps(f"psg{i}", [128, 512], F32) for i in range(2)]; b_psg = [P.buf() for _ in range(2)]
        psu = [P.ps(f"psu{i}", [128, 512], F32) for i in range(2)]; b_psu = [P.buf() for _ in range(2)]
        po = P.ps("po", [128, 1024], F32); b_po = P.buf()
        outs = []
        nx = 0
        nr = 0
        for s0 in range(0, NT, ST):
            tiles = list(range(s0, min(NT, s0 + ST)))
            N = len(tiles) * 128
            for ti, t in enumerate(tiles):
                i = nx % 2; nx += 1
                cls = 1 if t in ctx_tiles else 0
                P.dma("sp", xt[i], h[t * 128:(t + 1) * 128, :], writes=[b_xt[i]])
                nt.run(xt[i], b_xt[i], t, aT[:, :, ti * 128:(ti + 1) * 128], b_aT, Gs, Sh, b_gs, cls)
            for j in range(NJ):
                pi = j % 2
                for k in range(8):
                    P.op("pe", lambda e, pi=pi, j=j, k=k, N=N: e.matmul(
                        psg[pi][:, 0:N], lhsT=wg_sb[:, k, j * 128:(j + 1) * 128], rhs=aT[:, k, 0:N],
                        start=(k == 0), stop=(k == 7)), reads=[b_wg, b_aT], writes=[b_psg[pi]])
                for k in range(8):
                    P.op("pe", lambda e, pi=pi, j=j, k=k, N=N: e.matmul(
                        psu[pi][:, 0:N], lhsT=wu_sb[:, k, j * 128:(j + 1) * 128], rhs=aT[:, k, 0:N],
                        start=(k == 0), stop=(k == 7)), reads=[b_wu, b_aT], writes=[b_psu[pi]])
                P.op("act", lambda e, pi=pi, N=N: e.activation(out=sg[pi][:, 0:N], in_=psg[pi][:, 0:N], func=AF.Silu),
                     reads=[b_psg[pi]], writes=[b_sg[pi]])
                P.op("dve", lambda e, pi=pi, j=j, N=N: e.tensor_mul(out=hidT[:, j, 0:N], in0=sg[pi][:, 0:N], in1=psu[pi][:, 0:N]),
                     reads=[b_sg[pi], b_psu[pi]], writes=[b_hid])
            for ti, t in enumerate(tiles):
                i = nr % 2; nr += 1
                cls = 1 if t in ctx_tiles else 0
                P.dma("sp", xr[i], h[t * 128:(t + 1) * 128, :], writes=[b_xr[i]])
                for cb in range(2):
                    for j in range(NJ):
                        P.op("pe", lambda e, j=j, cb=cb, ti=ti: e.matmul(
                            po[:, cb * 512:(cb + 1) * 512], lhsT=hidT[:, j, ti * 128:(ti + 1) * 128],
                            rhs=wd_sb[:, j, cb * 512:(cb + 1) * 512], start=(j == 0), stop=(j == NJ - 1)),
                            reads=[b_hid, b_wd], writes=[b_po])
                o_t, b_o = rn.run(po[:], b_po, t, xr[i], b_xr[i], GG[:, cls, :], b_gg)
                outs.append(P.dma("sp", out[t * 128:(t + 1) * 128, :], o_t[:], reads=[b_o]))
        P.finish_wait("sp", outs)
        P.emit()
    return nc


def conv_fm(P, eng, vin, b_in, wsb, j, bias_ap, b_w, acc, b_acc, K, T, t0=0):
    P.op(eng, lambda e: e.tensor_scalar(out=acc, in0=vin[:, t0:t0 + T], scalar1=wsb[:, j, 0:1], scalar2=bias_ap,
                                        op0=ALU.mult, op1=ALU.add), reads=[b_in, b_w], writes=[b_acc])
    for k in range(1, K):
        P.op(eng, lambda e, k=k: e.scalar_tensor_tensor(out=acc, in0=vin[:, t0 + k:t0 + k + T], scalar=wsb[:, j, k:k + 1],
                                                        in1=acc, op0=ALU.mult, op1=ALU.add),
             reads=[b_in, b_w, b_acc], writes=[b_acc])


def build_k2a(TL=8192, TC=256, NCH=5, K=5):
    nc = new_nc()
    H = K - 1
    TT = TL + TC + 2 * H
    pre = nc.dram_tensor("pre", [NCH * 128, TT], F32, kind="ExternalInput").ap()
    cw = nc.dram_tensor("cw", [128, NCH, K], F32, kind="ExternalInput").ap()
    cb = nc.dram_tensor("cb", [128, NCH], F32, kind="ExternalInput").ap()
    out = nc.dram_tensor("out", [NCH * 128, TL + TC], F32, kind="ExternalOutput").ap()
    BL = 2048
    with ExitStack() as st:
        P = Prog(nc, st)
        wsb = P.sb("wsb", [128, NCH, K], F32); bsb = P.sb("bsb", [128, NCH], F32); b_w = P.buf()
        P.dma("sp", wsb[:], cw, writes=[b_w])
        P.dma("sp", bsb[:], cb, writes=[b_w])
        vin = [P.sb(f"vin{i}", [128, TT], F32) for i in range(2)]; b_vin = [P.buf() for _ in range(2)]
        acc = [P.sb(f"acc{i}", [128, BL], F32) for i in range(2)]; b_acc = [P.buf() for _ in range(2)]
        res = [P.sb(f"res{i}", [128, BL], F32) for i in range(2)]; b_res = [P.buf() for _ in range(2)]
        outs = []
        n = 0
        for j in range(NCH):
            vi = j % 2
            P.dma("sp", vin[vi][:], pre[j * 128:(j + 1) * 128, :], writes=[b_vin[vi]])
            blocks = [(t0, BL, t0) for t0 in range(0, TL, BL)] + [(TL + H, TC, TL)]
            for (i0, T, o0) in blocks:
                i = n % 2; n += 1
                eng = "dve" if i == 0 else "pool"
                conv_fm(P, "dve", vin[vi], b_vin[vi], wsb, j, bsb[:, j:j + 1], b_w, acc[i][:, 0:T], b_acc[i], K, T, t0=i0)
                P.op("act", lambda e, i=i, T=T: e.activation(out=res[i][:, 0:T], in_=acc[i][:, 0:T], func=AF.Silu),
                     reads=[b_acc[i]], writes=[b_res[i]])
                outs.append(P.dma("sp", out[j * 128:(j + 1) * 128, o0:o0 + T], res[i][:, 0:T], reads=[b_res[i]]))
        P.finish_wait("sp", outs)
        P.emit()
    return nc


def build_k5(T=2048, K=31):
    nc = new_nc()
    H = K - 1
    TT = T + H
    uT = nc.dram_tensor("uT", [1024, TT], F32, kind="ExternalInput").ap()
    cw = nc.dram_tensor("cw", [128, 4, K], F32, kind="ExternalInput").ap()
    cb = nc.dram_tensor("cb", [128, 4], F32, kind="ExternalInput").ap()
    lng = nc.dram_tensor("lng", [128, 4], F32, kind="ExternalInput").ap()
    lnb = nc.dram_tensor("lnb", [128, 4], F32, kind="ExternalInput").ap()
    out = nc.dram_tensor("out", [512, T], F32, kind="ExternalOutput").ap()
    with ExitStack() as st:
        P = Prog(nc, st)
        wsb = P.sb("wsb", [128, 4, K], F32); bsb = P.sb("bsb", [128, 4], F32); b_w = P.buf()
        gsb = P.sb("gsb", [128, 4], F32); lbsb = P.sb("lbsb", [128, 4], F32)
        P.dma("sp", wsb[:], cw, writes=[b_w]); P.dma("sp", bsb[:], cb, writes=[b_w])
        P.dma("sp", gsb[:], lng, writes=[b_w]); P.dma("sp", lbsb[:], lnb, writes=[b_w])
        ones = P.sb("ones", [128, 128], F32); b_ones = P.buf()
        P.op("pool", lambda e: e.memset(ones[:], 1.0 / 512), writes=[b_ones])
        a_sb = [P.sb(f"a{i}", [128, TT], F32) for i in range(2)]; b_a = [P.buf() for _ in range(2)]
        g_sb = [P.sb(f"g{i}", [128, TT], F32) for i in range(2)]; b_g = [P.buf() for _ in range(2)]
        cv = P.sb("cv", [128, 4, T], F32); b_cv = [P.buf() for _ in range(4)]
        for j in range(4):
            i = j % 2
            P.dma("sp", a_sb[i][:], uT[j * 128:(j + 1) * 128, :], writes=[b_a[i]])
            P.dma("sp", g_sb[i][:], uT[512 + j * 128:512 + (j + 1) * 128, :], writes=[b_g[i]])
            P.op("act", lambda e, i=i: e.activation(out=g_sb[i][:], in_=g_sb[i][:], func=AF.Sigmoid), reads=[b_g[i]], writes=[b_g[i]])
            eng = "dve" if i == 0 else "pool"
            P.op(eng, lambda e, i=i: e.tensor_mul(out=a_sb[i][:], in0=a_sb[i][:], in1=g_sb[i][:]), reads=[b_a[i], b_g[i]], writes=[b_a[i]])
            conv_fm(P, "dve", a_sb[i], b_a[i], wsb, j, bsb[:, j:j + 1], b_w, cv[:, j, :], b_cv[j], K, T)
        sq = P.sb("sq", [128, 4, 512], F32); b_sq = P.buf()
        pm = [P.ps(f"pm{i}", [128, 512], F32) for i in range(2)]; b_pm = [P.buf() for _ in range(2)]
        pq = [P.ps(f"pq{i}", [128, 512], F32) for i in range(2)]; b_pq = [P.buf() for _ in range(2)]
        rstd = P.sb("rstd", [128, 512], F32); b_rstd = P.buf()
        msq = P.sb("msq", [128, 512], F32); b_msq = P.buf()
        xc = [P.sb(f"xc{i}", [128, 512], F32) for i in range(2)]; b_xc = [P.buf() for _ in range(2)]
        outs = []
        n = 0
        for tb in range(T // 512):
            sl = slice(tb * 512, (tb + 1) * 512)
            pi = tb % 2
            P.op("act", lambda e, sl=sl: e.activation(out=sq[:], in_=cv[:, :, sl], func=AF.Square), reads=b_cv, writes=[b_sq])
            for j in range(4):
                P.op("pe", lambda e, j=j, sl=sl, pi=pi: e.matmul(pm[pi][:], lhsT=ones[:], rhs=cv[:, j, sl], start=(j == 0), stop=(j == 3)),
                     reads=[b_ones, b_cv[j]], writes=[b_pm[pi]])
            for j in range(4):
                P.op("pe", lambda e, j=j, pi=pi: e.matmul(pq[pi][:], lhsT=ones[:], rhs=sq[:, j, :], start=(j == 0), stop=(j == 3)),
                     reads=[b_ones, b_sq], writes=[b_pq[pi]])
            P.op("act", lambda e, pi=pi: e.activation(out=msq[:], in_=pm[pi][:], func=AF.Square), reads=[b_pm[pi]], writes=[b_msq])
            P.op("dve", lambda e, pi=pi: e.scalar_tensor_tensor(out=rstd[:], in0=pq[pi][:], scalar=EPS, in1=msq[:], op0=ALU.add, op1=ALU.subtract),
                 reads=[b_pq[pi], b_msq], writes=[b_rstd])
            P.op("act", lambda e: e.activation(out=rstd[:], in_=rstd[:], func=AF.Sqrt), reads=[b_rstd], writes=[b_rstd])
            P.op("dve", lambda e: e.reciprocal(out=rstd[:], in_=rstd[:]), reads=[b_rstd], writes=[b_rstd])
            for j in range(4):
                i = n % 2; n += 1
                P.op("dve", lambda e, i=i, j=j, sl=sl, pi=pi: e.tensor_sub(out=xc[i][:], in0=cv[:, j, sl], in1=pm[pi][:]),
                     reads=[b_cv[j], b_pm[pi]], writes=[b_xc[i]])
                P.op("pool", lambda e, i=i: e.tensor_mul(out=xc[i][:], in0=xc[i][:], in1=rstd[:]), reads=[b_xc[i], b_rstd], writes=[b_xc[i]])
                P.op("act", lambda e, i=i, j=j: e.activation(out=xc[i][:], in_=xc[i][:], func=AF.Silu, scale=gsb[:, j:j + 1], bias=lbsb[:, j:j + 1]),
                     reads=[b_xc[i], b_w], writes=[b_xc[i]])
                outs.append(P.dma("sp", out[j * 128:(j + 1) * 128, sl], xc[i][:], reads=[b_xc[i]]))
        P.finish_wait("sp", outs)
        P.emit()
    return nc


def build_k4(lambda_init, NU=2, TL=8192, TC=256, NQB=None):
    nc = new_nc()
    NKT = (TL + TC) // 128
    NCT = TC // 128
    NLT = TL // 128
    if NQB is None:
        NQB = TL // 512
    qk = nc.dram_tensor("qk", [NU, 2, TL, 128], F32, kind="ExternalInput").ap()
    v = nc.dram_tensor("v", [NU, TL, 128], F32, kind="ExternalInput").ap()
    kc = nc.dram_tensor("kc", [NU, TC, 128], F32, kind="ExternalInput").ap()
    vc = nc.dram_tensor("vc", [NU, TC, 128], F32, kind="ExternalInput").ap()
    csd = nc.dram_tensor("cs", [TL, 128], F32, kind="ExternalInput").ap()
    snd = nc.dram_tensor("sn", [TL, 128], F32, kind="ExternalInput").ap()
    lamRd = nc.dram_tensor("lamR", [128, 4, 64], F32, kind="ExternalInput").ap()
    subRd = nc.dram_tensor("subR", [128, 128], F32, kind="ExternalInput").ap()
    identd = nc.dram_tensor("ident", [128, 128], BF16, kind="ExternalInput").ap()
    out = nc.dram_tensor("out", [NU, TL, 128], F32, kind="ExternalOutput").ap()
    with ExitStack() as st:
        P = Prog(nc, st)
        ident = P.sb("ident", [128, 128], BF16); b_ident = P.buf()
        P.dma("sp", ident[:], identd, writes=[b_ident])
        lam = P.sb("lam", [128, 4, 64], F32); b_lam = P.buf()
        P.dma("sp", lam[:], lamRd, writes=[b_lam])
        GS = P.sb("GS", [128, 128], F32); b_gs = P.buf()
        P.dma("sp", GS[:], subRd, writes=[b_gs])
        P.op("dve", lambda e: e.tensor_scalar_mul(out=GS[:], in0=GS[:], scalar1=float(1.0 - lambda_init)), reads=[b_gs], writes=[b_gs])
        lp = P.sb("lp", [128, 2, 64], F32); ls = P.sb("ls", [128, 4], F32); b_ls = P.buf()
        P.op("dve", lambda e: e.tensor_mul(out=lp[:, 0, :], in0=lam[:, 0, :], in1=lam[:, 1, :]), reads=[b_lam], writes=[b_ls])
        P.op("dve", lambda e: e.tensor_mul(out=lp[:, 1, :], in0=lam[:, 2, :], in1=lam[:, 3, :]), reads=[b_lam, b_ls], writes=[b_ls])
        P.op("dve", lambda e: e.reduce_sum(out=ls[:, 0:2], in_=lp[:], axis=AX.X), reads=[b_ls], writes=[b_ls])
        P.op("act", lambda e: e.activation(out=ls[:, 0:2], in_=ls[:, 0:2], func=AF.Exp), reads=[b_ls], writes=[b_ls])
        P.op("dve", lambda e: e.tensor_sub(out=ls[:, 2:3], in0=ls[:, 1:2], in1=ls[:, 0:1]), reads=[b_ls], writes=[b_ls])
        P.op("dve", lambda e: e.tensor_scalar_add(out=ls[:, 3:4], in0=ls[:, 2:3], scalar1=float(-lambda_init)), reads=[b_ls], writes=[b_ls])
        neglam = ls[:, 3:4]

        kT = P.sb("kT", [128, TL + TC], BF16); b_kT = P.buf()
        qT = P.sb("qT", [128, TL], BF16); b_qT = P.buf()
        vaug = P.sb("vaug", [128, NKT, 129], BF16); b_va = P.buf()
        P.op("pool", lambda e: e.memset(vaug[:, :, 128:129], 1.0), writes=[b_va])
        ld = {n: [P.sb(f"ld_{n}{i}", [128, 128], F32) for i in range(2)] for n in ("q", "k", "v", "cs", "sn")}
        b_ld = {n: [P.buf() for _ in range(2)] for n in ld}
        t1 = {n: [P.sb(f"t1_{n}{i}", [128, 128], F32) for i in range(2)] for n in ("q", "k")}
        t2 = {n: [P.sb(f"t2_{n}{i}", [128, 128], F32) for i in range(2)] for n in ("q", "k")}
        b_t1 = {n: [P.buf() for _ in range(2)] for n in t1}
        b_t2 = {n: [P.buf() for _ in range(2)] for n in t1}
        rb = {n: [P.sb(f"rb_{n}{i}", [128, 128], BF16) for i in range(2)] for n in ("q", "k")}
        b_rb = {n: [P.buf() for _ in range(2)] for n in rb}
        psT = [P.ps(f"psT{i}", [128, 128], BF16) for i in range(2)]; b_psT = [P.buf() for _ in range(2)]
        ps_s = [P.ps(f"ps_s{i}", [128, 512], F32) for i in range(2)]; b_ps_s = [P.buf() for _ in range(2)]
        acc = [P.ps(f"acc{i}", [128, 129], F32) for i in range(4)]; b_acc = [P.buf() for _ in range(4)]
        E = [P.sb(f"E{i}", [128, 512], BF16) for i in range(2)]; b_E = [P.buf() for _ in range(2)]
        att0 = [P.sb(f"att0_{i}", [128, 128], F32) for i in range(4)]; b_att0 = [P.buf() for _ in range(4)]
        att1 = [P.sb(f"att1_{i}", [128, 128], F32) for i in range(2)]; b_att1 = [P.buf() for _ in range(2)]
        rec = P.sb("rec", [128, 8], F32); b_rec = [P.buf() for _ in range(8)]
        junk = P.sb("junk", [128, 128], F32); b_junk = P.buf()
        sst = P.sb("sst", [128, 2], F32); b_sst = [P.buf() for _ in range(2)]
        v5 = lambda ap: ap.rearrange("p (c a h f) -> p c a h f", c=2, a=2, h=2, f=16)
        outs = []
        npt = [0]

        def transpose_to(src_bf, b_src, dst_ap, b_dst):
            pi = npt[0] % 2; npt[0] += 1
            P.op("pe", lambda e: e.transpose(out=psT[pi][:], in_=src_bf, identity=ident[:]), reads=[b_src, b_ident], writes=[b_psT[pi]])
            P.op("act", lambda e: e.copy(out=dst_ap, in_=psT[pi][:]), reads=[b_psT[pi]], writes=[b_dst])

        for u in range(NU):
            for j in range(NCT):
                i = j % 2
                P.dma("sp", ld["k"][i][:], kc[u, j * 128:(j + 1) * 128, :], writes=[b_ld["k"][i]])
                P.dma("sp", ld["v"][i][:], vc[u, j * 128:(j + 1) * 128, :], writes=[b_ld["v"][i]])
                P.op("dve", lambda e, i=i: e.tensor_copy(out=rb["k"][i][:], in_=ld["k"][i][:]), reads=[b_ld["k"][i]], writes=[b_rb["k"][i]])
                transpose_to(rb["k"][i][:], b_rb["k"][i], kT[:, j * 128:(j + 1) * 128], b_kT)
                P.op("pool", lambda e, i=i, j=j: e.tensor_copy(out=vaug[:, j, 0:128], in_=ld["v"][i][:]), reads=[b_ld["v"][i]], writes=[b_va])
            for j in range(NLT):
                i = j % 2
                sl = slice(j * 128, (j + 1) * 128)
                P.dma("sp", ld["q"][i][:], qk[u, 0, sl, :], writes=[b_ld["q"][i]])
                P.dma("sp", ld["k"][i][:], qk[u, 1, sl, :], writes=[b_ld["k"][i]])
                P.dma("sp", ld["v"][i][:], v[u, sl, :], writes=[b_ld["v"][i]])
                P.dma("sp", ld["cs"][i][:], csd[sl, :], writes=[b_ld["cs"][i]])
                P.dma("sp", ld["sn"][i][:], snd[sl, :], writes=[b_ld["sn"][i]])
                for n in ("q", "k"):
                    x = ld[n][i]; a1 = t1[n][i]; a2 = t2[n][i]
                    P.op("dve", lambda e, x=x, a1=a1, i=i: e.tensor_mul(out=a1[:], in0=x[:], in1=ld["cs"][i][:]),
                         reads=[b_ld[n][i], b_ld["cs"][i]], writes=[b_t1[n][i]])
                    P.op("pool", lambda e, x=x, a2=a2, i=i: e.tensor_mul(out=v5(a2[:])[:, :, :, 0, :], in0=v5(x[:])[:, :, :, 1, :],
                                                                         in1=v5(ld["sn"][i][:])[:, :, :, 0, :]),
                         reads=[b_ld[n][i], b_ld["sn"][i]], writes=[b_t2[n][i]])
                    P.op("pool", lambda e, x=x, a2=a2, i=i: e.tensor_mul(out=v5(a2[:])[:, :, :, 1, :], in0=v5(x[:])[:, :, :, 0, :],
                                                                         in1=v5(ld["sn"][i][:])[:, :, :, 1, :]),
                         reads=[b_ld[n][i], b_ld["sn"][i]], writes=[b_t2[n][i]])
                    P.op("dve", lambda e, a1=a1, a2=a2, n=n, i=i: e.tensor_add(out=rb[n][i][:], in0=a1[:], in1=a2[:]),
                         reads=[b_t1[n][i], b_t2[n][i]], writes=[b_rb[n][i]])
                transpose_to(rb["q"][i][:], b_rb["q"][i], qT[:, sl], b_qT)
                transpose_to(rb["k"][i][:], b_rb["k"][i], kT[:, TC + j * 128:TC + (j + 1) * 128], b_kT)
                P.op("pool", lambda e, i=i, j=j: e.tensor_copy(out=vaug[:, NCT + j, 0:128], in_=ld["v"][i][:]), reads=[b_ld["v"][i]], writes=[b_va])
            for qb in range(NQB):
                qsl = slice(qb * 512, (qb + 1) * 512)
                for c in range(2):
                    cp = slice(c * 64, (c + 1) * 64)

                    def score(kt, cp=cp, qsl=qsl):
                        pi = kt % 2
                        P.op("pe", lambda e, kt=kt, pi=pi, cp=cp, qsl=qsl: e.matmul(ps_s[pi][:], lhsT=kT[cp, kt * 128:(kt + 1) * 128], rhs=qT[cp, qsl],
                                                                     start=True, stop=True), reads=[b_kT, b_qT], writes=[b_ps_s[pi]])
                    score(0)
                    for kt in range(NKT):
                        pi = kt % 2
                        if kt + 1 < NKT:
                            score(kt + 1)
                        P.op("act", lambda e, pi=pi: e.activation(out=E[pi][:], in_=ps_s[pi][:], func=AF.Exp, scale=0.125),
                             reads=[b_ps_s[pi]], writes=[b_E[pi]])
                        for qs in range(4):
                            P.op("pe", lambda e, pi=pi, qs=qs, kt=kt: e.matmul(acc[qs][:], lhsT=E[pi][:, qs * 128:(qs + 1) * 128], rhs=vaug[:, kt, :],
                                                                              start=(kt == 0), stop=(kt == NKT - 1)),
                                 reads=[b_E[pi], b_va], writes=[b_acc[qs]])
                    for qs in range(4):
                        r = c * 4 + qs
                        P.op("dve", lambda e, qs=qs, r=r: e.reciprocal(out=rec[:, r:r + 1], in_=acc[qs][:, 128:129]), reads=[b_acc[qs]], writes=[b_rec[r]])
                        if c == 0:
                            P.op("dve", lambda e, qs=qs, r=r: e.tensor_scalar_mul(out=att0[qs][:], in0=acc[qs][:, 0:128], scalar1=rec[:, r:r + 1]),
                                 reads=[b_acc[qs], b_rec[r]], writes=[b_att0[qs]])
                        else:
                            ai = qs % 2
                            P.op("dve", lambda e, qs=qs, r=r, ai=ai: e.tensor_scalar_mul(out=att1[ai][:], in0=acc[qs][:, 0:128], scalar1=rec[:, r:r + 1]),
                                 reads=[b_acc[qs], b_rec[r]], writes=[b_att1[ai]])
                            P.op("dve", lambda e, qs=qs, ai=ai: e.scalar_tensor_tensor(out=att1[ai][:], in0=att1[ai][:], scalar=neglam, in1=att0[qs][:],
                                                                                       op0=ALU.mult, op1=ALU.add),
                                 reads=[b_att1[ai], b_att0[qs], b_ls], writes=[b_att1[ai]])
                            P.op("pool", lambda e, ai=ai: e.memset(sst[:, ai:ai + 1], 0.0), writes=[b_sst[ai]])
                            P.op("act", lambda e, ai=ai: e.activation(out=junk[:], in_=att1[ai][:], func=AF.Square, accum_out=sst[:, ai:ai + 1]),
                                 reads=[b_att1[ai], b_sst[ai]], writes=[b_junk, b_sst[ai]])
                            P.op("dve", lambda e, ai=ai: e.tensor_scalar(out=sst[:, ai:ai + 1], in0=sst[:, ai:ai + 1], scalar1=1.0 / 128, scalar2=EPS,
                                                                         op0=ALU.mult, op1=ALU.add), reads=[b_sst[ai]], writes=[b_sst[ai]])
                            P.op("act", lambda e, ai=ai: e.activation(out=sst[:, ai:ai + 1], in_=sst[:, ai:ai + 1], func=AF.Sqrt), reads=[b_sst[ai]], writes=[b_sst[ai]])
                            P.op("dve", lambda e, ai=ai: e.reciprocal(out=sst[:, ai:ai + 1], in_=sst[:, ai:ai + 1]), reads=[b_sst[ai]], writes=[b_sst[ai]])
                            P.op("dve", lambda e, ai=ai: e.scalar_tensor_tensor(out=att1[ai][:], in0=att1[ai][:], scalar=sst[:, ai:ai + 1], in1=GS[:],
                                                                                op0=ALU.mult, op1=ALU.mult),
                                 reads=[b_att1[ai], b_sst[ai], b_gs], writes=[b_att1[ai]])
                            r0 = qb * 512 + qs * 128
                            outs.append(P.dma("sp", out[u, r0:r0 + 128, :], att1[ai][:], reads=[b_att1[ai]]))
        P.finish_wait("sp", outs)
        P.emit()
    return nc


def build_kf(TL=8192, TC=256):
    nc = new_nc()
    NL = TL // 128
    NCt = TC // 128
    GT = nc.dram_tensor("GT", [128, TL + TC], F32, kind="ExternalInput").ap()
    ccsc = nc.dram_tensor("ccsc", [128, 256], F32, kind="ExternalInput").ap()
    CTd = nc.dram_tensor("CT", [TL, TL], BF16, kind="ExternalInput").ap()
    STd = nc.dram_tensor("ST", [TL, TL], BF16, kind="ExternalInput").ap()
    CTc = nc.dram_tensor("CTc", [TC, TC], BF16, kind="ExternalInput").ap()
    STc = nc.dram_tensor("STc", [TC, TC], BF16, kind="ExternalInput").ap()
    out = nc.dram_tensor("out", [128, TL + TC], F32, kind="ExternalOutput").ap()
    with ExitStack() as st:
        P = Prog(nc, st)
        g32 = P.sb("g32", [128, TL + TC], F32); b_g32 = P.buf()
        P.dma("sp", g32[:], GT, writes=[b_g32])
        gbf = P.sb("gbf", [128, TL + TC], BF16); b_gbf = P.buf()
        P.op("dve", lambda e: e.tensor_copy(out=gbf[:], in_=g32[:]), reads=[b_g32], writes=[b_gbf])
        cc32 = P.sb("cc32", [128, 256], F32); ccb = P.sb("ccb", [128, 256], BF16); b_cc = P.buf()
        P.dma("sp", cc32[:], ccsc, writes=[b_cc])
        P.op("dve", lambda e: e.tensor_copy(out=ccb[:], in_=cc32[:]), reads=[b_cc], writes=[b_cc])
        H = P.sb("H", [128, NL + NCt, 256], BF16); b_H = P.buf()
        ph = [P.ps(f"ph{i}", [128, 256], F32) for i in range(2)]; b_ph = [P.buf() for _ in range(2)]
        for j in range(NL + NCt):
            pi = j % 2
            P.op("pe", lambda e, j=j, pi=pi: e.matmul(ph[pi][:], lhsT=gbf[:, j * 128:(j + 1) * 128], rhs=ccb[:], start=True, stop=True),
                 reads=[b_gbf, b_cc], writes=[b_ph[pi]])
            if pi == 0:
                P.op("act", lambda e, j=j, pi=pi: e.copy(out=H[:, j, :], in_=ph[pi][:]), reads=[b_ph[pi]], writes=[b_H])
            else:
                P.op("dve", lambda e, j=j, pi=pi: e.tensor_copy(out=H[:, j, :], in_=ph[pi][:]), reads=[b_ph[pi]], writes=[b_H])
        JB = 16
        cbuf = [P.sb(f"cbuf{i}", [128, JB, 512], BF16) for i in range(2)]; b_cbuf = [P.buf() for _ in range(2)]
        sbuf_ = [P.sb(f"sbuf{i}", [128, JB, 512], BF16) for i in range(2)]; b_sbuf = [P.buf() for _ in range(2)]
        po = [P.ps(f"po{i}", [128, 512], F32) for i in range(2)]; b_po = [P.buf() for _ in range(2)]
        ot = [P.sb(f"ot{i}", [128, 512], F32) for i in range(2)]; b_ot = [P.buf() for _ in range(2)]
        CTv = CTd.rearrange("(j p) f -> p j f", p=128)
        STv = STd.rearrange("(j p) f -> p j f", p=128)
        outs = []
        nb = 0
        for fb in range(TL // 512):
            fsl = slice(fb * 512, (fb + 1) * 512)
            pi = fb % 2
            for j0 in range(0, NL, JB):
                bi = nb % 2; nb += 1
                P.dma("sp", cbuf[bi][:], CTv[:, j0:j0 + JB, fsl], writes=[b_cbuf[bi]])
                P.dma("pool", sbuf_[bi][:], STv[:, j0:j0 + JB, fsl], writes=[b_sbuf[bi]])
                for jj in range(JB):
                    j = j0 + jj
                    P.op("pe", lambda e, j=j, jj=jj, bi=bi, pi=pi: e.matmul(po[pi][:], lhsT=H[:, j, 0:128], rhs=cbuf[bi][:, jj, :],
                                                                         start=(j == 0), stop=False), reads=[b_H, b_cbuf[bi]], writes=[b_po[pi]])
                    P.op("pe", lambda e, j=j, jj=jj, bi=bi, pi=pi: e.matmul(po[pi][:], lhsT=H[:, j, 128:256], rhs=sbuf_[bi][:, jj, :],
                                                                         start=False, stop=(j == NL - 1)), reads=[b_H, b_sbuf[bi]], writes=[b_po[pi]])
            P.op("act", lambda e, pi=pi: e.copy(out=ot[pi][:], in_=po[pi][:]), reads=[b_po[pi]], writes=[b_ot[pi]])
            outs.append(P.dma("sp", out[:, fsl], ot[pi][:], reads=[b_ot[pi]]))
        cc_ = P.sb("cc_", [128, NCt, TC], BF16); sc_ = P.sb("sc_", [128, NCt, TC], BF16); b_c2 = P.buf()
        P.dma("sp", cc_[:], CTc.rearrange("(j p) f -> p j f", p=128), writes=[b_c2])
        P.dma("sp", sc_[:], STc.rearrange("(j p) f -> p j f", p=128), writes=[b_c2])
        pi = 0
        for j in range(NCt):
            P.op("pe", lambda e, j=j: e.matmul(po[pi][:, 0:TC], lhsT=H[:, NL + j, 0:128], rhs=cc_[:, j, :], start=(j == 0), stop=False),
                 reads=[b_H, b_c2], writes=[b_po[pi]])
            P.op("pe", lambda e, j=j: e.matmul(po[pi][:, 0:TC], lhsT=H[:, NL + j, 128:256], rhs=sc_[:, j, :], start=False, stop=(j == NCt - 1)),
                 reads=[b_H, b_c2], writes=[b_po[pi]])
        P.op("act", lambda e: e.copy(out=ot[pi][:, 0:TC], in_=po[pi][:, 0:TC]), reads=[b_po[pi]], writes=[b_ot[pi]])
        outs.append(P.dma("sp", out[:, TL:TL + TC], ot[pi][:, 0:TC], reads=[b_ot[pi]]))
        P.finish_wait("sp", outs)
        P.emit()
    return nc


def build_ssd(NCH=66, NCTX=2):
    nc = new_nc()
    T = NCH * 128
    NC6 = NCH * 6
    xs = nc.dram_tensor("xs", [T, 384], F32, kind="ExternalInput").ap()
    Btm = nc.dram_tensor("Btm", [T, 128], F32, kind="ExternalInput").ap()
    BfT = nc.dram_tensor("BfT", [128, T], F32, kind="ExternalInput").ap()
    CfT = nc.dram_tensor("CfT", [128, T], F32, kind="ExternalInput").ap()
    dtd = nc.dram_tensor("dt", [128, 2, NCH, 6], F32, kind="ExternalInput").ap()
    prm = nc.dram_tensor("prm", [128, 3, 2, NCH, 6], F32, kind="ExternalInput").ap()
    trid = nc.dram_tensor("tri", [128, 3, 128], F32, kind="ExternalInput").ap()
    y = nc.dram_tensor("y", [2, T, 384], F32, kind="ExternalOutput").ap()
    with ExitStack() as st:
        P = Prog(nc, st)
        tri = P.sb("tri", [128, 3, 128], F32); b_tri = P.buf()
        P.dma("sp", tri[:], trid, writes=[b_tri])
        prs = P.sb("prs", [128, 3, 2, NCH, 6], F32); b_prs = P.buf()
        P.dma("sp", prs[:], prm, writes=[b_prs])
        dts = P.sb("dts", [128, 2, NCH, 6], F32); b_dts = P.buf()
        P.dma("sp", dts[:], dtd, writes=[b_dts])
        stg = P.sb("stg", [128, T], F32); b_stg = P.buf()
        Bf = P.sb("Bf", [128, T], BF16); Cf = P.sb("Cf", [128, T], BF16); Bt = P.sb("Bt", [128, NCH, 128], BF16)
        b_Bf, b_Cf, b_Bt = P.buf(), P.buf(), P.buf()
        P.dma("sp", stg[:], BfT, writes=[b_stg])
        P.op("dve", lambda e: e.tensor_copy(out=Bf[:], in_=stg[:]), reads=[b_stg], writes=[b_Bf])
        P.dma("sp", stg[:], CfT, writes=[b_stg])
        P.op("pool", lambda e: e.tensor_copy(out=Cf[:], in_=stg[:]), reads=[b_stg], writes=[b_Cf])
        P.dma("sp", stg[:].rearrange("p (c n) -> p c n", n=128), Btm.rearrange("(c p) n -> p c n", p=128), writes=[b_stg])
        P.op("dve", lambda e: e.tensor_copy(out=Bt[:], in_=stg[:].rearrange("p (c n) -> p c n", n=128)), reads=[b_stg], writes=[b_Bt])
        dtsp = P.sb("dtsp", [128, 2, NCH, 6], F32); dta = P.sb("dta", [128, 2, NCH, 6], F32); b_dt = P.buf()
        P.op("dve", lambda e: e.tensor_add(out=dtsp[:], in0=dts[:], in1=prs[:, 0]), reads=[b_dts, b_prs], writes=[b_dt])
        P.op("act", lambda e: e.activation(out=dtsp[:], in_=dtsp[:], func=AF.Exp), reads=[b_dt], writes=[b_dt])
        P.op("dve", lambda e: e.tensor_scalar_add(out=dtsp[:], in0=dtsp[:], scalar1=1.0), reads=[b_dt], writes=[b_dt])
        P.op("act", lambda e: e.activation(out=dtsp[:], in_=dtsp[:], func=AF.Ln), reads=[b_dt], writes=[b_dt])
        P.op("act", lambda e: e.activation(out=dta[:], in_=prs[:, 1], func=AF.Exp), reads=[b_prs, b_dt], writes=[b_dt])
        P.op("dve", lambda e: e.scalar_tensor_tensor(out=dta[:], in0=dta[:], scalar=-1.0, in1=dtsp[:], op0=ALU.mult, op1=ALU.mult),
             reads=[b_dt], writes=[b_dt])
        acs = P.sb("acs", [128, 2, NCH, 6], F32); eacs = P.sb("eacs", [128, 2, NCH, 6], F32)
        tend = P.sb("tend", [128, 2, NCH, 6], F32); cdec = P.sb("cdec", [128, 2, NCH, 6], F32); b_ac = P.buf()
        pb = [P.ps(f"pb{i}", [128, NC6], F32) for i in range(2)]; b_pb = [P.buf() for _ in range(2)]
        for d in range(2):
            P.op("pe", lambda e, d=d: e.matmul(pb[0][:], lhsT=tri[:, d, :], rhs=dta[:, d].rearrange("p c h -> p (c h)"), start=True, stop=True),
                 reads=[b_tri, b_dt], writes=[b_pb[0]])
            P.op("pe", lambda e, d=d: e.matmul(pb[1][:], lhsT=tri[:, 2, :], rhs=dta[:, d].rearrange("p c h -> p (c h)"), start=True, stop=True),
                 reads=[b_tri, b_dt], writes=[b_pb[1]])
            av = lambda t, d=d: t[:, d].rearrange("p c h -> p (c h)")
            P.op("dve", lambda e, d=d, av=av: e.tensor_copy(out=av(acs), in_=pb[0][:]), reads=[b_pb[0]], writes=[b_ac])
            P.op("act", lambda e, d=d, av=av: e.activation(out=av(eacs), in_=pb[0][:], func=AF.Exp), reads=[b_pb[0]], writes=[b_ac])
            P.op("act", lambda e, d=d, av=av: e.activation(out=av(cdec), in_=pb[1][:], func=AF.Exp), reads=[b_pb[1]], writes=[b_ac])
            P.op("dve", lambda e, d=d, av=av: e.tensor_sub(out=av(tend), in0=pb[1][:], in1=av(acs)), reads=[b_pb[1], b_ac], writes=[b_ac])
            P.op("act", lambda e, d=d, av=av: e.activation(out=av(tend), in_=av(tend), func=AF.Exp), reads=[b_ac], writes=[b_ac])
        xt = [P.sb(f"xt{i}", [128, 6, 64], F32) for i in range(2)]; b_xt = [P.buf() for _ in range(2)]
        Dm = P.sb("Dm", [128, 6, 128], F32); b_Dm = P.buf()
        pR = [P.ps(f"pR{i}", [128, 3, 128], F32) for i in range(2)]; b_pR = [P.buf() for _ in range(2)]
        pcb = P.ps("pcb", [128, 128], F32); b_pcb = P.buf()
        cbU = P.sb("cbU", [128, 128], F32); b_cbU = P.buf()
        arg = [P.sb(f"arg{i}", [128, 128], F32) for i in range(2)]; b_arg = [P.buf() for _ in range(2)]
        Wh = P.sb("Wh", [128, 6, 128], BF16); b_Wh = P.buf()
        xdt = P.sb("xdt", [128, 6, 64], BF16); b_xdt = P.buf()
        xdtE = P.sb("xdtE", [128, 6, 64], BF16); b_xdtE = P.buf()
        py = P.ps("py", [128, 6, 64], F32); b_py = P.buf()
        pyo = P.ps("pyo", [128, 6, 64], F32); b_pyo = P.buf()
        pst = P.ps("pst", [128, 6, 64], F32); b_pst = P.buf()
        t1 = P.sb("t1", [128, 6, 64], F32); b_t1 = P.buf()
        yo = [P.sb(f"yo{i}", [128, 6, 64], F32) for i in range(2)]; b_yo = [P.buf() for _ in range(2)]
        S = P.sb("S", [128, 6, 64], F32); Sb = P.sb("Sb", [128, 6, 64], BF16); b_S = P.buf(); b_Sb = P.buf()
        outs = []
        n = 0
        for d in range(2):
            ctx_order = list(range(NCTX)) if d == 0 else list(range(NCTX - 1, -1, -1))
            lat_order = list(range(NCTX, NCH)) if d == 0 else list(range(NCH - 1, NCTX - 1, -1))
            P.op("pool", lambda e: e.memset(S[:], 0.0), writes=[b_S])
            P.op("pool", lambda e: e.memset(Sb[:], 0.0), writes=[b_Sb])
            for c in ctx_order + lat_order:
                i = n % 2; n += 1
                csl = slice(c * 128, (c + 1) * 128)
                P.dma("sp", xt[i][:].rearrange("p h e -> p (h e)"), xs[csl, :], writes=[b_xt[i]])
                for h in range(6):
                    eng = "dve" if h % 2 == 0 else "pool"
                    P.op(eng, lambda e, h=h, c=c, d=d: e.tensor_scalar_mul(out=Dm[:, h, :], in0=tri[:, d, :], scalar1=dta[:, d, c, h:h + 1]),
                         reads=[b_tri, b_dt], writes=[b_Dm])
                for hh in range(2):
                    P.op("pe", lambda e, hh=hh: e.matmul(pR[hh][:].rearrange("p h l -> p (h l)"), lhsT=tri[:, 2, :],
                                                         rhs=Dm[:, hh * 3:(hh + 1) * 3, :].rearrange("p h l -> p (h l)"), start=True, stop=True),
                         reads=[b_tri, b_Dm], writes=[b_pR[hh]])
                P.op("pe", lambda e, csl=csl: e.matmul(pcb[:], lhsT=Bf[:, csl], rhs=Cf[:, csl], start=True, stop=True),
                     reads=[b_Bf, b_Cf], writes=[b_pcb])
                P.op("dve", lambda e, d=d: e.tensor_mul(out=cbU[:], in0=pcb[:], in1=tri[:, d, :]), reads=[b_pcb, b_tri], writes=[b_cbU])
                for h in range(6):
                    ai = h % 2
                    P.op("dve", lambda e, h=h, c=c, d=d, ai=ai: e.tensor_scalar(out=arg[ai][:], in0=pR[h // 3][:, h % 3, :], scalar1=acs[:, d, c, h:h + 1],
                                                                                scalar2=0.0, op0=ALU.subtract, op1=ALU.min),
                         reads=[b_pR[h // 3], b_ac], writes=[b_arg[ai]])
                    P.op("act", lambda e, ai=ai: e.activation(out=arg[ai][:], in_=arg[ai][:], func=AF.Exp), reads=[b_arg[ai]], writes=[b_arg[ai]])
                    P.op("pool", lambda e, h=h, ai=ai: e.tensor_mul(out=Wh[:, h, :], in0=arg[ai][:], in1=cbU[:]), reads=[b_arg[ai], b_cbU], writes=[b_Wh])
                    P.op("pool", lambda e, h=h, c=c, d=d, i=i: e.tensor_scalar_mul(out=xdt[:, h, :], in0=xt[i][:, h, :], scalar1=dtsp[:, d, c, h:h + 1]),
                         reads=[b_xt[i], b_dt], writes=[b_xdt])
                    P.op("pool", lambda e, h=h, c=c, d=d: e.tensor_scalar_mul(out=xdtE[:, h, :], in0=xdt[:, h, :], scalar1=tend[:, d, c, h:h + 1]),
                         reads=[b_xdt, b_ac], writes=[b_xdtE])
                for h in range(6):
                    P.op("pe", lambda e, h=h: e.matmul(py[:, h, :], lhsT=Wh[:, h, :], rhs=xdt[:, h, :], start=True, stop=True),
                         reads=[b_Wh, b_xdt], writes=[b_py])
                P.op("pe", lambda e, csl=csl: e.matmul(pyo[:].rearrange("p h e -> p (h e)"), lhsT=Cf[:, csl], rhs=Sb[:].rearrange("p h e -> p (h e)"),
                                                       start=True, stop=True), reads=[b_Cf, b_Sb], writes=[b_pyo])
                for h in range(6):
                    P.op("dve", lambda e, h=h, c=c, d=d: e.tensor_scalar_mul(out=t1[:, h, :], in0=pyo[:, h, :], scalar1=eacs[:, d, c, h:h + 1]),
                         reads=[b_pyo, b_ac], writes=[b_t1])
                    P.op("dve", lambda e, h=h, c=c, d=d, i=i: e.scalar_tensor_tensor(out=t1[:, h, :], in0=xt[i][:, h, :], scalar=prs[:, 2, d, c, h:h + 1],
                                                                                     in1=t1[:, h, :], op0=ALU.mult, op1=ALU.add),
                         reads=[b_xt[i], b_prs, b_t1], writes=[b_t1])
                P.op("dve", lambda e, i=i: e.tensor_add(out=yo[i][:], in0=t1[:], in1=py[:]), reads=[b_t1, b_py], writes=[b_yo[i]])
                outs.append(P.dma("sp", y[d, csl, :], yo[i][:].rearrange("p h e -> p (h e)"), reads=[b_yo[i]]))
                P.op("pe", lambda e, c=c: e.matmul(pst[:].rearrange("p h e -> p (h e)"), lhsT=Bt[:, c, :], rhs=xdtE[:].rearrange("p h e -> p (h e)"),
                                                   start=True, stop=True), reads=[b_Bt, b_xdtE], writes=[b_pst])
                for h in range(6):
                    P.op("dve", lambda e, h=h, c=c, d=d: e.scalar_tensor_tensor(out=S[:, h, :], in0=S[:, h, :], scalar=cdec[:, d, c, h:h + 1],
                                                                                in1=pst[:, h, :], op0=ALU.mult, op1=ALU.add),
                         reads=[b_S, b_ac, b_pst], writes=[b_S])
                P.op("act", lambda e: e.copy(out=Sb[:], in_=S[:]), reads=[b_S], writes=[b_Sb])
        P.finish_wait("sp", outs)
        P.emit()
    return nc


def build_k2c(NT):
    nc = new_nc()
    W = 1536
    y2 = nc.dram_tensor("y2", [2, NT * 128, W], F32, kind="ExternalInput").ap()
    z = nc.dram_tensor("z", [NT * 128, W], F32, kind="ExternalInput").ap()
    gnR = nc.dram_tensor("gnR", [128, W], F32, kind="ExternalInput").ap()
    out = nc.dram_tensor("out", [NT * 128, W], F32, kind="ExternalOutput").ap()
    with ExitStack() as st:
        P = Prog(nc, st)
        gn = P.sb("gn", [128, W], F32); b_gn = P.buf()
        P.dma("sp", gn[:], gnR, writes=[b_gn])
        ss = P.sb("ss", [128, NT, 4], F32); b_ss0 = P.buf(); b_ss = [P.buf() for _ in range(NT)]
        P.op("pool", lambda e: e.memset(ss[:], 0.0), writes=[b_ss0])
        junk = P.sb("junk", [128, 384], F32); b_junk = P.buf()
        y0 = [P.sb(f"y0_{i}", [128, W], F32) for i in range(2)]; b_y0 = [P.buf() for _ in range(2)]
        y1 = [P.sb(f"y1_{i}", [128, W], F32) for i in range(2)]; b_y1 = [P.buf() for _ in range(2)]
        zt = [P.sb(f"zt{i}", [128, W], F32) for i in range(2)]; b_zt = [P.buf() for _ in range(2)]
        ot = [P.sb(f"ot{i}", [128, W], F32) for i in range(2)]; b_ot = [P.buf() for _ in range(2)]
        outs = []
        for t in range(NT):
            i = t % 2
            rsl = slice(t * 128, (t + 1) * 128)
            P.dma("sp", y0[i][:], y2[0, rsl, :], writes=[b_y0[i]])
            P.dma("sp", y1[i][:], y2[1, rsl, :], writes=[b_y1[i]])
            P.dma("sp", zt[i][:], z[rsl, :], writes=[b_zt[i]])
            P.op("act", lambda e, i=i: e.activation(out=zt[i][:], in_=zt[i][:], func=AF.Silu), reads=[b_zt[i]], writes=[b_zt[i]])
            P.op("pool", lambda e, i=i: e.tensor_add(out=y0[i][:], in0=y0[i][:], in1=y1[i][:]), reads=[b_y0[i], b_y1[i]], writes=[b_y0[i]])
            P.op("dve", lambda e, i=i: e.tensor_mul(out=y0[i][:], in0=y0[i][:], in1=zt[i][:]), reads=[b_y0[i], b_zt[i]], writes=[b_y0[i]])
            for g in range(4):
                P.op("act", lambda e, i=i, g=g, t=t: e.activation(out=junk[:], in_=y0[i][:, g * 384:(g + 1) * 384], func=AF.Square,
                                                                  accum_out=ss[:, t, g:g + 1]),
                     reads=[b_y0[i], b_ss0], writes=[b_junk, b_ss[t]])
            P.op("dve", lambda e, t=t: e.tensor_scalar(out=ss[:, t, :], in0=ss[:, t, :], scalar1=1.0 / 384, scalar2=EPS, op0=ALU.mult, op1=ALU.add),
                 reads=[b_ss[t]], writes=[b_ss[t]])
            P.op("act", lambda e, t=t: e.activation(out=ss[:, t, :], in_=ss[:, t, :], func=AF.Sqrt), reads=[b_ss[t]], writes=[b_ss[t]])
            P.op("dve", lambda e, t=t: e.reciprocal(out=ss[:, t, :], in_=ss[:, t, :]), reads=[b_ss[t]], writes=[b_ss[t]])
            for g in range(4):
                gs = slice(g * 384, (g + 1) * 384)
                P.op("dve", lambda e, i=i, g=g, gs=gs, t=t: e.scalar_tensor_tensor(out=ot[i][:, gs], in0=y0[i][:, gs], scalar=ss[:, t, g:g + 1], in1=gn[:, gs],
                                                                                   op0=ALU.mult, op1=ALU.mult),
                     reads=[b_y0[i], b_ss[t], b_gn], writes=[b_ot[i]])
            outs.append(P.dma("sp", out[rsl, :], ot[i][:], reads=[b_ot[i]]))
        P.finish_wait("sp", outs)
        P.emit()
    return nc


def build_k0():
    nc = new_nc()
    NCOL = 768
    cT = nc.dram_tensor("cT", [128, 8, 3], F32, kind="ExternalInput").ap()
    mw = nc.dram_tensor("mw", [2, 1024, NCOL], F32, kind="ExternalInput").ap()
    mb = nc.dram_tensor("mb", [1, 2, NCOL], F32, kind="ExternalInput").ap()
    out = nc.dram_tensor("out", [3, 2, NCOL], F32, kind="ExternalOutput").ap()
    with ExitStack() as st:
        P = Prog(nc, st)
        ct_sb = P.sb("ct_sb", [128, 8, 3], F32)
        sc_sb = P.sb("sc_sb", [128, 8, 3], F32)
        w_sb = P.sb("w_sb", [128, 2, 8, NCOL], F32)
        b_sb = P.sb("b_sb", [1, 2, NCOL], F32)
        ones = P.sb("ones", [1, 4], F32)
        o_sb = P.sb("o_sb", [3, 2, NCOL], F32)
        ps = [P.ps(f"ps{i}", [3, 512], F32) for i in range(4)]
        b_ct, b_sc, b_w, b_b, b_ones, b_o = [P.buf() for _ in range(6)]
        b_ps = [P.buf() for _ in range(4)]

        P.dma("sp", ct_sb[:], cT, writes=[b_ct])
        P.dma("sp", w_sb[:], mw.rearrange("l (k p) n -> p l k n", p=128), writes=[b_w])
        P.dma("sp", b_sb[:], mb, writes=[b_b])
        P.op("dve", lambda e: e.memset(ones[:], 1.0), writes=[b_ones])
        P.op("act", lambda e: e.activation(out=sc_sb[:], in_=ct_sb[:], func=AF.Silu), reads=[b_ct], writes=[b_sc])
        pi = 0
        for l in range(2):
            for (c0, cn) in ((0, 512), (512, 256)):
                pt = ps[pi]; bp = b_ps[pi]; pi += 1
                for k in range(8):
                    P.op("pe", lambda e, pt=pt, k=k, l=l, c0=c0, cn=cn: e.matmul(
                        pt[:, 0:cn], lhsT=sc_sb[:, k, :], rhs=w_sb[:, l, k, c0:c0 + cn], start=(k == 0), stop=False),
                        reads=[b_sc, b_w], writes=[bp])
                P.op("pe", lambda e, pt=pt, l=l, c0=c0, cn=cn: e.matmul(
                    pt[:, 0:cn], lhsT=ones[:, 0:3], rhs=b_sb[:, l, c0:c0 + cn], start=False, stop=True),
                    reads=[b_ones, b_b], writes=[bp])
                P.op("dve", lambda e, pt=pt, l=l, c0=c0, cn=cn: e.tensor_copy(out=o_sb[:, l, c0:c0 + cn], in_=pt[:, 0:cn]),
                     reads=[bp], writes=[b_o])
        t = P.dma("sp", out, o_sb[:], reads=[b_o])
        P.finish_wait("sp", [t])
        P.emit()
    return nc


import math

G4 = [[0, 1, 2, 3], [4, 5, 6, 7]]
TL, TC = 8192, 256
TA = TL + TC
NTA = TA // 128
FMW = TA + 8
LAT0 = 262
I32 = mybir.dt.int32


def din(nc, name, shape, dt=F32):
    return nc.dram_tensor(name, list(shape), dt, kind="ExternalInput").ap()


def dscr(nc, name, shape, dt=F32):
    return nc.dram_tensor(name, list(shape), dt).ap()


def phase_mods(P, cT2, mw, mb, selc, modT_d, gate_d):
    P.push_scope()
    sc = P.sb("sc", [128, 8, 2], F32); b_sc = P.buf()
    P.dma("sp", sc[:], cT2, writes=[b_sc])
    P.op("act", lambda e: e.activation(out=sc[:], in_=sc[:], func=AF.Silu), reads=[b_sc], writes=[b_sc])
    sel = P.sb("sel", [2, 2 + 256], F32); b_sel = P.buf()
    P.dma("sp", sel[:], selc, writes=[b_sel])
    ones = P.sb("ones", [1, 2], F32); b_ones = P.buf()
    P.op("dve", lambda e: e.memset(ones[:], 1.0), writes=[b_ones])
    bsb = P.sb("bsb", [1, 2, 6144], F32); b_b = P.buf()
    P.dma("sp", bsb[:], mb, writes=[b_b])
    wb = [P.sb(f"wb{i}", [128, 8, 1024], F32) for i in range(2)]; b_wb = [P.buf() for _ in range(2)]
    row = [P.sb(f"row{i}", [2, 1024], F32) for i in range(2)]; b_row = [P.buf() for _ in range(2)]
    prow = [P.ps(f"prow{i}", [2, 512], F32) for i in range(2)]; b_prow = [P.buf() for _ in range(2)]
    pT = P.ps("pT", [128, 2, 8], F32); b_pT = P.buf()
    prep = [P.ps(f"prep{i}", [128, 512], F32) for i in range(2)]; b_prep = [P.buf() for _ in range(2)]
    modT = P.sb("modT", [128, 2, 2, 6, 8], F32); b_modT = P.buf()
    rep = [P.sb(f"rep{i}", [128, 1024], F32) for i in range(2)]; b_rep = [P.buf() for _ in range(2)]
    n = 0
    nr = 0
    outs = []
    for l in range(2):
        wv = mw[l].rearrange("(k p) n -> p k n", p=128)
        for jb in range(6):
            i = n % 2; n += 1
            P.dma("sp", wb[i][:], wv[:, :, jb * 1024:(jb + 1) * 1024], writes=[b_wb[i]])
            for half in range(2):
                c0 = half * 512
                for k in range(8):
                    P.op("pe", lambda e, i=i, k=k, c0=c0, half=half: e.matmul(prow[half][:], lhsT=sc[:, k, :], rhs=wb[i][:, k, c0:c0 + 512],
                                                                               start=(k == 0), stop=False),
                         reads=[b_sc, b_wb[i]], writes=[b_prow[half]])
                P.op("pe", lambda e, l=l, jb=jb, c0=c0, half=half: e.matmul(prow[half][:], lhsT=ones[:, 0:2], rhs=bsb[:, l, jb * 1024 + c0:jb * 1024 + c0 + 512],
                                                                           start=False, stop=True),
                     reads=[b_ones, b_b], writes=[b_prow[half]])
                P.op("dve", lambda e, i=i, c0=c0, half=half: e.tensor_copy(out=row[i][:, c0:c0 + 512], in_=prow[half][:]),
                     reads=[b_prow[half]], writes=[b_row[i]])
            for cls in range(2):
                for k in range(8):
                    P.op("pe", lambda e, i=i, cls=cls, k=k: e.matmul(pT[:, cls, k:k + 1], lhsT=row[i][:, k * 128:(k + 1) * 128], rhs=sel[:, cls:cls + 1],
                                                                     start=True, stop=True), reads=[b_row[i], b_sel], writes=[b_pT])
            P.op("dve", lambda e, l=l, jb=jb: e.tensor_copy(out=modT[:, l, :, jb, :], in_=pT[:]), reads=[b_pT], writes=[b_modT])
            if jb in (2, 5):
                for cls in range(2):
                    ri = nr % 2; nr += 1
                    for half in range(2):
                        c0 = half * 512
                        P.op("pe", lambda e, i=i, cls=cls, c0=c0, half=half: e.matmul(prep[half][:], lhsT=sel[:, 2 + cls * 128:2 + (cls + 1) * 128],
                                                                                    rhs=row[i][:, c0:c0 + 512], start=True, stop=True),
                             reads=[b_row[i], b_sel], writes=[b_prep[half]])
                        P.op("act", lambda e, ri=ri, c0=c0, half=half: e.copy(out=rep[ri][:, c0:c0 + 512], in_=prep[half][:]),
                             reads=[b_prep[half]], writes=[b_rep[ri]])
                    outs.append(P.dma("sp", gate_d[l, cls, 0 if jb == 2 else 1], rep[ri][:], reads=[b_rep[ri]]))
    for l in range(2):
        outs.append(P.dma("sp", modT_d[l], modT[:, l], reads=[b_modT]))
    P.barrier()
    P.pop_scope()


def load_mod(P, modT_dl, gT_dram, j_shift, j_scale):
    modsb = P.sb("modsb", [128, 2, 6, 8], F32); b_mod = P.buf()
    P.dma("sp", modsb[:], modT_dl, writes=[b_mod])
    gsb = P.sb("gsb", [128, 8], F32); b_g = P.buf()
    P.dma("sp", gsb[:], gT_dram, writes=[b_g])
    Gs = P.sb("Gs", [128, 2, 8], F32); Sh = P.sb("Sh", [128, 2, 8], F32); b_gs = P.buf()
    for cls in range(2):
        P.op("dve", lambda e, cls=cls: e.scalar_tensor_tensor(out=Gs[:, cls, :], in0=modsb[:, cls, j_scale, :], scalar=1.0,
                                                               in1=gsb[:], op0=ALU.add, op1=ALU.mult), reads=[b_mod, b_g], writes=[b_gs])
        P.op("dve", lambda e, cls=cls: e.tensor_copy(out=Sh[:, cls, :], in_=modsb[:, cls, j_shift, :]), reads=[b_mod], writes=[b_gs])
    return Gs, Sh, b_gs


def load_ident(P, identd, dt, name="ident"):
    t = P.sb(name, [128, 128], dt); b = P.buf()
    P.dma("sp", t[:], identd, writes=[b])
    return t, b


def phase_inproj(P, tile_srcs, tile_cls, groups, w, NFM, NTMC, modT_dl, gT, identd, fm_dst, tm_dst, fm_groups=None):
    P.push_scope()
    NW = NFM * 128 + NTMC
    ident, b_ident = load_ident(P, identd, BF16)
    Gs, Sh, b_gs = load_mod(P, modT_dl, gT, 0, 1)
    w_sb = P.sb("w_sb", [128, 8, NW], BF16); b_w = P.buf()
    stage = [P.sb(f"stage{i}", [128, NW], F32) for i in range(2)]; b_stage = [P.buf() for _ in range(2)]
    load_weight_bf16(P, w, w_sb, b_w, 8, NW, stage, b_stage)
    NT = len(tile_srcs)
    nt = NormT(P, "n_", ident, b_ident, NT)
    xt = [P.sb(f"xt{i}", [128, 1024], F32) for i in range(2)]; b_xt = [P.buf() for _ in range(2)]
    aT = [P.sb(f"aT{i}", [128, 8, 512], BF16) for i in range(2)]; b_aT = [P.buf() for _ in range(2)]
    tmo = [P.sb(f"tmo{i}", [128, NTMC], F32) for i in range(2)]; b_tmo = [P.buf() for _ in range(2)]
    fmo = [P.sb(f"fmo{i}", [128, 512], F32) for i in range(2)]; b_fmo = [P.buf() for _ in range(2)]
    ptm = [P.ps(f"ptm{i}", [128, 512], F32) for i in range(2)]; b_ptm = [P.buf() for _ in range(2)]
    pfm = [P.ps(f"pfm{i}", [128, 512], F32) for i in range(2)]; b_pfm = [P.buf() for _ in range(2)]
    nx = 0; ntm = 0; nfm = 0; no = 0
    tmblocks = [(c0, min(512, NTMC - c0)) for c0 in range(0, NTMC, 512)]
    for gi, tiles in enumerate(groups):
        ai = gi % 2
        N = len(tiles) * 128
        for ti, t in enumerate(tiles):
            i = nx % 2; nx += 1
            for (psl, src) in tile_srcs[t]:
                P.dma("sp", xt[i][psl, :], src, writes=[b_xt[i]])
            nt.run(xt[i][:], b_xt[i], t, aT[ai][:, :, ti * 128:(ti + 1) * 128], b_aT[ai], Gs, Sh, b_gs, tile_cls[t])
            oi = no % 2; no += 1
            for (c0, cn) in tmblocks:
                pi = ntm % 2; ntm += 1
                for k in range(8):
                    P.op("pe", lambda e, pi=pi, k=k, c0=c0, cn=cn, ai=ai, ti=ti: e.matmul(
                        ptm[pi][:, 0:cn], lhsT=aT[ai][:, k, ti * 128:(ti + 1) * 128], rhs=w_sb[:, k, NFM * 128 + c0:NFM * 128 + c0 + cn],
                        start=(k == 0), stop=(k == 7)), reads=[b_aT[ai], b_w], writes=[b_ptm[pi]])
                P.op("act", lambda e, pi=pi, c0=c0, cn=cn, oi=oi: e.copy(out=tmo[oi][:, c0:c0 + cn], in_=ptm[pi][:, 0:cn]),
                     reads=[b_ptm[pi]], djw=[b_tmo[oi]])
            P.dma("sp", tm_dst(t), tmo[oi][:], reads=[b_tmo[oi]])
        if fm_groups is not None and gi not in fm_groups:
            continue
        for c6 in range(NFM):
            pi = nfm % 2; nfm += 1
            for k in range(8):
                P.op("pe", lambda e, pi=pi, k=k, c6=c6, ai=ai, N=N: e.matmul(
                    pfm[pi][:, 0:N], lhsT=w_sb[:, k, c6 * 128:(c6 + 1) * 128], rhs=aT[ai][:, k, 0:N], start=(k == 0), stop=(k == 7)),
                    reads=[b_aT[ai], b_w], writes=[b_pfm[pi]])
            P.op("dve", lambda e, pi=pi, N=N: e.tensor_copy(out=fmo[pi][:, 0:N], in_=pfm[pi][:, 0:N]), reads=[b_pfm[pi]], writes=[b_fmo[pi]])
            P.dma("sp", fm_dst(c6, gi), fmo[pi][:, 0:N], reads=[b_fmo[pi]])
    P.barrier()
    P.pop_scope()


def phase_conv0(P, FM0, cw, cb, XBC):
    P.push_scope()
    K = 5
    wsb = P.sb("wsb", [128, 5, K], F32); bsb = P.sb("bsb", [128, 5], F32); b_w = P.buf()
    P.dma("sp", wsb[:], cw, writes=[b_w]); P.dma("sp", bsb[:], cb, writes=[b_w])
    zero = P.sb("zero", [128, 4], F32); b_z = P.buf()
    P.op("pool", lambda e: e.memset(zero[:], 0.0), writes=[b_z])
    vin = [P.sb(f"vin{i}", [128, FMW], F32) for i in range(2)]; b_vin = [P.buf() for _ in range(2)]
    acc = [P.sb(f"acc{i}", [128, 2048], F32) for i in range(2)]; b_acc = [P.buf() for _ in range(2)]
    res = [P.sb(f"res{i}", [128, 2048], F32) for i in range(2)]; b_res = [P.buf() for _ in range(2)]
    n = 0
    for j in range(5):
        vi = j % 2
        rows = slice(128 + j * 128, 128 + (j + 1) * 128)
        P.dma("sp", vin[vi][:, LAT0:LAT0 + TL], FM0[rows, LAT0:LAT0 + TL], writes=[b_vin[vi]])
        P.dma("sp", vin[vi][:, 2:2 + TC], FM0[rows, 2:2 + TC], djw=[b_vin[vi]])
        for (c0, cn) in ((0, 2), (258, 4), (FMW - 2, 2)):
            P.op("pool", lambda e, vi=vi, c0=c0, cn=cn: e.memset(vin[vi][:, c0:c0 + cn], 0.0), djw=[b_vin[vi]])
        blocks = [(0, TC, 0)] + [(260 + t0, 2048, TC + t0) for t0 in range(0, TL, 2048)]
        for (i0, T, o0) in blocks:
            i = n % 2; n += 1
            conv_fm(P, "dve", vin[vi], b_vin[vi], wsb, j, bsb[:, j:j + 1], b_w, acc[i][:, 0:T], b_acc[i], K, T, t0=i0)
            P.op("act", lambda e, i=i, T=T: e.activation(out=res[i][:, 0:T], in_=acc[i][:, 0:T], func=AF.Silu), reads=[b_acc[i]], writes=[b_res[i]])
            P.dma("sp", XBC[j * 128:(j + 1) * 128, o0:o0 + T], res[i][:, 0:T], reads=[b_res[i]])
    P.barrier()
    P.pop_scope()


def phase_fourier(P, FM0, ccsc, CTd, STd, CTc, STc, MIX0):
    P.push_scope()
    NL = TL // 128
    NCt = TC // 128
    g32 = P.sb("g32", [128, TL + TC], F32); b_g32 = P.buf()
    P.dma("sp", g32[:, 0:TL], FM0[0:128, LAT0:LAT0 + TL], writes=[b_g32])
    P.dma("sp", g32[:, TL:TL + TC], FM0[0:128, 2:2 + TC], writes=[b_g32])
    gbf = P.sb("gbf", [128, TL + TC], BF16); b_gbf = P.buf()
    P.op("dve", lambda e: e.tensor_copy(out=gbf[:], in_=g32[:]), reads=[b_g32], writes=[b_gbf])
    cc32 = P.sb("cc32", [128, 256], F32); ccb = P.sb("ccb", [128, 256], BF16); b_cc = P.buf()
    P.dma("sp", cc32[:], ccsc, writes=[b_cc])
    P.op("dve", lambda e: e.tensor_copy(out=ccb[:], in_=cc32[:]), reads=[b_cc], writes=[b_cc])
    H = P.sb("H", [128, NL + NCt, 256], BF16); b_H = P.buf()
    ph = [P.ps(f"ph{i}", [128, 256], F32) for i in range(2)]; b_ph = [P.buf() for _ in range(2)]
    for j in range(NL + NCt):
        pi = j % 2
        P.op("pe", lambda e, j=j, pi=pi: e.matmul(ph[pi][:], lhsT=gbf[:, j * 128:(j + 1) * 128], rhs=ccb[:], start=True, stop=True),
             reads=[b_gbf, b_cc], writes=[b_ph[pi]])
        if pi == 0:
            P.op("act", lambda e, j=j, pi=pi: e.copy(out=H[:, j, :], in_=ph[pi][:]), reads=[b_ph[pi]], djw=[b_H])
        else:
            P.op("dve", lambda e, j=j, pi=pi: e.tensor_copy(out=H[:, j, :], in_=ph[pi][:]), reads=[b_ph[pi]], djw=[b_H])
    JB = 16
    cbuf = [P.sb(f"cbuf{i}", [128, JB, 512], BF16) for i in range(2)]; b_cbuf = [P.buf() for _ in range(2)]
    sbuf_ = [P.sb(f"sbuf{i}", [128, JB, 512], BF16) for i in range(2)]; b_sbuf = [P.buf() for _ in range(2)]
    po = [P.ps(f"po{i}", [128, 512], F32) for i in range(2)]; b_po = [P.buf() for _ in range(2)]
    ot = [P.sb(f"ot{i}", [128, 512], BF16) for i in range(2)]; b_ot = [P.buf() for _ in range(2)]
    CTv = CTd.rearrange("(j p) f -> p j f", p=128)
    STv = STd.rearrange("(j p) f -> p j f", p=128)
    nb = 0
    for fb in range(TL // 512):
        fsl = slice(fb * 512, (fb + 1) * 512)
        pi = fb % 2
        for j0 in range(0, NL, JB):
            bi = nb % 2; nb += 1
            P.dma("sp", cbuf[bi][:], CTv[:, j0:j0 + JB, fsl], writes=[b_cbuf[bi]])
            P.dma("pool", sbuf_[bi][:], STv[:, j0:j0 + JB, fsl], writes=[b_sbuf[bi]])
            for jj in range(JB):
                j = j0 + jj
                P.op("pe", lambda e, j=j, jj=jj, bi=bi, pi=pi: e.matmul(po[pi][:], lhsT=H[:, j, 0:128], rhs=cbuf[bi][:, jj, :],
                                                                     start=(j == 0), stop=False), reads=[b_H, b_cbuf[bi]], writes=[b_po[pi]])
                P.op("pe", lambda e, j=j, jj=jj, bi=bi, pi=pi: e.matmul(po[pi][:], lhsT=H[:, j, 128:256], rhs=sbuf_[bi][:, jj, :],
                                                                     start=False, stop=(j == NL - 1)), reads=[b_H, b_sbuf[bi]], writes=[b_po[pi]])
        P.op("act", lambda e, pi=pi: e.copy(out=ot[pi][:], in_=po[pi][:]), reads=[b_po[pi]], writes=[b_ot[pi]])
        P.dma("sp", MIX0[1 + fb // 2, 0:128, (fb % 2) * 512:(fb % 2 + 1) * 512], ot[pi][:], reads=[b_ot[pi]])
    cc_ = P.sb("cc_", [128, NCt, TC], BF16); sc_ = P.sb("sc_", [128, NCt, TC], BF16); b_c2 = P.buf()
    P.dma("sp", cc_[:], CTc.rearrange("(j p) f -> p j f", p=128), writes=[b_c2])
    P.dma("sp", sc_[:], STc.rearrange("(j p) f -> p j f", p=128), writes=[b_c2])
    pi = 0
    for j in range(NCt):
        P.op("pe", lambda e, j=j: e.matmul(po[pi][:, 0:TC], lhsT=H[:, NL + j, 0:128], rhs=cc_[:, j, :], start=(j == 0), stop=False),
             reads=[b_H, b_c2], writes=[b_po[pi]])
        P.op("pe", lambda e, j=j: e.matmul(po[pi][:, 0:TC], lhsT=H[:, NL + j, 128:256], rhs=sc_[:, j, :], start=False, stop=(j == NCt - 1)),
             reads=[b_H, b_c2], writes=[b_po[pi]])
    P.op("act", lambda e: e.copy(out=ot[pi][:, 0:TC], in_=po[pi][:, 0:TC]), reads=[b_po[pi]], writes=[b_ot[pi]])
    P.dma("sp", MIX0[0, 0:128, 0:TC], ot[pi][:, 0:TC], reads=[b_ot[pi]])
    P.barrier()
    P.pop_scope()


def phase_ssd(P, XBC, ZDT, prm, trid, gnR, identf, YF, MIX0):
    P.push_scope()
    NCH = NTA
    NCTX = TC // 128
    NC6 = NCH * 6
    tri = P.sb("tri", [128, 3, 128], F32); b_tri = P.buf()
    P.dma("sp", tri[:], trid, writes=[b_tri])
    idf, b_idf = load_ident(P, identf, F32, "idf")
    prs = P.sb("prs", [128, 3, 2, NCH, 6], F32); b_prs = P.buf()
    P.dma("sp", prs[:], prm, writes=[b_prs])
    gn = P.sb("gn", [128, 384], F32); b_gn = P.buf()
    P.dma("sp", gn[:], gnR, writes=[b_gn])
    dts = P.sb("dts", [128, 2, NCH, 6], F32); b_dts = P.buf()
    zv = ZDT.rearrange("(c p) n -> p c n", p=128)
    for c in range(NCH):
        P.dma("sp", dts[:, :, c, :], ZDT[c * 128:(c + 1) * 128, 384:396].rearrange("p (d h) -> p d h", d=2), writes=[b_dts])
    stg = P.sb("stg", [128, TA], F32); b_stg = P.buf()
    Bf = P.sb("Bf", [128, TA], BF16); Cf = P.sb("Cf", [128, TA], BF16); Bt = P.sb("Bt", [128, NCH, 128], BF16)
    b_Bf, b_Cf, b_Bt = P.buf(), P.buf(), P.buf()
    pcb = P.ps("pcb", [128, 128], F32); b_pcb = P.buf()
    P.dma("sp", stg[:], XBC[512:640, :], writes=[b_stg])
    P.op("pool", lambda e: e.tensor_copy(out=Cf[:], in_=stg[:]), reads=[b_stg], writes=[b_Cf])
    P.dma("sp", stg[:], XBC[384:512, :], writes=[b_stg])
    P.op("dve", lambda e: e.tensor_copy(out=Bf[:], in_=stg[:]), reads=[b_stg], writes=[b_Bf])
    for c in range(NCH):
        P.op("pe", lambda e, c=c: e.transpose(out=pcb[:], in_=stg[:, c * 128:(c + 1) * 128], identity=idf[:]),
             reads=[b_stg, b_idf], writes=[b_pcb])
        P.op("act", lambda e, c=c: e.copy(out=Bt[:, c, :], in_=pcb[:]), reads=[b_pcb], djw=[b_Bt])
    dtsp = P.sb("dtsp", [128, 2, NCH, 6], F32); dta = P.sb("dta", [128, 2, NCH, 6], F32); b_dt = P.buf()
    P.op("dve", lambda e: e.tensor_add(out=dtsp[:], in0=dts[:], in1=prs[:, 0]), reads=[b_dts, b_prs], writes=[b_dt])
    P.op("act", lambda e: e.activation(out=dtsp[:], in_=dtsp[:], func=AF.Exp), reads=[b_dt], writes=[b_dt])
    P.op("dve", lambda e: e.tensor_scalar_add(out=dtsp[:], in0=dtsp[:], scalar1=1.0), reads=[b_dt], writes=[b_dt])
    P.op("act", lambda e: e.activation(out=dtsp[:], in_=dtsp[:], func=AF.Ln), reads=[b_dt], writes=[b_dt])
    P.op("act", lambda e: e.activation(out=dta[:], in_=prs[:, 1], func=AF.Exp), reads=[b_prs, b_dt], writes=[b_dt])
    P.op("dve", lambda e: e.scalar_tensor_tensor(out=dta[:], in0=dta[:], scalar=-1.0, in1=dtsp[:], op0=ALU.mult, op1=ALU.mult),
         reads=[b_dt], writes=[b_dt])
    acs = P.sb("acs", [128, 2, NCH, 6], F32); eacs = P.sb("eacs", [128, 2, NCH, 6], F32)
    tend = P.sb("tend", [128, 2, NCH, 6], F32); cdec = P.sb("cdec", [128, 2, NCH, 6], F32); b_ac = P.buf()
    pb = [P.ps(f"pb{i}", [128, NC6], F32) for i in range(2)]; b_pb = [P.buf() for _ in range(2)]
    for d in range(2):
        P.op("pe", lambda e, d=d: e.matmul(pb[0][:], lhsT=tri[:, d, :], rhs=dta[:, d].rearrange("p c h -> p (c h)"), start=True, stop=True),
             reads=[b_tri, b_dt], writes=[b_pb[0]])
        P.op("pe", lambda e, d=d: e.matmul(pb[1][:], lhsT=tri[:, 2, :], rhs=dta[:, d].rearrange("p c h -> p (c h)"), start=True, stop=True),
             reads=[b_tri, b_dt], writes=[b_pb[1]])
        av = lambda t, d=d: t[:, d].rearrange("p c h -> p (c h)")
        P.op("dve", lambda e, av=av: e.tensor_copy(out=av(acs), in_=pb[0][:]), reads=[b_pb[0]], writes=[b_ac])
        P.op("act", lambda e, av=av: e.activation(out=av(eacs), in_=pb[0][:], func=AF.Exp), reads=[b_pb[0]], writes=[b_ac])
        P.op("act", lambda e, av=av: e.activation(out=av(cdec), in_=pb[1][:], func=AF.Exp), reads=[b_pb[1]], writes=[b_ac])
        P.op("dve", lambda e, av=av: e.tensor_sub(out=av(tend), in0=pb[1][:], in1=av(acs)), reads=[b_pb[1], b_ac], writes=[b_ac])
        P.op("act", lambda e, av=av: e.activation(out=av(tend), in_=av(tend), func=AF.Exp), reads=[b_ac], writes=[b_ac])
    dtE = P.sb("dtE", [128, 2, NCH, 6], F32)
    P.op("dve", lambda e: e.tensor_mul(out=dtE[:], in0=dtsp[:], in1=tend[:]), reads=[b_dt, b_ac], writes=[b_ac])
    bc = lambda ap, n: ap.unsqueeze(2).to_broadcast([128, 6, n])
    xf = [P.sb(f"xf{i}", [128, 3, 128], F32) for i in range(2)]; b_xf = [P.buf() for _ in range(2)]
    xt = [P.sb(f"xt{i}", [128, 6, 64], F32) for i in range(2)]; b_xt = [P.buf() for _ in range(2)]
    Dm = [P.sb(f"Dm{i}", [128, 6, 128], F32) for i in range(2)]; b_Dm = [P.buf() for _ in range(2)]
    pR = P.ps("pR", [128, 6, 128], F32); b_pR = P.buf()
    cbU = [P.sb(f"cbU{i}", [128, 128], F32) for i in range(2)]; b_cbU = [P.buf() for _ in range(2)]
    arg = [P.sb(f"arg{i}", [128, 6, 128], F32) for i in range(2)]; b_arg = [P.buf() for _ in range(2)]
    Wh = [P.sb(f"Wh{i}", [128, 6, 128], BF16) for i in range(2)]; b_Wh = [P.buf() for _ in range(2)]
    xdt = [P.sb(f"xdt{i}", [128, 6, 64], BF16) for i in range(2)]; b_xdt = [P.buf() for _ in range(2)]
    xdtE = [P.sb(f"xdtE{i}", [128, 6, 64], BF16) for i in range(2)]; b_xdtE = [P.buf() for _ in range(2)]
    py = P.ps("py", [128, 6, 64], F32); b_py = P.buf()
    pyo = P.ps("pyo", [128, 6, 64], F32); b_pyo = P.buf()
    pst = P.ps("pst", [128, 6, 64], F32); b_pst = P.buf()
    tA = [P.sb(f"tA{i}", [128, 6, 64], F32) for i in range(2)]; b_tA = [P.buf() for _ in range(2)]
    tB = [P.sb(f"tB{i}", [128, 6, 64], F32) for i in range(2)]; b_tB = [P.buf() for _ in range(2)]
    yo = [P.sb(f"yo{i}", [128, 384], F32) for i in range(2)]; b_yo = [P.buf() for _ in range(2)]
    yfl = [P.sb(f"yfl{i}", [128, 384], F32) for i in range(2)]; b_yfl = [P.buf() for _ in range(2)]
    zt = [P.sb(f"zt{i}", [128, 384], F32) for i in range(2)]; b_zt = [P.buf() for _ in range(2)]
    ynb = [P.sb(f"ynb{i}", [128, 384], F32) for i in range(2)]; b_ynb = [P.buf() for _ in range(2)]
    ynT = [P.sb(f"ynT{i}", [128, 3, 128], BF16) for i in range(2)]; b_ynT = [P.buf() for _ in range(2)]
    junk = P.sb("junk", [128, 384], F32); b_junk = P.buf()
    ss = P.sb("ss", [128, NCH], F32); b_ss0 = P.buf(); b_ss = [P.buf() for _ in range(NCH)]
    P.op("pool", lambda e: e.memset(ss[:], 0.0), writes=[b_ss0])
    S = P.sb("S", [128, 6, 64], F32); Sb = P.sb("Sb", [128, 6, 64], BF16); b_S = P.buf(); b_Sb = P.buf()
    b_YF = [P.buf() for _ in range(NCH)]
    f2 = lambda t: t[:].rearrange("p h e -> p (h e)")
    n = 0
    for d in range(2):
        ctx_order = list(range(NCTX)) if d == 0 else list(range(NCTX - 1, -1, -1))
        lat_order = list(range(NCTX, NCH)) if d == 0 else list(range(NCH - 1, NCTX - 1, -1))
        P.op("pool", lambda e: e.memset(S[:], 0.0), writes=[b_S])
        P.op("pool", lambda e: e.memset(Sb[:], 0.0), writes=[b_Sb])
        for c in ctx_order + lat_order:
            i = n % 2; n += 1
            csl = slice(c * 128, (c + 1) * 128)
            P.dma("sp", xf[i][:], XBC[0:384, csl].rearrange("(j p) t -> p j t", p=128), writes=[b_xf[i]])
            if d == 1:
                P.dma("sp", yfl[i][:], YF[csl, :], reads=[b_YF[c]], writes=[b_yfl[i]])
                P.dma("sp", zt[i][:], ZDT[csl, 0:384], writes=[b_zt[i]])
            for j3 in range(3):
                P.op("pe", lambda e, i=i, j3=j3: e.transpose(out=pb[0][:, j3 * 128:(j3 + 1) * 128], in_=xf[i][:, j3, :], identity=idf[:]),
                     reads=[b_xf[i], b_idf], writes=[b_pb[0]])
            P.op("act", lambda e, i=i: e.copy(out=f2(xt[i]), in_=pb[0][:, 0:384]), reads=[b_pb[0]], writes=[b_xt[i]])
            P.op("pool", lambda e, i=i, c=c, d=d: e.tensor_mul(out=Dm[i][:], in0=tri[:, d, :].unsqueeze(1).to_broadcast([128, 6, 128]),
                                                               in1=bc(dta[:, d, c, :], 128)), reads=[b_tri, b_dt], writes=[b_Dm[i]])
            P.op("pe", lambda e, i=i: e.matmul(pR[:, 0:4, :].rearrange("p h l -> p (h l)"), lhsT=tri[:, 2, :],
                                               rhs=Dm[i][:, 0:4, :].rearrange("p h l -> p (h l)"), start=True, stop=True),
                 reads=[b_tri, b_Dm[i]], writes=[b_pR])
            P.op("pe", lambda e, i=i: e.matmul(pR[:, 4:6, :].rearrange("p h l -> p (h l)"), lhsT=tri[:, 2, :],
                                               rhs=Dm[i][:, 4:6, :].rearrange("p h l -> p (h l)"), start=True, stop=True),
                 reads=[b_tri, b_Dm[i]], writes=[b_pR])
            P.op("pe", lambda e, csl=csl: e.matmul(pcb[:], lhsT=Bf[:, csl], rhs=Cf[:, csl], start=True, stop=True),
                 reads=[b_Bf, b_Cf], writes=[b_pcb])
            P.op("dve", lambda e, i=i, d=d: e.tensor_mul(out=cbU[i][:], in0=pcb[:], in1=tri[:, d, :]), reads=[b_pcb, b_tri], writes=[b_cbU[i]])
            P.op("dve", lambda e, i=i, c=c, d=d: e.tensor_sub(out=arg[i][:], in0=pR[:], in1=bc(acs[:, d, c, :], 128)),
                 reads=[b_pR, b_ac], writes=[b_arg[i]])
            P.op("pool", lambda e, i=i: e.tensor_scalar_min(out=arg[i][:], in0=arg[i][:], scalar1=0.0), reads=[b_arg[i]], writes=[b_arg[i]])
            P.op("act", lambda e, i=i: e.activation(out=arg[i][:], in_=arg[i][:], func=AF.Exp), reads=[b_arg[i]], writes=[b_arg[i]])
            P.op("pool", lambda e, i=i: e.tensor_mul(out=Wh[i][:], in0=arg[i][:], in1=cbU[i][:].unsqueeze(1).to_broadcast([128, 6, 128])),
                 reads=[b_arg[i], b_cbU[i]], writes=[b_Wh[i]])
            P.op("dve", lambda e, i=i, c=c, d=d: e.tensor_mul(out=xdt[i][:], in0=xt[i][:], in1=bc(dtsp[:, d, c, :], 64)),
                 reads=[b_xt[i], b_dt], writes=[b_xdt[i]])
            P.op("pool", lambda e, i=i, c=c, d=d: e.tensor_mul(out=xdtE[i][:], in0=xt[i][:], in1=bc(dtE[:, d, c, :], 64)),
                 reads=[b_xt[i], b_ac], writes=[b_xdtE[i]])
            for h in range(6):
                P.op("pe", lambda e, h=h, i=i: e.matmul(py[:, h, :], lhsT=Wh[i][:, h, :], rhs=xdt[i][:, h, :], start=True, stop=True),
                     reads=[b_Wh[i], b_xdt[i]], writes=[b_py])
            P.op("pe", lambda e, csl=csl: e.matmul(f2(pyo), lhsT=Cf[:, csl], rhs=f2(Sb), start=True, stop=True),
                 reads=[b_Cf, b_Sb], writes=[b_pyo])
            P.op("pe", lambda e, c=c, i=i: e.matmul(f2(pst), lhsT=Bt[:, c, :], rhs=f2(xdtE[i]), start=True, stop=True),
                 reads=[b_Bt, b_xdtE[i]], writes=[b_pst])
            P.op("dve", lambda e, c=c, d=d: e.tensor_mul(out=S[:], in0=S[:], in1=bc(cdec[:, d, c, :], 64)), reads=[b_S, b_ac, b_pyo], writes=[b_S])
            P.op("dve", lambda e: e.tensor_add(out=S[:], in0=S[:], in1=pst[:]), reads=[b_S, b_pst], writes=[b_S])
            P.op("act", lambda e: e.copy(out=Sb[:], in_=S[:]), reads=[b_S, b_pyo], writes=[b_Sb])
            P.op("dve", lambda e, i=i, c=c, d=d: e.tensor_mul(out=tA[i][:], in0=pyo[:], in1=bc(eacs[:, d, c, :], 64)), reads=[b_pyo, b_ac], writes=[b_tA[i]])
            P.op("pool", lambda e, i=i, c=c, d=d: e.tensor_mul(out=tB[i][:], in0=xt[i][:], in1=bc(prs[:, 2, d, c, :], 64)), reads=[b_xt[i], b_prs], writes=[b_tB[i]])
            P.op("pool", lambda e, i=i: e.tensor_add(out=tA[i][:], in0=tA[i][:], in1=tB[i][:]), reads=[b_tA[i], b_tB[i]], writes=[b_tA[i]])
            P.op("dve", lambda e, i=i: e.tensor_add(out=yo[i][:], in0=f2(tA[i]), in1=f2(py)), reads=[b_tA[i], b_py], writes=[b_yo[i]])
            if d == 0:
                P.dma("sp", YF[csl, :], yo[i][:], reads=[b_yo[i]], writes=[b_YF[c]])
            else:
                P.op("act", lambda e, i=i: e.activation(out=zt[i][:], in_=zt[i][:], func=AF.Silu), reads=[b_zt[i]], writes=[b_zt[i]])
                P.op("pool", lambda e, i=i: e.tensor_add(out=yo[i][:], in0=yo[i][:], in1=yfl[i][:]), reads=[b_yo[i], b_yfl[i]], writes=[b_yo[i]])
                P.op("pool", lambda e, i=i: e.tensor_mul(out=yo[i][:], in0=yo[i][:], in1=zt[i][:]), reads=[b_yo[i], b_zt[i]], writes=[b_yo[i]])
                P.op("act", lambda e, i=i, c=c: e.activation(out=junk[:], in_=yo[i][:], func=AF.Square, accum_out=ss[:, c:c + 1]),
                     reads=[b_yo[i], b_ss0], writes=[b_junk, b_ss[c]])
                P.op("dve", lambda e, c=c: e.tensor_scalar(out=ss[:, c:c + 1], in0=ss[:, c:c + 1], scalar1=1.0 / 384, scalar2=EPS, op0=ALU.mult, op1=ALU.add),
                     reads=[b_ss[c]], writes=[b_ss[c]])
                P.op("act", lambda e, c=c: e.activation(out=ss[:, c:c + 1], in_=ss[:, c:c + 1], func=AF.Sqrt), reads=[b_ss[c]], writes=[b_ss[c]])
                P.op("dve", lambda e, c=c: e.reciprocal(out=ss[:, c:c + 1], in_=ss[:, c:c + 1]), reads=[b_ss[c]], writes=[b_ss[c]])
                P.op("dve", lambda e, i=i, c=c: e.scalar_tensor_tensor(out=ynb[i][:], in0=yo[i][:], scalar=ss[:, c:c + 1], in1=gn[:], op0=ALU.mult, op1=ALU.mult),
                     reads=[b_yo[i], b_ss[c], b_gn], writes=[b_ynb[i]])
                for j3 in range(3):
                    P.op("pe", lambda e, i=i, j3=j3: e.transpose(out=pb[1][:, j3 * 128:(j3 + 1) * 128], in_=ynb[i][:, j3 * 128:(j3 + 1) * 128], identity=idf[:]),
                         reads=[b_ynb[i], b_idf], writes=[b_pb[1]])
                P.op("act", lambda e, i=i: e.copy(out=ynT[i][:].rearrange("p j t -> p (j t)"), in_=pb[1][:, 0:384]), reads=[b_pb[1]], writes=[b_ynT[i]])
                if c < NCTX:
                    mdst = MIX0[0, 128:512, c * 128:(c + 1) * 128]
                else:
                    lt = c - NCTX
                    mdst = MIX0[1 + lt // 8, 128:512, (lt % 8) * 128:(lt % 8 + 1) * 128]
                P.dma("sp", mdst.rearrange("(j p) t -> p j t", p=128), ynT[i][:], reads=[b_ynT[i]])
    P.barrier()
    P.pop_scope()


def phase_outproj(P, CM, NT, mt_srcs, h_src, w, gR, gate_dl, HMID):
    P.push_scope()
    nk = CM // 128
    g_sb = P.sb("g_sb", [128, 1024], F32); b_g = P.buf()
    P.dma("sp", g_sb[:], gR, writes=[b_g])
    GG = P.sb("GG", [128, 2, 1024], F32); b_gg = P.buf()
    for cls in range(2):
        P.dma("sp", GG[:, cls, :], gate_dl[cls, 0], writes=[b_gg])
    for cls in range(2):
        P.op("dve", lambda e, cls=cls: e.tensor_mul(out=GG[:, cls, :], in0=GG[:, cls, :], in1=g_sb[:]), reads=[b_g, b_gg], writes=[b_gg])
    w_sb = P.sb("w_sb", [128, nk, 1024], BF16); b_w = P.buf()
    stage = [P.sb(f"stage{i}", [128, 1024], F32) for i in range(2)]; b_stage = [P.buf() for _ in range(2)]
    load_weight_bf16(P, w, w_sb, b_w, nk, 1024, stage, b_stage)
    rn = ResNorm(P, "r_", NT)
    mT = [P.sb(f"mT{i}", [128, nk, 128], BF16) for i in range(2)]; b_mT = [P.buf() for _ in range(2)]
    xt = [P.sb(f"xt{i}", [128, 1024], F32) for i in range(2)]; b_xt = [P.buf() for _ in range(2)]
    po = [P.ps(f"po{i}", [128, 1024], F32) for i in range(2)]; b_po = [P.buf() for _ in range(2)]
    for i in range(2):
        P.op("pool", lambda e, i=i: e.memset(mT[i][:], 0.0), writes=[b_mT[i]])
    for t in range(NT):
        i = t % 2
        cls = 0 if t < 16 else 1
        dst_fn, src_fn = mt_srcs(t)
        P.dma("sp", dst_fn(mT[i]), src_fn, writes=[b_mT[i]])
        P.dma("sp", xt[i][:], h_src[t * 128:(t + 1) * 128, :], writes=[b_xt[i]])
        for cb in range(2):
            for k in range(nk):
                P.op("pe", lambda e, i=i, k=k, cb=cb: e.matmul(
                    po[i][:, cb * 512:(cb + 1) * 512], lhsT=mT[i][:, k, :], rhs=w_sb[:, k, cb * 512:(cb + 1) * 512],
                    start=(k == 0), stop=(k == nk - 1)), reads=[b_mT[i], b_w], writes=[b_po[i]])
        o_t, b_o = rn.run(po[i][:], b_po[i], t, xt[i][:], b_xt[i], GG[:, cls, :], b_gg)
        P.dma("sp", HMID[t * 128:(t + 1) * 128, :], o_t[:], reads=[b_o])
    P.barrier()
    P.pop_scope()


def phase_ffn(P, HMID, NT, wg, wu, wd, modT_dl, gT, gR, gate_dl, identd, OUT, ctx_tiles):
    P.push_scope()
    FH = 2816
    NJ = FH // 128
    ident, b_ident = load_ident(P, identd, BF16)
    modsb = P.sb("modsb", [128, 2, 6, 8], F32); b_mod = P.buf()
    P.dma("sp", modsb[:], modT_dl, writes=[b_mod])
    gsb = P.sb("gsb", [128, 8], F32); b_g = P.buf()
    P.dma("sp", gsb[:], gT, writes=[b_g])
    Gs = P.sb("Gs", [128, 2, 8], F32); Sh = P.sb("Sh", [128, 2, 8], F32); b_gs = P.buf()
    for cls in range(2):
        P.op("dve", lambda e, cls=cls: e.scalar_tensor_tensor(out=Gs[:, cls, :], in0=modsb[:, cls, 4, :], scalar=1.0,
                                                               in1=gsb[:], op0=ALU.add, op1=ALU.mult), reads=[b_mod, b_g], writes=[b_gs])
        P.op("dve", lambda e, cls=cls: e.tensor_copy(out=Sh[:, cls, :], in_=modsb[:, cls, 3, :]), reads=[b_mod], writes=[b_gs])
    g_sb = P.sb("g_sb", [128, 1024], F32); b_g3 = P.buf()
    P.dma("sp", g_sb[:], gR, writes=[b_g3])
    GG = P.sb("GG", [128, 2, 1024], F32); b_gg = P.buf()
    for cls in range(2):
        P.dma("sp", GG[:, cls, :], gate_dl[cls, 1], writes=[b_gg])
    for cls in range(2):
        P.op("dve", lambda e, cls=cls: e.tensor_mul(out=GG[:, cls, :], in0=GG[:, cls, :], in1=g_sb[:]), reads=[b_g3, b_gg], writes=[b_gg])
    wg_sb = P.sb("wg_sb", [128, 8, FH], BF16); b_wg = P.buf()
    wu_sb = P.sb("wu_sb", [128, 8, FH], BF16); b_wu = P.buf()
    wd_sb = P.sb("wd_sb", [128, NJ, 1024], BF16); b_wd = P.buf()
    stage = [P.sb(f"stage{i}", [128, FH], F32) for i in range(2)]; b_stage = [P.buf() for _ in range(2)]
    load_weight_bf16(P, wg, wg_sb, b_wg, 8, FH, stage, b_stage, cast_engs=("pool", "dve"))
    load_weight_bf16(P, wu, wu_sb, b_wu, 8, FH, stage, b_stage, cast_engs=("pool", "dve"))
    load_weight_bf16(P, wd, wd_sb, b_wd, NJ, 1024, stage, b_stage, cast_engs=("pool", "dve"))
    nt = NormT(P, "n_", ident, b_ident, NT)
    rn = ResNorm(P, "r_", NT)
    ST = 2
    xt = [stage[0][:, i * 1024:(i + 1) * 1024] for i in range(2)]; b_xt = [P.alias(b_stage[0]) for _ in range(2)]
    xr = [stage[1][:, i * 1024:(i + 1) * 1024] for i in range(2)]; b_xr = [P.alias(b_stage[1]) for _ in range(2)]
    aT = P.sb("aT", [128, 8, ST * 128], BF16); b_aT = P.buf()
    hidT = P.sb("hidT", [128, NJ, ST * 128], BF16); b_hid = P.buf()
    sg = [P.sb(f"sg{i}", [128, ST * 128], F32) for i in range(2)]; b_sg = [P.buf() for _ in range(2)]
    psg = [P.ps(f"psg{i}", [128, 512], F32) for i in range(2)]; b_psg = [P.buf() for _ in range(2)]
    psu = [P.ps(f"psu{i}", [128, 512], F32) for i in range(2)]; b_psu = [P.buf() for _ in range(2)]
    po = P.ps("po", [128, 1024], F32); b_po = P.buf()
    outs = []
    nx = 0
    nr = 0
    for s0 in range(0, NT, ST):
        tiles = list(range(s0, min(NT, s0 + ST)))
        N = len(tiles) * 128
        for ti, t in enumerate(tiles):
            i = nx % 2; nx += 1
            cls = 1 if t in ctx_tiles else 0
            P.dma("sp", xt[i], HMID[t * 128:(t + 1) * 128, :], writes=[b_xt[i]])
            nt.run(xt[i], b_xt[i], t, aT[:, :, ti * 128:(ti + 1) * 128], b_aT, Gs, Sh, b_gs, cls)
        for j in range(NJ):
            pi = j % 2
            for k in range(8):
                P.op("pe", lambda e, pi=pi, j=j, k=k, N=N: e.matmul(
                    psg[pi][:, 0:N], lhsT=wg_sb[:, k, j * 128:(j + 1) * 128], rhs=aT[:, k, 0:N],
                    start=(k == 0), stop=(k == 7)), reads=[b_wg, b_aT], writes=[b_psg[pi]])
            for k in range(8):
                P.op("pe", lambda e, pi=pi, j=j, k=k, N=N: e.matmul(
                    psu[pi][:, 0:N], lhsT=wu_sb[:, k, j * 128:(j + 1) * 128], rhs=aT[:, k, 0:N],
                    start=(k == 0), stop=(k == 7)), reads=[b_wu, b_aT], writes=[b_psu[pi]])
            P.op("act", lambda e, pi=pi, N=N: e.activation(out=sg[pi][:, 0:N], in_=psg[pi][:, 0:N], func=AF.Silu),
                 reads=[b_psg[pi]], writes=[b_sg[pi]])
            P.op("dve", lambda e, pi=pi, j=j, N=N: e.tensor_mul(out=hidT[:, j, 0:N], in0=sg[pi][:, 0:N], in1=psu[pi][:, 0:N]),
                 reads=[b_sg[pi], b_psu[pi]], djw=[b_hid])
        for ti, t in enumerate(tiles):
            i = nr % 2; nr += 1
            cls = 1 if t in ctx_tiles else 0
            P.dma("sp", xr[i], HMID[t * 128:(t + 1) * 128, :], writes=[b_xr[i]])
            for cb in range(2):
                for j in range(NJ):
                    P.op("pe", lambda e, j=j, cb=cb, ti=ti: e.matmul(
                        po[:, cb * 512:(cb + 1) * 512], lhsT=hidT[:, j, ti * 128:(ti + 1) * 128],
                        rhs=wd_sb[:, j, cb * 512:(cb + 1) * 512], start=(j == 0), stop=(j == NJ - 1)),
                        reads=[b_hid, b_wd], writes=[b_po])
            o_t, b_o = rn.run(po[:], b_po, t, xr[i], b_xr[i], GG[:, cls, :], b_gg)
            outs.append(P.dma("sp", OUT[t * 128:(t + 1) * 128, :], o_t[:], reads=[b_o]))
    P.barrier()
    P.pop_scope()
    return outs


def allgather(P, srcs, dsts):
    for a, b in zip(srcs, dsts):
        P.cc("AllGather", a.opt(), b.opt(), G4)
    P.barrier()


def phase_attn(P, QKV, csd, snd, lamRd, subCd, identd, lambda_init, MIX1, NQB=None):
    P.push_scope()
    NU = 2
    NKT = TA // 128
    NCT = TC // 128
    NLT = TL // 128
    if NQB is None:
        NQB = TL // 512
    ident, b_ident = load_ident(P, identd, BF16)
    lam = P.sb("lam", [128, 4, 64], F32); b_lam = P.buf()
    P.dma("sp", lam[:], lamRd, writes=[b_lam])
    GS = P.sb("GS", [128, 1], F32); b_gs = P.buf()
    P.dma("sp", GS[:], subCd, writes=[b_gs])
    P.op("dve", lambda e: e.tensor_scalar_mul(out=GS[:], in0=GS[:], scalar1=float(1.0 - lambda_init)), reads=[b_gs], writes=[b_gs])
    lp = P.sb("lp", [128, 2, 64], F32); ls = P.sb("ls", [128, 4], F32); b_ls = P.buf()
    P.op("dve", lambda e: e.tensor_mul(out=lp[:, 0, :], in0=lam[:, 0, :], in1=lam[:, 1, :]), reads=[b_lam], writes=[b_ls])
    P.op("dve", lambda e: e.tensor_mul(out=lp[:, 1, :], in0=lam[:, 2, :], in1=lam[:, 3, :]), reads=[b_lam, b_ls], writes=[b_ls])
    P.op("dve", lambda e: e.reduce_sum(out=ls[:, 0:2], in_=lp[:], axis=AX.X), reads=[b_ls], writes=[b_ls])
    P.op("act", lambda e: e.activation(out=ls[:, 0:2], in_=ls[:, 0:2], func=AF.Exp), reads=[b_ls], writes=[b_ls])
    P.op("dve", lambda e: e.tensor_sub(out=ls[:, 2:3], in0=ls[:, 1:2], in1=ls[:, 0:1]), reads=[b_ls], writes=[b_ls])
    P.op("dve", lambda e: e.tensor_scalar_add(out=ls[:, 3:4], in0=ls[:, 2:3], scalar1=float(-lambda_init)), reads=[b_ls], writes=[b_ls])
    neglam = ls[:, 3:4]
    kT = P.sb("kT", [128, TA], BF16); b_kT = P.buf()
    NB = TL // 256
    qTz = P.sb("qTz", [128, NB, 2, 256], BF16); b_qT = P.buf()
    P.op("pool", lambda e: e.memset(qTz[:], 0.0), writes=[b_qT])
    vaug = P.sb("vaug", [128, NKT, 129], BF16); b_va = P.buf()
    P.op("pool", lambda e: e.memset(vaug[:, :, 128:129], 1.0), writes=[b_va])
    ld = {n: [P.sb(f"ld_{n}{i}", [128, 128], F32) for i in range(2)] for n in ("q", "k", "v", "cs", "sn")}
    b_ld = {n: [P.buf() for _ in range(2)] for n in ld}
    t1 = {n: [P.sb(f"t1_{n}{i}", [128, 128], F32) for i in range(2)] for n in ("q", "k")}
    t2 = {n: [P.sb(f"t2_{n}{i}", [128, 128], F32) for i in range(2)] for n in ("q", "k")}
    b_t1 = {n: [P.buf() for _ in range(2)] for n in t1}
    b_t2 = {n: [P.buf() for _ in range(2)] for n in t1}
    rb = {n: [P.sb(f"rb_{n}{i}", [128, 128], BF16) for i in range(2)] for n in ("q", "k")}
    b_rb = {n: [P.buf() for _ in range(2)] for n in rb}
    psT = [P.ps(f"psT{i}", [128, 128], BF16) for i in range(2)]; b_psT = [P.buf() for _ in range(2)]
    NPS = 4
    ps_s = [P.ps(f"ps_s{i}", [128, 512], F32) for i in range(NPS)]; b_ps_s = [P.buf() for _ in range(NPS)]
    accT = [P.ps("accT0", [128, 512], F32)] * 2; b_accT = [P.buf()] * 2
    pden = P.ps("pden", [128, 512], F32); b_pden = P.buf()
    E = [P.sb(f"E{i}", [128, 512], BF16) for i in range(4)]; b_E = [P.buf() for _ in range(4)]
    NES = 4
    esum = [P.sb(f"esum{i}", [128, 512], F32) for i in range(NES + 2)]; b_esum = [P.buf() for _ in range(NES + 2)]
    onesf = P.sb("onesf", [128, 2, 128], F32); b_onesf = P.buf()
    P.op("pool", lambda e: e.memset(onesf[:, 0, :], 1.0), writes=[b_onesf])
    P.op("pool", lambda e: e.memset(onesf[:, 1, :], 1.0 / 128), reads=[b_onesf], writes=[b_onesf])
    onesb = P.sb("onesb", [128, 128], BF16)
    P.op("pool", lambda e: e.memset(onesb[:], 1.0), reads=[b_onesf], writes=[b_onesf])
    rden = P.sb("rden", [128, 512], F32); b_rden = P.buf()
    att0T = P.sb("att0T", [128, 512], F32); b_att0T = P.buf()
    attT = P.sb("attT", [128, 512], F32); b_attT = P.buf()
    sqT = P.sb("sqT", [128, 512], F32); b_sqT = P.buf()
    oT = [P.sb(f"oT{i}", [128, 512], BF16) for i in range(2)]; b_oT = [P.buf() for _ in range(2)]
    v5 = lambda ap: ap.rearrange("p (c a h f) -> p c a h f", c=2, a=2, h=2, f=16)
    npt = [0]

    def transpose_to(src_bf, b_src, dst_ap, b_dst):
        pi = npt[0] % 2; npt[0] += 1
        P.op("pe", lambda e: e.transpose(out=psT[pi][:], in_=src_bf, identity=ident[:]), reads=[b_src, b_ident], writes=[b_psT[pi]])
        P.op("act", lambda e: e.copy(out=dst_ap, in_=psT[pi][:]), reads=[b_psT[pi]], djw=[b_dst])

    no = 0
    for u in range(NU):
        qc = slice(u * 128, (u + 1) * 128)
        kc_ = slice(256 + u * 128, 256 + (u + 1) * 128)
        vc_ = slice(512 + u * 128, 512 + (u + 1) * 128)
        for j in range(NCT):
            i = j % 2
            rs = slice(j * 128, (j + 1) * 128)
            P.dma("sp", ld["k"][i][:], QKV[rs, kc_], writes=[b_ld["k"][i]])
            P.dma("sp", ld["v"][i][:], QKV[rs, vc_], writes=[b_ld["v"][i]])
            P.op("dve", lambda e, i=i: e.tensor_copy(out=rb["k"][i][:], in_=ld["k"][i][:]), reads=[b_ld["k"][i]], writes=[b_rb["k"][i]])
            transpose_to(rb["k"][i][:], b_rb["k"][i], kT[:, j * 128:(j + 1) * 128], b_kT)
            P.op("pool", lambda e, i=i, j=j: e.tensor_copy(out=vaug[:, j, 0:128], in_=ld["v"][i][:]), reads=[b_ld["v"][i]], djw=[b_va])
        for j in range(NLT):
            i = j % 2
            sl = slice(j * 128, (j + 1) * 128)
            rs = slice(TC + j * 128, TC + (j + 1) * 128)
            P.dma("sp", ld["q"][i][:], QKV[rs, qc], writes=[b_ld["q"][i]])
            P.dma("sp", ld["k"][i][:], QKV[rs, kc_], writes=[b_ld["k"][i]])
            P.dma("sp", ld["v"][i][:], QKV[rs, vc_], writes=[b_ld["v"][i]])
            P.dma("sp", ld["cs"][i][:], csd[sl, :], writes=[b_ld["cs"][i]])
            P.dma("sp", ld["sn"][i][:], snd[sl, :], writes=[b_ld["sn"][i]])
            for n in ("q", "k"):
                x = ld[n][i]; a1 = t1[n][i]; a2 = t2[n][i]
                P.op("dve", lambda e, x=x, a1=a1, i=i: e.tensor_mul(out=a1[:], in0=x[:], in1=ld["cs"][i][:]),
                     reads=[b_ld[n][i], b_ld["cs"][i]], writes=[b_t1[n][i]])
                P.op("pool", lambda e, x=x, a2=a2, i=i: e.tensor_mul(out=v5(a2[:])[:, :, :, 0, :], in0=v5(x[:])[:, :, :, 1, :],
                                                                     in1=v5(ld["sn"][i][:])[:, :, :, 0, :]),
                     reads=[b_ld[n][i], b_ld["sn"][i]], writes=[b_t2[n][i]])
                P.op("pool", lambda e, x=x, a2=a2, i=i: e.tensor_mul(out=v5(a2[:])[:, :, :, 1, :], in0=v5(x[:])[:, :, :, 0, :],
                                                                     in1=v5(ld["sn"][i][:])[:, :, :, 1, :]),
                     reads=[b_ld[n][i], b_ld["sn"][i]], writes=[b_t2[n][i]])
                P.op("dve", lambda e, a1=a1, a2=a2, n=n, i=i: e.tensor_add(out=rb[n][i][:], in0=a1[:], in1=a2[:]),
                     reads=[b_t1[n][i], b_t2[n][i]], writes=[b_rb[n][i]])
            pi_ = npt[0] % 2; npt[0] += 1
            P.op("pe", lambda e, i=i, pi_=pi_: e.transpose(out=psT[pi_][:], in_=rb["q"][i][:], identity=ident[:]),
                 reads=[b_rb["q"][i], b_ident], writes=[b_psT[pi_]])
            qb_, qo_ = j // 2, (j % 2) * 128
            P.op("act", lambda e, pi_=pi_, qb_=qb_, qo_=qo_: e.copy(out=qTz[0:64, qb_, 0, qo_:qo_ + 128], in_=psT[pi_][0:64, :]),
                 reads=[b_psT[pi_]], djw=[b_qT])
            P.op("act", lambda e, pi_=pi_, qb_=qb_, qo_=qo_: e.copy(out=qTz[64:128, qb_, 1, qo_:qo_ + 128], in_=psT[pi_][64:128, :]),
                 reads=[b_psT[pi_]], djw=[b_qT])
            transpose_to(rb["k"][i][:], b_rb["k"][i], kT[:, TC + j * 128:TC + (j + 1) * 128], b_kT)
            P.op("pool", lambda e, i=i, j=j: e.tensor_copy(out=vaug[:, NCT + j, 0:128], in_=ld["v"][i][:]), reads=[b_ld["v"][i]], djw=[b_va])
        for qb in range(NQB * 2):
            oi = no % 2; no += 1
            ac = accT[0]; b_ac_ = b_accT[0]
            qrhs = qTz[:, qb].rearrange("p c t -> p (c t)")

            def score(kt, qrhs=qrhs):
                pi = kt % NPS
                P.op("pe", lambda e, kt=kt, pi=pi, qrhs=qrhs: e.matmul(ps_s[pi][:], lhsT=kT[:, kt * 128:(kt + 1) * 128], rhs=qrhs,
                                                                        start=True, stop=True), reads=[b_kT, b_qT], writes=[b_ps_s[pi]])
            for k0 in range(NPS - 1):
                score(k0)
            used = set()
            for kt in range(NKT):
                pi = kt % NPS
                ei = kt % 4
                if kt + NPS - 1 < NKT:
                    score(kt + NPS - 1)
                P.op("act", lambda e, pi=pi, ei=ei: e.activation(out=E[ei][:], in_=ps_s[pi][:], func=AF.Exp, scale=0.125),
                     reads=[b_ps_s[pi]], writes=[b_E[ei]])
                P.op("pe", lambda e, ei=ei, kt=kt, ac=ac: e.matmul(ac[:], lhsT=vaug[:, kt, 0:128], rhs=E[ei][:], start=(kt == 0), stop=(kt == NKT - 1)),
                     reads=[b_E[ei], b_va], writes=[b_ac_])
                if kt % 3 == 2:
                    eng = "pool"; si = NES + (kt // 3) % 2
                else:
                    eng = "dve"; si = (kt - kt // 3) % NES
                if si not in used:
                    used.add(si)
                    P.op(eng, lambda e, ei=ei, si=si: e.tensor_copy(out=esum[si][:], in_=E[ei][:]), reads=[b_E[ei]], writes=[b_esum[si]])
                else:
                    P.op(eng, lambda e, ei=ei, si=si: e.tensor_add(out=esum[si][:], in0=esum[si][:], in1=E[ei][:]), reads=[b_E[ei], b_esum[si]], writes=[b_esum[si]])
            for si in range(NES + 2):
                P.op("pe", lambda e, si=si: e.matmul(pden[:], lhsT=onesf[:, 0, :], rhs=esum[si][:], start=(si == 0), stop=(si == NES + 1)),
                     reads=[b_onesf, b_esum[si]], writes=[b_pden])
            P.op("dve", lambda e: e.reciprocal(out=rden[:], in_=pden[:]), reads=[b_pden], writes=[b_rden])
            P.op("dve", lambda e, ac=ac: e.tensor_mul(out=att0T[:], in0=ac[:], in1=rden[:]), reads=[b_ac_, b_rden], writes=[b_att0T])
            P.op("dve", lambda e: e.scalar_tensor_tensor(out=attT[:, 0:256], in0=att0T[:, 256:512], scalar=neglam, in1=att0T[:, 0:256], op0=ALU.mult, op1=ALU.add),
                 reads=[b_att0T, b_ls], writes=[b_attT])
            P.op("act", lambda e: e.activation(out=sqT[:, 0:256], in_=attT[:, 0:256], func=AF.Square), reads=[b_attT], writes=[b_sqT])
            P.op("pe", lambda e: e.matmul(pden[:, 0:256], lhsT=onesf[:, 1, :], rhs=sqT[:, 0:256], start=True, stop=True), reads=[b_onesf, b_sqT, b_rden], writes=[b_pden])
            P.op("dve", lambda e: e.tensor_scalar_add(out=rden[:, 0:256], in0=pden[:, 0:256], scalar1=EPS), reads=[b_pden, b_att0T], writes=[b_rden])
            P.op("act", lambda e: e.activation(out=rden[:, 0:256], in_=rden[:, 0:256], func=AF.Sqrt), reads=[b_rden], writes=[b_rden])
            P.op("dve", lambda e: e.reciprocal(out=rden[:, 0:256], in_=rden[:, 0:256]), reads=[b_rden], writes=[b_rden])
            P.op("dve", lambda e: e.tensor_mul(out=attT[:, 0:256], in0=attT[:, 0:256], in1=rden[:, 0:256]), reads=[b_attT, b_rden], writes=[b_attT])
            P.op("act", lambda e, oi=oi: e.activation(out=oT[oi][:, 0:256], in_=attT[:, 0:256], func=AF.Identity, scale=GS[:, 0:1]), reads=[b_attT, b_gs], writes=[b_oT[oi]])
            P.dma("sp", MIX1[1 + qb // 4, u * 128:(u + 1) * 128, (qb % 4) * 256:(qb % 4 + 1) * 256], oT[oi][:, 0:256], reads=[b_oT[oi]])
    P.barrier()
    P.pop_scope()


def phase_conformer(P, UT, cw, cb, lng, lnb, selm, ST, GST, MIX1):
    P.push_scope()
    K = 31
    TTp = TL + K - 1
    wsb = P.sb("wsb", [128, 1, K], F32); bsb = P.sb("bsb", [128, 1], F32); b_w = P.buf()
    gsb = P.sb("gsb", [128, 1], F32); lbsb = P.sb("lbsb", [128, 1], F32)
    P.dma("sp", wsb[:], cw, writes=[b_w]); P.dma("sp", bsb[:], cb, writes=[b_w])
    P.dma("sp", gsb[:], lng, writes=[b_w]); P.dma("sp", lbsb[:], lnb, writes=[b_w])
    ones = P.sb("ones", [128, 128], F32); b_ones = P.buf()
    P.op("pool", lambda e: e.memset(ones[:], 1.0), writes=[b_ones])
    a_sb = P.sb("a_sb", [128, TTp], F32); b_a = P.buf()
    g_sb = P.sb("g_sb", [128, TTp], F32); b_g = P.buf()
    P.dma("sp", a_sb[:, 15:15 + TL], UT[0:128, 15:15 + TL], writes=[b_a])
    P.dma("sp", g_sb[:, 15:15 + TL], UT[128:256, 15:15 + TL], writes=[b_g])
    for (c0, cn) in ((0, 15), (15 + TL, 15)):
        P.op("pool", lambda e, c0=c0, cn=cn: e.memset(a_sb[:, c0:c0 + cn], 0.0), djw=[b_a])
        P.op("pool", lambda e, c0=c0, cn=cn: e.memset(g_sb[:, c0:c0 + cn], 0.0), djw=[b_g])
    P.op("act", lambda e: e.activation(out=g_sb[:], in_=g_sb[:], func=AF.Sigmoid), reads=[b_g], writes=[b_g])
    P.op("pool", lambda e: e.tensor_mul(out=a_sb[:], in0=a_sb[:], in1=g_sb[:]), reads=[b_a, b_g], writes=[b_a])
    cv = g_sb
    b_cv = [P.alias(b_g) for _ in range(4)]
    for blk in range(4):
        conv_fm(P, "dve", a_sb, b_a, wsb, 0, bsb[:, 0:1], b_w, cv[:, blk * 2048:(blk + 1) * 2048], b_cv[blk], K, 2048, t0=blk * 2048)
    stt = P.sb("stt", [1, 2, TL], F32); b_stt = P.buf()
    sq = [P.sb(f"sq{i}", [128, 512], F32) for i in range(2)]; b_sq = [P.buf() for _ in range(2)]
    pm = [P.ps(f"pm{i}", [128, 512], F32) for i in range(2)]; b_pm = [P.buf() for _ in range(2)]
    pq = [P.ps(f"pq{i}", [128, 512], F32) for i in range(2)]; b_pq = [P.buf() for _ in range(2)]
    for tb in range(TL // 512):
        sl = slice(tb * 512, (tb + 1) * 512)
        pi = tb % 2
        bc = b_cv[tb // 4]
        P.op("act", lambda e, sl=sl, pi=pi: e.activation(out=sq[pi][:], in_=cv[:, sl], func=AF.Square), reads=[bc], writes=[b_sq[pi]])
        P.op("pe", lambda e, sl=sl, pi=pi: e.matmul(pm[pi][:], lhsT=ones[:], rhs=cv[:, sl], start=True, stop=True), reads=[b_ones, bc], writes=[b_pm[pi]])
        P.op("pe", lambda e, pi=pi: e.matmul(pq[pi][:], lhsT=ones[:], rhs=sq[pi][:], start=True, stop=True), reads=[b_ones, b_sq[pi]], writes=[b_pq[pi]])
        P.op("dve", lambda e, sl=sl, pi=pi: e.tensor_copy(out=stt[0:1, 0, sl], in_=pm[pi][0:1, :]), reads=[b_pm[pi]], writes=[b_stt])
        P.op("dve", lambda e, sl=sl, pi=pi: e.tensor_copy(out=stt[0:1, 1, sl], in_=pq[pi][0:1, :]), reads=[b_pq[pi]], writes=[b_stt])
    b_ST = P.buf()
    P.dma("sp", ST.rearrange("(o s) t -> o s t", o=1), stt[:], reads=[b_stt], writes=[b_ST])
    P.barrier()
    P.cc("AllGather", ST.opt(), GST.opt(), G4)
    P.barrier()
    gst = P.sb("gst", [8, TL], F32); b_gst = P.buf()
    P.dma("sp", gst[:], GST, writes=[b_gst])
    sm = P.sb("sm", [8, 2, 128], F32); b_sm = P.buf()
    P.dma("sp", sm[:], selm, writes=[b_sm])
    rstd = P.sb("rstd", [128, 512], F32); b_rstd = P.buf()
    msq = P.sb("msq", [128, 512], F32); b_msq = P.buf()
    xc = [P.sb(f"xc{i}", [128, 512], F32) for i in range(2)]; b_xc = [P.buf() for _ in range(2)]
    xo = [P.sb(f"xo{i}", [128, 512], BF16) for i in range(2)]; b_xo = [P.buf() for _ in range(2)]
    for tb in range(TL // 512):
        sl = slice(tb * 512, (tb + 1) * 512)
        pi = tb % 2
        bc = b_cv[tb // 4]
        P.op("pe", lambda e, sl=sl, pi=pi: e.matmul(pm[pi][:], lhsT=sm[:, 0, :], rhs=gst[:, sl], start=True, stop=True), reads=[b_sm, b_gst], writes=[b_pm[pi]])
        P.op("pe", lambda e, sl=sl, pi=pi: e.matmul(pq[pi][:], lhsT=sm[:, 1, :], rhs=gst[:, sl], start=True, stop=True), reads=[b_sm, b_gst], writes=[b_pq[pi]])
        P.op("act", lambda e, pi=pi: e.activation(out=msq[:], in_=pm[pi][:], func=AF.Square), reads=[b_pm[pi]], writes=[b_msq])
        P.op("dve", lambda e, pi=pi: e.scalar_tensor_tensor(out=rstd[:], in0=pq[pi][:], scalar=EPS, in1=msq[:], op0=ALU.add, op1=ALU.subtract),
             reads=[b_pq[pi], b_msq], writes=[b_rstd])
        P.op("act", lambda e: e.activation(out=rstd[:], in_=rstd[:], func=AF.Sqrt), reads=[b_rstd], writes=[b_rstd])
        P.op("dve", lambda e: e.reciprocal(out=rstd[:], in_=rstd[:]), reads=[b_rstd], writes=[b_rstd])
        P.op("dve", lambda e, sl=sl, pi=pi: e.tensor_sub(out=xc[pi][:], in0=cv[:, sl], in1=pm[pi][:]), reads=[bc, b_pm[pi]], writes=[b_xc[pi]])
        P.op("pool", lambda e, pi=pi: e.tensor_mul(out=xc[pi][:], in0=xc[pi][:], in1=rstd[:]), reads=[b_xc[pi], b_rstd], writes=[b_xc[pi]])
        P.op("act", lambda e, pi=pi: e.activation(out=xo[pi][:], in_=xc[pi][:], func=AF.Silu, scale=gsb[:, 0:1], bias=lbsb[:, 0:1]),
             reads=[b_xc[pi], b_w], writes=[b_xo[pi]])
        P.dma("sp", MIX1[1 + tb // 2, 256:384, (tb % 2) * 512:(tb % 2 + 1) * 512], xo[pi][:], reads=[b_xo[pi]])
    P.barrier()
    P.pop_scope()


def build_fused(stop_after=None, debug=(), NQB=None):
    nc = new_nc()
    lambda_init = 0.8 - 0.6 * math.exp(-0.3 * 1)
    xall = din(nc, "xall", [TA, 1024]); xown = din(nc, "xown", [17 * 128, 1024])
    cT2 = din(nc, "cT2", [128, 8, 2]); mw = din(nc, "mw", [2, 1024, 6144]); mb = din(nc, "mb", [1, 2, 6144])
    selc = din(nc, "selc", [2, 258])
    identb = din(nc, "identb", [128, 128], BF16); identf = din(nc, "identf", [128, 128])
    gT = din(nc, "gT", [2, 4, 128, 8]); gR = din(nc, "gR", [2, 4, 128, 1024])
    w_in0 = din(nc, "w_in0", [1024, 1164]); cw0 = din(nc, "cw0", [128, 5, 5]); cb0 = din(nc, "cb0", [128, 5])
    ccsc = din(nc, "ccsc", [128, 256]); CT = din(nc, "CT", [TL, TL], BF16); STt = din(nc, "STt", [TL, TL], BF16)
    CTc = din(nc, "CTc", [TC, TC], BF16); STc = din(nc, "STc", [TC, TC], BF16)
    prm = din(nc, "prm", [128, 3, 2, NTA, 6]); tri = din(nc, "tri", [128, 3, 128]); gnR = din(nc, "gnR", [128, 384])
    w_out0 = din(nc, "w_out0", [2048, 1024])
    wg = din(nc, "wg", [2, 1024, 2816]); wu = din(nc, "wu", [2, 1024, 2816]); wd = din(nc, "wd", [2, 2816, 1024])
    w_in1 = din(nc, "w_in1", [1024, 1024]); cs = din(nc, "cs", [TL, 128]); sn = din(nc, "sn", [TL, 128])
    lamR = din(nc, "lamR", [128, 4, 64]); subC = din(nc, "subC", [128, 1])
    cw1 = din(nc, "cw1", [128, 1, 31]); cb1 = din(nc, "cb1", [128, 1]); lng = din(nc, "lng", [128, 1]); lnb = din(nc, "lnb", [128, 1])
    selm = din(nc, "selm", [8, 2, 128]); w_out1 = din(nc, "w_out1", [1536, 1024])
    out = nc.dram_tensor("out", [2048, 1024], F32, kind="ExternalOutput").ap()
    modT_d = dscr(nc, "modT_d", [2, 128, 2, 6, 8]); gate_d = dscr(nc, "gate_d", [2, 2, 2, 128, 1024])
    FM0 = dscr(nc, "FM0", [768, FMW]); ZDT = dscr(nc, "ZDT", [TA, 396]); XBC = dscr(nc, "XBC", [640, TA])
    YF = dscr(nc, "YF", [TA, 384]); MIX0 = dscr(nc, "MIX0", [9, 512, 1024], BF16); GMIX0 = dscr(nc, "GMIX0", [9, 2048, 1024], BF16)
    HMID = dscr(nc, "HMID", [17 * 128, 1024]); H1 = dscr(nc, "H1", [18 * 128, 1024]); GH1 = dscr(nc, "GH1", [9, 1024, 1024])
    QKV = dscr(nc, "QKV", [TA, 768]); UT = dscr(nc, "UT", [256, TL + 30])
    MIX1 = dscr(nc, "MIX1", [9, 384, 1024], BF16); GMIX1 = dscr(nc, "GMIX1", [9, 1536, 1024], BF16)
    OWN0 = dscr(nc, "OWN0", [2, 2048, 1024], BF16); OWNC = dscr(nc, "OWNC", [2048, 64], BF16); OWN1 = dscr(nc, "OWN1", [2, 1536, 1024], BF16)
    STs = dscr(nc, "STs", [2, TL]); GST = dscr(nc, "GST", [8, TL]); HMID2 = dscr(nc, "HMID2", [2048, 1024])
    scr = dict(modT_d=modT_d, gate_d=gate_d, FM0=FM0, ZDT=ZDT, XBC=XBC, YF=YF, MIX0=MIX0, GMIX0=GMIX0, HMID=HMID, H1=H1, GH1=GH1,
               QKV=QKV, UT=UT, MIX1=MIX1, GMIX1=GMIX1, GST=GST, HMID2=HMID2)
    dbg_out = {}
    for name in debug:
        a = scr[name]
        dbg_out[name] = nc.dram_tensor("dbg_" + name, list(a.shape), a.dtype, kind="ExternalOutput").ap()
    with ExitStack() as st:
        P = Prog(nc, st)
        dyn = {}

        def setup(e):
            pid = nc.partition_id([mybir.EngineType.SP])
            q = pid % 4
            dyn["qrow0"] = e.snap(q * (2 * 2048), min_val=0, max_val=3 * 2 * 2048)
            dyn["qrow1"] = e.snap(q * (2 * 1536), min_val=0, max_val=3 * 2 * 1536)
            dyn["ctx0"] = e.snap(q * 64, min_val=0, max_val=192)
        P.raw("sp", setup)

        def finish():
            for name in debug:
                a = scr[name]
                if len(a.shape) > 2:
                    continue
            toks = []
            for name in debug:
                a, o = scr[name], dbg_out[name]
                if len(a.shape) == 3:
                    a = a.rearrange("c r t -> (c r) t"); o = o.rearrange("c r t -> (c r) t")
                toks.append(P.dma("sp", o, a))
            P.barrier()
            P.emit()
            return nc

        stages = ["mods", "inproj0", "conv0", "fourier", "ssd", "ag0", "outproj0", "ffn0", "ag1", "inproj1", "attn", "conf", "ag2", "outproj1", "ffn1"]
        last = stages.index(stop_after) if stop_after else len(stages) - 1

        def want(name):
            return stages.index(name) <= last

        phase_mods(P, cT2, mw, mb, selc, modT_d, gate_d)
        if not want("inproj0"):
            return finish()
        tile_srcs = [[(slice(0, 128), xall[t * 128:(t + 1) * 128, :])] for t in range(NTA)]
        tile_cls = [1 if t < 2 else 0 for t in range(NTA)]
        groups = [[0, 1]] + [[2 + 4 * g + i for i in range(4)] for g in range(16)]

        def fm_dst0(c6, gi):
            if gi == 0:
                return FM0[c6 * 128:(c6 + 1) * 128, 2:2 + TC]
            return FM0[c6 * 128:(c6 + 1) * 128, LAT0 + (gi - 1) * 512:LAT0 + gi * 512]
        phase_inproj(P, tile_srcs, tile_cls, groups, w_in0, 6, 396, modT_d[0], gT[0, 0], identb, fm_dst0,
                     lambda t: ZDT[t * 128:(t + 1) * 128, :])
        if not want("conv0"):
            return finish()
        phase_conv0(P, FM0, cw0, cb0, XBC)
        if not want("fourier"):
            return finish()
        phase_fourier(P, FM0, ccsc, CT, STt, CTc, STc, MIX0)
        if not want("ssd"):
            return finish()
        phase_ssd(P, XBC, ZDT, prm, tri, gnR, identf, YF, MIX0)
        if not want("ag0"):
            return finish()
        allgather(P, [MIX0[i] for i in range(9)], [GMIX0[i] for i in range(9)])
        if not want("outproj0"):
            return finish()
        g0f = GMIX0.rearrange("c r t -> (c r) t")
        P.dma("sp", OWN0.rearrange("c r t -> (c r) t"), lambda: g0f[2048:, :][bass.ds(dyn["qrow0"], 2 * 2048), :])
        P.dma("sp", OWNC, lambda: GMIX0[0][:, bass.ds(dyn["ctx0"], 64)])
        P.barrier()
        o0v = OWN0.rearrange("c (k p) t -> p c k t", p=128)
        ocv = OWNC.rearrange("(k p) t -> p k t", p=128)

        def mt0(t):
            if t < 16:
                return (lambda m: m[:], o0v[:, t // 8, :, (t % 8) * 128:(t % 8 + 1) * 128])
            return (lambda m: m[:, :, 0:64], ocv)
        phase_outproj(P, 2048, 17, mt0, xown, w_out0, gR[0, 1], gate_d[0], HMID)
        if not want("ffn0"):
            return finish()
        phase_ffn(P, HMID, 17, wg[0], wu[0], wd[0], modT_d[0], gT[0, 2], gR[0, 3], gate_d[0], identb, H1, (16,))
        if not want("ag1"):
            return finish()
        allgather(P, [H1[i * 256:(i + 1) * 256, :] for i in range(9)], [GH1[i] for i in range(9)])
        if not want("inproj1"):
            return finish()
        tile_srcs = [[(slice(0, 64), GH1[8, 0:64, :]), (slice(64, 128), GH1[8, 256:320, :])],
                     [(slice(0, 64), GH1[8, 512:576, :]), (slice(64, 128), GH1[8, 768:832, :])]]
        for j in range(64):
            r, t = j // 16, j % 16
            r0 = r * 256 + (t % 2) * 128
            tile_srcs.append([(slice(0, 128), GH1[t // 2, r0:r0 + 128, :])])
        phase_inproj(P, tile_srcs, tile_cls, groups, w_in1, 2, 768, modT_d[1], gT[1, 0], identb,
                     lambda c2, gi: UT[c2 * 128:(c2 + 1) * 128, 15 + (gi - 1) * 512:15 + gi * 512],
                     lambda t: QKV[t * 128:(t + 1) * 128, :], fm_groups=set(range(1, 17)))
        if not want("attn"):
            return finish()
        phase_attn(P, QKV, cs, sn, lamR, subC, identb, lambda_init, MIX1, NQB=NQB)
        if not want("conf"):
            return finish()
        phase_conformer(P, UT, cw1, cb1, lng, lnb, selm, STs, GST, MIX1)
        if not want("ag2"):
            return finish()
        allgather(P, [MIX1[i] for i in range(1, 9)], [GMIX1[i] for i in range(1, 9)])
        if not want("outproj1"):
            return finish()
        g1f = GMIX1.rearrange("c r t -> (c r) t")
        P.dma("sp", OWN1.rearrange("c r t -> (c r) t"), lambda: g1f[1536:, :][bass.ds(dyn["qrow1"], 2 * 1536), :])
        P.barrier()
        o1v = OWN1.rearrange("c (k p) t -> p c k t", p=128)

        def mt1(t):
            return (lambda m: m[:], o1v[:, t // 8, :, (t % 8) * 128:(t % 8 + 1) * 128])
        phase_outproj(P, 1536, 16, mt1, H1, w_out1, gR[1, 1], gate_d[1], HMID2)
        if not want("ffn1"):
            return finish()
        phase_ffn(P, HMID2, 16, wg[1], wu[1], wd[1], modT_d[1], gT[1, 2], gR[1, 3], gate_d[1], identb, out, ())
        return finish()


import math
import ml_dtypes

NCORES = 8
CORES = list(range(NCORES))
_NC_CACHE = {}


def featT(v):
    n = v.shape[0] // 128
    return np.ascontiguousarray(v.reshape(n, 128).T)


def rep(v):
    return np.ascontiguousarray(np.broadcast_to(v, (128,) + v.shape))


def dft_tabs(n):
    tab = np.arange(n, dtype=np.float64) * (2 * np.pi / n)
    idx = (np.arange(n, dtype=np.int64)[:, None] * np.arange(n, dtype=np.int64)[None, :]) % n
    c = (np.cos(tab) / math.sqrt(n)).astype(np.float32).astype(ml_dtypes.bfloat16)
    s = (np.sin(tab) / math.sqrt(n)).astype(np.float32).astype(ml_dtypes.bfloat16)
    return c[idx], s[idx]


def rope_tables():
    t = 8192
    row = np.repeat(np.arange(t // 64, dtype=np.float32), 64)
    col = np.tile(np.arange(64, dtype=np.float32), t // 64)
    inv = (10000.0 ** (-np.arange(16, dtype=np.float32) * 2.0 / 32)).astype(np.float32)
    ang = np.stack([row, col], -1)[:, :, None] * inv
    cos, sin = np.cos(ang).astype(np.float32), np.sin(ang).astype(np.float32)
    cs = np.broadcast_to(cos[:, None, :, None, :], (t, 2, 2, 2, 16)).reshape(t, 128)
    sg = np.array([-1.0, 1.0], np.float32)[None, None, None, :, None]
    sn = (np.broadcast_to(sin[:, None, :, None, :], (t, 2, 2, 2, 16)) * sg).reshape(t, 128)
    return np.ascontiguousarray(cs), np.ascontiguousarray(sn)


def prep_inputs(x, c, ctx, c_ctx, mod_w, mod_b, norm_g, ffn_w_gate, ffn_w_up, ffn_w_down,
                ev_w_in, ev_conv_w, ev_conv_b, ev_dt_bias, ev_a_log, ev_d_skip, ev_gnorm_g, ev_w_out,
                od_w_in, od_lambda, od_subln_g, od_conv_w, od_conv_b, od_cnorm_g, od_cnorm_b, od_w_out):
    identf = np.eye(128, dtype=np.float32)
    identb = identf.astype(ml_dtypes.bfloat16)
    selc = np.zeros((2, 258), np.float32)
    selc[0, 0] = 1; selc[1, 1] = 1; selc[0, 2:130] = 1; selc[1, 130:258] = 1
    gT = np.stack([np.stack([featT(norm_g[l, j]) for j in range(4)]) for l in range(2)])
    gR = np.stack([np.stack([rep(norm_g[l, j]) for j in range(4)]) for l in range(2)])
    CT, ST = dft_tabs(8192)
    CTc, STc = dft_tabs(256)
    kk = np.arange(128)
    ang = 2 * np.pi * np.outer(kk, kk) / 128
    ccsc = (np.concatenate([np.cos(ang), -np.sin(ang)], 1) / math.sqrt(128)).astype(np.float32)
    tri = np.zeros((128, 3, 128), np.float32)
    s_, l_ = np.meshgrid(np.arange(128), np.arange(128), indexing="ij")
    tri[:, 0] = (s_ <= l_); tri[:, 1] = (s_ >= l_); tri[:, 2] = 1.0
    cs, sn = rope_tables()
    selm = np.zeros((8, 2, 128), np.float32)
    selm[0::2, 0, :] = 1.0 / 512
    selm[1::2, 1, :] = 1.0 / 512
    p0 = np.concatenate([np.concatenate([np.arange(r * 128, (r + 1) * 128), 512 + np.arange(r * 384, (r + 1) * 384)]) for r in range(4)])
    p1 = np.concatenate([np.concatenate([np.arange(r * 256, (r + 1) * 256), 1024 + np.arange(r * 128, (r + 1) * 128)]) for r in range(4)])
    w_out0 = np.ascontiguousarray(ev_w_out[0][p0])
    w_out1 = np.ascontiguousarray(od_w_out[0][p1])
    shared = dict(mw=mod_w, mb=np.ascontiguousarray(mod_b[None]), selc=selc, identb=identb, identf=identf, gT=gT, gR=gR,
                  ccsc=ccsc, CT=CT, STt=ST, CTc=CTc, STc=STc, tri=tri, w_out0=w_out0, wg=ffn_w_gate, wu=ffn_w_up, wd=ffn_w_down,
                  cs=cs, sn=sn, lamR=rep(od_lambda[0]), subC=np.ascontiguousarray(od_subln_g[0].reshape(128, 1)), selm=selm, w_out1=w_out1)
    maps = []
    for core in CORES:
        b, g = core // 4, core % 4
        q = g
        m = dict(shared)
        m["xall"] = np.ascontiguousarray(np.concatenate([ctx[b], x[b]], 0))
        xo = np.zeros((17 * 128, 1024), np.float32)
        xo[:2048] = x[b, q * 2048:(q + 1) * 2048]; xo[2048:2112] = ctx[b, q * 64:(q + 1) * 64]
        m["xown"] = xo
        cv = np.stack([c[b], c_ctx], 0)
        m["cT2"] = np.ascontiguousarray(cv.reshape(2, 8, 128).transpose(2, 1, 0))
        dtcols = np.array([4608 + d * 24 + g * 6 + h for d in range(2) for h in range(6)])
        cols0 = np.concatenate([np.arange(g * 128, (g + 1) * 128), 2048 + np.arange(g * 384, (g + 1) * 384),
                                2048 + 1536 + np.arange(g * 128, (g + 1) * 128), 2048 + 2048 + np.arange(g * 128, (g + 1) * 128),
                                512 + np.arange(g * 384, (g + 1) * 384), dtcols])
        m["w_in0"] = np.ascontiguousarray(ev_w_in[0][:, cols0])
        ch = np.concatenate([np.arange(g * 384, (g + 1) * 384), 1536 + np.arange(g * 128, (g + 1) * 128), 2048 + np.arange(g * 128, (g + 1) * 128)])
        m["cw0"] = np.ascontiguousarray(ev_conv_w[0][:, ch].T.reshape(5, 128, 5).transpose(1, 0, 2))
        m["cb0"] = np.ascontiguousarray(ev_conv_b[0][ch].reshape(5, 128).T)
        prm = np.zeros((128, 3, 2, 66, 6), np.float32)
        for k_, a_ in enumerate((ev_dt_bias, ev_a_log, ev_d_skip)):
            prm[:, k_] = a_[0].reshape(2, 4, 6)[:, g, :][None, :, None, :]
        m["prm"] = prm
        m["gnR"] = rep(ev_gnorm_g[0][g * 384:(g + 1) * 384])
        hd0 = 2 * q
        cols1 = np.concatenate([3072 + np.arange(q * 128, (q + 1) * 128), 3072 + 512 + np.arange(q * 128, (q + 1) * 128),
                                np.arange(hd0 * 128, (hd0 + 2) * 128), 1024 + np.arange(hd0 * 128, (hd0 + 2) * 128),
                                2048 + np.arange(hd0 * 128, (hd0 + 2) * 128)])
        m["w_in1"] = np.ascontiguousarray(od_w_in[0][:, cols1])
        cq = slice(q * 128, (q + 1) * 128)
        m["cw1"] = np.ascontiguousarray(od_conv_w[0][:, cq].T.reshape(128, 1, 31))
        m["cb1"] = np.ascontiguousarray(od_conv_b[0][cq].reshape(128, 1))
        m["lng"] = np.ascontiguousarray(od_cnorm_g[0][cq].reshape(128, 1))
        m["lnb"] = np.ascontiguousarray(od_cnorm_b[0][cq].reshape(128, 1))
        maps.append(m)
    return maps


def kernel(**inputs):
    inputs = {k: np.ascontiguousarray(np.asarray(v, dtype=np.float32)) for k, v in inputs.items()}
    maps = prep_inputs(**inputs)
    if "fused" not in _NC_CACHE:
        _NC_CACHE["fused"] = build_fused()
    res = run_bass_kernel_spmd(_NC_CACHE["fused"], maps, core_ids=CORES)
    out = np.zeros((2, 8192, 1024), np.float32)
    for core in CORES:
        b, q = core // 4, core % 4
        out[b, q * 2048:(q + 1) * 2048] = res.results[core]["out"]
    return out
```

```python
import numpy as np
from contextlib import ExitStack
import concourse.bass as bass
import concourse.mybir as mybir
from concourse.bass_utils import run_bass_kernel_spmd

F32 = mybir.dt.float32
BF16 = mybir.dt.bfloat16
AF = mybir.ActivationFunctionType
ALU = mybir.AluOpType
AX = mybir.AxisListType


class Buf:
    __slots__ = ("name", "w", "r")

    def __init__(self, name=""):
        self.name = name
        self.w = {}
        self.r = {}


def _merge(dst, tok):
    k, v, e = tok
    if k not in dst or dst[k][0] < v:
        dst[k] = (v, e)


class Prog:
    ENGS = ("pe", "act", "dve", "pool", "sp")
    RING = 8

    def __init__(self, nc, stack):
        self.nc = nc
        self.stack = stack
        self.ops = {e: [] for e in self.ENGS}
        self.ccount = {e: 0 for e in self.ENGS}
        self.dcount = {e: 0 for e in self.ENGS}
        self.seen = {e: {} for e in self.ENGS}
        self.sems = {}
        self.nbuf = 0
        self.ncc = 0
        self._root_stack = stack
        self.scope_id = 0
        self._nscope = 0
        for e in self.ENGS:
            self.sems[("c", e)] = stack.enter_context(nc.semaphore("c_" + e))
        for e in ("sp", "pool", "act"):
            for i in range(self.RING):
                self.sems[("d", e, i)] = stack.enter_context(nc.semaphore(f"d_{e}{i}"))

    def sb(self, name, shape, dtype):
        return self.stack.enter_context(self.nc.sbuf_tensor(f"sb{self.scope_id}_" + name, list(shape), dtype))

    def ps(self, name, shape, dtype):
        return self.stack.enter_context(self.nc.psum_tensor(f"ps{self.scope_id}_" + name, list(shape), dtype))

    def buf(self, name=""):
        self.nbuf += 1
        return Buf(name or f"b{self.nbuf}")

    def alias(self, old):
        b = self.buf()
        for k, (v, e) in list(old.r.items()) + list(old.w.items()):
            _merge(b.r, (k, v, e))
        return b

    def _deps(self, eng, reads, writes, djw=()):
        deps = {}

        def add(k, v, e2):
            if eng == "pe" and e2 == "pe" and k[0] == "c":
                return
            if v > deps.get(k, 0):
                deps[k] = v
        for b in reads:
            for k, (v, e2) in b.w.items():
                add(k, v, e2)
        for b in writes:
            for k, (v, e2) in b.w.items():
                add(k, v, e2)
            for k, (v, e2) in b.r.items():
                add(k, v, e2)
        for b in djw:
            for k, (v, e2) in b.r.items():
                add(k, v, e2)
        seen = self.seen[eng]
        out = []
        for k, v in deps.items():
            if seen.get(k, 0) >= v:
                continue
            seen[k] = v
            out.append((k, v))
        return out

    def _record(self, tok, reads, writes, djw=()):
        for b in reads:
            _merge(b.r, tok)
        for b in writes:
            b.w = {tok[0]: (tok[1], tok[2])}
            b.r = {}
        for b in djw:
            _merge(b.w, tok)

    def op(self, eng, fn, reads=(), writes=(), djw=()):
        waits = self._deps(eng, reads, writes, djw)
        self.ccount[eng] += 1
        tok = (("c", eng), self.ccount[eng], eng)
        self.ops[eng].append(("c", fn, waits, tok))
        self._record(tok, reads, writes, djw)
        return tok

    def dma(self, eng, out_ap, in_ap, reads=(), writes=(), djw=(), **kw):
        waits = self._deps(eng, reads, writes, djw)
        j = self.dcount[eng]
        self.dcount[eng] += 1
        slot = j % self.RING
        k = ("d", eng, slot)
        need = 16 * (j // self.RING)
        if need > 0 and self.seen[eng].get(k, 0) < need:
            self.seen[eng][k] = need
            waits.append((k, need))
        tok = (k, 16 * (j // self.RING + 1), eng)

        def fn(e, out_ap=out_ap, in_ap=in_ap, kw=kw):
            o = out_ap() if callable(out_ap) else out_ap
            i = in_ap() if callable(in_ap) else in_ap
            return e.dma_start(out=o, in_=i, **kw)
        self.ops[eng].append(("d", fn, waits, tok))
        self._record(tok, reads, writes, djw)
        return tok

    def finish_wait(self, eng, toks):
        waits = []
        for (k, v, _e) in toks:
            if self.seen[eng].get(k, 0) < v:
                self.seen[eng][k] = v
                waits.append((k, v))
        self.ops[eng].append(("w", None, waits, None))

    def wait_all_dma(self, eng="sp"):
        toks = []
        for e in ("sp", "pool", "act"):
            n = self.dcount[e]
            for slot in range(self.RING):
                if n == 0:
                    continue
                last = ((n - 1 - slot) // self.RING) * self.RING + slot if n - 1 >= slot else -1
                if last >= 0:
                    toks.append((("d", e, slot), 16 * (last // self.RING + 1), e))
        self.finish_wait(eng, toks)

    def push_scope(self):
        self._outer = getattr(self, "_outer", [])
        self._outer.append(self.stack)
        self.stack = ExitStack()
        self.stack.__enter__()
        self._nscope += 1
        self.scope_id = self._nscope

    def pop_scope(self):
        self.stack.__exit__(None, None, None)
        self.stack = self._outer.pop()

    def all_tokens(self):
        toks = []
        for e in self.ENGS:
            if self.ccount[e] > 0:
                toks.append((("c", e), self.ccount[e], e))
        for e in ("sp", "pool", "act"):
            n = self.dcount[e]
            for slot in range(self.RING):
                if n - 1 >= slot:
                    last = ((n - 1 - slot) // self.RING) * self.RING + slot
                    toks.append((("d", e, slot), 16 * (last // self.RING + 1), e))
        for i in range(self.ncc):
            toks.append((("cc", i), 1, "pool"))
        return toks

    def barrier(self):
        toks = self.all_tokens()
        for e in self.ENGS:
            self.finish_wait(e, toks)

    def cc(self, kind, in_ap, out_ap, groups, reads=(), writes=()):
        waits = self._deps("pool", reads, writes)
        i = self.ncc
        self.ncc += 1
        k = ("cc", i)
        self.sems[k] = self._root_stack.enter_context(self.nc.semaphore(f"cc{i}"))
        tok = (k, 1, "pool")

        def fn(e):
            return e.collective_compute(kind, ALU.bypass, replica_groups=groups, ins=[in_ap], outs=[out_ap])
        self.ops["pool"].append(("x", fn, waits, tok))
        self._record(tok, reads, writes)
        return tok

    def raw(self, eng, fn):
        self.ops[eng].append(("r", fn, [], None))

    def emit(self):
        nc = self.nc
        prog = self
        with nc.Block() as block:
            def run(engname, e):
                for kind, fn, waits, tok in prog.ops[engname]:
                    for (k, v) in waits:
                        e.wait_ge(prog.sems[k], v)
                    if kind == "w":
                        continue
                    if kind == "r":
                        fn(e)
                        continue
                    ins = fn(e)
                    if kind == "c":
                        ins.then_inc(prog.sems[tok[0]], 1)
                    elif kind == "x":
                        ins.then_inc(prog.sems[tok[0]])
                    else:
                        ins.then_inc(prog.sems[tok[0]], 16)

            @block.tensor
            def _(e):
                run("pe", e)

            @block.scalar
            def _(e):
                run("act", e)

            @block.vector
            def _(e):
                run("dve", e)

            @block.gpsimd
            def _(e):
                run("pool", e)

            @block.sync
            def _(e):
                run("sp", e)


EPS = 1e-6


def new_nc():
    return bass.Bass("TRN2", target_bir_lowering=False)


def load_weight_bf16(P, w_dram, w_sb, b_w, nk, ncol, stage, b_stage, cast_engs=("pool",)):
    for k in range(nk):
        s = k % len(stage)
        P.dma("sp", stage[s][:, 0:ncol], w_dram[k * 128:(k + 1) * 128, :], writes=[b_stage[s]])
        eng = cast_engs[k % len(cast_engs)]
        P.op(eng, lambda e, s=s, k=k: e.tensor_copy(out=w_sb[:, k, :], in_=stage[s][:, 0:ncol]),
             reads=[b_stage[s]], writes=[b_w])


class NormT:
    def __init__(self, P, pfx, ident, b_ident, nt):
        self.P = P
        self.ident, self.b_ident = ident, b_ident
        self.junk = P.sb(pfx + "junk", [128, 1024], BF16)
        self.b_junk = P.buf()
        self.ss = P.sb(pfx + "ss", [128, nt], F32)
        self.b_ss = P.buf()
        self.rs = P.sb(pfx + "rs", [128, nt], F32)
        self.b_rs = [P.buf() for _ in range(nt)]
        self.xn = [P.sb(pfx + f"xn{i}", [128, 1024], BF16) for i in range(2)]
        self.b_xn = [P.buf() for _ in range(2)]
        self.psT = [P.ps(pfx + f"psT{i}", [128, 8, 128], BF16) for i in range(2)]
        self.b_psT = [P.buf() for _ in range(2)]
        P.op("pool", lambda e: e.memset(self.ss[:], 0.0), writes=[self.b_ss])
        self.n = 0

    def run(self, x_ap, b_x, t, aT_ap, b_aT, Gs, Sh, b_gs, cls):
        P = self.P
        i = self.n % 2
        self.n += 1
        ss, rs, xn, psT = self.ss, self.rs, self.xn[i], self.psT[i]
        P.op("act", lambda e: e.activation(out=self.junk[:], in_=x_ap, func=AF.Square, accum_out=ss[:, t:t + 1]),
             reads=[b_x, self.b_ss], writes=[self.b_junk, self.b_rs[t]])
        P.op("dve", lambda e: e.tensor_scalar(out=rs[:, t:t + 1], in0=ss[:, t:t + 1], scalar1=1.0 / 1024, scalar2=EPS,
                                              op0=ALU.mult, op1=ALU.add), reads=[self.b_rs[t]], writes=[self.b_rs[t]])
        P.op("act", lambda e: e.activation(out=rs[:, t:t + 1], in_=rs[:, t:t + 1], func=AF.Sqrt),
             reads=[self.b_rs[t]], writes=[self.b_rs[t]])
        P.op("dve", lambda e: e.reciprocal(out=rs[:, t:t + 1], in_=rs[:, t:t + 1]),
             reads=[self.b_rs[t]], writes=[self.b_rs[t]])
        P.op("dve", lambda e: e.tensor_scalar_mul(out=xn[:], in0=x_ap, scalar1=rs[:, t:t + 1]),
             reads=[b_x, self.b_rs[t]], writes=[self.b_xn[i]])
        for k in range(8):
            P.op("pe", lambda e, k=k: e.transpose(out=psT[:, k, :], in_=xn[:, k * 128:(k + 1) * 128], identity=self.ident[:]),
                 reads=[self.b_xn[i], self.b_ident], writes=[self.b_psT[i]])
        for k in range(8):
            P.op("act", lambda e, k=k: e.activation(out=aT_ap[:, k, :], in_=psT[:, k, :], func=AF.Identity,
                                                    scale=Gs[:, cls, k:k + 1], bias=Sh[:, cls, k:k + 1]),
                 reads=[self.b_psT[i], b_gs], djw=[b_aT])


def build_k1(NT, NOUT, ctx_tiles=(16,)):
    nc = new_nc()
    h = nc.dram_tensor("h", [NT * 128, 1024], F32, kind="ExternalInput").ap()
    w = nc.dram_tensor("w", [1024, NOUT], F32, kind="ExternalInput").ap()
    modT = nc.dram_tensor("modT", [128, 2, 2, 8], F32, kind="ExternalInput").ap()
    gT = nc.dram_tensor("gT", [128, 8], F32, kind="ExternalInput").ap()
    identd = nc.dram_tensor("ident", [128, 128], BF16, kind="ExternalInput").ap()
    out = nc.dram_tensor("out", [NT * 128, NOUT], F32, kind="ExternalOutput").ap()
    ncb = (NOUT + 511) // 512
    with ExitStack() as st:
        P = Prog(nc, st)
        ident = P.sb("ident", [128, 128], BF16); b_ident = P.buf()
        P.dma("sp", ident[:], identd, writes=[b_ident])
        modsb = P.sb("modsb", [128, 2, 2, 8], F32); b_mod = P.buf()
        P.dma("sp", modsb[:], modT, writes=[b_mod])
        gsb = P.sb("gsb", [128, 8], F32); b_g = P.buf()
        P.dma("sp", gsb[:], gT, writes=[b_g])
        Gs = P.sb("Gs", [128, 2, 8], F32); Sh = P.sb("Sh", [128, 2, 8], F32); b_gs = P.buf()
        for cls in range(2):
            P.op("dve", lambda e, cls=cls: e.scalar_tensor_tensor(out=Gs[:, cls, :], in0=modsb[:, cls, 1, :], scalar=1.0,
                                                                   in1=gsb[:], op0=ALU.add, op1=ALU.mult),
                 reads=[b_mod, b_g], writes=[b_gs])
            P.op("dve", lambda e, cls=cls: e.tensor_copy(out=Sh[:, cls, :], in_=modsb[:, cls, 0, :]),
                 reads=[b_mod], writes=[b_gs])
        w_sb = P.sb("w_sb", [128, 8, NOUT], BF16); b_w = P.buf()
        stage = [P.sb(f"stage{i}", [128, NOUT], F32) for i in range(2)]
        b_stage = [P.buf() for _ in range(2)]
        load_weight_bf16(P, w, w_sb, b_w, 8, NOUT, stage, b_stage)
        nt = NormT(P, "n_", ident, b_ident, NT)
        xt = [P.sb(f"xt{i}", [128, 1024], F32) for i in range(2)]; b_xt = [P.buf() for _ in range(2)]
        aT = [P.sb(f"aT{i}", [128, 8, 128], BF16) for i in range(2)]; b_aT = [P.buf() for _ in range(2)]
        ot = stage; b_ot = b_stage
        pso = [P.ps(f"pso{i}", [128, 512], F32) for i in range(4)]; b_pso = [P.buf() for _ in range(4)]
        outs = []
        nps = 0
        for t in range(NT):
            i = t % 2
            cls = 1 if t in ctx_tiles else 0
            P.dma("sp", xt[i][:], h[t * 128:(t + 1) * 128, :], writes=[b_xt[i]])
            nt.run(xt[i][:], b_xt[i], t, aT[i], b_aT[i], Gs, Sh, b_gs, cls)
            for cb in range(ncb):
                c0 = cb * 512; cn = min(512, NOUT - c0)
                pi = nps % 4; nps += 1
                for k in range(8):
                    P.op("pe", lambda e, pi=pi, k=k, c0=c0, cn=cn, i=i: e.matmul(
                        pso[pi][:, 0:cn], lhsT=aT[i][:, k, :], rhs=w_sb[:, k, c0:c0 + cn], start=(k == 0), stop=(k == 7)),
                        reads=[b_aT[i], b_w], writes=[b_pso[pi]])
                if cb % 2 == 0:
                    P.op("dve", lambda e, pi=pi, c0=c0, cn=cn, i=i: e.tensor_copy(out=ot[i][:, c0:c0 + cn], in_=pso[pi][:, 0:cn]),
                         reads=[b_pso[pi]], writes=[b_ot[i]])
                else:
                    P.op("act", lambda e, pi=pi, c0=c0, cn=cn, i=i: e.copy(out=ot[i][:, c0:c0 + cn], in_=pso[pi][:, 0:cn]),
                         reads=[b_pso[pi]], writes=[b_ot[i]])
            outs.append(P.dma("sp", out[t * 128:(t + 1) * 128, :], ot[i][:], reads=[b_ot[i]]))
        P.finish_wait("sp", outs)
        P.emit()
    return nc


class ResNorm:
    def __init__(self, P, pfx, nt):
        self.P = P
        self.junk = P.sb(pfx + "junk", [128, 1024], BF16); self.b_junk = P.buf()
        self.ss = P.sb(pfx + "ss", [128, nt], F32); self.b_ss = P.buf()
        self.rs = P.sb(pfx + "rs", [128, nt], F32); self.b_rs = [P.buf() for _ in range(nt)]
        self.tmp = [P.sb(pfx + f"tmp{i}", [128, 1024], F32) for i in range(2)]; self.b_tmp = [P.buf() for _ in range(2)]
        P.op("pool", lambda e: e.memset(self.ss[:], 0.0), writes=[self.b_ss])
        self.n = 0

    def run(self, po, b_po, t, hres, b_hres, GG_ap, b_gg, out_ap=None, b_out=None):
        P = self.P
        i = self.n % 2
        self.n += 1
        ss, rs, tmp = self.ss, self.rs, self.tmp[i]
        P.op("act", lambda e: e.activation(out=self.junk[:], in_=po, func=AF.Square, accum_out=ss[:, t:t + 1]),
             reads=[b_po, self.b_ss], writes=[self.b_junk, self.b_rs[t]])
        P.op("dve", lambda e: e.tensor_scalar(out=rs[:, t:t + 1], in0=ss[:, t:t + 1], scalar1=1.0 / 1024, scalar2=EPS,
                                              op0=ALU.mult, op1=ALU.add), reads=[self.b_rs[t]], writes=[self.b_rs[t]])
        P.op("act", lambda e: e.activation(out=rs[:, t:t + 1], in_=rs[:, t:t + 1], func=AF.Sqrt),
             reads=[self.b_rs[t]], writes=[self.b_rs[t]])
        P.op("dve", lambda e: e.reciprocal(out=rs[:, t:t + 1], in_=rs[:, t:t + 1]),
             reads=[self.b_rs[t]], writes=[self.b_rs[t]])
        P.op("dve", lambda e: e.scalar_tensor_tensor(out=tmp[:], in0=po, scalar=rs[:, t:t + 1], in1=GG_ap,
                                                     op0=ALU.mult, op1=ALU.mult),
             reads=[b_po, self.b_rs[t], b_gg], writes=[self.b_tmp[i]])
        if out_ap is None:
            P.op("pool", lambda e: e.tensor_add(out=tmp[:], in0=tmp[:], in1=hres),
                 reads=[self.b_tmp[i], b_hres], writes=[self.b_tmp[i]])
            return tmp, self.b_tmp[i]
        P.op("pool", lambda e: e.tensor_add(out=out_ap, in0=tmp[:], in1=hres),
             reads=[self.b_tmp[i], b_hres], writes=[b_out])
        return None


def build_k3a(NT, CM, ctx_tiles=(16,)):
    nc = new_nc()
    nk = CM // 128
    mixT = nc.dram_tensor("mixT", [CM, NT * 128], F32, kind="ExternalInput").ap()
    h = nc.dram_tensor("h", [NT * 128, 1024], F32, kind="ExternalInput").ap()
    w = nc.dram_tensor("w", [CM, 1024], F32, kind="ExternalInput").ap()
    gR = nc.dram_tensor("gR", [128, 1024], F32, kind="ExternalInput").ap()
    gateR = nc.dram_tensor("gateR", [128, 2, 1024], F32, kind="ExternalInput").ap()
    out = nc.dram_tensor("out", [NT * 128, 1024], F32, kind="ExternalOutput").ap()
    with ExitStack() as st:
        P = Prog(nc, st)
        g_sb = P.sb("g_sb", [128, 1024], F32); b_g = P.buf()
        P.dma("sp", g_sb[:], gR, writes=[b_g])
        GG = P.sb("GG", [128, 2, 1024], F32); b_gg = P.buf()
        P.dma("sp", GG[:], gateR, writes=[b_gg])
        for cls in range(2):
            P.op("dve", lambda e, cls=cls: e.tensor_mul(out=GG[:, cls, :], in0=GG[:, cls, :], in1=g_sb[:]),
                 reads=[b_g, b_gg], writes=[b_gg])
        w_sb = P.sb("w_sb", [128, nk, 1024], BF16); b_w = P.buf()
        stage = [P.sb(f"stage{i}", [128, 1024], F32) for i in range(2)]; b_stage = [P.buf() for _ in range(2)]
        load_weight_bf16(P, w, w_sb, b_w, nk, 1024, stage, b_stage)
        rn = ResNorm(P, "r_", NT)
        mst = [P.sb(f"mst{i}", [128, nk, 128], F32) for i in range(2)]; b_mst = [P.buf() for _ in range(2)]
        mT = [P.sb(f"mT{i}", [128, nk, 128], BF16) for i in range(2)]; b_mT = [P.buf() for _ in range(2)]
        xt = [P.sb(f"xt{i}", [128, 1024], F32) for i in range(2)]; b_xt = [P.buf() for _ in range(2)]
        ot = [P.sb(f"ot{i}", [128, 1024], F32) for i in range(2)]; b_ot = [P.buf() for _ in range(2)]
        po = [P.ps(f"po{i}", [128, 1024], F32) for i in range(2)]; b_po = [P.buf() for _ in range(2)]
        mv = mixT.rearrange("(k p) t -> p k t", p=128)
        outs = []
        for t in range(NT):
            i = t % 2
            cls = 1 if t in ctx_tiles else 0
            P.dma("sp", mst[i][:], mv[:, :, t * 128:(t + 1) * 128], writes=[b_mst[i]])
            P.dma("sp", xt[i][:], h[t * 128:(t + 1) * 128, :], writes=[b_xt[i]])
            P.op("pool", lambda e, i=i: e.tensor_copy(out=mT[i][:], in_=mst[i][:]), reads=[b_mst[i]], writes=[b_mT[i]])
            for cb in range(2):
                for k in range(nk):
                    P.op("pe", lambda e, i=i, k=k, cb=cb: e.matmul(
                        po[i][:, cb * 512:(cb + 1) * 512], lhsT=mT[i][:, k, :], rhs=w_sb[:, k, cb * 512:(cb + 1) * 512],
                        start=(k == 0), stop=(k == nk - 1)), reads=[b_mT[i], b_w], writes=[b_po[i]])
            rn.run(po[i][:], b_po[i], t, xt[i][:], b_xt[i], GG[:, cls, :], b_gg, ot[i][:], b_ot[i])
            outs.append(P.dma("sp", out[t * 128:(t + 1) * 128, :], ot[i][:], reads=[b_ot[i]]))
        P.finish_wait("sp", outs)
        P.emit()
    return nc


def build_k3b(NT, ctx_tiles=(16,)):
    nc = new_nc()
    FH = 2816
    NJ = FH // 128
    h = nc.dram_tensor("h", [NT * 128, 1024], F32, kind="ExternalInput").ap()
    wg = nc.dram_tensor("wg", [1024, FH], F32, kind="ExternalInput").ap()
    wu = nc.dram_tensor("wu", [1024, FH], F32, kind="ExternalInput").ap()
    wd = nc.dram_tensor("wd", [FH, 1024], F32, kind="ExternalInput").ap()
    modT = nc.dram_tensor("modT", [128, 2, 2, 8], F32, kind="ExternalInput").ap()
    gT = nc.dram_tensor("gT", [128, 8], F32, kind="ExternalInput").ap()
    gR = nc.dram_tensor("gR", [128, 1024], F32, kind="ExternalInput").ap()
    gateR = nc.dram_tensor("gateR", [128, 2, 1024], F32, kind="ExternalInput").ap()
    identd = nc.dram_tensor("ident", [128, 128], BF16, kind="ExternalInput").ap()
    out = nc.dram_tensor("out", [NT * 128, 1024], F32, kind="ExternalOutput").ap()
    with ExitStack() as st:
        P = Prog(nc, st)
        ident = P.sb("ident", [128, 128], BF16); b_ident = P.buf()
        P.dma("sp", ident[:], identd, writes=[b_ident])
        modsb = P.sb("modsb", [128, 2, 2, 8], F32); b_mod = P.buf()
        P.dma("sp", modsb[:], modT, writes=[b_mod])
        gsb = P.sb("gsb", [128, 8], F32); b_g = P.buf()
        P.dma("sp", gsb[:], gT, writes=[b_g])
        Gs = P.sb("Gs", [128, 2, 8], F32); Sh = P.sb("Sh", [128, 2, 8], F32); b_gs = P.buf()
        for cls in range(2):
            P.op("dve", lambda e, cls=cls: e.scalar_tensor_tensor(out=Gs[:, cls, :], in0=modsb[:, cls, 1, :], scalar=1.0,
                                                                   in1=gsb[:], op0=ALU.add, op1=ALU.mult),
                 reads=[b_mod, b_g], writes=[b_gs])
            P.op("dve", lambda e, cls=cls: e.tensor_copy(out=Sh[:, cls, :], in_=modsb[:, cls, 0, :]),
                 reads=[b_mod], writes=[b_gs])
        g_sb = P.sb("g_sb", [128, 1024], F32); b_g3 = P.buf()
        P.dma("sp", g_sb[:], gR, writes=[b_g3])
        GG = P.sb("GG", [128, 2, 1024], F32); b_gg = P.buf()
        P.dma("sp", GG[:], gateR, writes=[b_gg])
        for cls in range(2):
            P.op("dve", lambda e, cls=cls: e.tensor_mul(out=GG[:, cls, :], in0=GG[:, cls, :], in1=g_sb[:]),
                 reads=[b_g3, b_gg], writes=[b_gg])
        wg_sb = P.sb("wg_sb", [128, 8, FH], BF16); b_wg = P.buf()
        wu_sb = P.sb("wu_sb", [128, 8, FH], BF16); b_wu = P.buf()
        wd_sb = P.sb("wd_sb", [128, NJ, 1024], BF16); b_wd = P.buf()
        stage = [P.sb(f"stage{i}", [128, FH], F32) for i in range(2)]; b_stage = [P.buf() for _ in range(2)]
        load_weight_bf16(P, wg, wg_sb, b_wg, 8, FH, stage, b_stage, cast_engs=("pool", "dve"))
        load_weight_bf16(P, wu, wu_sb, b_wu, 8, FH, stage, b_stage, cast_engs=("pool", "dve"))
        load_weight_bf16(P, wd, wd_sb, b_wd, NJ, 1024, stage, b_stage, cast_engs=("pool", "dve"))
        nt = NormT(P, "n_", ident, b_ident, NT)
        rn = ResNorm(P, "r_", NT)
        ST = 2
        xt = [stage[0][:, i * 1024:(i + 1) * 1024] for i in range(2)]; b_xt = [P.alias(b_stage[0]) for _ in range(2)]
        xr = [stage[1][:, i * 1024:(i + 1) * 1024] for i in range(2)]; b_xr = [P.alias(b_stage[1]) for _ in range(2)]
        aT = P.sb("aT", [128, 8, ST * 128], BF16); b_aT = P.buf()
        hidT = P.sb("hidT", [128, NJ, ST * 128], BF16); b_hid = P.buf()
        sg = [P.sb(f"sg{i}", [128, ST * 128], F32) for i in range(2)]; b_sg = [P.buf() for _ in range(2)]
        psg = [P.ps(f"psg{i}", [128, 512], F32) for i in range(2)]; b_psg = [P.buf() for _ in range(2)]
        psu = [P.ps(f"psu{i}", [128, 512], F32) for i in range(2)]; b_psu = [P.buf() for _ in range(2)]
        po = P.ps("po", [128, 1024], F32); b_po = P.buf()
        outs = []
        nx = 0
        nr = 0
        for s0 in range(0, NT, ST):
            tiles = list(range(s0, min(NT, s0 + ST)))
            N = len(tiles) * 128
            for ti, t in enumerate(tiles):
                i = nx % 2; nx += 1
                cls = 1 if t in ctx_tiles else 0
                P.dma("sp", xt[i], h[t * 128:(t + 1) * 128, :], writes=[b_xt[i]])
                nt.run(xt[i], b_xt[i], t, aT[:, :, ti * 128:(ti + 1) * 128], b_aT, Gs, Sh, b_gs, cls)
            for j in range(NJ):
                pi = j % 2
                for k in range(8):
                    P.op("pe", lambda e, pi=pi, j=j, k=k, N=N: e.matmul(
                        psg[pi][:, 0:N], lhsT=wg_sb[:, k, j * 128:(j + 1) * 128], rhs=aT[:, k, 0:N],
                        start=(k == 0), stop=(k == 7)), reads=[b_wg, b_aT], writes=[b_psg[pi]])
                for k in range(8):
                    P.op("pe", lambda e, pi=pi, j=j, k=k, N=N: e.matmul(
                        psu[pi][:, 0:N], lhsT=wu_sb[:, k, j * 128:(j + 1) * 128], rhs=aT[:, k, 0:N],
                        start=(k == 0), stop=(k == 7)), reads=[b_wu, b_aT], writes=[b_psu[pi]])
                P.op("act", lambda e, pi=pi, N=N: e.activation(out=sg[pi][:, 0:N], in_=psg[pi][:, 0:N], func=AF.Silu),
                     reads=[b_psg[pi]], writes=[b_sg[pi]])
                P.op("dve", lambda e, pi=pi, j=j, N=N: e.tensor_mul(out=hidT[:, j, 0:N], in0=sg[pi][:, 0:N], in1=psu[pi][:, 0:N]),
                     reads=[b_sg[pi], b_psu[pi]], writes=[b_hid])
            for ti, t in enumerate(tiles):
                i = nr % 2; nr += 1
                cls = 1 if t in ctx_tiles else 0
                P.dma("sp", xr[i], h[t * 128:(t + 1) * 128, :], writes=[b_xr[i]])
                for cb in range(2):
                    for j in range(NJ):
                        P.op("pe", lambda e, j=j, cb=cb, ti=ti: e.matmul(
                            po[:, cb * 512:(cb + 1) * 512], lhsT=hidT[:, j, ti * 128:(ti + 1) * 128],
                            rhs=wd_sb[:, j, cb * 512:(cb + 1) * 512], start=(j == 0), stop=(j == NJ - 1)),
                            reads=[b_hid, b_wd], writes=[b_po])
                o_t, b_o = rn.run(po[:], b_po, t, xr[i], b_xr[i], GG[:, cls, :], b_gg)
                outs.append(P.dma("sp", out[t * 128:(t + 1) * 128, :], o_t[:], reads=[b_o]))
        P.finish_wait("sp", outs)
        P.emit()
    return nc


def conv_fm(P, eng, vin, b_in, wsb, j, bias_ap, b_w, acc, b_acc, K, T, t0=0):
    P.op(eng, lambda e: e.tensor_scalar(out=acc, in0=vin[:, t0:t0 + T], scalar1=wsb[:, j, 0:1], scalar2=bias_ap,
                                        op0=ALU.mult, op1=ALU.add), reads=[b_in, b_w], writes=[b_acc])
    for k in range(1, K):
        P.op(eng, lambda e, k=k: e.scalar_tensor_tensor(out=acc, in0=vin[:, t0 + k:t0 + k + T], scalar=wsb[:, j, k:k + 1],
                                                        in1=acc, op0=ALU.mult, op1=ALU.add),
             reads=[b_in, b_w, b_acc], writes=[b_acc])


def build_k2a(TL=8192, TC=256, NCH=5, K=5):
    nc = new_nc()
    H = K - 1
    TT = TL + TC + 2 * H
    pre = nc.dram_tensor("pre", [NCH * 128, TT], F32, kind="ExternalInput").ap()
    cw = nc.dram_tensor("cw", [128, NCH, K], F32, kind="ExternalInput").ap()
    cb = nc.dram_tensor("cb", [128, NCH], F32, kind="ExternalInput").ap()
    out = nc.dram_tensor("out", [NCH * 128, TL + TC], F32, kind="ExternalOutput").ap()
    BL = 2048
    with ExitStack() as st:
        P = Prog(nc, st)
        wsb = P.sb("wsb", [128, NCH, K], F32); bsb = P.sb("bsb", [128, NCH], F32); b_w = P.buf()
        P.dma("sp", wsb[:], cw, writes=[b_w])
        P.dma("sp", bsb[:], cb, writes=[b_w])
        vin = [P.sb(f"vin{i}", [128, TT], F32) for i in range(2)]; b_vin = [P.buf() for _ in range(2)]
        acc = [P.sb(f"acc{i}", [128, BL], F32) for i in range(2)]; b_acc = [P.buf() for _ in range(2)]
        res = [P.sb(f"res{i}", [128, BL], F32) for i in range(2)]; b_res = [P.buf() for _ in range(2)]
        outs = []
        n = 0
        for j in range(NCH):
            vi = j % 2
            P.dma("sp", vin[vi][:], pre[j * 128:(j + 1) * 128, :], writes=[b_vin[vi]])
            blocks = [(t0, BL, t0) for t0 in range(0, TL, BL)] + [(TL + H, TC, TL)]
            for (i0, T, o0) in blocks:
                i = n % 2; n += 1
                eng = "dve" if i == 0 else "pool"
                conv_fm(P, "dve", vin[vi], b_vin[vi], wsb, j, bsb[:, j:j + 1], b_w, acc[i][:, 0:T], b_acc[i], K, T, t0=i0)
                P.op("act", lambda e, i=i, T=T: e.activation(out=res[i][:, 0:T], in_=acc[i][:, 0:T], func=AF.Silu),
                     reads=[b_acc[i]], writes=[b_res[i]])
                outs.append(P.dma("sp", out[j * 128:(j + 1) * 128, o0:o0 + T], res[i][:, 0:T], reads=[b_res[i]]))
        P.finish_wait("sp", outs)
        P.emit()
    return nc


def build_k5(T=2048, K=31):
    nc = new_nc()
    H = K - 1
    TT = T + H
    uT = nc.dram_tensor("uT", [1024, TT], F32, kind="ExternalInput").ap()
    cw = nc.dram_tensor("cw", [128, 4, K], F32, kind="ExternalInput").ap()
    cb = nc.dram_tensor("cb", [128, 4], F32, kind="ExternalInput").ap()
    lng = nc.dram_tensor("lng", [128, 4], F32, kind="ExternalInput").ap()
    lnb = nc.dram_tensor("lnb", [128, 4], F32, kind="ExternalInput").ap()
    out = nc.dram_tensor("out", [512, T], F32, kind="ExternalOutput").ap()
    with ExitStack() as st:
        P = Prog(nc, st)
        wsb = P.sb("wsb", [128, 4, K], F32); bsb = P.sb("bsb", [128, 4], F32); b_w = P.buf()
        gsb = P.sb("gsb", [128, 4], F32); lbsb = P.sb("lbsb", [128, 4], F32)
        P.dma("sp", wsb[:], cw, writes=[b_w]); P.dma("sp", bsb[:], cb, writes=[b_w])
        P.dma("sp", gsb[:], lng, writes=[b_w]); P.dma("sp", lbsb[:], lnb, writes=[b_w])
        ones = P.sb("ones", [128, 128], F32); b_ones = P.buf()
        P.op("pool", lambda e: e.memset(ones[:], 1.0 / 512), writes=[b_ones])
        a_sb = [P.sb(f"a{i}", [128, TT], F32) for i in range(2)]; b_a = [P.buf() for _ in range(2)]
        g_sb = [P.sb(f"g{i}", [128, TT], F32) for i in range(2)]; b_g = [P.buf() for _ in range(2)]
        cv = P.sb("cv", [128, 4, T], F32); b_cv = [P.buf() for _ in range(4)]
        for j in range(4):
            i = j % 2
            P.dma("sp", a_sb[i][:], uT[j * 128:(j + 1) * 128, :], writes=[b_a[i]])
            P.dma("sp", g_sb[i][:], uT[512 + j * 128:512 + (j + 1) * 128, :], writes=[b_g[i]])
            P.op("act", lambda e, i=i: e.activation(out=g_sb[i][:], in_=g_sb[i][:], func=AF.Sigmoid), reads=[b_g[i]], writes=[b_g[i]])
            eng = "dve" if i == 0 else "pool"
            P.op(eng, lambda e, i=i: e.tensor_mul(out=a_sb[i][:], in0=a_sb[i][:], in1=g_sb[i][:]), reads=[b_a[i], b_g[i]], writes=[b_a[i]])
            conv_fm(P, "dve", a_sb[i], b_a[i], wsb, j, bsb[:, j:j + 1], b_w, cv[:, j, :], b_cv[j], K, T)
        sq = P.sb("sq", [128, 4, 512], F32); b_sq = P.buf()
        pm = [P.ps(f"pm{i}", [128, 512], F32) for i in range(2)]; b_pm = [P.buf() for _ in range(2)]
        pq = [P.ps(f"pq{i}", [128, 512], F32) for i in range(2)]; b_pq = [P.buf() for _ in range(2)]
        rstd = P.sb("rstd", [128, 512], F32); b_rstd = P.buf()
        msq = P.sb("msq", [128, 512], F32); b_msq = P.buf()
        xc = [P.sb(f"xc{i}", [128, 512], F32) for i in range(2)]; b_xc = [P.buf() for _ in range(2)]
        outs = []
        n = 0
        for tb in range(T // 512):
            sl = slice(tb * 512, (tb + 1) * 512)
            pi = tb % 2
            P.op("act", lambda e, sl=sl: e.activation(out=sq[:], in_=cv[:, :, sl], func=AF.Square), reads=b_cv, writes=[b_sq])
            for j in range(4):
                P.op("pe", lambda e, j=j, sl=sl, pi=pi: e.matmul(pm[pi][:], lhsT=ones[:], rhs=cv[:, j, sl], start=(j == 0), stop=(j == 3)),
                     reads=[b_ones, b_cv[j]], writes=[b_pm[pi]])
            for j in range(4):
                P.op("pe", lambda e, j=j, pi=pi: e.matmul(pq[pi][:], lhsT=ones[:], rhs=sq[:, j, :], start=(j == 0), stop=(j == 3)),
                     reads=[b_ones, b_sq], writes=[b_pq[pi]])
            P.op("act", lambda e, pi=pi: e.activation(out=msq[:], in_=pm[pi][:], func=AF.Square), reads=[b_pm[pi]], writes=[b_msq])
            P.op("dve", lambda e, pi=pi: e.scalar_tensor_tensor(out=rstd[:], in0=pq[pi][:], scalar=EPS, in1=msq[:], op0=ALU.add, op1=ALU.subtract),
                 reads=[b_pq[pi], b_msq], writes=[b_rstd])
            P.op("act", lambda e: e.activation(out=rstd[:], in_=rstd[:], func=AF.Sqrt), reads=[b_rstd], writes=[b_rstd])
            P.op("dve", lambda e: e.reciprocal(out=rstd[:], in_=rstd[:]), reads=[b_rstd], writes=[b_rstd])
            for j in range(4):
                i = n % 2; n += 1
                P.op("dve", lambda e, i=i, j=j, sl=sl, pi=pi: e.tensor_sub(out=xc[i][:], in0=cv[:, j, sl], in1=pm[pi][:]),
                     reads=[b_cv[j], b_pm[pi]], writes=[b_xc[i]])
                P.op("pool", lambda e, i=i: e.tensor_mul(out=xc[i][:], in0=xc[i][:], in1=rstd[:]), reads=[b_xc[i], b_rstd], writes=[b_xc[i]])
                P.op("act", lambda e, i=i, j=j: e.activation(out=xc[i][:], in_=xc[i][:], func=AF.Silu, scale=gsb[:, j:j + 1], bias=lbsb[:, j:j + 1]),
                     reads=[b_xc[i], b_w], writes=[b_xc[i]])
                outs.append(P.dma("sp", out[j * 128:(j + 1) * 128, sl], xc[i][:], reads=[b_xc[i]]))
        P.finish_wait("sp", outs)
        P.emit()
    return nc


def build_k4(lambda_init, NU=2, TL=8192, TC=256, NQB=None):
    nc = new_nc()
    NKT = (TL + TC) // 128
    NCT = TC // 128
    NLT = TL // 128
    if NQB is None:
        NQB = TL // 512
    qk = nc.dram_tensor("qk", [NU, 2, TL, 128], F32, kind="ExternalInput").ap()
    v = nc.dram_tensor("v", [NU, TL, 128], F32, kind="ExternalInput").ap()
    kc = nc.dram_tensor("kc", [NU, TC, 128], F32, kind="ExternalInput").ap()
    vc = nc.dram_tensor("vc", [NU, TC, 128], F32, kind="ExternalInput").ap()
    csd = nc.dram_tensor("cs", [TL, 128], F32, kind="ExternalInput").ap()
    snd = nc.dram_tensor("sn", [TL, 128], F32, kind="ExternalInput").ap()
    lamRd = nc.dram_tensor("lamR", [128, 4, 64], F32, kind="ExternalInput").ap()
    subRd = nc.dram_tensor("subR", [128, 128], F32, kind="ExternalInput").ap()
    identd = nc.dram_tensor("ident", [128, 128], BF16, kind="ExternalInput").ap()
    out = nc.dram_tensor("out", [NU, TL, 128], F32, kind="ExternalOutput").ap()
    with ExitStack() as st:
        P = Prog(nc, st)
        ident = P.sb("ident", [128, 128], BF16); b_ident = P.buf()
        P.dma("sp", ident[:], identd, writes=[b_ident])
        lam = P.sb("lam", [128, 4, 64], F32); b_lam = P.buf()
        P.dma("sp", lam[:], lamRd, writes=[b_lam])
        GS = P.sb("GS", [128, 128], F32); b_gs = P.buf()
        P.dma("sp", GS[:], subRd, writes=[b_gs])
        P.op("dve", lambda e: e.tensor_scalar_mul(out=GS[:], in0=GS[:], scalar1=float(1.0 - lambda_init)), reads=[b_gs], writes=[b_gs])
        lp = P.sb("lp", [128, 2, 64], F32); ls = P.sb("ls", [128, 4], F32); b_ls = P.buf()
        P.op("dve", lambda e: e.tensor_mul(out=lp[:, 0, :], in0=lam[:, 0, :], in1=lam[:, 1, :]), reads=[b_lam], writes=[b_ls])
        P.op("dve", lambda e: e.tensor_mul(out=lp[:, 1, :], in0=lam[:, 2, :], in1=lam[:, 3, :]), reads=[b_lam, b_ls], writes=[b_ls])
        P.op("dve", lambda e: e.reduce_sum(out=ls[:, 0:2], in_=lp[:], axis=AX.X), reads=[b_ls], writes=[b_ls])
        P.op("act", lambda e: e.activation(out=ls[:, 0:2], in_=ls[:, 0:2], func=AF.Exp), reads=[b_ls], writes=[b_ls])
        P.op("dve", lambda e: e.tensor_sub(out=ls[:, 2:3], in0=ls[:, 1:2], in1=ls[:, 0:1]), reads=[b_ls], writes=[b_ls])
        P.op("dve", lambda e: e.tensor_scalar_add(out=ls[:, 3:4], in0=ls[:, 2:3], scalar1=float(-lambda_init)), reads=[b_ls], writes=[b_ls])
        neglam = ls[:, 3:4]

        kT = P.sb("kT", [128, TL + TC], BF16); b_kT = P.buf()
        qT = P.sb("qT", [128, TL], BF16); b_qT = P.buf()
        vaug = P.sb("vaug", [128, NKT, 129], BF16); b_va = P.buf()
        P.op("pool", lambda e: e.memset(vaug[:, :, 128:129], 1.0), writes=[b_va])
        ld = {n: [P.sb(f"ld_{n}{i}", [128, 128], F32) for i in range(2)] for n in ("q", "k", "v", "cs", "sn")}
        b_ld = {n: [P.buf() for _ in range(2)] for n in ld}
        t1 = {n: [P.sb(f"t1_{n}{i}", [128, 128], F32) for i in range(2)] for n in ("q", "k")}
        t2 = {n: [P.sb(f"t2_{n}{i}", [128, 128], F32) for i in range(2)] for n in ("q", "k")}
        b_t1 = {n: [P.buf() for _ in range(2)] for n in t1}
        b_t2 = {n: [P.buf() for _ in range(2)] for n in t1}
        rb = {n: [P.sb(f"rb_{n}{i}", [128, 128], BF16) for i in range(2)] for n in ("q", "k")}
        b_rb = {n: [P.buf() for _ in range(2)] for n in rb}
        psT = [P.ps(f"psT{i}", [128, 128], BF16) for i in range(2)]; b_psT = [P.buf() for _ in range(2)]
        ps_s = [P.ps(f"ps_s{i}", [128, 512], F32) for i in range(2)]; b_ps_s = [P.buf() for _ in range(2)]
        acc = [P.ps(f"acc{i}", [128, 129], F32) for i in range(4)]; b_acc = [P.buf() for _ in range(4)]
        E = [P.sb(f"E{i}", [128, 512], BF16) for i in range(2)]; b_E = [P.buf() for _ in range(2)]
        att0 = [P.sb(f"att0_{i}", [128, 128], F32) for i in range(4)]; b_att0 = [P.buf() for _ in range(4)]
        att1 = [P.sb(f"att1_{i}", [128, 128], F32) for i in range(2)]; b_att1 = [P.buf() for _ in range(2)]
        rec = P.sb("rec", [128, 8], F32); b_rec = [P.buf() for _ in range(8)]
        junk = P.sb("junk", [128, 128], F32); b_junk = P.buf()
        sst = P.sb("sst", [128, 2], F32); b_sst = [P.buf() for _ in range(2)]
        v5 = lambda ap: ap.rearrange("p (c a h f) -> p c a h f", c=2, a=2, h=2, f=16)
        outs = []
        npt = [0]

        def transpose_to(src_bf, b_src, dst_ap, b_dst):
            pi = npt[0] % 2; npt[0] += 1
            P.op("pe", lambda e: e.transpose(out=psT[pi][:], in_=src_bf, identity=ident[:]), reads=[b_src, b_ident], writes=[b_psT[pi]])
            P.op("act", lambda e: e.copy(out=dst_ap, in_=psT[pi][:]), reads=[b_psT[pi]], writes=[b_dst])

        for u in range(NU):
            for j in range(NCT):
                i = j % 2
                P.dma("sp", ld["k"][i][:], kc[u, j * 128:(j + 1) * 128, :], writes=[b_ld["k"][i]])
                P.dma("sp", ld["v"][i][:], vc[u, j * 128:(j + 1) * 128, :], writes=[b_ld["v"][i]])
                P.op("dve", lambda e, i=i: e.tensor_copy(out=rb["k"][i][:], in_=ld["k"][i][:]), reads=[b_ld["k"][i]], writes=[b_rb["k"][i]])
                transpose_to(rb["k"][i][:], b_rb["k"][i], kT[:, j * 128:(j + 1) * 128], b_kT)
                P.op("pool", lambda e, i=i, j=j: e.tensor_copy(out=vaug[:, j, 0:128], in_=ld["v"][i][:]), reads=[b_ld["v"][i]], writes=[b_va])
            for j in range(NLT):
                i = j % 2
                sl = slice(j * 128, (j + 1) * 128)
                P.dma("sp", ld["q"][i][:], qk[u, 0, sl, :], writes=[b_ld["q"][i]])
                P.dma("sp", ld["k"][i][:], qk[u, 1, sl, :], writes=[b_ld["k"][i]])
                P.dma("sp", ld["v"][i][:], v[u, sl, :], writes=[b_ld["v"][i]])
                P.dma("sp", ld["cs"][i][:], csd[sl, :], writes=[b_ld["cs"][i]])
                P.dma("sp", ld["sn"][i][:], snd[sl, :], writes=[b_ld["sn"][i]])
                for n in ("q", "k"):
                    x = ld[n][i]; a1 = t1[n][i]; a2 = t2[n][i]
                    P.op("dve", lambda e, x=x, a1=a1, i=i: e.tensor_mul(out=a1[:], in0=x[:], in1=ld["cs"][i][:]),
                         reads=[b_ld[n][i], b_ld["cs"][i]], writes=[b_t1[n][i]])
                    P.op("pool", lambda e, x=x, a2=a2, i=i: e.tensor_mul(out=v5(a2[:])[:, :, :, 0, :], in0=v5(x[:])[:, :, :, 1, :],
                                                                         in1=v5(ld["sn"][i][:])[:, :, :, 0, :]),
                         reads=[b_ld[n][i], b_ld["sn"][i]], writes=[b_t2[n][i]])
                    P.op("pool", lambda e, x=x, a2=a2, i=i: e.tensor_mul(out=v5(a2[:])[:, :, :, 1, :], in0=v5(x[:])[:, :, :, 0, :],
                                                                         in1=v5(ld["sn"][i][:])[:, :, :, 1, :]),
                         reads=[b_ld[n][i], b_ld["sn"][i]], writes=[b_t2[n][i]])
                    P.op("dve", lambda e, a1=a1, a2=a2, n=n, i=i: e.tensor_add(out=rb[n][i][:], in0=a1[:], in1=a2[:]),
                         reads=[b_t1[n][i], b_t2[n][i]], writes=[b_rb[n][i]])
                transpose_to(rb["q"][i][:], b_rb["q"][i], qT[:, sl], b_qT)
                transpose_to(rb["k"][i][:], b_rb["k"][i], kT[:, TC + j * 128:TC + (j + 1) * 128], b_kT)
                P.op("pool", lambda e, i=i, j=j: e.tensor_copy(out=vaug[:, NCT + j, 0:128], in_=ld["v"][i][:]), reads=[b_ld["v"][i]], writes=[b_va])
            for qb in range(NQB):
                qsl = slice(qb * 512, (qb + 1) * 512)
                for c in range(2):
                    cp = slice(c * 64, (c + 1) * 64)

                    def score(kt, cp=cp, qsl=qsl):
                        pi = kt % 2
                        P.op("pe", lambda e, kt=kt, pi=pi, cp=cp, qsl=qsl: e.matmul(ps_s[pi][:], lhsT=kT[cp, kt * 128:(kt + 1) * 128], rhs=qT[cp, qsl],
                                                                     start=True, stop=True), reads=[b_kT, b_qT], writes=[b_ps_s[pi]])
                    score(0)
                    for kt in range(NKT):
                        pi = kt % 2
                        if kt + 1 < NKT:
                            score(kt + 1)
                        P.op("act", lambda e, pi=pi: e.activation(out=E[pi][:], in_=ps_s[pi][:], func=AF.Exp, scale=0.125),
                             reads=[b_ps_s[pi]], writes=[b_E[pi]])
                        for qs in range(4):
                            P.op("pe", lambda e, pi=pi, qs=qs, kt=kt: e.matmul(acc[qs][:], lhsT=E[pi][:, qs * 128:(qs + 1) * 128], rhs=vaug[:, kt, :],
                                                                              start=(kt == 0), stop=(kt == NKT - 1)),
                                 reads=[b_E[pi], b_va], writes=[b_acc[qs]])
                    for qs in range(4):
                        r = c * 4 + qs
                        P.op("dve", lambda e, qs=qs, r=r: e.reciprocal(out=rec[:, r:r + 1], in_=acc[qs][:, 128:129]), reads=[b_acc[qs]], writes=[b_rec[r]])
                        if c == 0:
                            P.op("dve", lambda e, qs=qs, r=r: e.tensor_scalar_mul(out=att0[qs][:], in0=acc[qs][:, 0:128], scalar1=rec[:, r:r + 1]),
                                 reads=[b_acc[qs], b_rec[r]], writes=[b_att0[qs]])
                        else:
                            ai = qs % 2
                            P.op("dve", lambda e, qs=qs, r=r, ai=ai: e.tensor_scalar_mul(out=att1[ai][:], in0=acc[qs][:, 0:128], scalar1=rec[:, r:r + 1]),
                                 reads=[b_acc[qs], b_rec[r]], writes=[b_att1[ai]])
                            P.op("dve", lambda e, qs=qs, ai=ai: e.scalar_tensor_tensor(out=att1[ai][:], in0=att1[ai][:], scalar=neglam, in1=att0[qs][:],
                                                                                       op0=ALU.mult, op1=ALU.add),
                                 reads=[b_att1[ai], b_att0[qs], b_ls], writes=[b_att1[ai]])
                            P.op("pool", lambda e, ai=ai: e.memset(sst[:, ai:ai + 1], 0.0), writes=[b_sst[ai]])
                            P.op("act", lambda e, ai=ai: e.activation(out=junk[:], in_=att1[ai][:], func=AF.Square, accum_out=sst[:, ai:ai + 1]),
                                 reads=[b_att1[ai], b_sst[ai]], writes=[b_junk, b_sst[ai]])
                            P.op("dve", lambda e, ai=ai: e.tensor_scalar(out=sst[:, ai:ai + 1], in0=sst[:, ai:ai + 1], scalar1=1.0 / 128, scalar2=EPS,
                                                                         op0=ALU.mult, op1=ALU.add), reads=[b_sst[ai]], writes=[b_sst[ai]])
                            P.op("act", lambda e, ai=ai: e.activation(out=sst[:, ai:ai + 1], in_=sst[:, ai:ai + 1], func=AF.Sqrt), reads=[b_sst[ai]], writes=[b_sst[ai]])
                            P.op("dve", lambda e, ai=ai: e.reciprocal(out=sst[:, ai:ai + 1], in_=sst[:, ai:ai + 1]), reads=[b_sst[ai]], writes=[b_sst[ai]])
                            P.op("dve", lambda e, ai=ai: e.scalar_tensor_tensor(out=att1[ai][:], in0=att1[ai][:], scalar=sst[:, ai:ai + 1], in1=GS[:],
                                                                                op0=ALU.mult, op1=ALU.mult),
                                 reads=[b_att1[ai], b_sst[ai], b_gs], writes=[b_att1[ai]])
                            r0 = qb * 512 + qs * 128
                            outs.append(P.dma("sp", out[u, r0:r0 + 128, :], att1[ai][:], reads=[b_att1[ai]]))
        P.finish_wait("sp", outs)
        P.emit()
    return nc


def build_kf(TL=8192, TC=256):
    nc = new_nc()
    NL = TL // 128
    NCt = TC // 128
    GT = nc.dram_tensor("GT", [128, TL + TC], F32, kind="ExternalInput").ap()
    ccsc = nc.dram_tensor("ccsc", [128, 256], F32, kind="ExternalInput").ap()
    CTd = nc.dram_tensor("CT", [TL, TL], BF16, kind="ExternalInput").ap()
    STd = nc.dram_tensor("ST", [TL, TL], BF16, kind="ExternalInput").ap()
    CTc = nc.dram_tensor("CTc", [TC, TC], BF16, kind="ExternalInput").ap()
    STc = nc.dram_tensor("STc", [TC, TC], BF16, kind="ExternalInput").ap()
    out = nc.dram_tensor("out", [128, TL + TC], F32, kind="ExternalOutput").ap()
    with ExitStack() as st:
        P = Prog(nc, st)
        g32 = P.sb("g32", [128, TL + TC], F32); b_g32 = P.buf()
        P.dma("sp", g32[:], GT, writes=[b_g32])
        gbf = P.sb("gbf", [128, TL + TC], BF16); b_gbf = P.buf()
        P.op("dve", lambda e: e.tensor_copy(out=gbf[:], in_=g32[:]), reads=[b_g32], writes=[b_gbf])
        cc32 = P.sb("cc32", [128, 256], F32); ccb = P.sb("ccb", [128, 256], BF16); b_cc = P.buf()
        P.dma("sp", cc32[:], ccsc, writes=[b_cc])
        P.op("dve", lambda e: e.tensor_copy(out=ccb[:], in_=cc32[:]), reads=[b_cc], writes=[b_cc])
        H = P.sb("H", [128, NL + NCt, 256], BF16); b_H = P.buf()
        ph = [P.ps(f"ph{i}", [128, 256], F32) for i in range(2)]; b_ph = [P.buf() for _ in range(2)]
        for j in range(NL + NCt):
            pi = j % 2
            P.op("pe", lambda e, j=j, pi=pi: e.matmul(ph[pi][:], lhsT=gbf[:, j * 128:(j + 1) * 128], rhs=ccb[:], start=True, stop=True),
                 reads=[b_gbf, b_cc], writes=[b_ph[pi]])
            if pi == 0:
                P.op("act", lambda e, j=j, pi=pi: e.copy(out=H[:, j, :], in_=ph[pi][:]), reads=[b_ph[pi]], writes=[b_H])
            else:
                P.op("dve", lambda e, j=j, pi=pi: e.tensor_copy(out=H[:, j, :], in_=ph[pi][:]), reads=[b_ph[pi]], writes=[b_H])
        JB = 16
        cbuf = [P.sb(f"cbuf{i}", [128, JB, 512], BF16) for i in range(2)]; b_cbuf = [P.buf() for _ in range(2)]
        sbuf_ = [P.sb(f"sbuf{i}", [128, JB, 512], BF16) for i in range(2)]; b_sbuf = [P.buf() for _ in range(2)]
        po = [P.ps(f"po{i}", [128, 512], F32) for i in range(2)]; b_po = [P.buf() for _ in range(2)]
        ot = [P.sb(f"ot{i}", [128, 512], F32) for i in range(2)]; b_ot = [P.buf() for _ in range(2)]
        CTv = CTd.rearrange("(j p) f -> p j f", p=128)
        STv = STd.rearrange("(j p) f -> p j f", p=128)
        outs = []
        nb = 0
        for fb in range(TL // 512):
            fsl = slice(fb * 512, (fb + 1) * 512)
            pi = fb % 2
            for j0 in range(0, NL, JB):
                bi = nb % 2; nb += 1
                P.dma("sp", cbuf[bi][:], CTv[:, j0:j0 + JB, fsl], writes=[b_cbuf[bi]])
                P.dma("pool", sbuf_[bi][:], STv[:, j0:j0 + JB, fsl], writes=[b_sbuf[bi]])
                for jj in range(JB):
                    j = j0 + jj
                    P.op("pe", lambda e, j=j, jj=jj, bi=bi, pi=pi: e.matmul(po[pi][:], lhsT=H[:, j, 0:128], rhs=cbuf[bi][:, jj, :],
                                                                         start=(j == 0), stop=False), reads=[b_H, b_cbuf[bi]], writes=[b_po[pi]])
                    P.op("pe", lambda e, j=j, jj=jj, bi=bi, pi=pi: e.matmul(po[pi][:], lhsT=H[:, j, 128:256], rhs=sbuf_[bi][:, jj, :],
                                                                         start=False, stop=(j == NL - 1)), reads=[b_H, b_sbuf[bi]], writes=[b_po[pi]])
            P.op("act", lambda e, pi=pi: e.copy(out=ot[pi][:], in_=po[pi][:]), reads=[b_po[pi]], writes=[b_ot[pi]])
            outs.append(P.dma("sp", out[:, fsl], ot[pi][:], reads=[b_ot[pi]]))
        cc_ = P.sb("cc_", [128, NCt, TC], BF16); sc_ = P.sb("sc_", [128, NCt, TC], BF16); b_c2 = P.buf()
        P.dma("sp", cc_[:], CTc.rearrange("(j p) f -> p j f", p=128), writes=[b_c2])
        P.dma("sp", sc_[:], STc.rearrange("(j p) f -> p j f", p=128), writes=[b_c2])
        pi = 0
        for j in range(NCt):
            P.op("pe", lambda e, j=j: e.matmul(po[pi][:, 0:TC], lhsT=H[:, NL + j, 0:128], rhs=cc_[:, j, :], start=(j == 0), stop=False),
                 reads=[b_H, b_c2], writes=[b_po[pi]])
            P.op("pe", lambda e, j=j: e.matmul(po[pi][:, 0:TC], lhsT=H[:, NL + j, 128:256], rhs=sc_[:, j, :], start=False, stop=(j == NCt - 1)),
                 reads=[b_H, b_c2], writes=[b_po[pi]])
        P.op("act", lambda e: e.copy(out=ot[pi][:, 0:TC], in_=po[pi][:, 0:TC]), reads=[b_po[pi]], writes=[b_ot[pi]])
        outs.append(P.dma("sp", out[:, TL:TL + TC], ot[pi][:, 0:TC], reads=[b_ot[pi]]))
        P.finish_wait("sp", outs)
        P.emit()
    return nc


def build_ssd(NCH=66, NCTX=2):
    nc = new_nc()
    T = NCH * 128
    NC6 = NCH * 6
    xs = nc.dram_tensor("xs", [T, 384], F32, kind="ExternalInput").ap()
    Btm = nc.dram_tensor("Btm", [T, 128], F32, kind="ExternalInput").ap()
    BfT = nc.dram_tensor("BfT", [128, T], F32, kind="ExternalInput").ap()
    CfT = nc.dram_tensor("CfT", [128, T], F32, kind="ExternalInput").ap()
    dtd = nc.dram_tensor("dt", [128, 2, NCH, 6], F32, kind="ExternalInput").ap()
    prm = nc.dram_tensor("prm", [128, 3, 2, NCH, 6], F32, kind="ExternalInput").ap()
    trid = nc.dram_tensor("tri", [128, 3, 128], F32, kind="ExternalInput").ap()
    y = nc.dram_tensor("y", [2, T, 384], F32, kind="ExternalOutput").ap()
    with ExitStack() as st:
        P = Prog(nc, st)
        tri = P.sb("tri", [128, 3, 128], F32); b_tri = P.buf()
        P.dma("sp", tri[:], trid, writes=[b_tri])
        prs = P.sb("prs", [128, 3, 2, NCH, 6], F32); b_prs = P.buf()
        P.dma("sp", prs[:], prm, writes=[b_prs])
        dts = P.sb("dts", [128, 2, NCH, 6], F32); b_dts = P.buf()
        P.dma("sp", dts[:], dtd, writes=[b_dts])
        stg = P.sb("stg", [128, T], F32); b_stg = P.buf()
        Bf = P.sb("Bf", [128, T], BF16); Cf = P.sb("Cf", [128, T], BF16); Bt = P.sb("Bt", [128, NCH, 128], BF16)
        b_Bf, b_Cf, b_Bt = P.buf(), P.buf(), P.buf()
        P.dma("sp", stg[:], BfT, writes=[b_stg])
        P.op("dve", lambda e: e.tensor_copy(out=Bf[:], in_=stg[:]), reads=[b_stg], writes=[b_Bf])
        P.dma("sp", stg[:], CfT, writes=[b_stg])
        P.op("pool", lambda e: e.tensor_copy(out=Cf[:], in_=stg[:]), reads=[b_stg], writes=[b_Cf])
        P.dma("sp", stg[:].rearrange("p (c n) -> p c n", n=128), Btm.rearrange("(c p) n -> p c n", p=128), writes=[b_stg])
        P.op("dve", lambda e: e.tensor_copy(out=Bt[:], in_=stg[:].rearrange("p (c n) -> p c n", n=128)), reads=[b_stg], writes=[b_Bt])
        dtsp = P.sb("dtsp", [128, 2, NCH, 6], F32); dta = P.sb("dta", [128, 2, NCH, 6], F32); b_dt = P.buf()
        P.op("dve", lambda e: e.tensor_add(out=dtsp[:], in0=dts[:], in1=prs[:, 0]), reads=[b_dts, b_prs], writes=[b_dt])
        P.op("act", lambda e: e.activation(out=dtsp[:], in_=dtsp[:], func=AF.Exp), reads=[b_dt], writes=[b_dt])
        P.op("dve", lambda e: e.tensor_scalar_add(out=dtsp[:], in0=dtsp[:], scalar1=1.0), reads=[b_dt], writes=[b_dt])
        P.op("act", lambda e: e.activation(out=dtsp[:], in_=dtsp[:], func=AF.Ln), reads=[b_dt], writes=[b_dt])
        P.op("act", lambda e: e.activation(out=dta[:], in_=prs[:, 1], func=AF.Exp), reads=[b_prs, b_dt], writes=[b_dt])
        P.op("dve", lambda e: e.scalar_tensor_tensor(out=dta[:], in0=dta[:], scalar=-1.0, in1=dtsp[:], op0=ALU.mult, op1=ALU.mult),
             reads=[b_dt], writes=[b_dt])
        acs = P.sb("acs", [128, 2, NCH, 6], F32); eacs = P.sb("eacs", [128, 2, NCH, 6], F32)
        tend = P.sb("tend", [128, 2, NCH, 6], F32); cdec = P.sb("cdec", [128, 2, NCH, 6], F32); b_ac = P.buf()
        pb = [P.ps(f"pb{i}", [128, NC6], F32) for i in range(2)]; b_pb = [P.buf() for _ in range(2)]
        for d in range(2):
            P.op("pe", lambda e, d=d: e.matmul(pb[0][:], lhsT=tri[:, d, :], rhs=dta[:, d].rearrange("p c h -> p (c h)"), start=True, stop=True),
                 reads=[b_tri, b_dt], writes=[b_pb[0]])
            P.op("pe", lambda e, d=d: e.matmul(pb[1][:], lhsT=tri[:, 2, :], rhs=dta[:, d].rearrange("p c h -> p (c h)"), start=True, stop=True),
                 reads=[b_tri, b_dt], writes=[b_pb[1]])
            av = lambda t, d=d: t[:, d].rearrange("p c h -> p (c h)")
            P.op("dve", lambda e, d=d, av=av: e.tensor_copy(out=av(acs), in_=pb[0][:]), reads=[b_pb[0]], writes=[b_ac])
            P.op("act", lambda e, d=d, av=av: e.activation(out=av(eacs), in_=pb[0][:], func=AF.Exp), reads=[b_pb[0]], writes=[b_ac])
            P.op("act", lambda e, d=d, av=av: e.activation(out=av(cdec), in_=pb[1][:], func=AF.Exp), reads=[b_pb[1]], writes=[b_ac])
            P.op("dve", lambda e, d=d, av=av: e.tensor_sub(out=av(tend), in0=pb[1][:], in1=av(acs)), reads=[b_pb[1], b_ac], writes=[b_ac])
            P.op("act", lambda e, d=d, av=av: e.activation(out=av(tend), in_=av(tend), func=AF.Exp), reads=[b_ac], writes=[b_ac])
        xt = [P.sb(f"xt{i}", [128, 6, 64], F32) for i in range(2)]; b_xt = [P.buf() for _ in range(2)]
        Dm = P.sb("Dm", [128, 6, 128], F32); b_Dm = P.buf()
        pR = [P.ps(f"pR{i}", [128, 3, 128], F32) for i in range(2)]; b_pR = [P.buf() for _ in range(2)]
        pcb = P.ps("pcb", [128, 128], F32); b_pcb = P.buf()
        cbU = P.sb("cbU", [128, 128], F32); b_cbU = P.buf()
        arg = [P.sb(f"arg{i}", [128, 128], F32) for i in range(2)]; b_arg = [P.buf() for _ in range(2)]
        Wh = P.sb("Wh", [128, 6, 128], BF16); b_Wh = P.buf()
        xdt = P.sb("xdt", [128, 6, 64], BF16); b_xdt = P.buf()
        xdtE = P.sb("xdtE", [128, 6, 64], BF16); b_xdtE = P.buf()
        py = P.ps("py", [128, 6, 64], F32); b_py = P.buf()
        pyo = P.ps("pyo", [128, 6, 64], F32); b_pyo = P.buf()
        pst = P.ps("pst", [128, 6, 64], F32); b_pst = P.buf()
        t1 = P.sb("t1", [128, 6, 64], F32); b_t1 = P.buf()
        yo = [P.sb(f"yo{i}", [128, 6, 64], F32) for i in range(2)]; b_yo = [P.buf() for _ in range(2)]
        S = P.sb("S", [128, 6, 64], F32); Sb = P.sb("Sb", [128, 6, 64], BF16); b_S = P.buf(); b_Sb = P.buf()
        outs = []
        n = 0
        for d in range(2):
            ctx_order = list(range(NCTX)) if d == 0 else list(range(NCTX - 1, -1, -1))
            lat_order = list(range(NCTX, NCH)) if d == 0 else list(range(NCH - 1, NCTX - 1, -1))
            P.op("pool", lambda e: e.memset(S[:], 0.0), writes=[b_S])
            P.op("pool", lambda e: e.memset(Sb[:], 0.0), writes=[b_Sb])
            for c in ctx_order + lat_order:
                i = n % 2; n += 1
                csl = slice(c * 128, (c + 1) * 128)
                P.dma("sp", xt[i][:].rearrange("p h e -> p (h e)"), xs[csl, :], writes=[b_xt[i]])
                for h in range(6):
                    eng = "dve" if h % 2 == 0 else "pool"
                    P.op(eng, lambda e, h=h, c=c, d=d: e.tensor_scalar_mul(out=Dm[:, h, :], in0=tri[:, d, :], scalar1=dta[:, d, c, h:h + 1]),
                         reads=[b_tri, b_dt], writes=[b_Dm])
                for hh in range(2):
                    P.op("pe", lambda e, hh=hh: e.matmul(pR[hh][:].rearrange("p h l -> p (h l)"), lhsT=tri[:, 2, :],
                                                         rhs=Dm[:, hh * 3:(hh + 1) * 3, :].rearrange("p h l -> p (h l)"), start=True, stop=True),
                         reads=[b_tri, b_Dm], writes=[b_pR[hh]])
                P.op("pe", lambda e, csl=csl: e.matmul(pcb[:], lhsT=Bf[:, csl], rhs=Cf[:, csl], start=True, stop=True),
                     reads=[b_Bf, b_Cf], writes=[b_pcb])
                P.op("dve", lambda e, d=d: e.tensor_mul(out=cbU[:], in0=pcb[:], in1=tri[:, d, :]), reads=[b_pcb, b_tri], writes=[b_cbU])
                for h in range(6):
                    ai = h % 2
                    P.op("dve", lambda e, h=h, c=c, d=d, ai=ai: e.tensor_scalar(out=arg[ai][:], in0=pR[h // 3][:, h % 3, :], scalar1=acs[:, d, c, h:h + 1],
                                                                                scalar2=0.0, op0=ALU.subtract, op1=ALU.min),
                         reads=[b_pR[h // 3], b_ac], writes=[b_arg[ai]])
                    P.op("act", lambda e, ai=ai: e.activation(out=arg[ai][:], in_=arg[ai][:], func=AF.Exp), reads=[b_arg[ai]], writes=[b_arg[ai]])
                    P.op("pool", lambda e, h=h, ai=ai: e.tensor_mul(out=Wh[:, h, :], in0=arg[ai][:], in1=cbU[:]), reads=[b_arg[ai], b_cbU], writes=[b_Wh])
                    P.op("pool", lambda e, h=h, c=c, d=d, i=i: e.tensor_scalar_mul(out=xdt[:, h, :], in0=xt[i][:, h, :], scalar1=dtsp[:, d, c, h:h + 1]),
                         reads=[b_xt[i], b_dt], writes=[b_xdt])
                    P.op("pool", lambda e, h=h, c=c, d=d: e.tensor_scalar_mul(out=xdtE[:, h, :], in0=xdt[:, h, :], scalar1=tend[:, d, c, h:h + 1]),
                         reads=[b_xdt, b_ac], writes=[b_xdtE])
                for h in range(6):
                    P.op("pe", lambda e, h=h: e.matmul(py[:, h, :], lhsT=Wh[:, h, :], rhs=xdt[:, h, :], start=True, stop=True),
                         reads=[b_Wh, b_xdt], writes=[b_py])
                P.op("pe", lambda e, csl=csl: e.matmul(pyo[:].rearrange("p h e -> p (h e)"), lhsT=Cf[:, csl], rhs=Sb[:].rearrange("p h e -> p (h e)"),
                                                       start=True, stop=True), reads=[b_Cf, b_Sb], writes=[b_pyo])
                for h in range(6):
                    P.op("dve", lambda e, h=h, c=c, d=d: e.tensor_scalar_mul(out=t1[:, h, :], in0=pyo[:, h, :], scalar1=eacs[:, d, c, h:h + 1]),
                         reads=[b_pyo, b_ac], writes=[b_t1])
                    P.op("dve", lambda e, h=h, c=c, d=d, i=i: e.scalar_tensor_tensor(out=t1[:, h, :], in0=xt[i][:, h, :], scalar=prs[:, 2, d, c, h:h + 1],
                                                                                     in1=t1[:, h, :], op0=ALU.mult, op1=ALU.add),
                         reads=[b_xt[i], b_prs, b_t1], writes=[b_t1])
                P.op("dve", lambda e, i=i: e.tensor_add(out=yo[i][:], in0=t1[:], in1=py[:]), reads=[b_t1, b_py], writes=[b_yo[i]])
                outs.append(P.dma("sp", y[d, csl, :], yo[i][:].rearrange("p h e -> p (h e)"), reads=[b_yo[i]]))
                P.op("pe", lambda e, c=c: e.matmul(pst[:].rearrange("p h e -> p (h e)"), lhsT=Bt[:, c, :], rhs=xdtE[:].rearrange("p h e -> p (h e)"),
                                                   start=True, stop=True), reads=[b_Bt, b_xdtE], writes=[b_pst])
                for h in range(6):
                    P.op("dve", lambda e, h=h, c=c, d=d: e.scalar_tensor_tensor(out=S[:, h, :], in0=S[:, h, :], scalar=cdec[:, d, c, h:h + 1],
                                                                                in1=pst[:, h, :], op0=ALU.mult, op1=ALU.add),
                         reads=[b_S, b_ac, b_pst], writes=[b_S])
                P.op("act", lambda e: e.copy(out=Sb[:], in_=S[:]), reads=[b_S], writes=[b_Sb])
        P.finish_wait("sp", outs)
        P.emit()
    return nc


def build_k2c(NT):
    nc = new_nc()
    W = 1536
    y2 = nc.dram_tensor("y2", [2, NT * 128, W], F32, kind="ExternalInput").ap()
    z = nc.dram_tensor("z", [NT * 128, W], F32, kind="ExternalInput").ap()
    gnR = nc.dram_tensor("gnR", [128, W], F32, kind="ExternalInput").ap()
    out = nc.dram_tensor("out", [NT * 128, W], F32, kind="ExternalOutput").ap()
    with ExitStack() as st:
        P = Prog(nc, st)
        gn = P.sb("gn", [128, W], F32); b_gn = P.buf()
        P.dma("sp", gn[:], gnR, writes=[b_gn])
        ss = P.sb("ss", [128, NT, 4], F32); b_ss0 = P.buf(); b_ss = [P.buf() for _ in range(NT)]
        P.op("pool", lambda e: e.memset(ss[:], 0.0), writes=[b_ss0])
        junk = P.sb("junk", [128, 384], F32); b_junk = P.buf()
        y0 = [P.sb(f"y0_{i}", [128, W], F32) for i in range(2)]; b_y0 = [P.buf() for _ in range(2)]
        y1 = [P.sb(f"y1_{i}", [128, W], F32) for i in range(2)]; b_y1 = [P.buf() for _ in range(2)]
        zt = [P.sb(f"zt{i}", [128, W], F32) for i in range(2)]; b_zt = [P.buf() for _ in range(2)]
        ot = [P.sb(f"ot{i}", [128, W], F32) for i in range(2)]; b_ot = [P.buf() for _ in range(2)]
        outs = []
        for t in range(NT):
            i = t % 2
            rsl = slice(t * 128, (t + 1) * 128)
            P.dma("sp", y0[i][:], y2[0, rsl, :], writes=[b_y0[i]])
            P.dma("sp", y1[i][:], y2[1, rsl, :], writes=[b_y1[i]])
            P.dma("sp", zt[i][:], z[rsl, :], writes=[b_zt[i]])
            P.op("act", lambda e, i=i: e.activation(out=zt[i][:], in_=zt[i][:], func=AF.Silu), reads=[b_zt[i]], writes=[b_zt[i]])
            P.op("pool", lambda e, i=i: e.tensor_add(out=y0[i][:], in0=y0[i][:], in1=y1[i][:]), reads=[b_y0[i], b_y1[i]], writes=[b_y0[i]])
            P.op("dve", lambda e, i=i: e.tensor_mul(out=y0[i][:], in0=y0[i][:], in1=zt[i][:]), reads=[b_y0[i], b_zt[i]], writes=[b_y0[i]])
            for g in range(4):
                P.op("act", lambda e, i=i, g=g, t=t: e.activation(out=junk[:], in_=y0[i][:, g * 384:(g + 1) * 384], func=AF.Square,
                                                                  accum_out=ss[:, t, g:g + 1]),
                     reads=[b_y0[i], b_ss0], writes=[b_junk, b_ss[t]])
            P.op("dve", lambda e, t=t: e.tensor_scalar(out=ss[:, t, :], in0=ss[:, t, :], scalar1=1.0 / 384, scalar2=EPS, op0=ALU.mult, op1=ALU.add),
                 reads=[b_ss[t]], writes=[b_ss[t]])
            P.op("act", lambda e, t=t: e.activation(out=ss[:, t, :], in_=ss[:, t, :], func=AF.Sqrt), reads=[b_ss[t]], writes=[b_ss[t]])
            P.op("dve", lambda e, t=t: e.reciprocal(out=ss[:, t, :], in_=ss[:, t, :]), reads=[b_ss[t]], writes=[b_ss[t]])
            for g in range(4):
                gs = slice(g * 384, (g + 1) * 384)
                P.op("dve", lambda e, i=i, g=g, gs=gs, t=t: e.scalar_tensor_tensor(out=ot[i][:, gs], in0=y0[i][:, gs], scalar=ss[:, t, g:g + 1], in1=gn[:, gs],
                                                                                   op0=ALU.mult, op1=ALU.mult),
                     reads=[b_y0[i], b_ss[t], b_gn], writes=[b_ot[i]])
            outs.append(P.dma("sp", out[rsl, :], ot[i][:], reads=[b_ot[i]]))
        P.finish_wait("sp", outs)
        P.emit()
    return nc


def build_k0():
    nc = new_nc()
    NCOL = 768
    cT = nc.dram_tensor("cT", [128, 8, 3], F32, kind="ExternalInput").ap()
    mw = nc.dram_tensor("mw", [2, 1024, NCOL], F32, kind="ExternalInput").ap()
    mb = nc.dram_tensor("mb", [1, 2, NCOL], F32, kind="ExternalInput").ap()
    out = nc.dram_tensor("out", [3, 2, NCOL], F32, kind="ExternalOutput").ap()
    with ExitStack() as st:
        P = Prog(nc, st)
        ct_sb = P.sb("ct_sb", [128, 8, 3], F32)
        sc_sb = P.sb("sc_sb", [128, 8, 3], F32)
        w_sb = P.sb("w_sb", [128, 2, 8, NCOL], F32)
        b_sb = P.sb("b_sb", [1, 2, NCOL], F32)
        ones = P.sb("ones", [1, 4], F32)
        o_sb = P.sb("o_sb", [3, 2, NCOL], F32)
        ps = [P.ps(f"ps{i}", [3, 512], F32) for i in range(4)]
        b_ct, b_sc, b_w, b_b, b_ones, b_o = [P.buf() for _ in range(6)]
        b_ps = [P.buf() for _ in range(4)]

        P.dma("sp", ct_sb[:], cT, writes=[b_ct])
        P.dma("sp", w_sb[:], mw.rearrange("l (k p) n -> p l k n", p=128), writes=[b_w])
        P.dma("sp", b_sb[:], mb, writes=[b_b])
        P.op("dve", lambda e: e.memset(ones[:], 1.0), writes=[b_ones])
        P.op("act", lambda e: e.activation(out=sc_sb[:], in_=ct_sb[:], func=AF.Silu), reads=[b_ct], writes=[b_sc])
        pi = 0
        for l in range(2):
            for (c0, cn) in ((0, 512), (512, 256)):
                pt = ps[pi]; bp = b_ps[pi]; pi += 1
                for k in range(8):
                    P.op("pe", lambda e, pt=pt, k=k, l=l, c0=c0, cn=cn: e.matmul(
                        pt[:, 0:cn], lhsT=sc_sb[:, k, :], rhs=w_sb[:, l, k, c0:c0 + cn], start=(k == 0), stop=False),
                        reads=[b_sc, b_w], writes=[bp])
                P.op("pe", lambda e, pt=pt, l=l, c0=c0, cn=cn: e.matmul(
                    pt[:, 0:cn], lhsT=ones[:, 0:3], rhs=b_sb[:, l, c0:c0 + cn], start=False, stop=True),
                    reads=[b_ones, b_b], writes=[bp])
                P.op("dve", lambda e, pt=pt, l=l, c0=c0, cn=cn: e.tensor_copy(out=o_sb[:, l, c0:c0 + cn], in_=pt[:, 0:cn]),
                     reads=[bp], writes=[b_o])
        t = P.dma("sp", out, o_sb[:], reads=[b_o])
        P.finish_wait("sp", [t])
        P.emit()
    return nc


import math

G4 = [[0, 1, 2, 3], [4, 5, 6, 7]]
TL, TC = 8192, 256
TA = TL + TC
NTA = TA // 128
FMW = TA + 8
LAT0 = 262
I32 = mybir.dt.int32


def din(nc, name, shape, dt=F32):
    return nc.dram_tensor(name, list(shape), dt, kind="ExternalInput").ap()


def dscr(nc, name, shape, dt=F32):
    return nc.dram_tensor(name, list(shape), dt).ap()


def phase_mods(P, cT2, mw, mb, selc, modT_d, gate_d):
    P.push_scope()
    sc = P.sb("sc", [128, 8, 2], F32); b_sc = P.buf()
    P.dma("sp", sc[:], cT2, writes=[b_sc])
    P.op("act", lambda e: e.activation(out=sc[:], in_=sc[:], func=AF.Silu), reads=[b_sc], writes=[b_sc])
    sel = P.sb("sel", [2, 2 + 256], F32); b_sel = P.buf()
    P.dma("sp", sel[:], selc, writes=[b_sel])
    ones = P.sb("ones", [1, 2], F32); b_ones = P.buf()
    P.op("dve", lambda e: e.memset(ones[:], 1.0), writes=[b_ones])
    bsb = P.sb("bsb", [1, 2, 6144], F32); b_b = P.buf()
    P.dma("sp", bsb[:], mb, writes=[b_b])
    wb = [P.sb(f"wb{i}", [128, 8, 1024], F32) for i in range(2)]; b_wb = [P.buf() for _ in range(2)]
    row = [P.sb(f"row{i}", [2, 1024], F32) for i in range(2)]; b_row = [P.buf() for _ in range(2)]
    prow = [P.ps(f"prow{i}", [2, 512], F32) for i in range(2)]; b_prow = [P.buf() for _ in range(2)]
    pT = P.ps("pT", [128, 2, 8], F32); b_pT = P.buf()
    prep = [P.ps(f"prep{i}", [128, 512], F32) for i in range(2)]; b_prep = [P.buf() for _ in range(2)]
    modT = P.sb("modT", [128, 2, 2, 6, 8], F32); b_modT = P.buf()
    rep = [P.sb(f"rep{i}", [128, 1024], F32) for i in range(2)]; b_rep = [P.buf() for _ in range(2)]
    n = 0
    nr = 0
    outs = []
    for l in range(2):
        wv = mw[l].rearrange("(k p) n -> p k n", p=128)
        for jb in range(6):
            i = n % 2; n += 1
            P.dma("sp", wb[i][:], wv[:, :, jb * 1024:(jb + 1) * 1024], writes=[b_wb[i]])
            for half in range(2):
                c0 = half * 512
                for k in range(8):
                    P.op("pe", lambda e, i=i, k=k, c0=c0, half=half: e.matmul(prow[half][:], lhsT=sc[:, k, :], rhs=wb[i][:, k, c0:c0 + 512],
                                                                               start=(k == 0), stop=False),
                         reads=[b_sc, b_wb[i]], writes=[b_prow[half]])
                P.op("pe", lambda e, l=l, jb=jb, c0=c0, half=half: e.matmul(prow[half][:], lhsT=ones[:, 0:2], rhs=bsb[:, l, jb * 1024 + c0:jb * 1024 + c0 + 512],
                                                                           start=False, stop=True),
                     reads=[b_ones, b_b], writes=[b_prow[half]])
                P.op("dve", lambda e, i=i, c0=c0, half=half: e.tensor_copy(out=row[i][:, c0:c0 + 512], in_=prow[half][:]),
                     reads=[b_prow[half]], writes=[b_row[i]])
            for cls in range(2):
                for k in range(8):
                    P.op("pe", lambda e, i=i, cls=cls, k=k: e.matmul(pT[:, cls, k:k + 1], lhsT=row[i][:, k * 128:(k + 1) * 128], rhs=sel[:, cls:cls + 1],
                                                                     start=True, stop=True), reads=[b_row[i], b_sel], writes=[b_pT])
            P.op("dve", lambda e, l=l, jb=jb: e.tensor_copy(out=modT[:, l, :, jb, :], in_=pT[:]), reads=[b_pT], writes=[b_modT])
            if jb in (2, 5):
                for cls in range(2):
                    ri = nr % 2; nr += 1
                    for half in range(2):
                        c0 = half * 512
                        P.op("pe", lambda e, i=i, cls=cls, c0=c0, half=half: e.matmul(prep[half][:], lhsT=sel[:, 2 + cls * 128:2 + (cls + 1) * 128],
                                                                                    rhs=row[i][:, c0:c0 + 512], start=True, stop=True),
                             reads=[b_row[i], b_sel], writes=[b_prep[half]])
                        P.op("act", lambda e, ri=ri, c0=c0, half=half: e.copy(out=rep[ri][:, c0:c0 + 512], in_=prep[half][:]),
                             reads=[b_prep[half]], writes=[b_rep[ri]])
                    outs.append(P.dma("sp", gate_d[l, cls, 0 if jb == 2 else 1], rep[ri][:], reads=[b_rep[ri]]))
    for l in range(2):
        outs.append(P.dma("sp", modT_d[l], modT[:, l], reads=[b_modT]))
    P.barrier()
    P.pop_scope()


def load_mod(P, modT_dl, gT_dram, j_shift, j_scale):
    modsb = P.sb("modsb", [128, 2, 6, 8], F32); b_mod = P.buf()
    P.dma("sp", modsb[:], modT_dl, writes=[b_mod])
    gsb = P.sb("gsb", [128, 8], F32); b_g = P.buf()
    P.dma("sp", gsb[:], gT_dram, writes=[b_g])
    Gs = P.sb("Gs", [128, 2, 8], F32); Sh = P.sb("Sh", [128, 2, 8], F32); b_gs = P.buf()
    for cls in range(2):
        P.op("dve", lambda e, cls=cls: e.scalar_tensor_tensor(out=Gs[:, cls, :], in0=modsb[:, cls, j_scale, :], scalar=1.0,
                                                               in1=gsb[:], op0=ALU.add, op1=ALU.mult), reads=[b_mod, b_g], writes=[b_gs])
        P.op("dve", lambda e, cls=cls: e.tensor_copy(out=Sh[:, cls, :], in_=modsb[:, cls, j_shift, :]), reads=[b_mod], writes=[b_gs])
    return Gs, Sh, b_gs


def load_ident(P, identd, dt, name="ident"):
    t = P.sb(name, [128, 128], dt); b = P.buf()
    P.dma("sp", t[:], identd, writes=[b])
    return t, b


def phase_inproj(P, tile_srcs, tile_cls, groups, w, NFM, NTMC, modT_dl, gT, identd, fm_dst, tm_dst, fm_groups=None):
    P.push_scope()
    NW = NFM * 128 + NTMC
    ident, b_ident = load_ident(P, identd, BF16)
    Gs, Sh, b_gs = load_mod(P, modT_dl, gT, 0, 1)
    w_sb = P.sb("w_sb", [128, 8, NW], BF16); b_w = P.buf()
    stage = [P.sb(f"stage{i}", [128, NW], F32) for i in range(2)]; b_stage = [P.buf() for _ in range(2)]
    load_weight_bf16(P, w, w_sb, b_w, 8, NW, stage, b_stage)
    NT = len(tile_srcs)
    nt = NormT(P, "n_", ident, b_ident, NT)
    xt = [P.sb(f"xt{i}", [128, 1024], F32) for i in range(2)]; b_xt = [P.buf() for _ in range(2)]
    aT = [P.sb(f"aT{i}", [128, 8, 512], BF16) for i in range(2)]; b_aT = [P.buf() for _ in range(2)]
    tmo = [P.sb(f"tmo{i}", [128, NTMC], F32) for i in range(2)]; b_tmo = [P.buf() for _ in range(2)]
    fmo = [P.sb(f"fmo{i}", [128, 512], F32) for i in range(2)]; b_fmo = [P.buf() for _ in range(2)]
    ptm = [P.ps(f"ptm{i}", [128, 512], F32) for i in range(2)]; b_ptm = [P.buf() for _ in range(2)]
    pfm = [P.ps(f"pfm{i}", [128, 512], F32) for i in range(2)]; b_pfm = [P.buf() for _ in range(2)]
    nx = 0; ntm = 0; nfm = 0; no = 0
    tmblocks = [(c0, min(512, NTMC - c0)) for c0 in range(0, NTMC, 512)]
    for gi, tiles in enumerate(groups):
        ai = gi % 2
        N = len(tiles) * 128
        for ti, t in enumerate(tiles):
            i = nx % 2; nx += 1
            for (psl, src) in tile_srcs[t]:
                P.dma("sp", xt[i][psl, :], src, writes=[b_xt[i]])
            nt.run(xt[i][:], b_xt[i], t, aT[ai][:, :, ti * 128:(ti + 1) * 128], b_aT[ai], Gs, Sh, b_gs, tile_cls[t])
            oi = no % 2; no += 1
            for (c0, cn) in tmblocks:
                pi = ntm % 2; ntm += 1
                for k in range(8):
                    P.op("pe", lambda e, pi=pi, k=k, c0=c0, cn=cn, ai=ai, ti=ti: e.matmul(
                        ptm[pi][:, 0:cn], lhsT=aT[ai][:, k, ti * 128:(ti + 1) * 128], rhs=w_sb[:, k, NFM * 128 + c0:NFM * 128 + c0 + cn],
                        start=(k == 0), stop=(k == 7)), reads=[b_aT[ai], b_w], writes=[b_ptm[pi]])
                P.op("act", lambda e, pi=pi, c0=c0, cn=cn, oi=oi: e.copy(out=tmo[oi][:, c0:c0 + cn], in_=ptm[pi][:, 0:cn]),
                     reads=[b_ptm[pi]], djw=[b_tmo[oi]])
            P.dma("sp", tm_dst(t), tmo[oi][:], reads=[b_tmo[oi]])
        if fm_groups is not None and gi not in fm_groups:
            continue
        for c6 in range(NFM):
            pi = nfm % 2; nfm += 1
            for k in range(8):
                P.op("pe", lambda e, pi=pi, k=k, c6=c6, ai=ai, N=N: e.matmul(
                    pfm[pi][:, 0:N], lhsT=w_sb[:, k, c6 * 128:(c6 + 1) * 128], rhs=aT[ai][:, k, 0:N], start=(k == 0), stop=(k == 7)),
                    reads=[b_aT[ai], b_w], writes=[b_pfm[pi]])
            P.op("dve", lambda e, pi=pi, N=N: e.tensor_copy(out=fmo[pi][:, 0:N], in_=pfm[pi][:, 0:N]), reads=[b_pfm[pi]], writes=[b_fmo[pi]])
            P.dma("sp", fm_dst(c6, gi), fmo[pi][:, 0:N], reads=[b_fmo[pi]])
    P.barrier()
    P.pop_scope()


def phase_conv0(P, FM0, cw, cb, XBC):
    P.push_scope()
    K = 5
    wsb = P.sb("wsb", [128, 5, K], F32); bsb = P.sb("bsb", [128, 5], F32); b_w = P.buf()
    P.dma("sp", wsb[:], cw, writes=[b_w]); P.dma("sp", bsb[:], cb, writes=[b_w])
    zero = P.sb("zero", [128, 4], F32); b_z = P.buf()
    P.op("pool", lambda e: e.memset(zero[:], 0.0), writes=[b_z])
    vin = [P.sb(f"vin{i}", [128, FMW], F32) for i in range(2)]; b_vin = [P.buf() for _ in range(2)]
    acc = [P.sb(f"acc{i}", [128, 2048], F32) for i in range(2)]; b_acc = [P.buf() for _ in range(2)]
    res = [P.sb(f"res{i}", [128, 2048], F32) for i in range(2)]; b_res = [P.buf() for _ in range(2)]
    n = 0
    for j in range(5):
        vi = j % 2
        rows = slice(128 + j * 128, 128 + (j + 1) * 128)
        P.dma("sp", vin[vi][:, LAT0:LAT0 + TL], FM0[rows, LAT0:LAT0 + TL], writes=[b_vin[vi]])
        P.dma("sp", vin[vi][:, 2:2 + TC], FM0[rows, 2:2 + TC], djw=[b_vin[vi]])
        for (c0, cn) in ((0, 2), (258, 4), (FMW - 2, 2)):
            P.op("pool", lambda e, vi=vi, c0=c0, cn=cn: e.memset(vin[vi][:, c0:c0 + cn], 0.0), djw=[b_vin[vi]])
        blocks = [(0, TC, 0)] + [(260 + t0, 2048, TC + t0) for t0 in range(0, TL, 2048)]
        for (i0, T, o0) in blocks:
            i = n % 2; n += 1
            conv_fm(P, "dve", vin[vi], b_vin[vi], wsb, j, bsb[:, j:j + 1], b_w, acc[i][:, 0:T], b_acc[i], K, T, t0=i0)
            P.op("act", lambda e, i=i, T=T: e.activation(out=res[i][:, 0:T], in_=acc[i][:, 0:T], func=AF.Silu), reads=[b_acc[i]], writes=[b_res[i]])
            P.dma("sp", XBC[j * 128:(j + 1) * 128, o0:o0 + T], res[i][:, 0:T], reads=[b_res[i]])
    P.barrier()
    P.pop_scope()


def phase_fourier(P, FM0, ccsc, CTd, STd, CTc, STc, MIX0):
    P.push_scope()
    NL = TL // 128
    NCt = TC // 128
    g32 = P.sb("g32", [128, TL + TC], F32); b_g32 = P.buf()
    P.dma("sp", g32[:, 0:TL], FM0[0:128, LAT0:LAT0 + TL], writes=[b_g32])
    P.dma("sp", g32[:, TL:TL + TC], FM0[0:128, 2:2 + TC], writes=[b_g32])
    gbf = P.sb("gbf", [128, TL + TC], BF16); b_gbf = P.buf()
    P.op("dve", lambda e: e.tensor_copy(out=gbf[:], in_=g32[:]), reads=[b_g32], writes=[b_gbf])
    cc32 = P.sb("cc32", [128, 256], F32); ccb = P.sb("ccb", [128, 256], BF16); b_cc = P.buf()
    P.dma("sp", cc32[:], ccsc, writes=[b_cc])
    P.op("dve", lambda e: e.tensor_copy(out=ccb[:], in_=cc32[:]), reads=[b_cc], writes=[b_cc])
    H = P.sb("H", [128, NL + NCt, 256], BF16); b_H = P.buf()
    ph = [P.ps(f"ph{i}", [128, 256], F32) for i in range(2)]; b_ph = [P.buf() for _ in range(2)]
    for j in range(NL + NCt):
        pi = j % 2
        P.op("pe", lambda e, j=j, pi=pi: e.matmul(ph[pi][:], lhsT=gbf[:, j * 128:(j + 1) * 128], rhs=ccb[:], start=True, stop=True),
             reads=[b_gbf, b_cc], writes=[b_ph[pi]])
        if pi == 0:
            P.op("act", lambda e, j=j, pi=pi: e.copy(out=H[:, j, :], in_=ph[pi][:]), reads=[b_ph[pi]], djw=[b_H])
        else:
            P.op("dve", lambda e, j=j, pi=pi: e.tensor_copy(out=H[:, j, :], in_=ph[pi][:]), reads=[b_ph[pi]], djw=[b_H])
    JB = 16
    cbuf = [P.sb(f"cbuf{i}", [128, JB, 512], BF16) for i in range(2)]; b_cbuf = [P.buf() for _ in range(2)]
    sbuf_ = [P.sb(f"sbuf{i}", [128, JB, 512], BF16) for i in range(2)]; b_sbuf = [P.buf() for _ in range(2)]
    po = [P.ps(f"po{i}", [128, 512], F32) for i in range(2)]; b_po = [P.buf() for _ in range(2)]
    ot = [P.sb(f"ot{i}", [128, 512], BF16) for i in range(2)]; b_ot = [P.buf() for _ in range(2)]
    CTv = CTd.rearrange("(j p) f -> p j f", p=128)
    STv = STd.rearrange("(j p) f -> p j f", p=128)
    nb = 0
    for fb in range(TL // 512):
        fsl = slice(fb * 512, (fb + 1) * 512)
        pi = fb % 2
        for j0 in range(0, NL, JB):
            bi = nb % 2; nb += 1
            P.dma("sp", cbuf[bi][:], CTv[:, j0:j0 + JB, fsl], writes=[b_cbuf[bi]])
            P.dma("pool", sbuf_[bi][:], STv[:, j0:j0 + JB, fsl], writes=[b_sbuf[bi]])
            for jj in range(JB):
                j = j0 + jj
                P.op("pe", lambda e, j=j, jj=jj, bi=bi, pi=pi: e.matmul(po[pi][:], lhsT=H[:, j, 0:128], rhs=cbuf[bi][:, jj, :],
                                                                     start=(j == 0), stop=False), reads=[b_H, b_cbuf[bi]], writes=[b_po[pi]])
                P.op("pe", lambda e, j=j, jj=jj, bi=bi, pi=pi: e.matmul(po[pi][:], lhsT=H[:, j, 128:256], rhs=sbuf_[bi][:, jj, :],
                                                                     start=False, stop=(j == NL - 1)), reads=[b_H, b_sbuf[bi]], writes=[b_po[pi]])
        P.op("act", lambda e, pi=pi: e.copy(out=ot[pi][:], in_=po[pi][:]), reads=[b_po[pi]], writes=[b_ot[pi]])
        P.dma("sp", MIX0[1 + fb // 2, 0:128, (fb % 2) * 512:(fb % 2 + 1) * 512], ot[pi][:], reads=[b_ot[pi]])
    cc_ = P.sb("cc_", [128, NCt, TC], BF16); sc_ = P.sb("sc_", [128, NCt, TC], BF16); b_c2 = P.buf()
    P.dma("sp", cc_[:], CTc.rearrange("(j p) f -> p j f", p=128), writes=[b_c2])
    P.dma("sp", sc_[:], STc.rearrange("(j p) f -> p j f", p=128), writes=[b_c2])
    pi = 0
    for j in range(NCt):
        P.op("pe", lambda e, j=j: e.matmul(po[pi][:, 0:TC], lhsT=H[:, NL + j, 0:128], rhs=cc_[:, j, :], start=(j == 0), stop=False),
             reads=[b_H, b_c2], writes=[b_po[pi]])
        P.op("pe", lambda e, j=j: e.matmul(po[pi][:, 0:TC], lhsT=H[:, NL + j, 128:256], rhs=sc_[:, j, :], start=False, stop=(j == NCt - 1)),
             reads=[b_H, b_c2], writes=[b_po[pi]])
    P.op("act", lambda e: e.copy(out=ot[pi][:, 0:TC], in_=po[pi][:, 0:TC]), reads=[b_po[pi]], writes=[b_ot[pi]])
    P.dma("sp", MIX0[0, 0:128, 0:TC], ot[pi][:, 0:TC], reads=[b_ot[pi]])
    P.barrier()
    P.pop_scope()


def phase_ssd(P, XBC, ZDT, prm, trid, gnR, identf, YF, MIX0, GMIX0=None):
    P.push_scope()
    NCH = NTA
    NCTX = TC // 128
    NC6 = NCH * 6
    tri = P.sb("tri", [128, 3, 128], F32); b_tri = P.buf()
    P.dma("sp", tri[:], trid, writes=[b_tri])
    idf, b_idf = load_ident(P, identf, F32, "idf")
    prs = P.sb("prs", [128, 3, 2, NCH, 6], F32); b_prs = P.buf()
    P.dma("sp", prs[:], prm, writes=[b_prs])
    gn = P.sb("gn", [128, 384], F32); b_gn = P.buf()
    P.dma("sp", gn[:], gnR, writes=[b_gn])
    dts = P.sb("dts", [128, 2, NCH, 6], F32); b_dts = P.buf()
    zv = ZDT.rearrange("(c p) n -> p c n", p=128)
    for c in range(NCH):
        P.dma("sp", dts[:, :, c, :], ZDT[c * 128:(c + 1) * 128, 384:396].rearrange("p (d h) -> p d h", d=2), writes=[b_dts])
    stg = P.sb("stg", [128, TA], F32); b_stg = P.buf()
    Bf = P.sb("Bf", [128, TA], BF16); Cf = P.sb("Cf", [128, TA], BF16); Bt = P.sb("Bt", [128, NCH, 128], BF16)
    b_Bf, b_Cf, b_Bt = P.buf(), P.buf(), P.buf()
    pcb = P.ps("pcb", [128, 128], F32); b_pcb = P.buf()
    P.dma("sp", stg[:], XBC[512:640, :], writes=[b_stg])
    P.op("pool", lambda e: e.tensor_copy(out=Cf[:], in_=stg[:]), reads=[b_stg], writes=[b_Cf])
    P.dma("sp", stg[:], XBC[384:512, :], writes=[b_stg])
    P.op("dve", lambda e: e.tensor_copy(out=Bf[:], in_=stg[:]), reads=[b_stg], writes=[b_Bf])
    for c in range(NCH):
        P.op("pe", lambda e, c=c: e.transpose(out=pcb[:], in_=stg[:, c * 128:(c + 1) * 128], identity=idf[:]),
             reads=[b_stg, b_idf], writes=[b_pcb])
        P.op("act", lambda e, c=c: e.copy(out=Bt[:, c, :], in_=pcb[:]), reads=[b_pcb], djw=[b_Bt])
    dtsp = P.sb("dtsp", [128, 2, NCH, 6], F32); dta = P.sb("dta", [128, 2, NCH, 6], F32); b_dt = P.buf()
    P.op("dve", lambda e: e.tensor_add(out=dtsp[:], in0=dts[:], in1=prs[:, 0]), reads=[b_dts, b_prs], writes=[b_dt])
    P.op("act", lambda e: e.activation(out=dtsp[:], in_=dtsp[:], func=AF.Exp), reads=[b_dt], writes=[b_dt])
    P.op("dve", lambda e: e.tensor_scalar_add(out=dtsp[:], in0=dtsp[:], scalar1=1.0), reads=[b_dt], writes=[b_dt])
    P.op("act", lambda e: e.activation(out=dtsp[:], in_=dtsp[:], func=AF.Ln), reads=[b_dt], writes=[b_dt])
    P.op("act", lambda e: e.activation(out=dta[:], in_=prs[:, 1], func=AF.Exp), reads=[b_prs, b_dt], writes=[b_dt])
    P.op("dve", lambda e: e.scalar_tensor_tensor(out=dta[:], in0=dta[:], scalar=-1.0, in1=dtsp[:], op0=ALU.mult, op1=ALU.mult),
         reads=[b_dt], writes=[b_dt])
    acs = P.sb("acs", [128, 2, NCH, 6], F32); eacs = P.sb("eacs", [128, 2, NCH, 6], F32)
    tend = P.sb("tend", [128, 2, NCH, 6], F32); cdec = P.sb("cdec", [128, 2, NCH, 6], F32); b_ac = P.buf()
    pb = [P.ps(f"pb{i}", [128, NC6], F32) for i in range(2)]; b_pb = [P.buf() for _ in range(2)]
    for d in range(2):
        P.op("pe", lambda e, d=d: e.matmul(pb[0][:], lhsT=tri[:, d, :], rhs=dta[:, d].rearrange("p c h -> p (c h)"), start=True, stop=True),
             reads=[b_tri, b_dt], writes=[b_pb[0]])
        P.op("pe", lambda e, d=d: e.matmul(pb[1][:], lhsT=tri[:, 2, :], rhs=dta[:, d].rearrange("p c h -> p (c h)"), start=True, stop=True),
             reads=[b_tri, b_dt], writes=[b_pb[1]])
        av = lambda t, d=d: t[:, d].rearrange("p c h -> p (c h)")
        P.op("dve", lambda e, av=av: e.tensor_copy(out=av(acs), in_=pb[0][:]), reads=[b_pb[0]], writes=[b_ac])
        P.op("act", lambda e, av=av: e.activation(out=av(eacs), in_=pb[0][:], func=AF.Exp), reads=[b_pb[0]], writes=[b_ac])
        P.op("act", lambda e, av=av: e.activation(out=av(cdec), in_=pb[1][:], func=AF.Exp), reads=[b_pb[1]], writes=[b_ac])
        P.op("dve", lambda e, av=av: e.tensor_sub(out=av(tend), in0=pb[1][:], in1=av(acs)), reads=[b_pb[1], b_ac], writes=[b_ac])
        P.op("act", lambda e, av=av: e.activation(out=av(tend), in_=av(tend), func=AF.Exp), reads=[b_ac], writes=[b_ac])
    dtE = P.sb("dtE", [128, 2, NCH, 6], F32)
    P.op("dve", lambda e: e.tensor_mul(out=dtE[:], in0=dtsp[:], in1=tend[:]), reads=[b_dt, b_ac], writes=[b_ac])
    bc = lambda ap, n: ap.unsqueeze(2).to_broadcast([128, 6, n])
    xf = [P.sb(f"xf{i}", [128, 3, 128], F32) for i in range(2)]; b_xf = [P.buf() for _ in range(2)]
    xt = [P.sb(f"xt{i}", [128, 6, 64], F32) for i in range(2)]; b_xt = [P.buf() for _ in range(2)]
    Dm = [P.sb(f"Dm{i}", [128, 6, 128], F32) for i in range(2)]; b_Dm = [P.buf() for _ in range(2)]
    pR = P.ps("pR", [128, 6, 128], F32); b_pR = P.buf()
    cbU = [P.sb(f"cbU{i}", [128, 128], F32) for i in range(2)]; b_cbU = [P.buf() for _ in range(2)]
    arg = [P.sb(f"arg{i}", [128, 6, 128], F32) for i in range(2)]; b_arg = [P.buf() for _ in range(2)]
    Wh = [P.sb(f"Wh{i}", [128, 6, 128], BF16) for i in range(2)]; b_Wh = [P.buf() for _ in range(2)]
    xdt = [P.sb(f"xdt{i}", [128, 6, 64], BF16) for i in range(2)]; b_xdt = [P.buf() for _ in range(2)]
    xdtE = [P.sb(f"xdtE{i}", [128, 6, 64], BF16) for i in range(2)]; b_xdtE = [P.buf() for _ in range(2)]
    py = P.ps("py", [128, 6, 64], F32); b_py = P.buf()
    pyo = P.ps("pyo", [128, 6, 64], F32); b_pyo = P.buf()
    pst = P.ps("pst", [128, 6, 64], F32); b_pst = P.buf()
    tA = [P.sb(f"tA{i}", [128, 6, 64], F32) for i in range(2)]; b_tA = [P.buf() for _ in range(2)]
    tB = [P.sb(f"tB{i}", [128, 6, 64], F32) for i in range(2)]; b_tB = [P.buf() for _ in range(2)]
    yo = [P.sb(f"yo{i}", [128, 384], F32) for i in range(2)]; b_yo = [P.buf() for _ in range(2)]
    yfl = [P.sb(f"yfl{i}", [128, 384], F32) for i in range(2)]; b_yfl = [P.buf() for _ in range(2)]
    zt = [P.sb(f"zt{i}", [128, 384], F32) for i in range(2)]; b_zt = [P.buf() for _ in range(2)]
    ynb = [P.sb(f"ynb{i}", [128, 384], F32) for i in range(2)]; b_ynb = [P.buf() for _ in range(2)]
    ynT = [P.sb(f"ynT{i}", [128, 3, 128], BF16) for i in range(2)]; b_ynT = [P.buf() for _ in range(2)]
    junk = P.sb("junk", [128, 384], F32); b_junk = P.buf()
    ss = P.sb("ss", [128, NCH], F32); b_ss0 = P.buf(); b_ss = [P.buf() for _ in range(NCH)]
    P.op("pool", lambda e: e.memset(ss[:], 0.0), writes=[b_ss0])
    S = P.sb("S", [128, 6, 64], F32); Sb = P.sb("Sb", [128, 6, 64], BF16); b_S = P.buf(); b_Sb = P.buf()
    b_YF = [P.buf() for _ in range(NCH)]
    b_m0 = [P.buf() for _ in range(9)]; n_m0 = [0] * 9
    f2 = lambda t: t[:].rearrange("p h e -> p (h e)")
    n = 0
    for d in range(2):
        ctx_order = list(range(NCTX)) if d == 0 else list(range(NCTX - 1, -1, -1))
        lat_order = list(range(NCTX, NCH)) if d == 0 else list(range(NCH - 1, NCTX - 1, -1))
        P.op("pool", lambda e: e.memset(S[:], 0.0), writes=[b_S])
        P.op("pool", lambda e: e.memset(Sb[:], 0.0), writes=[b_Sb])
        for c in ctx_order + lat_order:
            i = n % 2; n += 1
            csl = slice(c * 128, (c + 1) * 128)
            P.dma("sp", xf[i][:], XBC[0:384, csl].rearrange("(j p) t -> p j t", p=128), writes=[b_xf[i]])
            if d == 1:
                P.dma("sp", yfl[i][:], YF[csl, :], reads=[b_YF[c]], writes=[b_yfl[i]])
                P.dma("sp", zt[i][:], ZDT[csl, 0:384], writes=[b_zt[i]])
            for j3 in range(3):
                P.op("pe", lambda e, i=i, j3=j3: e.transpose(out=pb[0][:, j3 * 128:(j3 + 1) * 128], in_=xf[i][:, j3, :], identity=idf[:]),
                     reads=[b_xf[i], b_idf], writes=[b_pb[0]])
            P.op("act", lambda e, i=i: e.copy(out=f2(xt[i]), in_=pb[0][:, 0:384]), reads=[b_pb[0]], writes=[b_xt[i]])
            P.op("pool", lambda e, i=i, c=c, d=d: e.tensor_mul(out=Dm[i][:], in0=tri[:, d, :].unsqueeze(1).to_broadcast([128, 6, 128]),
                                                               in1=bc(dta[:, d, c, :], 128)), reads=[b_tri, b_dt], writes=[b_Dm[i]])
            P.op("pe", lambda e, i=i: e.matmul(pR[:, 0:4, :].rearrange("p h l -> p (h l)"), lhsT=tri[:, 2, :],
                                               rhs=Dm[i][:, 0:4, :].rearrange("p h l -> p (h l)"), start=True, stop=True),
                 reads=[b_tri, b_Dm[i]], writes=[b_pR])
            P.op("pe", lambda e, i=i: e.matmul(pR[:, 4:6, :].rearrange("p h l -> p (h l)"), lhsT=tri[:, 2, :],
                                               rhs=Dm[i][:, 4:6, :].rearrange("p h l -> p (h l)"), start=True, stop=True),
                 reads=[b_tri, b_Dm[i]], writes=[b_pR])
            P.op("pe", lambda e, csl=csl: e.matmul(pcb[:], lhsT=Bf[:, csl], rhs=Cf[:, csl], start=True, stop=True),
                 reads=[b_Bf, b_Cf], writes=[b_pcb])
            P.op("dve", lambda e, i=i, d=d: e.tensor_mul(out=cbU[i][:], in0=pcb[:], in1=tri[:, d, :]), reads=[b_pcb, b_tri], writes=[b_cbU[i]])
            P.op("dve", lambda e, i=i, c=c, d=d: e.tensor_sub(out=arg[i][:], in0=pR[:], in1=bc(acs[:, d, c, :], 128)),
                 reads=[b_pR, b_ac], writes=[b_arg[i]])
            P.op("pool", lambda e, i=i: e.tensor_scalar_min(out=arg[i][:], in0=arg[i][:], scalar1=0.0), reads=[b_arg[i]], writes=[b_arg[i]])
            P.op("act", lambda e, i=i: e.activation(out=arg[i][:], in_=arg[i][:], func=AF.Exp), reads=[b_arg[i]], writes=[b_arg[i]])
            P.op("pool", lambda e, i=i: e.tensor_mul(out=Wh[i][:], in0=arg[i][:], in1=cbU[i][:].unsqueeze(1).to_broadcast([128, 6, 128])),
                 reads=[b_arg[i], b_cbU[i]], writes=[b_Wh[i]])
            P.op("dve", lambda e, i=i, c=c, d=d: e.tensor_mul(out=xdt[i][:], in0=xt[i][:], in1=bc(dtsp[:, d, c, :], 64)),
                 reads=[b_xt[i], b_dt], writes=[b_xdt[i]])
            P.op("pool", lambda e, i=i, c=c, d=d: e.tensor_mul(out=xdtE[i][:], in0=xt[i][:], in1=bc(dtE[:, d, c, :], 64)),
                 reads=[b_xt[i], b_ac], writes=[b_xdtE[i]])
            for h in range(6):
                P.op("pe", lambda e, h=h, i=i: e.matmul(py[:, h, :], lhsT=Wh[i][:, h, :], rhs=xdt[i][:, h, :], start=True, stop=True),
                     reads=[b_Wh[i], b_xdt[i]], writes=[b_py])
            P.op("pe", lambda e, csl=csl: e.matmul(f2(pyo), lhsT=Cf[:, csl], rhs=f2(Sb), start=True, stop=True),
                 reads=[b_Cf, b_Sb], writes=[b_pyo])
            P.op("pe", lambda e, c=c, i=i: e.matmul(f2(pst), lhsT=Bt[:, c, :], rhs=f2(xdtE[i]), start=True, stop=True),
                 reads=[b_Bt, b_xdtE[i]], writes=[b_pst])
            P.op("dve", lambda e, c=c, d=d: e.tensor_mul(out=S[:], in0=S[:], in1=bc(cdec[:, d, c, :], 64)), reads=[b_S, b_ac, b_pyo], writes=[b_S])
            P.op("dve", lambda e: e.tensor_add(out=S[:], in0=S[:], in1=pst[:]), reads=[b_S, b_pst], writes=[b_S])
            P.op("act", lambda e: e.copy(out=Sb[:], in_=S[:]), reads=[b_S, b_pyo], writes=[b_Sb])
            P.op("dve", lambda e, i=i, c=c, d=d: e.tensor_mul(out=tA[i][:], in0=pyo[:], in1=bc(eacs[:, d, c, :], 64)), reads=[b_pyo, b_ac], writes=[b_tA[i]])
            P.op("pool", lambda e, i=i, c=c, d=d: e.tensor_mul(out=tB[i][:], in0=xt[i][:], in1=bc(prs[:, 2, d, c, :], 64)), reads=[b_xt[i], b_prs], writes=[b_tB[i]])
            P.op("pool", lambda e, i=i: e.tensor_add(out=tA[i][:], in0=tA[i][:], in1=tB[i][:]), reads=[b_tA[i], b_tB[i]], writes=[b_tA[i]])
            P.op("dve", lambda e, i=i: e.tensor_add(out=yo[i][:], in0=f2(tA[i]), in1=f2(py)), reads=[b_tA[i], b_py], writes=[b_yo[i]])
            if d == 0:
                P.dma("sp", YF[csl, :], yo[i][:], reads=[b_yo[i]], writes=[b_YF[c]])
            else:
                P.op("act", lambda e, i=i: e.activation(out=zt[i][:], in_=zt[i][:], func=AF.Silu), reads=[b_zt[i]], writes=[b_zt[i]])
                P.op("pool", lambda e, i=i: e.tensor_add(out=yo[i][:], in0=yo[i][:], in1=yfl[i][:]), reads=[b_yo[i], b_yfl[i]], writes=[b_yo[i]])
                P.op("pool", lambda e, i=i: e.tensor_mul(out=yo[i][:], in0=yo[i][:], in1=zt[i][:]), reads=[b_yo[i], b_zt[i]], writes=[b_yo[i]])
                P.op("act", lambda e, i=i, c=c: e.activation(out=junk[:], in_=yo[i][:], func=AF.Square, accum_out=ss[:, c:c + 1]),
                     reads=[b_yo[i], b_ss0], writes=[b_junk, b_ss[c]])
                P.op("dve", lambda e, c=c: e.tensor_scalar(out=ss[:, c:c + 1], in0=ss[:, c:c + 1], scalar1=1.0 / 384, scalar2=EPS, op0=ALU.mult, op1=ALU.add),
                     reads=[b_ss[c]], writes=[b_ss[c]])
                P.op("act", lambda e, c=c: e.activation(out=ss[:, c:c + 1], in_=ss[:, c:c + 1], func=AF.Sqrt), reads=[b_ss[c]], writes=[b_ss[c]])
                P.op("dve", lambda e, c=c: e.reciprocal(out=ss[:, c:c + 1], in_=ss[:, c:c + 1]), reads=[b_ss[c]], writes=[b_ss[c]])
                P.op("dve", lambda e, i=i, c=c: e.scalar_tensor_tensor(out=ynb[i][:], in0=yo[i][:], scalar=ss[:, c:c + 1], in1=gn[:], op0=ALU.mult, op1=ALU.mult),
                     reads=[b_yo[i], b_ss[c], b_gn], writes=[b_ynb[i]])
                for j3 in range(3):
                    P.op("pe", lambda e, i=i, j3=j3: e.transpose(out=pb[1][:, j3 * 128:(j3 + 1) * 128], in_=ynb[i][:, j3 * 128:(j3 + 1) * 128], identity=idf[:]),
                         reads=[b_ynb[i], b_idf], writes=[b_pb[1]])
                P.op("act", lambda e, i=i: e.copy(out=ynT[i][:].rearrange("p j t -> p (j t)"), in_=pb[1][:, 0:384]), reads=[b_pb[1]], writes=[b_ynT[i]])
                if c < NCTX:
                    mci = 0
                    mdst = MIX0[0, 128:512, c * 128:(c + 1) * 128]
                else:
                    lt = c - NCTX
                    mci = 1 + lt // 8
                    mdst = MIX0[mci, 128:512, (lt % 8) * 128:(lt % 8 + 1) * 128]
                P.dma("sp", mdst.rearrange("(j p) t -> p j t", p=128), ynT[i][:], reads=[b_ynT[i]], djw=[b_m0[mci]])
                n_m0[mci] += 1
                if GMIX0 is not None and n_m0[mci] == (NCTX if mci == 0 else 8):
                    P.cc("AllGather", MIX0[mci].opt(), GMIX0[mci].opt(), G4, reads=[b_m0[mci]])
    P.barrier()
    P.pop_scope()


def phase_outproj(P, CM, NT, mt_srcs, h_src, w, gR, gate_dl, HMID):
    P.push_scope()
    nk = CM // 128
    g_sb = P.sb("g_sb", [128, 1024], F32); b_g = P.buf()
    P.dma("sp", g_sb[:], gR, writes=[b_g])
    GG = P.sb("GG", [128, 2, 1024], F32); b_gg = P.buf()
    for cls in range(2):
        P.dma("sp", GG[:, cls, :], gate_dl[cls, 0], writes=[b_gg])
    for cls in range(2):
        P.op("dve", lambda e, cls=cls: e.tensor_mul(out=GG[:, cls, :], in0=GG[:, cls, :], in1=g_sb[:]), reads=[b_g, b_gg], writes=[b_gg])
    w_sb = P.sb("w_sb", [128, nk, 1024], BF16); b_w = P.buf()
    stage = [P.sb(f"stage{i}", [128, 1024], F32) for i in range(2)]; b_stage = [P.buf() for _ in range(2)]
    load_weight_bf16(P, w, w_sb, b_w, nk, 1024, stage, b_stage)
    rn = ResNorm(P, "r_", NT)
    mT = [P.sb(f"mT{i}", [128, nk, 128], BF16) for i in range(2)]; b_mT = [P.buf() for _ in range(2)]
    xt = [P.sb(f"xt{i}", [128, 1024], F32) for i in range(2)]; b_xt = [P.buf() for _ in range(2)]
    po = [P.ps(f"po{i}", [128, 1024], F32) for i in range(2)]; b_po = [P.buf() for _ in range(2)]
    for i in range(2):
        P.op("pool", lambda e, i=i: e.memset(mT[i][:], 0.0), writes=[b_mT[i]])
    for t in range(NT):
        i = t % 2
        cls = 0 if t < 16 else 1
        dst_fn, src_fn = mt_srcs(t)
        P.dma("sp", dst_fn(mT[i]), src_fn, writes=[b_mT[i]])
        P.dma("sp", xt[i][:], h_src[t * 128:(t + 1) * 128, :], writes=[b_xt[i]])
        for cb in range(2):
            for k in range(nk):
                P.op("pe", lambda e, i=i, k=k, cb=cb: e.matmul(
                    po[i][:, cb * 512:(cb + 1) * 512], lhsT=mT[i][:, k, :], rhs=w_sb[:, k, cb * 512:(cb + 1) * 512],
                    start=(k == 0), stop=(k == nk - 1)), reads=[b_mT[i], b_w], writes=[b_po[i]])
        o_t, b_o = rn.run(po[i][:], b_po[i], t, xt[i][:], b_xt[i], GG[:, cls, :], b_gg)
        P.dma("sp", HMID[t * 128:(t + 1) * 128, :], o_t[:], reads=[b_o])
    P.barrier()
    P.pop_scope()


def phase_ffn(P, HMID, NT, wg, wu, wd, modT_dl, gT, gR, gate_dl, identd, OUT, ctx_tiles):
    P.push_scope()
    FH = 2816
    NJ = FH // 128
    ident, b_ident = load_ident(P, identd, BF16)
    modsb = P.sb("modsb", [128, 2, 6, 8], F32); b_mod = P.buf()
    P.dma("sp", modsb[:], modT_dl, writes=[b_mod])
    gsb = P.sb("gsb", [128, 8], F32); b_g = P.buf()
    P.dma("sp", gsb[:], gT, writes=[b_g])
    Gs = P.sb("Gs", [128, 2, 8], F32); Sh = P.sb("Sh", [128, 2, 8], F32); b_gs = P.buf()
    for cls in range(2):
        P.op("dve", lambda e, cls=cls: e.scalar_tensor_tensor(out=Gs[:, cls, :], in0=modsb[:, cls, 4, :], scalar=1.0,
                                                               in1=gsb[:], op0=ALU.add, op1=ALU.mult), reads=[b_mod, b_g], writes=[b_gs])
        P.op("dve", lambda e, cls=cls: e.tensor_copy(out=Sh[:, cls, :], in_=modsb[:, cls, 3, :]), reads=[b_mod], writes=[b_gs])
    g_sb = P.sb("g_sb", [128, 1024], F32); b_g3 = P.buf()
    P.dma("sp", g_sb[:], gR, writes=[b_g3])
    GG = P.sb("GG", [128, 2, 1024], F32); b_gg = P.buf()
    for cls in range(2):
        P.dma("sp", GG[:, cls, :], gate_dl[cls, 1], writes=[b_gg])
    for cls in range(2):
        P.op("dve", lambda e, cls=cls: e.tensor_mul(out=GG[:, cls, :], in0=GG[:, cls, :], in1=g_sb[:]), reads=[b_g3, b_gg], writes=[b_gg])
    wg_sb = P.sb("wg_sb", [128, 8, FH], BF16); b_wg = P.buf()
    wu_sb = P.sb("wu_sb", [128, 8, FH], BF16); b_wu = P.buf()
    wd_sb = P.sb("wd_sb", [128, NJ, 1024], BF16); b_wd = P.buf()
    stage = [P.sb(f"stage{i}", [128, FH], F32) for i in range(2)]; b_stage = [P.buf() for _ in range(2)]
    load_weight_bf16(P, wg, wg_sb, b_wg, 8, FH, stage, b_stage, cast_engs=("pool", "dve"))
    load_weight_bf16(P, wu, wu_sb, b_wu, 8, FH, stage, b_stage, cast_engs=("pool", "dve"))
    load_weight_bf16(P, wd, wd_sb, b_wd, NJ, 1024, stage, b_stage, cast_engs=("pool", "dve"))
    nt = NormT(P, "n_", ident, b_ident, NT)
    rn = ResNorm(P, "r_", NT)
    ST = 2
    xt = [stage[0][:, i * 1024:(i + 1) * 1024] for i in range(2)]; b_xt = [P.alias(b_stage[0]) for _ in range(2)]
    xr = [stage[1][:, i * 1024:(i + 1) * 1024] for i in range(2)]; b_xr = [P.alias(b_stage[1]) for _ in range(2)]
    aT = P.sb("aT", [128, 8, ST * 128], BF16); b_aT = P.buf()
    hidT = P.sb("hidT", [128, NJ, ST * 128], BF16); b_hid = P.buf()
    sg = [P.sb(f"sg{i}", [128, ST * 128], F32) for i in range(2)]; b_sg = [P.buf() for _ in range(2)]
    psg = [P.ps(f"psg{i}", [128, 512], F32) for i in range(2)]; b_psg = [P.buf() for _ in range(2)]
    psu = [P.ps(f"psu{i}", [128, 512], F32) for i in range(2)]; b_psu = [P.buf() for _ in range(2)]
    po = P.ps("po", [128, 1024], F32); b_po = P.buf()
    outs = []
    nx = 0
    nr = 0
    for s0 in range(0, NT, ST):
        tiles = list(range(s0, min(NT, s0 + ST)))
        N = len(tiles) * 128
        for ti, t in enumerate(tiles):
            i = nx % 2; nx += 1
            cls = 1 if t in ctx_tiles else 0
            P.dma("sp", xt[i], HMID[t * 128:(t + 1) * 128, :], writes=[b_xt[i]])
            nt.run(xt[i], b_xt[i], t, aT[:, :, ti * 128:(ti + 1) * 128], b_aT, Gs, Sh, b_gs, cls)
        for j in range(NJ):
            pi = j % 2
            for k in range(8):
                P.op("pe", lambda e, pi=pi, j=j, k=k, N=N: e.matmul(
                    psg[pi][:, 0:N], lhsT=wg_sb[:, k, j * 128:(j + 1) * 128], rhs=aT[:, k, 0:N],
                    start=(k == 0), stop=(k == 7)), reads=[b_wg, b_aT], writes=[b_psg[pi]])
            for k in range(8):
                P.op("pe", lambda e, pi=pi, j=j, k=k, N=N: e.matmul(
                    psu[pi][:, 0:N], lhsT=wu_sb[:, k, j * 128:(j + 1) * 128], rhs=aT[:, k, 0:N],
                    start=(k == 0), stop=(k == 7)), reads=[b_wu, b_aT], writes=[b_psu[pi]])
            P.op("act", lambda e, pi=pi, N=N: e.activation(out=sg[pi][:, 0:N], in_=psg[pi][:, 0:N], func=AF.Silu),
                 reads=[b_psg[pi]], writes=[b_sg[pi]])
            P.op("dve", lambda e, pi=pi, j=j, N=N: e.tensor_mul(out=hidT[:, j, 0:N], in0=sg[pi][:, 0:N], in1=psu[pi][:, 0:N]),
                 reads=[b_sg[pi], b_psu[pi]], djw=[b_hid])
        for ti, t in enumerate(tiles):
            i = nr % 2; nr += 1
            cls = 1 if t in ctx_tiles else 0
            P.dma("sp", xr[i], HMID[t * 128:(t + 1) * 128, :], writes=[b_xr[i]])
            for cb in range(2):
                for j in range(NJ):
                    P.op("pe", lambda e, j=j, cb=cb, ti=ti: e.matmul(
                        po[:, cb * 512:(cb + 1) * 512], lhsT=hidT[:, j, ti * 128:(ti + 1) * 128],
                        rhs=wd_sb[:, j, cb * 512:(cb + 1) * 512], start=(j == 0), stop=(j == NJ - 1)),
                        reads=[b_hid, b_wd], writes=[b_po])
            o_t, b_o = rn.run(po[:], b_po, t, xr[i], b_xr[i], GG[:, cls, :], b_gg)
            outs.append(P.dma("sp", OUT[t * 128:(t + 1) * 128, :], o_t[:], reads=[b_o]))
    P.barrier()
    P.pop_scope()
    return outs


def allgather(P, srcs, dsts):
    for a, b in zip(srcs, dsts):
        P.cc("AllGather", a.opt(), b.opt(), G4)
    P.barrier()


def phase_attn(P, QKV, csd, snd, lamRd, subCd, identd, lambda_init, MIX1, NQB=None):
    P.push_scope()
    NU = 2
    NKT = TA // 128
    NCT = TC // 128
    NLT = TL // 128
    if NQB is None:
        NQB = TL // 512
    ident, b_ident = load_ident(P, identd, BF16)
    lam = P.sb("lam", [128, 4, 64], F32); b_lam = P.buf()
    P.dma("sp", lam[:], lamRd, writes=[b_lam])
    GS = P.sb("GS", [128, 1], F32); b_gs = P.buf()
    P.dma("sp", GS[:], subCd, writes=[b_gs])
    P.op("dve", lambda e: e.tensor_scalar_mul(out=GS[:], in0=GS[:], scalar1=float(1.0 - lambda_init)), reads=[b_gs], writes=[b_gs])
    lp = P.sb("lp", [128, 2, 64], F32); ls = P.sb("ls", [128, 4], F32); b_ls = P.buf()
    P.op("dve", lambda e: e.tensor_mul(out=lp[:, 0, :], in0=lam[:, 0, :], in1=lam[:, 1, :]), reads=[b_lam], writes=[b_ls])
    P.op("dve", lambda e: e.tensor_mul(out=lp[:, 1, :], in0=lam[:, 2, :], in1=lam[:, 3, :]), reads=[b_lam, b_ls], writes=[b_ls])
    P.op("dve", lambda e: e.reduce_sum(out=ls[:, 0:2], in_=lp[:], axis=AX.X), reads=[b_ls], writes=[b_ls])
    P.op("act", lambda e: e.activation(out=ls[:, 0:2], in_=ls[:, 0:2], func=AF.Exp), reads=[b_ls], writes=[b_ls])
    P.op("dve", lambda e: e.tensor_sub(out=ls[:, 2:3], in0=ls[:, 1:2], in1=ls[:, 0:1]), reads=[b_ls], writes=[b_ls])
    P.op("dve", lambda e: e.tensor_scalar_add(out=ls[:, 3:4], in0=ls[:, 2:3], scalar1=float(-lambda_init)), reads=[b_ls], writes=[b_ls])
    neglam = ls[:, 3:4]
    kT = P.sb("kT", [128, TA], BF16); b_kT = P.buf()
    NB = TL // 256
    qTz = P.sb("qTz", [128, NB, 2, 256], BF16); b_qT = P.buf()
    P.op("pool", lambda e: e.memset(qTz[:], 0.0), writes=[b_qT])
    vaug = P.sb("vaug", [128, NKT, 129], BF16); b_va = P.buf()
    P.op("pool", lambda e: e.memset(vaug[:, :, 128:129], 1.0), writes=[b_va])
    ld = {n: [P.sb(f"ld_{n}{i}", [128, 128], F32) for i in range(2)] for n in ("q", "k", "v", "cs", "sn")}
    b_ld = {n: [P.buf() for _ in range(2)] for n in ld}
    t1 = {n: [P.sb(f"t1_{n}{i}", [128, 128], F32) for i in range(2)] for n in ("q", "k")}
    t2 = {n: [P.sb(f"t2_{n}{i}", [128, 128], F32) for i in range(2)] for n in ("q", "k")}
    b_t1 = {n: [P.buf() for _ in range(2)] for n in t1}
    b_t2 = {n: [P.buf() for _ in range(2)] for n in t1}
    rb = {n: [P.sb(f"rb_{n}{i}", [128, 128], BF16) for i in range(2)] for n in ("q", "k")}
    b_rb = {n: [P.buf() for _ in range(2)] for n in rb}
    psT = [P.ps(f"psT{i}", [128, 128], BF16) for i in range(2)]; b_psT = [P.buf() for _ in range(2)]
    NPS = 4
    ps_s = [P.ps(f"ps_s{i}", [128, 512], F32) for i in range(NPS)]; b_ps_s = [P.buf() for _ in range(NPS)]
    accT = [P.ps("accT0", [128, 512], F32)] * 2; b_accT = [P.buf()] * 2
    pden = P.ps("pden", [128, 512], F32); b_pden = P.buf()
    E = [P.sb(f"E{i}", [128, 512], BF16) for i in range(4)]; b_E = [P.buf() for _ in range(4)]
    NES = 4
    esum = [P.sb(f"esum{i}", [128, 512], F32) for i in range(NES)]; b_esum = [P.buf() for _ in range(NES)]
    onesf = P.sb("onesf", [128, 2, 128], F32); b_onesf = P.buf()
    P.op("pool", lambda e: e.memset(onesf[:, 0, :], 1.0), writes=[b_onesf])
    P.op("pool", lambda e: e.memset(onesf[:, 1, :], 1.0 / 128), reads=[b_onesf], writes=[b_onesf])
    onesb = P.sb("onesb", [128, 128], BF16)
    P.op("pool", lambda e: e.memset(onesb[:], 1.0), reads=[b_onesf], writes=[b_onesf])
    rden = P.sb("rden", [128, 512], F32); b_rden = P.buf()
    att0T = P.sb("att0T", [128, 512], F32); b_att0T = P.buf()
    attT = P.sb("attT", [128, 512], F32); b_attT = P.buf()
    sqT = P.sb("sqT", [128, 512], F32); b_sqT = P.buf()
    oT = [P.sb(f"oT{i}", [128, 512], BF16) for i in range(2)]; b_oT = [P.buf() for _ in range(2)]
    v5 = lambda ap: ap.rearrange("p (c a h f) -> p c a h f", c=2, a=2, h=2, f=16)
    npt = [0]

    def transpose_to(src_bf, b_src, dst_ap, b_dst):
        pi = npt[0] % 2; npt[0] += 1
        P.op("pe", lambda e: e.transpose(out=psT[pi][:], in_=src_bf, identity=ident[:]), reads=[b_src, b_ident], writes=[b_psT[pi]])
        P.op("act", lambda e: e.copy(out=dst_ap, in_=psT[pi][:]), reads=[b_psT[pi]], djw=[b_dst])

    no = 0
    for u in range(NU):
        qc = slice(u * 128, (u + 1) * 128)
        kc_ = slice(256 + u * 128, 256 + (u + 1) * 128)
        vc_ = slice(512 + u * 128, 512 + (u + 1) * 128)
        for j in range(NCT):
            i = j % 2
            rs = slice(j * 128, (j + 1) * 128)
            P.dma("sp", ld["k"][i][:], QKV[rs, kc_], writes=[b_ld["k"][i]])
            P.dma("sp", ld["v"][i][:], QKV[rs, vc_], writes=[b_ld["v"][i]])
            P.op("dve", lambda e, i=i: e.tensor_copy(out=rb["k"][i][:], in_=ld["k"][i][:]), reads=[b_ld["k"][i]], writes=[b_rb["k"][i]])
            transpose_to(rb["k"][i][:], b_rb["k"][i], kT[:, j * 128:(j + 1) * 128], b_kT)
            P.op("pool", lambda e, i=i, j=j: e.tensor_copy(out=vaug[:, j, 0:128], in_=ld["v"][i][:]), reads=[b_ld["v"][i]], djw=[b_va])
        for j in range(NLT):
            i = j % 2
            sl = slice(j * 128, (j + 1) * 128)
            rs = slice(TC + j * 128, TC + (j + 1) * 128)
            P.dma("sp", ld["q"][i][:], QKV[rs, qc], writes=[b_ld["q"][i]])
            P.dma("sp", ld["k"][i][:], QKV[rs, kc_], writes=[b_ld["k"][i]])
            P.dma("sp", ld["v"][i][:], QKV[rs, vc_], writes=[b_ld["v"][i]])
            P.dma("sp", ld["cs"][i][:], csd[sl, :], writes=[b_ld["cs"][i]])
            P.dma("sp", ld["sn"][i][:], snd[sl, :], writes=[b_ld["sn"][i]])
            for n in ("q", "k"):
                x = ld[n][i]; a1 = t1[n][i]; a2 = t2[n][i]
                P.op("dve", lambda e, x=x, a1=a1, i=i: e.tensor_mul(out=a1[:], in0=x[:], in1=ld["cs"][i][:]),
                     reads=[b_ld[n][i], b_ld["cs"][i]], writes=[b_t1[n][i]])
                P.op("pool", lambda e, x=x, a2=a2, i=i: e.tensor_mul(out=v5(a2[:])[:, :, :, 0, :], in0=v5(x[:])[:, :, :, 1, :],
                                                                     in1=v5(ld["sn"][i][:])[:, :, :, 0, :]),
                     reads=[b_ld[n][i], b_ld["sn"][i]], writes=[b_t2[n][i]])
                P.op("pool", lambda e, x=x, a2=a2, i=i: e.tensor_mul(out=v5(a2[:])[:, :, :, 1, :], in0=v5(x[:])[:, :, :, 0, :],
                                                                     in1=v5(ld["sn"][i][:])[:, :, :, 1, :]),
                     reads=[b_ld[n][i], b_ld["sn"][i]], writes=[b_t2[n][i]])
                P.op("dve", lambda e, a1=a1, a2=a2, n=n, i=i: e.tensor_add(out=rb[n][i][:], in0=a1[:], in1=a2[:]),
                     reads=[b_t1[n][i], b_t2[n][i]], writes=[b_rb[n][i]])
            pi_ = npt[0] % 2; npt[0] += 1
            P.op("pe", lambda e, i=i, pi_=pi_: e.transpose(out=psT[pi_][:], in_=rb["q"][i][:], identity=ident[:]),
                 reads=[b_rb["q"][i], b_ident], writes=[b_psT[pi_]])
            qb_, qo_ = j // 2, (j % 2) * 128
            P.op("act", lambda e, pi_=pi_, qb_=qb_, qo_=qo_: e.copy(out=qTz[0:64, qb_, 0, qo_:qo_ + 128], in_=psT[pi_][0:64, :]),
                 reads=[b_psT[pi_]], djw=[b_qT])
            P.op("act", lambda e, pi_=pi_, qb_=qb_, qo_=qo_: e.copy(out=qTz[64:128, qb_, 1, qo_:qo_ + 128], in_=psT[pi_][64:128, :]),
                 reads=[b_psT[pi_]], djw=[b_qT])
            transpose_to(rb["k"][i][:], b_rb["k"][i], kT[:, TC + j * 128:TC + (j + 1) * 128], b_kT)
            P.op("pool", lambda e, i=i, j=j: e.tensor_copy(out=vaug[:, NCT + j, 0:128], in_=ld["v"][i][:]), reads=[b_ld["v"][i]], djw=[b_va])
        for qb in range(NQB * 2):
            oi = no % 2; no += 1
            ac = accT[0]; b_ac_ = b_accT[0]
            qrhs = qTz[:, qb].rearrange("p c t -> p (c t)")

            def score(kt, qrhs=qrhs):
                pi = kt % NPS
                P.op("pe", lambda e, kt=kt, pi=pi, qrhs=qrhs: e.matmul(ps_s[pi][:], lhsT=kT[:, kt * 128:(kt + 1) * 128], rhs=qrhs,
                                                                        start=True, stop=True), reads=[b_kT, b_qT], writes=[b_ps_s[pi]])
            for k0 in range(NPS - 1):
                score(k0)
            for kt in range(NKT):
                pi = kt % NPS
                ei = kt % 4
                if kt + NPS - 1 < NKT:
                    score(kt + NPS - 1)
                P.op("act", lambda e, pi=pi, ei=ei: e.activation(out=E[ei][:], in_=ps_s[pi][:], func=AF.Exp, scale=0.125),
                     reads=[b_ps_s[pi]], writes=[b_E[ei]])
                P.op("pe", lambda e, ei=ei, kt=kt, ac=ac: e.matmul(ac[:], lhsT=vaug[:, kt, 0:128], rhs=E[ei][:], start=(kt == 0), stop=(kt == NKT - 1)),
                     reads=[b_E[ei], b_va], writes=[b_ac_])
                si = kt % NES
                if kt < NES:
                    P.op("dve", lambda e, ei=ei, si=si: e.tensor_copy(out=esum[si][:], in_=E[ei][:]), reads=[b_E[ei]], writes=[b_esum[si]])
                else:
                    P.op("dve", lambda e, ei=ei, si=si: e.tensor_add(out=esum[si][:], in0=esum[si][:], in1=E[ei][:]), reads=[b_E[ei], b_esum[si]], writes=[b_esum[si]])
            for si in range(NES):
                P.op("pe", lambda e, si=si: e.matmul(pden[:], lhsT=onesf[:, 0, :], rhs=esum[si][:], start=(si == 0), stop=(si == NES - 1)),
                     reads=[b_onesf, b_esum[si]], writes=[b_pden])
            P.op("dve", lambda e: e.reciprocal(out=rden[:], in_=pden[:]), reads=[b_pden], writes=[b_rden])
            P.op("dve", lambda e, ac=ac: e.tensor_mul(out=att0T[:], in0=ac[:], in1=rden[:]), reads=[b_ac_, b_rden], writes=[b_att0T])
            P.op("dve", lambda e: e.scalar_tensor_tensor(out=attT[:, 0:256], in0=att0T[:, 256:512], scalar=neglam, in1=att0T[:, 0:256], op0=ALU.mult, op1=ALU.add),
                 reads=[b_att0T, b_ls], writes=[b_attT])
            P.op("act", lambda e: e.activation(out=sqT[:, 0:256], in_=attT[:, 0:256], func=AF.Square), reads=[b_attT], writes=[b_sqT])
            P.op("pe", lambda e: e.matmul(pden[:, 0:256], lhsT=onesf[:, 1, :], rhs=sqT[:, 0:256], start=True, stop=True), reads=[b_onesf, b_sqT, b_rden], writes=[b_pden])
            P.op("dve", lambda e: e.tensor_scalar_add(out=rden[:, 0:256], in0=pden[:, 0:256], scalar1=EPS), reads=[b_pden, b_att0T], writes=[b_rden])
            P.op("act", lambda e: e.activation(out=rden[:, 0:256], in_=rden[:, 0:256], func=AF.Sqrt), reads=[b_rden], writes=[b_rden])
            P.op("dve", lambda e: e.reciprocal(out=rden[:, 0:256], in_=rden[:, 0:256]), reads=[b_rden], writes=[b_rden])
            P.op("dve", lambda e: e.tensor_mul(out=attT[:, 0:256], in0=attT[:, 0:256], in1=rden[:, 0:256]), reads=[b_attT, b_rden], writes=[b_attT])
            P.op("act", lambda e, oi=oi: e.activation(out=oT[oi][:, 0:256], in_=attT[:, 0:256], func=AF.Identity, scale=GS[:, 0:1]), reads=[b_attT, b_gs], writes=[b_oT[oi]])
            P.dma("sp", MIX1[1 + qb // 4, u * 128:(u + 1) * 128, (qb % 4) * 256:(qb % 4 + 1) * 256], oT[oi][:, 0:256], reads=[b_oT[oi]])
    P.barrier()
    P.pop_scope()


def phase_conformer(P, UT, cw, cb, lng, lnb, selm, ST, GST, MIX1):
    P.push_scope()
    K = 31
    TTp = TL + K - 1
    wsb = P.sb("wsb", [128, 1, K], F32); bsb = P.sb("bsb", [128, 1], F32); b_w = P.buf()
    gsb = P.sb("gsb", [128, 1], F32); lbsb = P.sb("lbsb", [128, 1], F32)
    P.dma("sp", wsb[:], cw, writes=[b_w]); P.dma("sp", bsb[:], cb, writes=[b_w])
    P.dma("sp", gsb[:], lng, writes=[b_w]); P.dma("sp", lbsb[:], lnb, writes=[b_w])
    ones = P.sb("ones", [128, 128], F32); b_ones = P.buf()
    P.op("pool", lambda e: e.memset(ones[:], 1.0), writes=[b_ones])
    a_sb = P.sb("a_sb", [128, TTp], F32); b_a = P.buf()
    g_sb = P.sb("g_sb", [128, TTp], F32); b_g = P.buf()
    P.dma("sp", a_sb[:, 15:15 + TL], UT[0:128, 15:15 + TL], writes=[b_a])
    P.dma("sp", g_sb[:, 15:15 + TL], UT[128:256, 15:15 + TL], writes=[b_g])
    for (c0, cn) in ((0, 15), (15 + TL, 15)):
        P.op("pool", lambda e, c0=c0, cn=cn: e.memset(a_sb[:, c0:c0 + cn], 0.0), djw=[b_a])
        P.op("pool", lambda e, c0=c0, cn=cn: e.memset(g_sb[:, c0:c0 + cn], 0.0), djw=[b_g])
    P.op("act", lambda e: e.activation(out=g_sb[:], in_=g_sb[:], func=AF.Sigmoid), reads=[b_g], writes=[b_g])
    P.op("pool", lambda e: e.tensor_mul(out=a_sb[:], in0=a_sb[:], in1=g_sb[:]), reads=[b_a, b_g], writes=[b_a])
    cv = g_sb
    b_cv = [P.alias(b_g) for _ in range(4)]
    for blk in range(4):
        conv_fm(P, "dve", a_sb, b_a, wsb, 0, bsb[:, 0:1], b_w, cv[:, blk * 2048:(blk + 1) * 2048], b_cv[blk], K, 2048, t0=blk * 2048)
    stt = P.sb("stt", [1, 2, TL], F32); b_stt = P.buf()
    sq = [P.sb(f"sq{i}", [128, 512], F32) for i in range(2)]; b_sq = [P.buf() for _ in range(2)]
    pm = [P.ps(f"pm{i}", [128, 512], F32) for i in range(2)]; b_pm = [P.buf() for _ in range(2)]
    pq = [P.ps(f"pq{i}", [128, 512], F32) for i in range(2)]; b_pq = [P.buf() for _ in range(2)]
    for tb in range(TL // 512):
        sl = slice(tb * 512, (tb + 1) * 512)
        pi = tb % 2
        bc = b_cv[tb // 4]
        P.op("act", lambda e, sl=sl, pi=pi: e.activation(out=sq[pi][:], in_=cv[:, sl], func=AF.Square), reads=[bc], writes=[b_sq[pi]])
        P.op("pe", lambda e, sl=sl, pi=pi: e.matmul(pm[pi][:], lhsT=ones[:], rhs=cv[:, sl], start=True, stop=True), reads=[b_ones, bc], writes=[b_pm[pi]])
        P.op("pe", lambda e, pi=pi: e.matmul(pq[pi][:], lhsT=ones[:], rhs=sq[pi][:], start=True, stop=True), reads=[b_ones, b_sq[pi]], writes=[b_pq[pi]])
        P.op("dve", lambda e, sl=sl, pi=pi: e.tensor_copy(out=stt[0:1, 0, sl], in_=pm[pi][0:1, :]), reads=[b_pm[pi]], writes=[b_stt])
        P.op("dve", lambda e, sl=sl, pi=pi: e.tensor_copy(out=stt[0:1, 1, sl], in_=pq[pi][0:1, :]), reads=[b_pq[pi]], writes=[b_stt])
    b_ST = P.buf()
    P.dma("sp", ST.rearrange("(o s) t -> o s t", o=1), stt[:], reads=[b_stt], writes=[b_ST])
    P.barrier()
    P.cc("AllGather", ST.opt(), GST.opt(), G4)
    P.barrier()
    gst = P.sb("gst", [8, TL], F32); b_gst = P.buf()
    P.dma("sp", gst[:], GST, writes=[b_gst])
    sm = P.sb("sm", [8, 2, 128], F32); b_sm = P.buf()
    P.dma("sp", sm[:], selm, writes=[b_sm])
    rstd = P.sb("rstd", [128, 512], F32); b_rstd = P.buf()
    msq = P.sb("msq", [128, 512], F32); b_msq = P.buf()
    xc = [P.sb(f"xc{i}", [128, 512], F32) for i in range(2)]; b_xc = [P.buf() for _ in range(2)]
    xo = [P.sb(f"xo{i}", [128, 512], BF16) for i in range(2)]; b_xo = [P.buf() for _ in range(2)]
    for tb in range(TL // 512):
        sl = slice(tb * 512, (tb + 1) * 512)
        pi = tb % 2
        bc = b_cv[tb // 4]
        P.op("pe", lambda e, sl=sl, pi=pi: e.matmul(pm[pi][:], lhsT=sm[:, 0, :], rhs=gst[:, sl], start=True, stop=True), reads=[b_sm, b_gst], writes=[b_pm[pi]])
        P.op("pe", lambda e, sl=sl, pi=pi: e.matmul(pq[pi][:], lhsT=sm[:, 1, :], rhs=gst[:, sl], start=True, stop=True), reads=[b_sm, b_gst], writes=[b_pq[pi]])
        P.op("act", lambda e, pi=pi: e.activation(out=msq[:], in_=pm[pi][:], func=AF.Square), reads=[b_pm[pi]], writes=[b_msq])
        P.op("dve", lambda e, pi=pi: e.scalar_tensor_tensor(out=rstd[:], in0=pq[pi][:], scalar=EPS, in1=msq[:], op0=ALU.add, op1=ALU.subtract),
             reads=[b_pq[pi], b_msq], writes=[b_rstd])
        P.op("act", lambda e: e.activation(out=rstd[:], in_=rstd[:], func=AF.Sqrt), reads=[b_rstd], writes=[b_rstd])
        P.op("dve", lambda e: e.reciprocal(out=rstd[:], in_=rstd[:]), reads=[b_rstd], writes=[b_rstd])
        P.op("dve", lambda e, sl=sl, pi=pi: e.tensor_sub(out=xc[pi][:], in0=cv[:, sl], in1=pm[pi][:]), reads=[bc, b_pm[pi]], writes=[b_xc[pi]])
        P.op("pool", lambda e, pi=pi: e.tensor_mul(out=xc[pi][:], in0=xc[pi][:], in1=rstd[:]), reads=[b_xc[pi], b_rstd], writes=[b_xc[pi]])
        P.op("act", lambda e, pi=pi: e.activation(out=xo[pi][:], in_=xc[pi][:], func=AF.Silu, scale=gsb[:, 0:1], bias=lbsb[:, 0:1]),
             reads=[b_xc[pi], b_w], writes=[b_xo[pi]])
        P.dma("sp", MIX1[1 + tb // 2, 256:384, (tb % 2) * 512:(tb % 2 + 1) * 512], xo[pi][:], reads=[b_xo[pi]])
    P.barrier()
    P.pop_scope()


def build_fused(stop_after=None, debug=(), NQB=None):
    nc = new_nc()
    lambda_init = 0.8 - 0.6 * math.exp(-0.3 * 1)
    xall = din(nc, "xall", [TA, 1024]); xown = din(nc, "xown", [17 * 128, 1024])
    cT2 = din(nc, "cT2", [128, 8, 2]); mw = din(nc, "mw", [2, 1024, 6144]); mb = din(nc, "mb", [1, 2, 6144])
    selc = din(nc, "selc", [2, 258])
    identb = din(nc, "identb", [128, 128], BF16); identf = din(nc, "identf", [128, 128])
    gT = din(nc, "gT", [2, 4, 128, 8]); gR = din(nc, "gR", [2, 4, 128, 1024])
    w_in0 = din(nc, "w_in0", [1024, 1164]); cw0 = din(nc, "cw0", [128, 5, 5]); cb0 = din(nc, "cb0", [128, 5])
    ccsc = din(nc, "ccsc", [128, 256]); CT = din(nc, "CT", [TL, TL], BF16); STt = din(nc, "STt", [TL, TL], BF16)
    CTc = din(nc, "CTc", [TC, TC], BF16); STc = din(nc, "STc", [TC, TC], BF16)
    prm = din(nc, "prm", [128, 3, 2, NTA, 6]); tri = din(nc, "tri", [128, 3, 128]); gnR = din(nc, "gnR", [128, 384])
    w_out0 = din(nc, "w_out0", [2048, 1024])
    wg = din(nc, "wg", [2, 1024, 2816]); wu = din(nc, "wu", [2, 1024, 2816]); wd = din(nc, "wd", [2, 2816, 1024])
    w_in1 = din(nc, "w_in1", [1024, 1024]); cs = din(nc, "cs", [TL, 128]); sn = din(nc, "sn", [TL, 128])
    lamR = din(nc, "lamR", [128, 4, 64]); subC = din(nc, "subC", [128, 1])
    cw1 = din(nc, "cw1", [128, 1, 31]); cb1 = din(nc, "cb1", [128, 1]); lng = din(nc, "lng", [128, 1]); lnb = din(nc, "lnb", [128, 1])
    selm = din(nc, "selm", [8, 2, 128]); w_out1 = din(nc, "w_out1", [1536, 1024])
    out = nc.dram_tensor("out", [2048, 1024], F32, kind="ExternalOutput").ap()
    modT_d = dscr(nc, "modT_d", [2, 128, 2, 6, 8]); gate_d = dscr(nc, "gate_d", [2, 2, 2, 128, 1024])
    FM0 = dscr(nc, "FM0", [768, FMW]); ZDT = dscr(nc, "ZDT", [TA, 396]); XBC = dscr(nc, "XBC", [640, TA])
    YF = dscr(nc, "YF", [TA, 384]); MIX0 = dscr(nc, "MIX0", [9, 512, 1024], BF16); GMIX0 = dscr(nc, "GMIX0", [9, 2048, 1024], BF16)
    HMID = dscr(nc, "HMID", [17 * 128, 1024]); H1 = dscr(nc, "H1", [18 * 128, 1024]); GH1 = dscr(nc, "GH1", [9, 1024, 1024])
    QKV = dscr(nc, "QKV", [TA, 768]); UT = dscr(nc, "UT", [256, TL + 30])
    MIX1 = dscr(nc, "MIX1", [9, 384, 1024], BF16); GMIX1 = dscr(nc, "GMIX1", [9, 1536, 1024], BF16)
    OWN0 = dscr(nc, "OWN0", [2, 2048, 1024], BF16); OWNC = dscr(nc, "OWNC", [2048, 64], BF16); OWN1 = dscr(nc, "OWN1", [2, 1536, 1024], BF16)
    STs = dscr(nc, "STs", [2, TL]); GST = dscr(nc, "GST", [8, TL]); HMID2 = dscr(nc, "HMID2", [2048, 1024])
    scr = dict(modT_d=modT_d, gate_d=gate_d, FM0=FM0, ZDT=ZDT, XBC=XBC, YF=YF, MIX0=MIX0, GMIX0=GMIX0, HMID=HMID, H1=H1, GH1=GH1,
               QKV=QKV, UT=UT, MIX1=MIX1, GMIX1=GMIX1, GST=GST, HMID2=HMID2)
    dbg_out = {}
    for name in debug:
        a = scr[name]
        dbg_out[name] = nc.dram_tensor("dbg_" + name, list(a.shape), a.dtype, kind="ExternalOutput").ap()
    with ExitStack() as st:
        P = Prog(nc, st)
        dyn = {}

        def setup(e):
            pid = nc.partition_id([mybir.EngineType.SP])
            q = pid % 4
            dyn["qrow0"] = e.snap(q * (2 * 2048), min_val=0, max_val=3 * 2 * 2048)
            dyn["qrow1"] = e.snap(q * (2 * 1536), min_val=0, max_val=3 * 2 * 1536)
            dyn["ctx0"] = e.snap(q * 64, min_val=0, max_val=192)
        P.raw("sp", setup)

        def finish():
            for name in debug:
                a = scr[name]
                if len(a.shape) > 2:
                    continue
            toks = []
            for name in debug:
                a, o = scr[name], dbg_out[name]
                if len(a.shape) == 3:
                    a = a.rearrange("c r t -> (c r) t"); o = o.rearrange("c r t -> (c r) t")
                toks.append(P.dma("sp", o, a))
            P.barrier()
            P.emit()
            return nc

        stages = ["mods", "inproj0", "conv0", "fourier", "ssd", "ag0", "outproj0", "ffn0", "ag1", "inproj1", "attn", "conf", "ag2", "outproj1", "ffn1"]
        last = stages.index(stop_after) if stop_after else len(stages) - 1

        def want(name):
            return stages.index(name) <= last

        phase_mods(P, cT2, mw, mb, selc, modT_d, gate_d)
        if not want("inproj0"):
            return finish()
        tile_srcs = [[(slice(0, 128), xall[t * 128:(t + 1) * 128, :])] for t in range(NTA)]
        tile_cls = [1 if t < 2 else 0 for t in range(NTA)]
        groups = [[0, 1]] + [[2 + 4 * g + i for i in range(4)] for g in range(16)]

        def fm_dst0(c6, gi):
            if gi == 0:
                return FM0[c6 * 128:(c6 + 1) * 128, 2:2 + TC]
            return FM0[c6 * 128:(c6 + 1) * 128, LAT0 + (gi - 1) * 512:LAT0 + gi * 512]
        phase_inproj(P, tile_srcs, tile_cls, groups, w_in0, 6, 396, modT_d[0], gT[0, 0], identb, fm_dst0,
                     lambda t: ZDT[t * 128:(t + 1) * 128, :])
        if not want("conv0"):
            return finish()
        phase_conv0(P, FM0, cw0, cb0, XBC)
        if not want("fourier"):
            return finish()
        phase_fourier(P, FM0, ccsc, CT, STt, CTc, STc, MIX0)
        if not want("ssd"):
            return finish()
        phase_ssd(P, XBC, ZDT, prm, tri, gnR, identf, YF, MIX0, GMIX0)
        if not want("ag0"):
            return finish()
        if not want("outproj0"):
            return finish()
        g0f = GMIX0.rearrange("c r t -> (c r) t")
        P.dma("sp", OWN0.rearrange("c r t -> (c r) t"), lambda: g0f[2048:, :][bass.ds(dyn["qrow0"], 2 * 2048), :])
        P.dma("sp", OWNC, lambda: GMIX0[0][:, bass.ds(dyn["ctx0"], 64)])
        P.barrier()
        o0v = OWN0.rearrange("c (k p) t -> p c k t", p=128)
        ocv = OWNC.rearrange("(k p) t -> p k t", p=128)

        def mt0(t):
            if t < 16:
                return (lambda m: m[:], o0v[:, t // 8, :, (t % 8) * 128:(t % 8 + 1) * 128])
            return (lambda m: m[:, :, 0:64], ocv)
        phase_outproj(P, 2048, 17, mt0, xown, w_out0, gR[0, 1], gate_d[0], HMID)
        if not want("ffn0"):
            return finish()
        phase_ffn(P, HMID, 17, wg[0], wu[0], wd[0], modT_d[0], gT[0, 2], gR[0, 3], gate_d[0], identb, H1, (16,))
        if not want("ag1"):
            return finish()
        allgather(P, [H1[i * 256:(i + 1) * 256, :] for i in range(9)], [GH1[i] for i in range(9)])
        if not want("inproj1"):
            return finish()
        tile_srcs = [[(slice(0, 64), GH1[8, 0:64, :]), (slice(64, 128), GH1[8, 256:320, :])],
                     [(slice(0, 64), GH1[8, 512:576, :]), (slice(64, 128), GH1[8, 768:832, :])]]
        for j in range(64):
            r, t = j // 16, j % 16
            r0 = r * 256 + (t % 2) * 128
            tile_srcs.append([(slice(0, 128), GH1[t // 2, r0:r0 + 128, :])])
        phase_inproj(P, tile_srcs, tile_cls, groups, w_in1, 2, 768, modT_d[1], gT[1, 0], identb,
                     lambda c2, gi: UT[c2 * 128:(c2 + 1) * 128, 15 + (gi - 1) * 512:15 + gi * 512],
                     lambda t: QKV[t * 128:(t + 1) * 128, :], fm_groups=set(range(1, 17)))
        if not want("attn"):
            return finish()
        phase_attn(P, QKV, cs, sn, lamR, subC, identb, lambda_init, MIX1, NQB=NQB)
        if not want("conf"):
            return finish()
        phase_conformer(P, UT, cw1, cb1, lng, lnb, selm, STs, GST, MIX1)
        if not want("ag2"):
            return finish()
        allgather(P, [MIX1[i] for i in range(1, 9)], [GMIX1[i] for i in range(1, 9)])
        if not want("outproj1"):
            return finish()
        g1f = GMIX1.rearrange("c r t -> (c r) t")
        P.dma("sp", OWN1.rearrange("c r t -> (c r) t"), lambda: g1f[1536:, :][bass.ds(dyn["qrow1"], 2 * 1536), :])
        P.barrier()
        o1v = OWN1.rearrange("c (k p) t -> p c k t", p=128)

        def mt1(t):
            return (lambda m: m[:], o1v[:, t // 8, :, (t % 8) * 128:(t % 8 + 1) * 128])
        phase_outproj(P, 1536, 16, mt1, H1, w_out1, gR[1, 1], gate_d[1], HMID2)
        if not want("ffn1"):
            return finish()
        phase_ffn(P, HMID2, 16, wg[1], wu[1], wd[1], modT_d[1], gT[1, 2], gR[1, 3], gate_d[1], identb, out, ())
        return finish()


import math
import ml_dtypes

NCORES = 8
CORES = list(range(NCORES))
_NC_CACHE = {}


def featT(v):
    n = v.shape[0] // 128
    return np.ascontiguousarray(v.reshape(n, 128).T)


def rep(v):
    return np.ascontiguousarray(np.broadcast_to(v, (128,) + v.shape))


def dft_tabs(n):
    tab = np.arange(n, dtype=np.float64) * (2 * np.pi / n)
    idx = (np.arange(n, dtype=np.int64)[:, None] * np.arange(n, dtype=np.int64)[None, :]) % n
    c = (np.cos(tab) / math.sqrt(n)).astype(np.float32).astype(ml_dtypes.bfloat16)
    s = (np.sin(tab) / math.sqrt(n)).astype(np.float32).astype(ml_dtypes.bfloat16)
    return c[idx], s[idx]


def rope_tables():
    t = 8192
    row = np.repeat(np.arange(t // 64, dtype=np.float32), 64)
    col = np.tile(np.arange(64, dtype=np.float32), t // 64)
    inv = (10000.0 ** (-np.arange(16, dtype=np.float32) * 2.0 / 32)).astype(np.float32)
    ang = np.stack([row, col], -1)[:, :, None] * inv
    cos, sin = np.cos(ang).astype(np.float32), np.sin(ang).astype(np.float32)
    cs = np.broadcast_to(cos[:, None, :, None, :], (t, 2, 2, 2, 16)).reshape(t, 128)
    sg = np.array([-1.0, 1.0], np.float32)[None, None, None, :, None]
    sn = (np.broadcast_to(sin[:, None, :, None, :], (t, 2, 2, 2, 16)) * sg).reshape(t, 128)
    return np.ascontiguousarray(cs), np.ascontiguousarray(sn)


def prep_inputs(x, c, ctx, c_ctx, mod_w, mod_b, norm_g, ffn_w_gate, ffn_w_up, ffn_w_down,
                ev_w_in, ev_conv_w, ev_conv_b, ev_dt_bias, ev_a_log, ev_d_skip, ev_gnorm_g, ev_w_out,
                od_w_in, od_lambda, od_subln_g, od_conv_w, od_conv_b, od_cnorm_g, od_cnorm_b, od_w_out):
    identf = np.eye(128, dtype=np.float32)
    identb = identf.astype(ml_dtypes.bfloat16)
    selc = np.zeros((2, 258), np.float32)
    selc[0, 0] = 1; selc[1, 1] = 1; selc[0, 2:130] = 1; selc[1, 130:258] = 1
    gT = np.stack([np.stack([featT(norm_g[l, j]) for j in range(4)]) for l in range(2)])
    gR = np.stack([np.stack([rep(norm_g[l, j]) for j in range(4)]) for l in range(2)])
    CT, ST = dft_tabs(8192)
    CTc, STc = dft_tabs(256)
    kk = np.arange(128)
    ang = 2 * np.pi * np.outer(kk, kk) / 128
    ccsc = (np.concatenate([np.cos(ang), -np.sin(ang)], 1) / math.sqrt(128)).astype(np.float32)
    tri = np.zeros((128, 3, 128), np.float32)
    s_, l_ = np.meshgrid(np.arange(128), np.arange(128), indexing="ij")
    tri[:, 0] = (s_ <= l_); tri[:, 1] = (s_ >= l_); tri[:, 2] = 1.0
    cs, sn = rope_tables()
    selm = np.zeros((8, 2, 128), np.float32)
    selm[0::2, 0, :] = 1.0 / 512
    selm[1::2, 1, :] = 1.0 / 512
    p0 = np.concatenate([np.concatenate([np.arange(r * 128, (r + 1) * 128), 512 + np.arange(r * 384, (r + 1) * 384)]) for r in range(4)])
    p1 = np.concatenate([np.concatenate([np.arange(r * 256, (r + 1) * 256), 1024 + np.arange(r * 128, (r + 1) * 128)]) for r in range(4)])
    w_out0 = np.ascontiguousarray(ev_w_out[0][p0])
    w_out1 = np.ascontiguousarray(od_w_out[0][p1])
    shared = dict(mw=mod_w, mb=np.ascontiguousarray(mod_b[None]), selc=selc, identb=identb, identf=identf, gT=gT, gR=gR,
                  ccsc=ccsc, CT=CT, STt=ST, CTc=CTc, STc=STc, tri=tri, w_out0=w_out0, wg=ffn_w_gate, wu=ffn_w_up, wd=ffn_w_down,
                  cs=cs, sn=sn, lamR=rep(od_lambda[0]), subC=np.ascontiguousarray(od_subln_g[0].reshape(128, 1)), selm=selm, w_out1=w_out1)
    maps = []
    for core in CORES:
        b, g = core // 4, core % 4
        q = g
        m = dict(shared)
        m["xall"] = np.ascontiguousarray(np.concatenate([ctx[b], x[b]], 0))
        xo = np.zeros((17 * 128, 1024), np.float32)
        xo[:2048] = x[b, q * 2048:(q + 1) * 2048]; xo[2048:2112] = ctx[b, q * 64:(q + 1) * 64]
        m["xown"] = xo
        cv = np.stack([c[b], c_ctx], 0)
        m["cT2"] = np.ascontiguousarray(cv.reshape(2, 8, 128).transpose(2, 1, 0))
        dtcols = np.array([4608 + d * 24 + g * 6 + h for d in range(2) for h in range(6)])
        cols0 = np.concatenate([np.arange(g * 128, (g + 1) * 128), 2048 + np.arange(g * 384, (g + 1) * 384),
                                2048 + 1536 + np.arange(g * 128, (g + 1) * 128), 2048 + 2048 + np.arange(g * 128, (g + 1) * 128),
                                512 + np.arange(g * 384, (g + 1) * 384), dtcols])
        m["w_in0"] = np.ascontiguousarray(ev_w_in[0][:, cols0])
        ch = np.concatenate([np.arange(g * 384, (g + 1) * 384), 1536 + np.arange(g * 128, (g + 1) * 128), 2048 + np.arange(g * 128, (g + 1) * 128)])
        m["cw0"] = np.ascontiguousarray(ev_conv_w[0][:, ch].T.reshape(5, 128, 5).transpose(1, 0, 2))
        m["cb0"] = np.ascontiguousarray(ev_conv_b[0][ch].reshape(5, 128).T)
        prm = np.zeros((128, 3, 2, 66, 6), np.float32)
        for k_, a_ in enumerate((ev_dt_bias, ev_a_log, ev_d_skip)):
            prm[:, k_] = a_[0].reshape(2, 4, 6)[:, g, :][None, :, None, :]
        m["prm"] = prm
        m["gnR"] = rep(ev_gnorm_g[0][g * 384:(g + 1) * 384])
        hd0 = 2 * q
        cols1 = np.concatenate([3072 + np.arange(q * 128, (q + 1) * 128), 3072 + 512 + np.arange(q * 128, (q + 1) * 128),
                                np.arange(hd0 * 128, (hd0 + 2) * 128), 1024 + np.arange(hd0 * 128, (hd0 + 2) * 128),
                                2048 + np.arange(hd0 * 128, (hd0 + 2) * 128)])
        m["w_in1"] = np.ascontiguousarray(od_w_in[0][:, cols1])
        cq = slice(q * 128, (q + 1) * 128)
        m["cw1"] = np.ascontiguousarray(od_conv_w[0][:, cq].T.reshape(128, 1, 31))
        m["cb1"] = np.ascontiguousarray(od_conv_b[0][cq].reshape(128, 1))
        m["lng"] = np.ascontiguousarray(od_cnorm_g[0][cq].reshape(128, 1))
        m["lnb"] = np.ascontiguousarray(od_cnorm_b[0][cq].reshape(128, 1))
        maps.append(m)
    return maps


def kernel(**inputs):
    inputs = {k: np.ascontiguousarray(np.asarray(v, dtype=np.float32)) for k, v in inputs.items()}
    maps = prep_inputs(**inputs)
    if "fused" not in _NC_CACHE:
        _NC_CACHE["fused"] = build_fused()
    res = run_bass_kernel_spmd(_NC_CACHE["fused"], maps, core_ids=CORES)
    out = np.zeros((2, 8192, 1024), np.float32)
    for core in CORES:
        b, q = core // 4, core % 4
        out[b, q * 2048:(q + 1) * 2048] = res.results[core]["out"]
    return out
```

```python
import numpy as np
from contextlib import ExitStack
import concourse.bass as bass
import concourse.mybir as mybir
from concourse.bass_utils import run_bass_kernel_spmd

F32 = mybir.dt.float32
BF16 = mybir.dt.bfloat16
AF = mybir.ActivationFunctionType
ALU = mybir.AluOpType
AX = mybir.AxisListType


class Buf:
    __slots__ = ("name", "w", "r")

    def __init__(self, name=""):
        self.name = name
        self.w = {}
        self.r = {}


def _merge(dst, tok):
    k, v, e = tok
    if k not in dst or dst[k][0] < v:
        dst[k] = (v, e)


class Prog:
    ENGS = ("pe", "act", "dve", "pool", "sp")
    RING = 8

    def __init__(self, nc, stack):
        self.nc = nc
        self.stack = stack
        self.ops = {e: [] for e in self.ENGS}
        self.ccount = {e: 0 for e in self.ENGS}
        self.dcount = {e: 0 for e in self.ENGS}
        self.seen = {e: {} for e in self.ENGS}
        self.sems = {}
        self.nbuf = 0
        self.ncc = 0
        self._root_stack = stack
        self.scope_id = 0
        self._nscope = 0
        for e in self.ENGS:
            self.sems[("c", e)] = stack.enter_context(nc.semaphore("c_" + e))
        for e in ("sp", "pool", "act"):
            for i in range(self.RING):
                self.sems[("d", e, i)] = stack.enter_context(nc.semaphore(f"d_{e}{i}"))

    def sb(self, name, shape, dtype):
        return self.stack.enter_context(self.nc.sbuf_tensor(f"sb{self.scope_id}_" + name, list(shape), dtype))

    def ps(self, name, shape, dtype):
        return self.stack.enter_context(self.nc.psum_tensor(f"ps{self.scope_id}_" + name, list(shape), dtype))

    def buf(self, name=""):
        self.nbuf += 1
        return Buf(name or f"b{self.nbuf}")

    def alias(self, old):
        b = self.buf()
        for k, (v, e) in list(old.r.items()) + list(old.w.items()):
            _merge(b.r, (k, v, e))
        return b

    def _deps(self, eng, reads, writes, djw=()):
        deps = {}

        def add(k, v, e2):
            if eng == "pe" and e2 == "pe" and k[0] == "c":
                return
            if v > deps.get(k, 0):
                deps[k] = v
        for b in reads:
            for k, (v, e2) in b.w.items():
                add(k, v, e2)
        for b in writes:
            for k, (v, e2) in b.w.items():
                add(k, v, e2)
            for k, (v, e2) in b.r.items():
                add(k, v, e2)
        for b in djw:
            for k, (v, e2) in b.r.items():
                add(k, v, e2)
        seen = self.seen[eng]
        out = []
        for k, v in deps.items():
            if seen.get(k, 0) >= v:
                continue
            seen[k] = v
            out.append((k, v))
        return out

    def _record(self, tok, reads, writes, djw=()):
        for b in reads:
            _merge(b.r, tok)
        for b in writes:
            b.w = {tok[0]: (tok[1], tok[2])}
            b.r = {}
        for b in djw:
            _merge(b.w, tok)

    def op(self, eng, fn, reads=(), writes=(), djw=()):
        waits = self._deps(eng, reads, writes, djw)
        self.ccount[eng] += 1
        tok = (("c", eng), self.ccount[eng], eng)
        self.ops[eng].append(("c", fn, waits, tok))
        self._record(tok, reads, writes, djw)
        return tok

    def dma(self, eng, out_ap, in_ap, reads=(), writes=(), djw=(), **kw):
        waits = self._deps(eng, reads, writes, djw)
        j = self.dcount[eng]
        self.dcount[eng] += 1
        slot = j % self.RING
        k = ("d", eng, slot)
        need = 16 * (j // self.RING)
        if need > 0 and self.seen[eng].get(k, 0) < need:
            self.seen[eng][k] = need
            waits.append((k, need))
        tok = (k, 16 * (j // self.RING + 1), eng)

        def fn(e, out_ap=out_ap, in_ap=in_ap, kw=kw):
            o = out_ap() if callable(out_ap) else out_ap
            i = in_ap() if callable(in_ap) else in_ap
            return e.dma_start(out=o, in_=i, **kw)
        self.ops[eng].append(("d", fn, waits, tok))
        self._record(tok, reads, writes, djw)
        return tok

    def finish_wait(self, eng, toks):
        waits = []
        for (k, v, _e) in toks:
            if self.seen[eng].get(k, 0) < v:
                self.seen[eng][k] = v
                waits.append((k, v))
        self.ops[eng].append(("w", None, waits, None))

    def wait_all_dma(self, eng="sp"):
        toks = []
        for e in ("sp", "pool", "act"):
            n = self.dcount[e]
            for slot in range(self.RING):
                if n == 0:
                    continue
                last = ((n - 1 - slot) // self.RING) * self.RING + slot if n - 1 >= slot else -1
                if last >= 0:
                    toks.append((("d", e, slot), 16 * (last // self.RING + 1), e))
        self.finish_wait(eng, toks)

    def push_scope(self):
        self._outer = getattr(self, "_outer", [])
        self._outer.append(self.stack)
        self.stack = ExitStack()
        self.stack.__enter__()
        self._nscope += 1
        self.scope_id = self._nscope

    def pop_scope(self):
        self.stack.__exit__(None, None, None)
        self.stack = self._outer.pop()

    def all_tokens(self):
        toks = []
        for e in self.ENGS:
            if self.ccount[e] > 0:
                toks.append((("c", e), self.ccount[e], e))
        for e in ("sp", "pool", "act"):
            n = self.dcount[e]
            for slot in range(self.RING):
                if n - 1 >= slot:
                    last = ((n - 1 - slot) // self.RING) * self.RING + slot
                    toks.append((("d", e, slot), 16 * (last // self.RING + 1), e))
        for i in range(self.ncc):
            toks.append((("cc", i), 1, "pool"))
        return toks

    def barrier(self):
        toks = self.all_tokens()
        for e in self.ENGS:
            self.finish_wait(e, toks)

    def cc(self, kind, in_ap, out_ap, groups, reads=(), writes=()):
        waits = self._deps("pool", reads, writes)
        i = self.ncc
        self.ncc += 1
        k = ("cc", i)
        self.sems[k] = self._root_stack.enter_context(self.nc.semaphore(f"cc{i}"))
        tok = (k, 1, "pool")

        def fn(e):
            return e.collective_compute(kind, ALU.bypass, replica_groups=groups, ins=[in_ap], outs=[out_ap])
        self.ops["pool"].append(("x", fn, waits, tok))
        self._record(tok, reads, writes)
        return tok

    def raw(self, eng, fn):
        self.ops[eng].append(("r", fn, [], None))

    def emit(self):
        nc = self.nc
        prog = self
        with nc.Block() as block:
            def run(engname, e):
                for kind, fn, waits, tok in prog.ops[engname]:
                    for (k, v) in waits:
                        e.wait_ge(prog.sems[k], v)
                    if kind == "w":
                        continue
                    if kind == "r":
                        fn(e)
                        continue
                    ins = fn(e)
                    if kind == "c":
                        ins.then_inc(prog.sems[tok[0]], 1)
                    elif kind == "x":
                        ins.then_inc(prog.sems[tok[0]])
                    else:
                        ins.then_inc(prog.sems[tok[0]], 16)

            @block.tensor
            def _(e):
                run("pe", e)

            @block.scalar
            def _(e):
                run("act", e)

            @block.vector
            def _(e):
                run("dve", e)

            @block.gpsimd
            def _(e):
                run("pool", e)

            @block.sync
            def _(e):
                run("sp", e)


EPS = 1e-6


def new_nc():
    return bass.Bass("TRN2", target_bir_lowering=False)


def load_weight_bf16(P, w_dram, w_sb, b_w, nk, ncol, stage, b_stage, cast_engs=("pool",)):
    for k in range(nk):
        s = k % len(stage)
        P.dma("sp", stage[s][:, 0:ncol], w_dram[k * 128:(k + 1) * 128, :], writes=[b_stage[s]])
        eng = cast_engs[k % len(cast_engs)]
        P.op(eng, lambda e, s=s, k=k: e.tensor_copy(out=w_sb[:, k, :], in_=stage[s][:, 0:ncol]),
             reads=[b_stage[s]], writes=[b_w])


class NormT:
    def __init__(self, P, pfx, ident, b_ident, nt):
        self.P = P
        self.ident, self.b_ident = ident, b_ident
        self.junk = P.sb(pfx + "junk", [128, 1024], BF16)
        self.b_junk = P.buf()
        self.ss = P.sb(pfx + "ss", [128, nt], F32)
        self.b_ss = P.buf()
        self.rs = P.sb(pfx + "rs", [128, nt], F32)
        self.b_rs = [P.buf() for _ in range(nt)]
        self.xn = [P.sb(pfx + f"xn{i}", [128, 1024], BF16) for i in range(2)]
        self.b_xn = [P.buf() for _ in range(2)]
        self.psT = [P.ps(pfx + f"psT{i}", [128, 8, 128], BF16) for i in range(2)]
        self.b_psT = [P.buf() for _ in range(2)]
        P.op("pool", lambda e: e.memset(self.ss[:], 0.0), writes=[self.b_ss])
        self.n = 0

    def run(self, x_ap, b_x, t, aT_ap, b_aT, Gs, Sh, b_gs, cls):
        P = self.P
        i = self.n % 2
        self.n += 1
        ss, rs, xn, psT = self.ss, self.rs, self.xn[i], self.psT[i]
        P.op("act", lambda e: e.activation(out=self.junk[:], in_=x_ap, func=AF.Square, accum_out=ss[:, t:t + 1]),
             reads=[b_x, self.b_ss], writes=[self.b_junk, self.b_rs[t]])
        P.op("dve", lambda e: e.tensor_scalar(out=rs[:, t:t + 1], in0=ss[:, t:t + 1], scalar1=1.0 / 1024, scalar2=EPS,
                                              op0=ALU.mult, op1=ALU.add), reads=[self.b_rs[t]], writes=[self.b_rs[t]])
        P.op("act", lambda e: e.activation(out=rs[:, t:t + 1], in_=rs[:, t:t + 1], func=AF.Sqrt),
             reads=[self.b_rs[t]], writes=[self.b_rs[t]])
        P.op("dve", lambda e: e.reciprocal(out=rs[:, t:t + 1], in_=rs[:, t:t + 1]),
             reads=[self.b_rs[t]], writes=[self.b_rs[t]])
        P.op("dve", lambda e: e.tensor_scalar_mul(out=xn[:], in0=x_ap, scalar1=rs[:, t:t + 1]),
             reads=[b_x, self.b_rs[t]], writes=[self.b_xn[i]])
        for k in range(8):
            P.op("pe", lambda e, k=k: e.transpose(out=psT[:, k, :], in_=xn[:, k * 128:(k + 1) * 128], identity=self.ident[:]),
                 reads=[self.b_xn[i], self.b_ident], writes=[self.b_psT[i]])
        for k in range(8):
            P.op("act", lambda e, k=k: e.activation(out=aT_ap[:, k, :], in_=psT[:, k, :], func=AF.Identity,
                                                    scale=Gs[:, cls, k:k + 1], bias=Sh[:, cls, k:k + 1]),
                 reads=[self.b_psT[i], b_gs], djw=[b_aT])


def build_k1(NT, NOUT, ctx_tiles=(16,)):
    nc = new_nc()
    h = nc.dram_tensor("h", [NT * 128, 1024], F32, kind="ExternalInput").ap()
    w = nc.dram_tensor("w", [1024, NOUT], F32, kind="ExternalInput").ap()
    modT = nc.dram_tensor("modT", [128, 2, 2, 8], F32, kind="ExternalInput").ap()
    gT = nc.dram_tensor("gT", [128, 8], F32, kind="ExternalInput").ap()
    identd = nc.dram_tensor("ident", [128, 128], BF16, kind="ExternalInput").ap()
    out = nc.dram_tensor("out", [NT * 128, NOUT], F32, kind="ExternalOutput").ap()
    ncb = (NOUT + 511) // 512
    with ExitStack() as st:
        P = Prog(nc, st)
        ident = P.sb("ident", [128, 128], BF16); b_ident = P.buf()
        P.dma("sp", ident[:], identd, writes=[b_ident])
        modsb = P.sb("modsb", [128, 2, 2, 8], F32); b_mod = P.buf()
        P.dma("sp", modsb[:], modT, writes=[b_mod])
        gsb = P.sb("gsb", [128, 8], F32); b_g = P.buf()
        P.dma("sp", gsb[:], gT, writes=[b_g])
        Gs = P.sb("Gs", [128, 2, 8], F32); Sh = P.sb("Sh", [128, 2, 8], F32); b_gs = P.buf()
        for cls in range(2):
            P.op("dve", lambda e, cls=cls: e.scalar_tensor_tensor(out=Gs[:, cls, :], in0=modsb[:, cls, 1, :], scalar=1.0,
                                                                   in1=gsb[:], op0=ALU.add, op1=ALU.mult),
                 reads=[b_mod, b_g], writes=[b_gs])
            P.op("dve", lambda e, cls=cls: e.tensor_copy(out=Sh[:, cls, :], in_=modsb[:, cls, 0, :]),
                 reads=[b_mod], writes=[b_gs])
        w_sb = P.sb("w_sb", [128, 8, NOUT], BF16); b_w = P.buf()
        stage = [P.sb(f"stage{i}", [128, NOUT], F32) for i in range(2)]
        b_stage = [P.buf() for _ in range(2)]
        load_weight_bf16(P, w, w_sb, b_w, 8, NOUT, stage, b_stage)
        nt = NormT(P, "n_", ident, b_ident, NT)
        xt = [P.sb(f"xt{i}", [128, 1024], F32) for i in range(2)]; b_xt = [P.buf() for _ in range(2)]
        aT = [P.sb(f"aT{i}", [128, 8, 128], BF16) for i in range(2)]; b_aT = [P.buf() for _ in range(2)]
        ot = stage; b_ot = b_stage
        pso = [P.ps(f"pso{i}", [128, 512], F32) for i in range(4)]; b_pso = [P.buf() for _ in range(4)]
        outs = []
        nps = 0
        for t in range(NT):
            i = t % 2
            cls = 1 if t in ctx_tiles else 0
            P.dma("sp", xt[i][:], h[t * 128:(t + 1) * 128, :], writes=[b_xt[i]])
            nt.run(xt[i][:], b_xt[i], t, aT[i], b_aT[i], Gs, Sh, b_gs, cls)
            for cb in range(ncb):
                c0 = cb * 512; cn = min(512, NOUT - c0)
                pi = nps % 4; nps += 1
                for k in range(8):
                    P.op("pe", lambda e, pi=pi, k=k, c0=c0, cn=cn, i=i: e.matmul(
                        pso[pi][:, 0:cn], lhsT=aT[i][:, k, :], rhs=w_sb[:, k, c0:c0 + cn], start=(k == 0), stop=(k == 7)),
                        reads=[b_aT[i], b_w], writes=[b_pso[pi]])
                if cb % 2 == 0:
                    P.op("dve", lambda e, pi=pi, c0=c0, cn=cn, i=i: e.tensor_copy(out=ot[i][:, c0:c0 + cn], in_=pso[pi][:, 0:cn]),
                         reads=[b_pso[pi]], writes=[b_ot[i]])
                else:
                    P.op("act", lambda e, pi=pi, c0=c0, cn=cn, i=i: e.copy(out=ot[i][:, c0:c0 + cn], in_=pso[pi][:, 0:cn]),
                         reads=[b_pso[pi]], writes=[b_ot[i]])
            outs.append(P.dma("sp", out[t * 128:(t + 1) * 128, :], ot[i][:], reads=[b_ot[i]]))
        P.finish_wait("sp", outs)
        P.emit()
    return nc


class ResNorm:
    def __init__(self, P, pfx, nt):
        self.P = P
        self.junk = P.sb(pfx + "junk", [128, 1024], BF16); self.b_junk = P.buf()
        self.ss = P.sb(pfx + "ss", [128, nt], F32); self.b_ss = P.buf()
        self.rs = P.sb(pfx + "rs", [128, nt], F32); self.b_rs = [P.buf() for _ in range(nt)]
        self.tmp = [P.sb(pfx + f"tmp{i}", [128, 1024], F32) for i in range(2)]; self.b_tmp = [P.buf() for _ in range(2)]
        P.op("pool", lambda e: e.memset(self.ss[:], 0.0), writes=[self.b_ss])
        self.n = 0

    def run(self, po, b_po, t, hres, b_hres, GG_ap, b_gg, out_ap=None, b_out=None):
        P = self.P
        i = self.n % 2
        self.n += 1
        ss, rs, tmp = self.ss, self.rs, self.tmp[i]
        P.op("act", lambda e: e.activation(out=self.junk[:], in_=po, func=AF.Square, accum_out=ss[:, t:t + 1]),
             reads=[b_po, self.b_ss], writes=[self.b_junk, self.b_rs[t]])
        P.op("dve", lambda e: e.tensor_scalar(out=rs[:, t:t + 1], in0=ss[:, t:t + 1], scalar1=1.0 / 1024, scalar2=EPS,
                                              op0=ALU.mult, op1=ALU.add), reads=[self.b_rs[t]], writes=[self.b_rs[t]])
        P.op("act", lambda e: e.activation(out=rs[:, t:t + 1], in_=rs[:, t:t + 1], func=AF.Sqrt),
             reads=[self.b_rs[t]], writes=[self.b_rs[t]])
        P.op("dve", lambda e: e.reciprocal(out=rs[:, t:t + 1], in_=rs[:, t:t + 1]),
             reads=[self.b_rs[t]], writes=[self.b_rs[t]])
        P.op("dve", lambda e: e.scalar_tensor_tensor(out=tmp[:], in0=po, scalar=rs[:, t:t + 1], in1=GG_ap,
                                                     op0=ALU.mult, op1=ALU.mult),
             reads=[b_po, self.b_rs[t], b_gg], writes=[self.b_tmp[i]])
        if out_ap is None:
            P.op("pool", lambda e: e.tensor_add(out=tmp[:], in0=tmp[:], in1=hres),
                 reads=[self.b_tmp[i], b_hres], writes=[self.b_tmp[i]])
            return tmp, self.b_tmp[i]
        P.op("pool", lambda e: e.tensor_add(out=out_ap, in0=tmp[:], in1=hres),
             reads=[self.b_tmp[i], b_hres], writes=[b_out])
        return None


def build_k3a(NT, CM, ctx_tiles=(16,)):
    nc = new_nc()
    nk = CM // 128
    mixT = nc.dram_tensor("mixT", [CM, NT * 128], F32, kind="ExternalInput").ap()
    h = nc.dram_tensor("h", [NT * 128, 1024], F32, kind="ExternalInput").ap()
    w = nc.dram_tensor("w", [CM, 1024], F32, kind="ExternalInput").ap()
    gR = nc.dram_tensor("gR", [128, 1024], F32, kind="ExternalInput").ap()
    gateR = nc.dram_tensor("gateR", [128, 2, 1024], F32, kind="ExternalInput").ap()
    out = nc.dram_tensor("out", [NT * 128, 1024], F32, kind="ExternalOutput").ap()
    with ExitStack() as st:
        P = Prog(nc, st)
        g_sb = P.sb("g_sb", [128, 1024], F32); b_g = P.buf()
        P.dma("sp", g_sb[:], gR, writes=[b_g])
        GG = P.sb("GG", [128, 2, 1024], F32); b_gg = P.buf()
        P.dma("sp", GG[:], gateR, writes=[b_gg])
        for cls in range(2):
            P.op("dve", lambda e, cls=cls: e.tensor_mul(out=GG[:, cls, :], in0=GG[:, cls, :], in1=g_sb[:]),
                 reads=[b_g, b_gg], writes=[b_gg])
        w_sb = P.sb("w_sb", [128, nk, 1024], BF16); b_w = P.buf()
        stage = [P.sb(f"stage{i}", [128, 1024], F32) for i in range(2)]; b_stage = [P.buf() for _ in range(2)]
        load_weight_bf16(P, w, w_sb, b_w, nk, 1024, stage, b_stage)
        rn = ResNorm(P, "r_", NT)
        mst = [P.sb(f"mst{i}", [128, nk, 128], F32) for i in range(2)]; b_mst = [P.buf() for _ in range(2)]
        mT = [P.sb(f"mT{i}", [128, nk, 128], BF16) for i in range(2)]; b_mT = [P.buf() for _ in range(2)]
        xt = [P.sb(f"xt{i}", [128, 1024], F32) for i in range(2)]; b_xt = [P.buf() for _ in range(2)]
        ot = [P.sb(f"ot{i}", [128, 1024], F32) for i in range(2)]; b_ot = [P.buf() for _ in range(2)]
        po = [P.ps(f"po{i}", [128, 1024], F32) for i in range(2)]; b_po = [P.buf() for _ in range(2)]
        mv = mixT.rearrange("(k p) t -> p k t", p=128)
        outs = []
        for t in range(NT):
            i = t % 2
            cls = 1 if t in ctx_tiles else 0
            P.dma("sp", mst[i][:], mv[:, :, t * 128:(t + 1) * 128], writes=[b_mst[i]])
            P.dma("sp", xt[i][:], h[t * 128:(t + 1) * 128, :], writes=[b_xt[i]])
            P.op("pool", lambda e, i=i: e.tensor_copy(out=mT[i][:], in_=mst[i][:]), reads=[b_mst[i]], writes=[b_mT[i]])
            for cb in range(2):
                for k in range(nk):
                    P.op("pe", lambda e, i=i, k=k, cb=cb: e.matmul(
                        po[i][:, cb * 512:(cb + 1) * 512], lhsT=mT[i][:, k, :], rhs=w_sb[:, k, cb * 512:(cb + 1) * 512],
                        start=(k == 0), stop=(k == nk - 1)), reads=[b_mT[i], b_w], writes=[b_po[i]])
            rn.run(po[i][:], b_po[i], t, xt[i][:], b_xt[i], GG[:, cls, :], b_gg, ot[i][:], b_ot[i])
            outs.append(P.dma("sp", out[t * 128:(t + 1) * 128, :], ot[i][:], reads=[b_ot[i]]))
        P.finish_wait("sp", outs)
        P.emit()
    return nc


def build_k3b(NT, ctx_tiles=(16,)):
    nc = new_nc()
    FH = 2816
    NJ = FH // 128
    h = nc.dram_tensor("h", [NT * 128, 1024], F32, kind="ExternalInput").ap()
    wg = nc.dram_tensor("wg", [1024, FH], F32, kind="ExternalInput").ap()
    wu = nc.dram_tensor("wu", [1024, FH], F32, kind="ExternalInput").ap()
    wd = nc.dram_tensor("wd", [FH, 1024], F32, kind="ExternalInput").ap()
    modT = nc.dram_tensor("modT", [128, 2, 2, 8], F32, kind="ExternalInput").ap()
    gT = nc.dram_tensor("gT", [128, 8], F32, kind="ExternalInput").ap()
    gR = nc.dram_tensor("gR", [128, 1024], F32, kind="ExternalInput").ap()
    gateR = nc.dram_tensor("gateR", [128, 2, 1024], F32, kind="ExternalInput").ap()
    identd = nc.dram_tensor("ident", [128, 128], BF16, kind="ExternalInput").ap()
    out = nc.dram_tensor("out", [NT * 128, 1024], F32, kind="ExternalOutput").ap()
    with ExitStack() as st:
        P = Prog(nc, st)
        ident = P.sb("ident", [128, 128], BF16); b_ident = P.buf()
        P.dma("sp", ident[:], identd, writes=[b_ident])
        modsb = P.sb("modsb", [128, 2, 2, 8], F32); b_mod = P.buf()
        P.dma("sp", modsb[:], modT, writes=[b_mod])
        gsb = P.sb("gsb", [128, 8], F32); b_g = P.buf()
        P.dma("sp", gsb[:], gT, writes=[b_g])
        Gs = P.sb("Gs", [128, 2, 8], F32); Sh = P.sb("Sh", [128, 2, 8], F32); b_gs = P.buf()
        for cls in range(2):
            P.op("dve", lambda e, cls=cls: e.scalar_tensor_tensor(out=Gs[:, cls, :], in0=modsb[:, cls, 1, :], scalar=1.0,
                                                                   in1=gsb[:], op0=ALU.add, op1=ALU.mult),
                 reads=[b_mod, b_g], writes=[b_gs])
            P.op("dve", lambda e, cls=cls: e.tensor_copy(out=Sh[:, cls, :], in_=modsb[:, cls, 0, :]),
                 reads=[b_mod], writes=[b_gs])
        g_sb = P.sb("g_sb", [128, 1024], F32); b_g3 = P.buf()
        P.dma("sp", g_sb[:], gR, writes=[b_g3])
        GG = P.sb("GG", [128, 2, 1024], F32); b_gg = P.buf()
        P.dma("sp", GG[:], gateR, writes=[b_gg])
        for cls in range(2):
            P.op("dve", lambda e, cls=cls: e.tensor_mul(out=GG[:, cls, :], in0=GG[:, cls, :], in1=g_sb[:]),
                 reads=[b_g3, b_gg], writes=[b_gg])
        wg_sb = P.sb("wg_sb", [128, 8, FH], BF16); b_wg = P.buf()
        wu_sb = P.sb("wu_sb", [128, 8, FH], BF16); b_wu = P.buf()
        wd_sb = P.sb("wd_sb", [128, NJ, 1024], BF16); b_wd = P.buf()
        stage = [P.sb(f"stage{i}", [128, FH], F32) for i in range(2)]; b_stage = [P.buf() for _ in range(2)]
        load_weight_bf16(P, wg, wg_sb, b_wg, 8, FH, stage, b_stage, cast_engs=("pool", "dve"))
        load_weight_bf16(P, wu, wu_sb, b_wu, 8, FH, stage, b_stage, cast_engs=("pool", "dve"))
        load_weight_bf16(P, wd, wd_sb, b_wd, NJ, 1024, stage, b_stage, cast_engs=("pool", "dve"))
        nt = NormT(P, "n_", ident, b_ident, NT)
        rn = ResNorm(P, "r_", NT)
        ST = 2
        xt = [stage[0][:, i * 1024:(i + 1) * 1024] for i in range(2)]; b_xt = [P.alias(b_stage[0]) for _ in range(2)]
        xr = [stage[1][:, i * 1024:(i + 1) * 1024] for i in range(2)]; b_xr = [P.alias(b_stage[1]) for _ in range(2)]
        aT = P.sb("aT", [128, 8, ST * 128], BF16); b_aT = P.buf()
        hidT = P.sb("hidT", [128, NJ, ST * 128], BF16); b_hid = P.buf()
        sg = [P.sb(f"sg{i}", [128, ST * 128], F32) for i in range(2)]; b_sg = [P.buf() for _ in range(2)]
        psg = [P.ps(f"psg{i}", [128, 512], F32) for i in range(2)]; b_psg = [P.buf() for _ in range(2)]
        psu = [P.ps(f"psu{i}", [128, 512], F32) for i in range(2)]; b_psu = [P.buf() for _ in range(2)]
        po = P.ps("po", [128, 1024], F32); b_po = P.buf()
        outs = []
        nx = 0
        nr = 0
        for s0 in range(0, NT, ST):
            tiles = list(range(s0, min(NT, s0 + ST)))
            N = len(tiles) * 128
            for ti, t in enumerate(tiles):
                i = nx % 2; nx += 1
                cls = 1 if t in ctx_tiles else 0
                P.dma("sp", xt[i], h[t * 128:(t + 1) * 128, :], writes=[b_xt[i]])
                nt.run(xt[i], b_xt[i], t, aT[:, :, ti * 128:(ti + 1) * 128], b_aT, Gs, Sh, b_gs, cls)
            for j in range(NJ):
                pi = j % 2
                for k in range(8):
                    P.op("pe", lambda e, pi=pi, j=j, k=k, N=N: e.matmul(
                        psg[pi][:, 0:N], lhsT=wg_sb[:, k, j * 128:(j + 1) * 128], rhs=aT[:, k, 0:N],
                        start=(k == 0), stop=(k == 7)), reads=[b_wg, b_aT], writes=[b_psg[pi]])
                for k in range(8):
                    P.op("pe", lambda e, pi=pi, j=j, k=k, N=N: e.matmul(
                        psu[pi][:, 0:N], lhsT=wu_sb[:, k, j * 128:(j + 1) * 128], rhs=aT[:, k, 0:N],
                        start=(k == 0), stop=(k == 7)), reads=[b_wu, b_aT], writes=[b_psu[pi]])
                P.op("act", lambda e, pi=pi, N=N: e.activation(out=sg[pi][:, 0:N], in_=psg[pi][:, 0:N], func=AF.Silu),
                     reads=[b_psg[pi]], writes=[b_sg[pi]])
                P.op("dve", lambda e, pi=pi, j=j, N=N: e.tensor_mul(out=hidT[:, j, 0:N], in0=sg[pi][:, 0:N], in1=psu[pi][:, 0:N]),
                     reads=[b_sg[pi], b_psu[pi]], writes=[b_hid])
            for ti, t in enumerate(tiles):
                i = nr % 2; nr += 1
                cls = 1 if t in ctx_tiles else 0
                P.dma("sp", xr[i], h[t * 128:(t + 1) * 128, :], writes=[b_xr[i]])
                for cb in range(2):
                    for j in range(NJ):
                        P.op("pe", lambda e, j=j, cb=cb, ti=ti: e.matmul(
                            po[:, cb * 512:(cb + 1) * 512], lhsT=hidT[:, j, ti * 128:(ti + 1) * 128],
                            rhs=wd_sb[:, j, cb * 512:(cb + 1) * 512], start=(j == 0), stop=(j == NJ - 1)),
                            reads=[b_hid, b_wd], writes=[b_po])
                o_t, b_o = rn.run(po[:], b_po, t, xr[i], b_xr[i], GG[:, cls, :], b_gg)
                outs.append(P.dma("sp", out[t * 128:(t + 1) * 128, :], o_t[:], reads=[b_o]))
        P.finish_wait("sp", outs)
        P.emit()
    return nc


def conv_fm(P, eng, vin, b_in, wsb, j, bias_ap, b_w, acc, b_acc, K, T, t0=0):
    P.op(eng, lambda e: e.tensor_scalar(out=acc, in0=vin[:, t0:t0 + T], scalar1=wsb[:, j, 0:1], scalar2=bias_ap,
                                        op0=ALU.mult, op1=ALU.add), reads=[b_in, b_w], writes=[b_acc])
    for k in range(1, K):
        P.op(eng, lambda e, k=k: e.scalar_tensor_tensor(out=acc, in0=vin[:, t0 + k:t0 + k + T], scalar=wsb[:, j, k:k + 1],
                                                        in1=acc, op0=ALU.mult, op1=ALU.add),
             reads=[b_in, b_w, b_acc], writes=[b_acc])


def build_k2a(TL=8192, TC=256, NCH=5, K=5):
    nc = new_nc()
    H = K - 1
    TT = TL + TC + 2 * H
    pre = nc.dram_tensor("pre", [NCH * 128, TT], F32, kind="ExternalInput").ap()
    cw = nc.dram_tensor("cw", [128, NCH, K], F32, kind="ExternalInput").ap()
    cb = nc.dram_tensor("cb", [128, NCH], F32, kind="ExternalInput").ap()
    out = nc.dram_tensor("out", [NCH * 128, TL + TC], F32, kind="ExternalOutput").ap()
    BL = 2048
    with ExitStack() as st:
        P = Prog(nc, st)
        wsb = P.sb("wsb", [128, NCH, K], F32); bsb = P.sb("bsb", [128, NCH], F32); b_w = P.buf()
        P.dma("sp", wsb[:], cw, writes=[b_w])
        P.dma("sp", bsb[:], cb, writes=[b_w])
        vin = [P.sb(f"vin{i}", [128, TT], F32) for i in range(2)]; b_vin = [P.buf() for _ in range(2)]
        acc = [P.sb(f"acc{i}", [128, BL], F32) for i in range(2)]; b_acc = [P.buf() for _ in range(2)]
        res = [P.sb(f"res{i}", [128, BL], F32) for i in range(2)]; b_res = [P.buf() for _ in range(2)]
        outs = []
        n = 0
        for j in range(NCH):
            vi = j % 2
            P.dma("sp", vin[vi][:], pre[j * 128:(j + 1) * 128, :], writes=[b_vin[vi]])
            blocks = [(t0, BL, t0) for t0 in range(0, TL, BL)] + [(TL + H, TC, TL)]
            for (i0, T, o0) in blocks:
                i = n % 2; n += 1
                eng = "dve" if i == 0 else "pool"
                conv_fm(P, "dve", vin[vi], b_vin[vi], wsb, j, bsb[:, j:j + 1], b_w, acc[i][:, 0:T], b_acc[i], K, T, t0=i0)
                P.op("act", lambda e, i=i, T=T: e.activation(out=res[i][:, 0:T], in_=acc[i][:, 0:T], func=AF.Silu),
                     reads=[b_acc[i]], writes=[b_res[i]])
                outs.append(P.dma("sp", out[j * 128:(j + 1) * 128, o0:o0 + T], res[i][:, 0:T], reads=[b_res[i]]))
        P.finish_wait("sp", outs)
        P.emit()
    return nc


def build_k5(T=2048, K=31):
    nc = new_nc()
    H = K - 1
    TT = T + H
    uT = nc.dram_tensor("uT", [1024, TT], F32, kind="ExternalInput").ap()
    cw = nc.dram_tensor("cw", [128, 4, K], F32, kind="ExternalInput").ap()
    cb = nc.dram_tensor("cb", [128, 4], F32, kind="ExternalInput").ap()
    lng = nc.dram_tensor("lng", [128, 4], F32, kind="ExternalInput").ap()
    lnb = nc.dram_tensor("lnb", [128, 4], F32, kind="ExternalInput").ap()
    out = nc.dram_tensor("out", [512, T], F32, kind="ExternalOutput").ap()
    with ExitStack() as st:
        P = Prog(nc, st)
        wsb = P.sb("wsb", [128, 4, K], F32); bsb = P.sb("bsb", [128, 4], F32); b_w = P.buf()
        gsb = P.sb("gsb", [128, 4], F32); lbsb = P.sb("lbsb", [128, 4], F32)
        P.dma("sp", wsb[:], cw, writes=[b_w]); P.dma("sp", bsb[:], cb, writes=[b_w])
        P.dma("sp", gsb[:], lng, writes=[b_w]); P.dma("sp", lbsb[:], lnb, writes=[b_w])
        ones = P.sb("ones", [128, 128], F32); b_ones = P.buf()
        P.op("pool", lambda e: e.memset(ones[:], 1.0 / 512), writes=[b_ones])
        a_sb = [P.sb(f"a{i}", [128, TT], F32) for i in range(2)]; b_a = [P.buf() for _ in range(2)]
        g_sb = [P.sb(f"g{i}", [128, TT], F32) for i in range(2)]; b_g = [P.buf() for _ in range(2)]
        cv = P.sb("cv", [128, 4, T], F32); b_cv = [P.buf() for _ in range(4)]
        for j in range(4):
            i = j % 2
            P.dma("sp", a_sb[i][:], uT[j * 128:(j + 1) * 128, :], writes=[b_a[i]])
            P.dma("sp", g_sb[i][:], uT[512 + j * 128:512 + (j + 1) * 128, :], writes=[b_g[i]])
            P.op("act", lambda e, i=i: e.activation(out=g_sb[i][:], in_=g_sb[i][:], func=AF.Sigmoid), reads=[b_g[i]], writes=[b_g[i]])
            eng = "dve" if i == 0 else "pool"
            P.op(eng, lambda e, i=i: e.tensor_mul(out=a_sb[i][:], in0=a_sb[i][:], in1=g_sb[i][:]), reads=[b_a[i], b_g[i]], writes=[b_a[i]])
            conv_fm(P, "dve", a_sb[i], b_a[i], wsb, j, bsb[:, j:j + 1], b_w, cv[:, j, :], b_cv[j], K, T)
        sq = P.sb("sq", [128, 4, 512], F32); b_sq = P.buf()
        pm = [P.ps(f"pm{i}", [128, 512], F32) for i in range(2)]; b_pm = [P.buf() for _ in range(2)]
        pq = [P.ps(f"pq{i}", [128, 512], F32) for i in range(2)]; b_pq = [P.buf() for _ in range(2)]
        rstd = P.sb("rstd", [128, 512], F32); b_rstd = P.buf()
        msq = P.sb("msq", [128, 512], F32); b_msq = P.buf()
        xc = [P.sb(f"xc{i}", [128, 512], F32) for i in range(2)]; b_xc = [P.buf() for _ in range(2)]
        outs = []
        n = 0
        for tb in range(T // 512):
            sl = slice(tb * 512, (tb + 1) * 512)
            pi = tb % 2
            P.op("act", lambda e, sl=sl: e.activation(out=sq[:], in_=cv[:, :, sl], func=AF.Square), reads=b_cv, writes=[b_sq])
            for j in range(4):
                P.op("pe", lambda e, j=j, sl=sl, pi=pi: e.matmul(pm[pi][:], lhsT=ones[:], rhs=cv[:, j, sl], start=(j == 0), stop=(j == 3)),
                     reads=[b_ones, b_cv[j]], writes=[b_pm[pi]])
            for j in range(4):
                P.op("pe", lambda e, j=j, pi=pi: e.matmul(pq[pi][:], lhsT=ones[:], rhs=sq[:, j, :], start=(j == 0), stop=(j == 3)),
                     reads=[b_ones, b_sq], writes=[b_pq[pi]])
            P.op("act", lambda e, pi=pi: e.activation(out=msq[:], in_=pm[pi][:], func=AF.Square), reads=[b_pm[pi]], writes=[b_msq])
            P.op("dve", lambda e, pi=pi: e.scalar_tensor_tensor(out=rstd[:], in0=pq[pi][:], scalar=EPS, in1=msq[:], op0=ALU.add, op1=ALU.subtract),
                 reads=[b_pq[pi], b_msq], writes=[b_rstd])
            P.op("act", lambda e: e.activation(out=rstd[:], in_=rstd[:], func=AF.Sqrt), reads=[b_rstd], writes=[b_rstd])
            P.op("dve", lambda e: e.reciprocal(out=rstd[:], in_=rstd[:]), reads=[b_rstd], writes=[b_rstd])
            for j in range(4):
                i = n % 2; n += 1
                P.op("dve", lambda e, i=i, j=j, sl=sl, pi=pi: e.tensor_sub(out=xc[i][:], in0=cv[:, j, sl], in1=pm[pi][:]),
                     reads=[b_cv[j], b_pm[pi]], writes=[b_xc[i]])
                P.op("pool", lambda e, i=i: e.tensor_mul(out=xc[i][:], in0=xc[i][:], in1=rstd[:]), reads=[b_xc[i], b_rstd], writes=[b_xc[i]])
                P.op("act", lambda e, i=i, j=j: e.activation(out=xc[i][:], in_=xc[i][:], func=AF.Silu, scale=gsb[:, j:j + 1], bias=lbsb[:, j:j + 1]),
                     reads=[b_xc[i], b_w], writes=[b_xc[i]])
                outs.append(P.dma("sp", out[j * 128:(j + 1) * 128, sl], xc[i][:], reads=[b_xc[i]]))
        P.finish_wait("sp", outs)
        P.emit()
    return nc


def build_k4(lambda_init, NU=2, TL=8192, TC=256, NQB=None):
    nc = new_nc()
    NKT = (TL + TC) // 128
    NCT = TC // 128
    NLT = TL // 128
    if NQB is None:
        NQB = TL // 512
    qk = nc.dram_tensor("qk", [NU, 2, TL, 128], F32, kind="ExternalInput").ap()
    v = nc.dram_tensor("v", [NU, TL, 128], F32, kind="ExternalInput").ap()
    kc = nc.dram_tensor("kc", [NU, TC, 128], F32, kind="ExternalInput").ap()
    vc = nc.dram_tensor("vc", [NU, TC, 128], F32, kind="ExternalInput").ap()
    csd = nc.dram_tensor("cs", [TL, 128], F32, kind="ExternalInput").ap()
    snd = nc.dram_tensor("sn", [TL, 128], F32, kind="ExternalInput").ap()
    lamRd = nc.dram_tensor("lamR", [128, 4, 64], F32, kind="ExternalInput").ap()
    subRd = nc.dram_tensor("subR", [128, 128], F32, kind="ExternalInput").ap()
    identd = nc.dram_tensor("ident", [128, 128], BF16, kind="ExternalInput").ap()
    out = nc.dram_tensor("out", [NU, TL, 128], F32, kind="ExternalOutput").ap()
    with ExitStack() as st:
        P = Prog(nc, st)
        ident = P.sb("ident", [128, 128], BF16); b_ident = P.buf()
        P.dma("sp", ident[:], identd, writes=[b_ident])
        lam = P.sb("lam", [128, 4, 64], F32); b_lam = P.buf()
        P.dma("sp", lam[:], lamRd, writes=[b_lam])
        GS = P.sb("GS", [128, 128], F32); b_gs = P.buf()
        P.dma("sp", GS[:], subRd, writes=[b_gs])
        P.op("dve", lambda e: e.tensor_scalar_mul(out=GS[:], in0=GS[:], scalar1=float(1.0 - lambda_init)), reads=[b_gs], writes=[b_gs])
        lp = P.sb("lp", [128, 2, 64], F32); ls = P.sb("ls", [128, 4], F32); b_ls = P.buf()
        P.op("dve", lambda e: e.tensor_mul(out=lp[:, 0, :], in0=lam[:, 0, :], in1=lam[:, 1, :]), reads=[b_lam], writes=[b_ls])
        P.op("dve", lambda e: e.tensor_mul(out=lp[:, 1, :], in0=lam[:, 2, :], in1=lam[:, 3, :]), reads=[b_lam, b_ls], writes=[b_ls])
        P.op("dve", lambda e: e.reduce_sum(out=ls[:, 0:2], in_=lp[:], axis=AX.X), reads=[b_ls], writes=[b_ls])
        P.op("act", lambda e: e.activation(out=ls[:, 0:2], in_=ls[:, 0:2], func=AF.Exp), reads=[b_ls], writes=[b_ls])
        P.op("dve", lambda e: e.tensor_sub(out=ls[:, 2:3], in0=ls[:, 1:2], in1=ls[:, 0:1]), reads=[b_ls], writes=[b_ls])
        P.op("dve", lambda e: e.tensor_scalar_add(out=ls[:, 3:4], in0=ls[:, 2:3], scalar1=float(-lambda_init)), reads=[b_ls], writes=[b_ls])
        neglam = ls[:, 3:4]

        kT = P.sb("kT", [128, TL + TC], BF16); b_kT = P.buf()
        qT = P.sb("qT", [128, TL], BF16); b_qT = P.buf()
        vaug = P.sb("vaug", [128, NKT, 129], BF16); b_va = P.buf()
        P.op("pool", lambda e: e.memset(vaug[:, :, 128:129], 1.0), writes=[b_va])
        ld = {n: [P.sb(f"ld_{n}{i}", [128, 128], F32) for i in range(2)] for n in ("q", "k", "v", "cs", "sn")}
        b_ld = {n: [P.buf() for _ in range(2)] for n in ld}
        t1 = {n: [P.sb(f"t1_{n}{i}", [128, 128], F32) for i in range(2)] for n in ("q", "k")}
        t2 = {n: [P.sb(f"t2_{n}{i}", [128, 128], F32) for i in range(2)] for n in ("q", "k")}
        b_t1 = {n: [P.buf() for _ in range(2)] for n in t1}
        b_t2 = {n: [P.buf() for _ in range(2)] for n in t1}
        rb = {n: [P.sb(f"rb_{n}{i}", [128, 128], BF16) for i in range(2)] for n in ("q", "k")}
        b_rb = {n: [P.buf() for _ in range(2)] for n in rb}
        psT = [P.ps(f"psT{i}", [128, 128], BF16) for i in range(2)]; b_psT = [P.buf() for _ in range(2)]
        ps_s = [P.ps(f"ps_s{i}", [128, 512], F32) for i in range(2)]; b_ps_s = [P.buf() for _ in range(2)]
        acc = [P.ps(f"acc{i}", [128, 129], F32) for i in range(4)]; b_acc = [P.buf() for _ in range(4)]
        E = [P.sb(f"E{i}", [128, 512], BF16) for i in range(2)]; b_E = [P.buf() for _ in range(2)]
        att0 = [P.sb(f"att0_{i}", [128, 128], F32) for i in range(4)]; b_att0 = [P.buf() for _ in range(4)]
        att1 = [P.sb(f"att1_{i}", [128, 128], F32) for i in range(2)]; b_att1 = [P.buf() for _ in range(2)]
        rec = P.sb("rec", [128, 8], F32); b_rec = [P.buf() for _ in range(8)]
        junk = P.sb("junk", [128, 128], F32); b_junk = P.buf()
        sst = P.sb("sst", [128, 2], F32); b_sst = [P.buf() for _ in range(2)]
        v5 = lambda ap: ap.rearrange("p (c a h f) -> p c a h f", c=2, a=2, h=2, f=16)
        outs = []
        npt = [0]

        def transpose_to(src_bf, b_src, dst_ap, b_dst):
            pi = npt[0] % 2; npt[0] += 1
            P.op("pe", lambda e: e.transpose(out=psT[pi][:], in_=src_bf, identity=ident[:]), reads=[b_src, b_ident], writes=[b_psT[pi]])
            P.op("act", lambda e: e.copy(out=dst_ap, in_=psT[pi][:]), reads=[b_psT[pi]], writes=[b_dst])

        for u in range(NU):
            for j in range(NCT):
                i = j % 2
                P.dma("sp", ld["k"][i][:], kc[u, j * 128:(j + 1) * 128, :], writes=[b_ld["k"][i]])
                P.dma("sp", ld["v"][i][:], vc[u, j * 128:(j + 1) * 128, :], writes=[b_ld["v"][i]])
                P.op("dve", lambda e, i=i: e.tensor_copy(out=rb["k"][i][:], in_=ld["k"][i][:]), reads=[b_ld["k"][i]], writes=[b_rb["k"][i]])
                transpose_to(rb["k"][i][:], b_rb["k"][i], kT[:, j * 128:(j + 1) * 128], b_kT)
                P.op("pool", lambda e, i=i, j=j: e.tensor_copy(out=vaug[:, j, 0:128], in_=ld["v"][i][:]), reads=[b_ld["v"][i]], writes=[b_va])
            for j in range(NLT):
                i = j % 2
                sl = slice(j * 128, (j + 1) * 128)
                P.dma("sp", ld["q"][i][:], qk[u, 0, sl, :], writes=[b_ld["q"][i]])
                P.dma("sp", ld["k"][i][:], qk[u, 1, sl, :], writes=[b_ld["k"][i]])
                P.dma("sp", ld["v"][i][:], v[u, sl, :], writes=[b_ld["v"][i]])
                P.dma("sp", ld["cs"][i][:], csd[sl, :], writes=[b_ld["cs"][i]])
                P.dma("sp", ld["sn"][i][:], snd[sl, :], writes=[b_ld["sn"][i]])
                for n in ("q", "k"):
                    x = ld[n][i]; a1 = t1[n][i]; a2 = t2[n][i]
                    P.op("dve", lambda e, x=x, a1=a1, i=i: e.tensor_mul(out=a1[:], in0=x[:], in1=ld["cs"][i][:]),
                         reads=[b_ld[n][i], b_ld["cs"][i]], writes=[b_t1[n][i]])
                    P.op("pool", lambda e, x=x, a2=a2, i=i: e.tensor_mul(out=v5(a2[:])[:, :, :, 0, :], in0=v5(x[:])[:, :, :, 1, :],
                                                                         in1=v5(ld["sn"][i][:])[:, :, :, 0, :]),
                         reads=[b_ld[n][i], b_ld["sn"][i]], writes=[b_t2[n][i]])
                    P.op("pool", lambda e, x=x, a2=a2, i=i: e.tensor_mul(out=v5(a2[:])[:, :, :, 1, :], in0=v5(x[:])[:, :, :, 0, :],
                                                                         in1=v5(ld["sn"][i][:])[:, :, :, 1, :]),
                         reads=[b_ld[n][i], b_ld["sn"][i]], writes=[b_t2[n][i]])
                    P.op("dve", lambda e, a1=a1, a2=a2, n=n, i=i: e.tensor_add(out=rb[n][i][:], in0=a1[:], in1=a2[:]),
                         reads=[b_t1[n][i], b_t2[n][i]], writes=[b_rb[n][i]])
                transpose_to(rb["q"][i][:], b_rb["q"][i], qT[:, sl], b_qT)
                transpose_to(rb["k"][i][:], b_rb["k"][i], kT[:, TC + j * 128:TC + (j + 1) * 128], b_kT)
                P.op("pool", lambda e, i=i, j=j: e.tensor_copy(out=vaug[:, NCT + j, 0:128], in_=ld["v"][i][:]), reads=[b_ld["v"][i]], writes=[b_va])
            for qb in range(NQB):
                qsl = slice(qb * 512, (qb + 1) * 512)
                for c in range(2):
                    cp = slice(c * 64, (c + 1) * 64)

                    def score(kt, cp=cp, qsl=qsl):
                        pi = kt % 2
                        P.op("pe", lambda e, kt=kt, pi=pi, cp=cp, qsl=qsl: e.matmul(ps_s[pi][:], lhsT=kT[cp, kt * 128:(kt + 1) * 128], rhs=qT[cp, qsl],
                                                                     start=True, stop=True), reads=[b_kT, b_qT], writes=[b_ps_s[pi]])
                    score(0)
                    for kt in range(NKT):
                        pi = kt % 2
                        if kt + 1 < NKT:
                            score(kt + 1)
                        P.op("act", lambda e, pi=pi: e.activation(out=E[pi][:], in_=ps_s[pi][:], func=AF.Exp, scale=0.125),
                             reads=[b_ps_s[pi]], writes=[b_E[pi]])
                        for qs in range(4):
                            P.op("pe", lambda e, pi=pi, qs=qs, kt=kt: e.matmul(acc[qs][:], lhsT=E[pi][:, qs * 128:(qs + 1) * 128], rhs=vaug[:, kt, :],
                                                                              start=(kt == 0), stop=(kt == NKT - 1)),
                                 reads=[b_E[pi], b_va], writes=[b_acc[qs]])
                    for qs in range(4):
                        r = c * 4 + qs
                        P.op("dve", lambda e, qs=qs, r=r: e.reciprocal(out=rec[:, r:r + 1], in_=acc[qs][:, 128:129]), reads=[b_acc[qs]], writes=[b_rec[r]])
                        if c == 0:
                            P.op("dve", lambda e, qs=qs, r=r: e.tensor_scalar_mul(out=att0[qs][:], in0=acc[qs][:, 0:128], scalar1=rec[:, r:r + 1]),
                                 reads=[b_acc[qs], b_rec[r]], writes=[b_att0[qs]])
                        else:
                            ai = qs % 2
                            P.op("dve", lambda e, qs=qs, r=r, ai=ai: e.tensor_scalar_mul(out=att1[ai][:], in0=acc[qs][:, 0:128], scalar1=rec[:, r:r + 1]),
                                 reads=[b_acc[qs], b_rec[r]], writes=[b_att1[ai]])
                            P.op("dve", lambda e, qs=qs, ai=ai: e.scalar_tensor_tensor(out=att1[ai][:], in0=att1[ai][:], scalar=neglam, in1=att0[qs][:],
                                                                                       op0=ALU.mult, op1=ALU.add),
                                 reads=[b_att1[ai], b_att0[qs], b_ls], writes=[b_att1[ai]])
                            P.op("pool", lambda e, ai=ai: e.memset(sst[:, ai:ai + 1], 0.0), writes=[b_sst[ai]])
                            P.op("act", lambda e, ai=ai: e.activation(out=junk[:], in_=att1[ai][:], func=AF.Square, accum_out=sst[:, ai:ai + 1]),
                                 reads=[b_att1[ai], b_sst[ai]], writes=[b_junk, b_sst[ai]])
                            P.op("dve", lambda e, ai=ai: e.tensor_scalar(out=sst[:, ai:ai + 1], in0=sst[:, ai:ai + 1], scalar1=1.0 / 128, scalar2=EPS,
                                                                         op0=ALU.mult, op1=ALU.add), reads=[b_sst[ai]], writes=[b_sst[ai]])
                            P.op("act", lambda e, ai=ai: e.activation(out=sst[:, ai:ai + 1], in_=sst[:, ai:ai + 1], func=AF.Sqrt), reads=[b_sst[ai]], writes=[b_sst[ai]])
                            P.op("dve", lambda e, ai=ai: e.reciprocal(out=sst[:, ai:ai + 1], in_=sst[:, ai:ai + 1]), reads=[b_sst[ai]], writes=[b_sst[ai]])
                            P.op("dve", lambda e, ai=ai: e.scalar_tensor_tensor(out=att1[ai][:], in0=att1[ai][:], scalar=sst[:, ai:ai + 1], in1=GS[:],
                                                                                op0=ALU.mult, op1=ALU.mult),
                                 reads=[b_att1[ai], b_sst[ai], b_gs], writes=[b_att1[ai]])
                            r0 = qb * 512 + qs * 128
                            outs.append(P.dma("sp", out[u, r0:r0 + 128, :], att1[ai][:], reads=[b_att1[ai]]))
        P.finish_wait("sp", outs)
        P.emit()
    return nc


def build_kf(TL=8192, TC=256):
    nc = new_nc()
    NL = TL // 128
    NCt = TC // 128
    GT = nc.dram_tensor("GT", [128, TL + TC], F32, kind="ExternalInput").ap()
    ccsc = nc.dram_tensor("ccsc", [128, 256], F32, kind="ExternalInput").ap()
    CTd = nc.dram_tensor("CT", [TL, TL], BF16, kind="ExternalInput").ap()
    STd = nc.dram_tensor("ST", [TL, TL], BF16, kind="ExternalInput").ap()
    CTc = nc.dram_tensor("CTc", [TC, TC], BF16, kind="ExternalInput").ap()
    STc = nc.dram_tensor("STc", [TC, TC], BF16, kind="ExternalInput").ap()
    out = nc.dram_tensor("out", [128, TL + TC], F32, kind="ExternalOutput").ap()
    with ExitStack() as st:
        P = Prog(nc, st)
        g32 = P.sb("g32", [128, TL + TC], F32); b_g32 = P.buf()
        P.dma("sp", g32[:], GT, writes=[b_g32])
        gbf = P.sb("gbf", [128, TL + TC], BF16); b_gbf = P.buf()
        P.op("dve", lambda e: e.tensor_copy(out=gbf[:], in_=g32[:]), reads=[b_g32], writes=[b_gbf])
        cc32 = P.sb("cc32", [128, 256], F32); ccb = P.sb("ccb", [128, 256], BF16); b_cc = P.buf()
        P.dma("sp", cc32[:], ccsc, writes=[b_cc])
        P.op("dve", lambda e: e.tensor_copy(out=ccb[:], in_=cc32[:]), reads=[b_cc], writes=[b_cc])
        H = P.sb("H", [128, NL + NCt, 256], BF16); b_H = P.buf()
        ph = [P.ps(f"ph{i}", [128, 256], F32) for i in range(2)]; b_ph = [P.buf() for _ in range(2)]
        for j in range(NL + NCt):
            pi = j % 2
            P.op("pe", lambda e, j=j, pi=pi: e.matmul(ph[pi][:], lhsT=gbf[:, j * 128:(j + 1) * 128], rhs=ccb[:], start=True, stop=True),
                 reads=[b_gbf, b_cc], writes=[b_ph[pi]])
            if pi == 0:
                P.op("act", lambda e, j=j, pi=pi: e.copy(out=H[:, j, :], in_=ph[pi][:]), reads=[b_ph[pi]], writes=[b_H])
            else:
                P.op("dve", lambda e, j=j, pi=pi: e.tensor_copy(out=H[:, j, :], in_=ph[pi][:]), reads=[b_ph[pi]], writes=[b_H])
        JB = 16
        cbuf = [P.sb(f"cbuf{i}", [128, JB, 512], BF16) for i in range(2)]; b_cbuf = [P.buf() for _ in range(2)]
        sbuf_ = [P.sb(f"sbuf{i}", [128, JB, 512], BF16) for i in range(2)]; b_sbuf = [P.buf() for _ in range(2)]
        po = [P.ps(f"po{i}", [128, 512], F32) for i in range(2)]; b_po = [P.buf() for _ in range(2)]
        ot = [P.sb(f"ot{i}", [128, 512], F32) for i in range(2)]; b_ot = [P.buf() for _ in range(2)]
        CTv = CTd.rearrange("(j p) f -> p j f", p=128)
        STv = STd.rearrange("(j p) f -> p j f", p=128)
        outs = []
        nb = 0
        for fb in range(TL // 512):
            fsl = slice(fb * 512, (fb + 1) * 512)
            pi = fb % 2
            for j0 in range(0, NL, JB):
                bi = nb % 2; nb += 1
                P.dma("sp", cbuf[bi][:], CTv[:, j0:j0 + JB, fsl], writes=[b_cbuf[bi]])
                P.dma("pool", sbuf_[bi][:], STv[:, j0:j0 + JB, fsl], writes=[b_sbuf[bi]])
                for jj in range(JB):
                    j = j0 + jj
                    P.op("pe", lambda e, j=j, jj=jj, bi=bi, pi=pi: e.matmul(po[pi][:], lhsT=H[:, j, 0:128], rhs=cbuf[bi][:, jj, :],
                                                                         start=(j == 0), stop=False), reads=[b_H, b_cbuf[bi]], writes=[b_po[pi]])
                    P.op("pe", lambda e, j=j, jj=jj, bi=bi, pi=pi: e.matmul(po[pi][:], lhsT=H[:, j, 128:256], rhs=sbuf_[bi][:, jj, :],
                                                                         start=False, stop=(j == NL - 1)), reads=[b_H, b_sbuf[bi]], writes=[b_po[pi]])
            P.op("act", lambda e, pi=pi: e.copy(out=ot[pi][:], in_=po[pi][:]), reads=[b_po[pi]], writes=[b_ot[pi]])
            outs.append(P.dma("sp", out[:, fsl], ot[pi][:], reads=[b_ot[pi]]))
        cc_ = P.sb("cc_", [128, NCt, TC], BF16); sc_ = P.sb("sc_", [128, NCt, TC], BF16); b_c2 = P.buf()
        P.dma("sp", cc_[:], CTc.rearrange("(j p) f -> p j f", p=128), writes=[b_c2])
        P.dma("sp", sc_[:], STc.rearrange("(j p) f -> p j f", p=128), writes=[b_c2])
        pi = 0
        for j in range(NCt):
            P.op("pe", lambda e, j=j: e.matmul(po[pi][:, 0:TC], lhsT=H[:, NL + j, 0:128], rhs=cc_[:, j, :], start=(j == 0), stop=False),
                 reads=[b_H, b_c2], writes=[b_po[pi]])
            P.op("pe", lambda e, j=j: e.matmul(po[pi][:, 0:TC], lhsT=H[:, NL + j, 128:256], rhs=sc_[:, j, :], start=False, stop=(j == NCt - 1)),
                 reads=[b_H, b_c2], writes=[b_po[pi]])
        P.op("act", lambda e: e.copy(out=ot[pi][:, 0:TC], in_=po[pi][:, 0:TC]), reads=[b_po[pi]], writes=[b_ot[pi]])
        outs.append(P.dma("sp", out[:, TL:TL + TC], ot[pi][:, 0:TC], reads=[b_ot[pi]]))
        P.finish_wait("sp", outs)
        P.emit()
    return nc


def build_ssd(NCH=66, NCTX=2):
    nc = new_nc()
    T = NCH * 128
    NC6 = NCH * 6
    xs = nc.dram_tensor("xs", [T, 384], F32, kind="ExternalInput").ap()
    Btm = nc.dram_tensor("Btm", [T, 128], F32, kind="ExternalInput").ap()
    BfT = nc.dram_tensor("BfT", [128, T], F32, kind="ExternalInput").ap()
    CfT = nc.dram_tensor("CfT", [128, T], F32, kind="ExternalInput").ap()
    dtd = nc.dram_tensor("dt", [128, 2, NCH, 6], F32, kind="ExternalInput").ap()
    prm = nc.dram_tensor("prm", [128, 3, 2, NCH, 6], F32, kind="ExternalInput").ap()
    trid = nc.dram_tensor("tri", [128, 3, 128], F32, kind="ExternalInput").ap()
    y = nc.dram_tensor("y", [2, T, 384], F32, kind="ExternalOutput").ap()
    with ExitStack() as st:
        P = Prog(nc, st)
        tri = P.sb("tri", [128, 3, 128], F32); b_tri = P.buf()
        P.dma("sp", tri[:], trid, writes=[b_tri])
        prs = P.sb("prs", [128, 3, 2, NCH, 6], F32); b_prs = P.buf()
        P.dma("sp", prs[:], prm, writes=[b_prs])
        dts = P.sb("dts", [128, 2, NCH, 6], F32); b_dts = P.buf()
        P.dma("sp", dts[:], dtd, writes=[b_dts])
        stg = P.sb("stg", [128, T], F32); b_stg = P.buf()
        Bf = P.sb("Bf", [128, T], BF16); Cf = P.sb("Cf", [128, T], BF16); Bt = P.sb("Bt", [128, NCH, 128], BF16)
        b_Bf, b_Cf, b_Bt = P.buf(), P.buf(), P.buf()
        P.dma("sp", stg[:], BfT, writes=[b_stg])
        P.op("dve", lambda e: e.tensor_copy(out=Bf[:], in_=stg[:]), reads=[b_stg], writes=[b_Bf])
        P.dma("sp", stg[:], CfT, writes=[b_stg])
        P.op("pool", lambda e: e.tensor_copy(out=Cf[:], in_=stg[:]), reads=[b_stg], writes=[b_Cf])
        P.dma("sp", stg[:].rearrange("p (c n) -> p c n", n=128), Btm.rearrange("(c p) n -> p c n", p=128), writes=[b_stg])
        P.op("dve", lambda e: e.tensor_copy(out=Bt[:], in_=stg[:].rearrange("p (c n) -> p c n", n=128)), reads=[b_stg], writes=[b_Bt])
        dtsp = P.sb("dtsp", [128, 2, NCH, 6], F32); dta = P.sb("dta", [128, 2, NCH, 6], F32); b_dt = P.buf()
        P.op("dve", lambda e: e.tensor_add(out=dtsp[:], in0=dts[:], in1=prs[:, 0]), reads=[b_dts, b_prs], writes=[b_dt])
        P.op("act", lambda e: e.activation(out=dtsp[:], in_=dtsp[:], func=AF.Exp), reads=[b_dt], writes=[b_dt])
        P.op("dve", lambda e: e.tensor_scalar_add(out=dtsp[:], in0=dtsp[:], scalar1=1.0), reads=[b_dt], writes=[b_dt])
        P.op("act", lambda e: e.activation(out=dtsp[:], in_=dtsp[:], func=AF.Ln), reads=[b_dt], writes=[b_dt])
        P.op("act", lambda e: e.activation(out=dta[:], in_=prs[:, 1], func=AF.Exp), reads=[b_prs, b_dt], writes=[b_dt])
        P.op("dve", lambda e: e.scalar_tensor_tensor(out=dta[:], in0=dta[:], scalar=-1.0, in1=dtsp[:], op0=ALU.mult, op1=ALU.mult),
             reads=[b_dt], writes=[b_dt])
        acs = P.sb("acs", [128, 2, NCH, 6], F32); eacs = P.sb("eacs", [128, 2, NCH, 6], F32)
        tend = P.sb("tend", [128, 2, NCH, 6], F32); cdec = P.sb("cdec", [128, 2, NCH, 6], F32); b_ac = P.buf()
        pb = [P.ps(f"pb{i}", [128, NC6], F32) for i in range(2)]; b_pb = [P.buf() for _ in range(2)]
        for d in range(2):
            P.op("pe", lambda e, d=d: e.matmul(pb[0][:], lhsT=tri[:, d, :], rhs=dta[:, d].rearrange("p c h -> p (c h)"), start=True, stop=True),
                 reads=[b_tri, b_dt], writes=[b_pb[0]])
            P.op("pe", lambda e, d=d: e.matmul(pb[1][:], lhsT=tri[:, 2, :], rhs=dta[:, d].rearrange("p c h -> p (c h)"), start=True, stop=True),
                 reads=[b_tri, b_dt], writes=[b_pb[1]])
            av = lambda t, d=d: t[:, d].rearrange("p c h -> p (c h)")
            P.op("dve", lambda e, d=d, av=av: e.tensor_copy(out=av(acs), in_=pb[0][:]), reads=[b_pb[0]], writes=[b_ac])
            P.op("act", lambda e, d=d, av=av: e.activation(out=av(eacs), in_=pb[0][:], func=AF.Exp), reads=[b_pb[0]], writes=[b_ac])
            P.op("act", lambda e, d=d, av=av: e.activation(out=av(cdec), in_=pb[1][:], func=AF.Exp), reads=[b_pb[1]], writes=[b_ac])
            P.op("dve", lambda e, d=d, av=av: e.tensor_sub(out=av(tend), in0=pb[1][:], in1=av(acs)), reads=[b_pb[1], b_ac], writes=[b_ac])
            P.op("act", lambda e, d=d, av=av: e.activation(out=av(tend), in_=av(tend), func=AF.Exp), reads=[b_ac], writes=[b_ac])
        xt = [P.sb(f"xt{i}", [128, 6, 64], F32) for i in range(2)]; b_xt = [P.buf() for _ in range(2)]
        Dm = P.sb("Dm", [128, 6, 128], F32); b_Dm = P.buf()
        pR = [P.ps(f"pR{i}", [128, 3, 128], F32) for i in range(2)]; b_pR = [P.buf() for _ in range(2)]
        pcb = P.ps("pcb", [128, 128], F32); b_pcb = P.buf()
        cbU = P.sb("cbU", [128, 128], F32); b_cbU = P.buf()
        arg = [P.sb(f"arg{i}", [128, 128], F32) for i in range(2)]; b_arg = [P.buf() for _ in range(2)]
        Wh = P.sb("Wh", [128, 6, 128], BF16); b_Wh = P.buf()
        xdt = P.sb("xdt", [128, 6, 64], BF16); b_xdt = P.buf()
        xdtE = P.sb("xdtE", [128, 6, 64], BF16); b_xdtE = P.buf()
        py = P.ps("py", [128, 6, 64], F32); b_py = P.buf()
        pyo = P.ps("pyo", [128, 6, 64], F32); b_pyo = P.buf()
        pst = P.ps("pst", [128, 6, 64], F32); b_pst = P.buf()
        t1 = P.sb("t1", [128, 6, 64], F32); b_t1 = P.buf()
        yo = [P.sb(f"yo{i}", [128, 6, 64], F32) for i in range(2)]; b_yo = [P.buf() for _ in range(2)]
        S = P.sb("S", [128, 6, 64], F32); Sb = P.sb("Sb", [128, 6, 64], BF16); b_S = P.buf(); b_Sb = P.buf()
        outs = []
        n = 0
        for d in range(2):
            ctx_order = list(range(NCTX)) if d == 0 else list(range(NCTX - 1, -1, -1))
            lat_order = list(range(NCTX, NCH)) if d == 0 else list(range(NCH - 1, NCTX - 1, -1))
            P.op("pool", lambda e: e.memset(S[:], 0.0), writes=[b_S])
            P.op("pool", lambda e: e.memset(Sb[:], 0.0), writes=[b_Sb])
            for c in ctx_order + lat_order:
                i = n % 2; n += 1
                csl = slice(c * 128, (c + 1) * 128)
                P.dma("sp", xt[i][:].rearrange("p h e -> p (h e)"), xs[csl, :], writes=[b_xt[i]])
                for h in range(6):
                    eng = "dve" if h % 2 == 0 else "pool"
                    P.op(eng, lambda e, h=h, c=c, d=d: e.tensor_scalar_mul(out=Dm[:, h, :], in0=tri[:, d, :], scalar1=dta[:, d, c, h:h + 1]),
                         reads=[b_tri, b_dt], writes=[b_Dm])
                for hh in range(2):
                    P.op("pe", lambda e, hh=hh: e.matmul(pR[hh][:].rearrange("p h l -> p (h l)"), lhsT=tri[:, 2, :],
                                                         rhs=Dm[:, hh * 3:(hh + 1) * 3, :].rearrange("p h l -> p (h l)"), start=True, stop=True),
                         reads=[b_tri, b_Dm], writes=[b_pR[hh]])
                P.op("pe", lambda e, csl=csl: e.matmul(pcb[:], lhsT=Bf[:, csl], rhs=Cf[:, csl], start=True, stop=True),
                     reads=[b_Bf, b_Cf], writes=[b_pcb])
                P.op("dve", lambda e, d=d: e.tensor_mul(out=cbU[:], in0=pcb[:], in1=tri[:, d, :]), reads=[b_pcb, b_tri], writes=[b_cbU])
                for h in range(6):
                    ai = h % 2
                    P.op("dve", lambda e, h=h, c=c, d=d, ai=ai: e.tensor_scalar(out=arg[ai][:], in0=pR[h // 3][:, h % 3, :], scalar1=acs[:, d, c, h:h + 1],
                                                                                scalar2=0.0, op0=ALU.subtract, op1=ALU.min),
                         reads=[b_pR[h // 3], b_ac], writes=[b_arg[ai]])
                    P.op("act", lambda e, ai=ai: e.activation(out=arg[ai][:], in_=arg[ai][:], func=AF.Exp), reads=[b_arg[ai]], writes=[b_arg[ai]])
                    P.op("pool", lambda e, h=h, ai=ai: e.tensor_mul(out=Wh[:, h, :], in0=arg[ai][:], in1=cbU[:]), reads=[b_arg[ai], b_cbU], writes=[b_Wh])
                    P.op("pool", lambda e, h=h, c=c, d=d, i=i: e.tensor_scalar_mul(out=xdt[:, h, :], in0=xt[i][:, h, :], scalar1=dtsp[:, d, c, h:h + 1]),
                         reads=[b_xt[i], b_dt], writes=[b_xdt])
                    P.op("pool", lambda e, h=h, c=c, d=d: e.tensor_scalar_mul(out=xdtE[:, h, :], in0=xdt[:, h, :], scalar1=tend[:, d, c, h:h + 1]),
                         reads=[b_xdt, b_ac], writes=[b_xdtE])
                for h in range(6):
                    P.op("pe", lambda e, h=h: e.matmul(py[:, h, :], lhsT=Wh[:, h, :], rhs=xdt[:, h, :], start=True, stop=True),
                         reads=[b_Wh, b_xdt], writes=[b_py])
                P.op("pe", lambda e, csl=csl: e.matmul(pyo[:].rearrange("p h e -> p (h e)"), lhsT=Cf[:, csl], rhs=Sb[:].rearrange("p h e -> p (h e)"),
                                                       start=True, stop=True), reads=[b_Cf, b_Sb], writes=[b_pyo])
                for h in range(6):
                    P.op("dve", lambda e, h=h, c=c, d=d: e.tensor_scalar_mul(out=t1[:, h, :], in0=pyo[:, h, :], scalar1=eacs[:, d, c, h:h + 1]),
                         reads=[b_pyo, b_ac], writes=[b_t1])
                    P.op("dve", lambda e, h=h, c=c, d=d, i=i: e.scalar_tensor_tensor(out=t1[:, h, :], in0=xt[i][:, h, :], scalar=prs[:, 2, d, c, h:h + 1],
                                                                                     in1=t1[:, h, :], op0=ALU.mult, op1=ALU.add),
                         reads=[b_xt[i], b_prs, b_t1], writes=[b_t1])
                P.op("dve", lambda e, i=i: e.tensor_add(out=yo[i][:], in0=t1[:], in1=py[:]), reads=[b_t1, b_py], writes=[b_yo[i]])
                outs.append(P.dma("sp", y[d, csl, :], yo[i][:].rearrange("p h e -> p (h e)"), reads=[b_yo[i]]))
                P.op("pe", lambda e, c=c: e.matmul(pst[:].rearrange("p h e -> p (h e)"), lhsT=Bt[:, c, :], rhs=xdtE[:].rearrange("p h e -> p (h e)"),
                                                   start=True, stop=True), reads=[b_Bt, b_xdtE], writes=[b_pst])
                for h in range(6):
                    P.op("dve", lambda e, h=h, c=c, d=d: e.scalar_tensor_tensor(out=S[:, h, :], in0=S[:, h, :], scalar=cdec[:, d, c, h:h + 1],
                                                                                in1=pst[:, h, :], op0=ALU.mult, op1=ALU.add),
                         reads=[b_S, b_ac, b_pst], writes=[b_S])
                P.op("act", lambda e: e.copy(out=Sb[:], in_=S[:]), reads=[b_S], writes=[b_Sb])
        P.finish_wait("sp", outs)
        P.emit()
    return nc


def build_k2c(NT):
    nc = new_nc()
    W = 1536
    y2 = nc.dram_tensor("y2", [2, NT * 128, W], F32, kind="ExternalInput").ap()
    z = nc.dram_tensor("z", [NT * 128, W], F32, kind="ExternalInput").ap()
    gnR = nc.dram_tensor("gnR", [128, W], F32, kind="ExternalInput").ap()
    out = nc.dram_tensor("out", [NT * 128, W], F32, kind="ExternalOutput").ap()
    with ExitStack() as st:
        P = Prog(nc, st)
        gn = P.sb("gn", [128, W], F32); b_gn = P.buf()
        P.dma("sp", gn[:], gnR, writes=[b_gn])
        ss = P.sb("ss", [128, NT, 4], F32); b_ss0 = P.buf(); b_ss = [P.buf() for _ in range(NT)]
        P.op("pool", lambda e: e.memset(ss[:], 0.0), writes=[b_ss0])
        junk = P.sb("junk", [128, 384], F32); b_junk = P.buf()
        y0 = [P.sb(f"y0_{i}", [128, W], F32) for i in range(2)]; b_y0 = [P.buf() for _ in range(2)]
        y1 = [P.sb(f"y1_{i}", [128, W], F32) for i in range(2)]; b_y1 = [P.buf() for _ in range(2)]
        zt = [P.sb(f"zt{i}", [128, W], F32) for i in range(2)]; b_zt = [P.buf() for _ in range(2)]
        ot = [P.sb(f"ot{i}", [128, W], F32) for i in range(2)]; b_ot = [P.buf() for _ in range(2)]
        outs = []
        for t in range(NT):
            i = t % 2
            rsl = slice(t * 128, (t + 1) * 128)
            P.dma("sp", y0[i][:], y2[0, rsl, :], writes=[b_y0[i]])
            P.dma("sp", y1[i][:], y2[1, rsl, :], writes=[b_y1[i]])
            P.dma("sp", zt[i][:], z[rsl, :], writes=[b_zt[i]])
            P.op("act", lambda e, i=i: e.activation(out=zt[i][:], in_=zt[i][:], func=AF.Silu), reads=[b_zt[i]], writes=[b_zt[i]])
            P.op("pool", lambda e, i=i: e.tensor_add(out=y0[i][:], in0=y0[i][:], in1=y1[i][:]), reads=[b_y0[i], b_y1[i]], writes=[b_y0[i]])
            P.op("dve", lambda e, i=i: e.tensor_mul(out=y0[i][:], in0=y0[i][:], in1=zt[i][:]), reads=[b_y0[i], b_zt[i]], writes=[b_y0[i]])
            for g in range(4):
                P.op("act", lambda e, i=i, g=g, t=t: e.activation(out=junk[:], in_=y0[i][:, g * 384:(g + 1) * 384], func=AF.Square,
                                                                  accum_out=ss[:, t, g:g + 1]),
                     reads=[b_y0[i], b_ss0], writes=[b_junk, b_ss[t]])
            P.op("dve", lambda e, t=t: e.tensor_scalar(out=ss[:, t, :], in0=ss[:, t, :], scalar1=1.0 / 384, scalar2=EPS, op0=ALU.mult, op1=ALU.add),
                 reads=[b_ss[t]], writes=[b_ss[t]])
            P.op("act", lambda e, t=t: e.activation(out=ss[:, t, :], in_=ss[:, t, :], func=AF.Sqrt), reads=[b_ss[t]], writes=[b_ss[t]])
            P.op("dve", lambda e, t=t: e.reciprocal(out=ss[:, t, :], in_=ss[:, t, :]), reads=[b_ss[t]], writes=[b_ss[t]])
            for g in range(4):
                gs = slice(g * 384, (g + 1) * 384)
                P.op("dve", lambda e, i=i, g=g, gs=gs, t=t: e.scalar_tensor_tensor(out=ot[i][:, gs], in0=y0[i][:, gs], scalar=ss[:, t, g:g + 1], in1=gn[:, gs],
                                                                                   op0=ALU.mult, op1=ALU.mult),
                     reads=[b_y0[i], b_ss[t], b_gn], writes=[b_ot[i]])
            outs.append(P.dma("sp", out[rsl, :], ot[i][:], reads=[b_ot[i]]))
        P.finish_wait("sp", outs)
        P.emit()
    return nc


def build_k0():
    nc = new_nc()
    NCOL = 768
    cT = nc.dram_tensor("cT", [128, 8, 3], F32, kind="ExternalInput").ap()
    mw = nc.dram_tensor("mw", [2, 1024, NCOL], F32, kind="ExternalInput").ap()
    mb = nc.dram_tensor("mb", [1, 2, NCOL], F32, kind="ExternalInput").ap()
    out = nc.dram_tensor("out", [3, 2, NCOL], F32, kind="ExternalOutput").ap()
    with ExitStack() as st:
        P = Prog(nc, st)
        ct_sb = P.sb("ct_sb", [128, 8, 3], F32)
        sc_sb = P.sb("sc_sb", [128, 8, 3], F32)
        w_sb = P.sb("w_sb", [128, 2, 8, NCOL], F32)
        b_sb = P.sb("b_sb", [1, 2, NCOL], F32)
        ones = P.sb("ones", [1, 4], F32)
        o_sb = P.sb("o_sb", [3, 2, NCOL], F32)
        ps = [P.ps(f"ps{i}", [3, 512], F32) for i in range(4)]
        b_ct, b_sc, b_w, b_b, b_ones, b_o = [P.buf() for _ in range(6)]
        b_ps = [P.buf() for _ in range(4)]

        P.dma("sp", ct_sb[:], cT, writes=[b_ct])
        P.dma("sp", w_sb[:], mw.rearrange("l (k p) n -> p l k n", p=128), writes=[b_w])
        P.dma("sp", b_sb[:], mb, writes=[b_b])
        P.op("dve", lambda e: e.memset(ones[:], 1.0), writes=[b_ones])
        P.op("act", lambda e: e.activation(out=sc_sb[:], in_=ct_sb[:], func=AF.Silu), reads=[b_ct], writes=[b_sc])
        pi = 0
        for l in range(2):
            for (c0, cn) in ((0, 512), (512, 256)):
                pt = ps[pi]; bp = b_ps[pi]; pi += 1
                for k in range(8):
                    P.op("pe", lambda e, pt=pt, k=k, l=l, c0=c0, cn=cn: e.matmul(
                        pt[:, 0:cn], lhsT=sc_sb[:, k, :], rhs=w_sb[:, l, k, c0:c0 + cn], start=(k == 0), stop=False),
                        reads=[b_sc, b_w], writes=[bp])
                P.op("pe", lambda e, pt=pt, l=l, c0=c0, cn=cn: e.matmul(
                    pt[:, 0:cn], lhsT=ones[:, 0:3], rhs=b_sb[:, l, c0:c0 + cn], start=False, stop=True),
                    reads=[b_ones, b_b], writes=[bp])
                P.op("dve", lambda e, pt=pt, l=l, c0=c0, cn=cn: e.tensor_copy(out=o_sb[:, l, c0:c0 + cn], in_=pt[:, 0:cn]),
                     reads=[bp], writes=[b_o])
        t = P.dma("sp", out, o_sb[:], reads=[b_o])
        P.finish_wait("sp", [t])
        P.emit()
    return nc


import math

G4 = [[0, 1, 2, 3], [4, 5, 6, 7]]
TL, TC = 8192, 256
TA = TL + TC
NTA = TA // 128
FMW = TA + 8
LAT0 = 262
I32 = mybir.dt.int32


def din(nc, name, shape, dt=F32):
    return nc.dram_tensor(name, list(shape), dt, kind="ExternalInput").ap()


def dscr(nc, name, shape, dt=F32):
    return nc.dram_tensor(name, list(shape), dt).ap()


def phase_mods(P, cT2, mw, mb, selc, modT_d, gate_d):
    P.push_scope()
    sc = P.sb("sc", [128, 8, 2], F32); b_sc = P.buf()
    P.dma("sp", sc[:], cT2, writes=[b_sc])
    P.op("act", lambda e: e.activation(out=sc[:], in_=sc[:], func=AF.Silu), reads=[b_sc], writes=[b_sc])
    sel = P.sb("sel", [2, 2 + 256], F32); b_sel = P.buf()
    P.dma("sp", sel[:], selc, writes=[b_sel])
    ones = P.sb("ones", [1, 2], F32); b_ones = P.buf()
    P.op("dve", lambda e: e.memset(ones[:], 1.0), writes=[b_ones])
    bsb = P.sb("bsb", [1, 2, 6144], F32); b_b = P.buf()
    P.dma("sp", bsb[:], mb, writes=[b_b])
    wb = [P.sb(f"wb{i}", [128, 8, 1024], F32) for i in range(2)]; b_wb = [P.buf() for _ in range(2)]
    row = [P.sb(f"row{i}", [2, 1024], F32) for i in range(2)]; b_row = [P.buf() for _ in range(2)]
    prow = [P.ps(f"prow{i}", [2, 512], F32) for i in range(2)]; b_prow = [P.buf() for _ in range(2)]
    pT = P.ps("pT", [128, 2, 8], F32); b_pT = P.buf()
    prep = [P.ps(f"prep{i}", [128, 512], F32) for i in range(2)]; b_prep = [P.buf() for _ in range(2)]
    modT = P.sb("modT", [128, 2, 2, 6, 8], F32); b_modT = P.buf()
    rep = [P.sb(f"rep{i}", [128, 1024], F32) for i in range(2)]; b_rep = [P.buf() for _ in range(2)]
    n = 0
    nr = 0
    outs = []
    for l in range(2):
        wv = mw[l].rearrange("(k p) n -> p k n", p=128)
        for jb in range(6):
            i = n % 2; n += 1
            P.dma("sp", wb[i][:], wv[:, :, jb * 1024:(jb + 1) * 1024], writes=[b_wb[i]])
            for half in range(2):
                c0 = half * 512
                for k in range(8):
                    P.op("pe", lambda e, i=i, k=k, c0=c0, half=half: e.matmul(prow[half][:], lhsT=sc[:, k, :], rhs=wb[i][:, k, c0:c0 + 512],
                                                                               start=(k == 0), stop=False),
                         reads=[b_sc, b_wb[i]], writes=[b_prow[half]])
                P.op("pe", lambda e, l=l, jb=jb, c0=c0, half=half: e.matmul(prow[half][:], lhsT=ones[:, 0:2], rhs=bsb[:, l, jb * 1024 + c0:jb * 1024 + c0 + 512],
                                                                           start=False, stop=True),
                     reads=[b_ones, b_b], writes=[b_prow[half]])
                P.op("dve", lambda e, i=i, c0=c0, half=half: e.tensor_copy(out=row[i][:, c0:c0 + 512], in_=prow[half][:]),
                     reads=[b_prow[half]], writes=[b_row[i]])
            for cls in range(2):
                for k in range(8):
                    P.op("pe", lambda e, i=i, cls=cls, k=k: e.matmul(pT[:, cls, k:k + 1], lhsT=row[i][:, k * 128:(k + 1) * 128], rhs=sel[:, cls:cls + 1],
                                                                     start=True, stop=True), reads=[b_row[i], b_sel], writes=[b_pT])
            P.op("dve", lambda e, l=l, jb=jb: e.tensor_copy(out=modT[:, l, :, jb, :], in_=pT[:]), reads=[b_pT], writes=[b_modT])
            if jb in (2, 5):
                for cls in range(2):
                    ri = nr % 2; nr += 1
                    for half in range(2):
                        c0 = half * 512
                        P.op("pe", lambda e, i=i, cls=cls, c0=c0, half=half: e.matmul(prep[half][:], lhsT=sel[:, 2 + cls * 128:2 + (cls + 1) * 128],
                                                                                    rhs=row[i][:, c0:c0 + 512], start=True, stop=True),
                             reads=[b_row[i], b_sel], writes=[b_prep[half]])
                        P.op("act", lambda e, ri=ri, c0=c0, half=half: e.copy(out=rep[ri][:, c0:c0 + 512], in_=prep[half][:]),
                             reads=[b_prep[half]], writes=[b_rep[ri]])
                    outs.append(P.dma("sp", gate_d[l, cls, 0 if jb == 2 else 1], rep[ri][:], reads=[b_rep[ri]]))
    for l in range(2):
        outs.append(P.dma("sp", modT_d[l], modT[:, l], reads=[b_modT]))
    P.barrier()
    P.pop_scope()


def load_mod(P, modT_dl, gT_dram, j_shift, j_scale):
    modsb = P.sb("modsb", [128, 2, 6, 8], F32); b_mod = P.buf()
    P.dma("sp", modsb[:], modT_dl, writes=[b_mod])
    gsb = P.sb("gsb", [128, 8], F32); b_g = P.buf()
    P.dma("sp", gsb[:], gT_dram, writes=[b_g])
    Gs = P.sb("Gs", [128, 2, 8], F32); Sh = P.sb("Sh", [128, 2, 8], F32); b_gs = P.buf()
    for cls in range(2):
        P.op("dve", lambda e, cls=cls: e.scalar_tensor_tensor(out=Gs[:, cls, :], in0=modsb[:, cls, j_scale, :], scalar=1.0,
                                                               in1=gsb[:], op0=ALU.add, op1=ALU.mult), reads=[b_mod, b_g], writes=[b_gs])
        P.op("dve", lambda e, cls=cls: e.tensor_copy(out=Sh[:, cls, :], in_=modsb[:, cls, j_shift, :]), reads=[b_mod], writes=[b_gs])
    return Gs, Sh, b_gs


def load_ident(P, identd, dt, name="ident"):
    t = P.sb(name, [128, 128], dt); b = P.buf()
    P.dma("sp", t[:], identd, writes=[b])
    return t, b


def phase_inproj(P, tile_srcs, tile_cls, groups, w, NFM, NTMC, modT_dl, gT, identd, fm_dst, tm_dst, fm_groups=None):
    P.push_scope()
    NW = NFM * 128 + NTMC
    ident, b_ident = load_ident(P, identd, BF16)
    Gs, Sh, b_gs = load_mod(P, modT_dl, gT, 0, 1)
    w_sb = P.sb("w_sb", [128, 8, NW], BF16); b_w = P.buf()
    stage = [P.sb(f"stage{i}", [128, NW], F32) for i in range(2)]; b_stage = [P.buf() for _ in range(2)]
    load_weight_bf16(P, w, w_sb, b_w, 8, NW, stage, b_stage)
    NT = len(tile_srcs)
    nt = NormT(P, "n_", ident, b_ident, NT)
    xt = [P.sb(f"xt{i}", [128, 1024], F32) for i in range(2)]; b_xt = [P.buf() for _ in range(2)]
    aT = [P.sb(f"aT{i}", [128, 8, 512], BF16) for i in range(2)]; b_aT = [P.buf() for _ in range(2)]
    tmo = [P.sb(f"tmo{i}", [128, NTMC], F32) for i in range(2)]; b_tmo = [P.buf() for _ in range(2)]
    fmo = [P.sb(f"fmo{i}", [128, 512], F32) for i in range(2)]; b_fmo = [P.buf() for _ in range(2)]
    ptm = [P.ps(f"ptm{i}", [128, 512], F32) for i in range(2)]; b_ptm = [P.buf() for _ in range(2)]
    pfm = [P.ps(f"pfm{i}", [128, 512], F32) for i in range(2)]; b_pfm = [P.buf() for _ in range(2)]
    nx = 0; ntm = 0; nfm = 0; no = 0
    tmblocks = [(c0, min(512, NTMC - c0)) for c0 in range(0, NTMC, 512)]
    for gi, tiles in enumerate(groups):
        ai = gi % 2
        N = len(tiles) * 128
        for ti, t in enumerate(tiles):
            i = nx % 2; nx += 1
            for (psl, src) in tile_srcs[t]:
                P.dma("sp", xt[i][psl, :], src, writes=[b_xt[i]])
            nt.run(xt[i][:], b_xt[i], t, aT[ai][:, :, ti * 128:(ti + 1) * 128], b_aT[ai], Gs, Sh, b_gs, tile_cls[t])
            oi = no % 2; no += 1
            for (c0, cn) in tmblocks:
                pi = ntm % 2; ntm += 1
                for k in range(8):
                    P.op("pe", lambda e, pi=pi, k=k, c0=c0, cn=cn, ai=ai, ti=ti: e.matmul(
                        ptm[pi][:, 0:cn], lhsT=aT[ai][:, k, ti * 128:(ti + 1) * 128], rhs=w_sb[:, k, NFM * 128 + c0:NFM * 128 + c0 + cn],
                        start=(k == 0), stop=(k == 7)), reads=[b_aT[ai], b_w], writes=[b_ptm[pi]])
                P.op("act", lambda e, pi=pi, c0=c0, cn=cn, oi=oi: e.copy(out=tmo[oi][:, c0:c0 + cn], in_=ptm[pi][:, 0:cn]),
                     reads=[b_ptm[pi]], djw=[b_tmo[oi]])
            P.dma("sp", tm_dst(t), tmo[oi][:], reads=[b_tmo[oi]])
        if fm_groups is not None and gi not in fm_groups:
            continue
        for c6 in range(NFM):
            pi = nfm % 2; nfm += 1
            for k in range(8):
                P.op("pe", lambda e, pi=pi, k=k, c6=c6, ai=ai, N=N: e.matmul(
                    pfm[pi][:, 0:N], lhsT=w_sb[:, k, c6 * 128:(c6 + 1) * 128], rhs=aT[ai][:, k, 0:N], start=(k == 0), stop=(k == 7)),
                    reads=[b_aT[ai], b_w], writes=[b_pfm[pi]])
            P.op("dve", lambda e, pi=pi, N=N: e.tensor_copy(out=fmo[pi][:, 0:N], in_=pfm[pi][:, 0:N]), reads=[b_pfm[pi]], writes=[b_fmo[pi]])
            P.dma("sp", fm_dst(c6, gi), fmo[pi][:, 0:N], reads=[b_fmo[pi]])
    P.barrier()
    P.pop_scope()


def phase_conv0(P, FM0, cw, cb, XBC):
    P.push_scope()
    K = 5
    wsb = P.sb("wsb", [128, 5, K], F32); bsb = P.sb("bsb", [128, 5], F32); b_w = P.buf()
    P.dma("sp", wsb[:], cw, writes=[b_w]); P.dma("sp", bsb[:], cb, writes=[b_w])
    zero = P.sb("zero", [128, 4], F32); b_z = P.buf()
    P.op("pool", lambda e: e.memset(zero[:], 0.0), writes=[b_z])
    vin = [P.sb(f"vin{i}", [128, FMW], F32) for i in range(2)]; b_vin = [P.buf() for _ in range(2)]
    acc = [P.sb(f"acc{i}", [128, 2048], F32) for i in range(2)]; b_acc = [P.buf() for _ in range(2)]
    res = [P.sb(f"res{i}", [128, 2048], F32) for i in range(2)]; b_res = [P.buf() for _ in range(2)]
    n = 0
    for j in range(5):
        vi = j % 2
        rows = slice(128 + j * 128, 128 + (j + 1) * 128)
        P.dma("sp", vin[vi][:, LAT0:LAT0 + TL], FM0[rows, LAT0:LAT0 + TL], writes=[b_vin[vi]])
        P.dma("sp", vin[vi][:, 2:2 + TC], FM0[rows, 2:2 + TC], djw=[b_vin[vi]])
        for (c0, cn) in ((0, 2), (258, 4), (FMW - 2, 2)):
            P.op("pool", lambda e, vi=vi, c0=c0, cn=cn: e.memset(vin[vi][:, c0:c0 + cn], 0.0), djw=[b_vin[vi]])
        blocks = [(0, TC, 0)] + [(260 + t0, 2048, TC + t0) for t0 in range(0, TL, 2048)]
        for (i0, T, o0) in blocks:
            i = n % 2; n += 1
            conv_fm(P, "dve", vin[vi], b_vin[vi], wsb, j, bsb[:, j:j + 1], b_w, acc[i][:, 0:T], b_acc[i], K, T, t0=i0)
            P.op("act", lambda e, i=i, T=T: e.activation(out=res[i][:, 0:T], in_=acc[i][:, 0:T], func=AF.Silu), reads=[b_acc[i]], writes=[b_res[i]])
            P.dma("sp", XBC[j * 128:(j + 1) * 128, o0:o0 + T], res[i][:, 0:T], reads=[b_res[i]])
    P.barrier()
    P.pop_scope()


def phase_fourier(P, FM0, ccsc, CTd, STd, CTc, STc, MIX0):
    P.push_scope()
    NL = TL // 128
    NCt = TC // 128
    g32 = P.sb("g32", [128, TL + TC], F32); b_g32 = P.buf()
    P.dma("sp", g32[:, 0:TL], FM0[0:128, LAT0:LAT0 + TL], writes=[b_g32])
    P.dma("sp", g32[:, TL:TL + TC], FM0[0:128, 2:2 + TC], writes=[b_g32])
    gbf = P.sb("gbf", [128, TL + TC], BF16); b_gbf = P.buf()
    P.op("dve", lambda e: e.tensor_copy(out=gbf[:], in_=g32[:]), reads=[b_g32], writes=[b_gbf])
    cc32 = P.sb("cc32", [128, 256], F32); ccb = P.sb("ccb", [128, 256], BF16); b_cc = P.buf()
    P.dma("sp", cc32[:], ccsc, writes=[b_cc])
    P.op("dve", lambda e: e.tensor_copy(out=ccb[:], in_=cc32[:]), reads=[b_cc], writes=[b_cc])
    H = P.sb("H", [128, NL + NCt, 256], BF16); b_H = P.buf()
    ph = [P.ps(f"ph{i}", [128, 256], F32) for i in range(2)]; b_ph = [P.buf() for _ in range(2)]
    for j in range(NL + NCt):
        pi = j % 2
        P.op("pe", lambda e, j=j, pi=pi: e.matmul(ph[pi][:], lhsT=gbf[:, j * 128:(j + 1) * 128], rhs=ccb[:], start=True, stop=True),
             reads=[b_gbf, b_cc], writes=[b_ph[pi]])
        if pi == 0:
            P.op("act", lambda e, j=j, pi=pi: e.copy(out=H[:, j, :], in_=ph[pi][:]), reads=[b_ph[pi]], djw=[b_H])
        else:
            P.op("dve", lambda e, j=j, pi=pi: e.tensor_copy(out=H[:, j, :], in_=ph[pi][:]), reads=[b_ph[pi]], djw=[b_H])
    JB = 16
    cbuf = [P.sb(f"cbuf{i}", [128, JB, 512], BF16) for i in range(2)]; b_cbuf = [P.buf() for _ in range(2)]
    sbuf_ = [P.sb(f"sbuf{i}", [128, JB, 512], BF16) for i in range(2)]; b_sbuf = [P.buf() for _ in range(2)]
    po = [P.ps(f"po{i}", [128, 512], F32) for i in range(2)]; b_po = [P.buf() for _ in range(2)]
    ot = [P.sb(f"ot{i}", [128, 512], BF16) for i in range(2)]; b_ot = [P.buf() for _ in range(2)]
    CTv = CTd.rearrange("(j p) f -> p j f", p=128)
    STv = STd.rearrange("(j p) f -> p j f", p=128)
    nb = 0
    for fb in range(TL // 512):
        fsl = slice(fb * 512, (fb + 1) * 512)
        pi = fb % 2
        for j0 in range(0, NL, JB):
            bi = nb % 2; nb += 1
            P.dma("sp", cbuf[bi][:], CTv[:, j0:j0 + JB, fsl], writes=[b_cbuf[bi]])
            P.dma("pool", sbuf_[bi][:], STv[:, j0:j0 + JB, fsl], writes=[b_sbuf[bi]])
            for jj in range(JB):
                j = j0 + jj
                P.op("pe", lambda e, j=j, jj=jj, bi=bi, pi=pi: e.matmul(po[pi][:], lhsT=H[:, j, 0:128], rhs=cbuf[bi][:, jj, :],
                                                                     start=(j == 0), stop=False), reads=[b_H, b_cbuf[bi]], writes=[b_po[pi]])
                P.op("pe", lambda e, j=j, jj=jj, bi=bi, pi=pi: e.matmul(po[pi][:], lhsT=H[:, j, 128:256], rhs=sbuf_[bi][:, jj, :],
                                                                     start=False, stop=(j == NL - 1)), reads=[b_H, b_sbuf[bi]], writes=[b_po[pi]])
        P.op("act", lambda e, pi=pi: e.copy(out=ot[pi][:], in_=po[pi][:]), reads=[b_po[pi]], writes=[b_ot[pi]])
        P.dma("sp", MIX0[1 + fb // 2, 0:128, (fb % 2) * 512:(fb % 2 + 1) * 512], ot[pi][:], reads=[b_ot[pi]])
    cc_ = P.sb("cc_", [128, NCt, TC], BF16); sc_ = P.sb("sc_", [128, NCt, TC], BF16); b_c2 = P.buf()
    P.dma("sp", cc_[:], CTc.rearrange("(j p) f -> p j f", p=128), writes=[b_c2])
    P.dma("sp", sc_[:], STc.rearrange("(j p) f -> p j f", p=128), writes=[b_c2])
    pi = 0
    for j in range(NCt):
        P.op("pe", lambda e, j=j: e.matmul(po[pi][:, 0:TC], lhsT=H[:, NL + j, 0:128], rhs=cc_[:, j, :], start=(j == 0), stop=False),
             reads=[b_H, b_c2], writes=[b_po[pi]])
        P.op("pe", lambda e, j=j: e.matmul(po[pi][:, 0:TC], lhsT=H[:, NL + j, 128:256], rhs=sc_[:, j, :], start=False, stop=(j == NCt - 1)),
             reads=[b_H, b_c2], writes=[b_po[pi]])
    P.op("act", lambda e: e.copy(out=ot[pi][:, 0:TC], in_=po[pi][:, 0:TC]), reads=[b_po[pi]], writes=[b_ot[pi]])
    P.dma("sp", MIX0[0, 0:128, 0:TC], ot[pi][:, 0:TC], reads=[b_ot[pi]])
    P.barrier()
    P.pop_scope()


def phase_ssd(P, XBC, ZDT, prm, trid, gnR, identf, YF, MIX0, GMIX0=None):
    P.push_scope()
    NCH = NTA
    NCTX = TC // 128
    NC6 = NCH * 6
    tri = P.sb("tri", [128, 3, 128], F32); b_tri = P.buf()
    P.dma("sp", tri[:], trid, writes=[b_tri])
    idf, b_idf = load_ident(P, identf, F32, "idf")
    prs = P.sb("prs", [128, 3, 2, NCH, 6], F32); b_prs = P.buf()
    P.dma("sp", prs[:], prm, writes=[b_prs])
    gn = P.sb("gn", [128, 384], F32); b_gn = P.buf()
    P.dma("sp", gn[:], gnR, writes=[b_gn])
    dts = P.sb("dts", [128, 2, NCH, 6], F32); b_dts = P.buf()
    zv = ZDT.rearrange("(c p) n -> p c n", p=128)
    for c in range(NCH):
        P.dma("sp", dts[:, :, c, :], ZDT[c * 128:(c + 1) * 128, 384:396].rearrange("p (d h) -> p d h", d=2), writes=[b_dts])
    stg = P.sb("stg", [128, TA], F32); b_stg = P.buf()
    Bf = P.sb("Bf", [128, TA], BF16); Cf = P.sb("Cf", [128, TA], BF16); Bt = P.sb("Bt", [128, NCH, 128], BF16)
    b_Bf, b_Cf, b_Bt = P.buf(), P.buf(), P.buf()
    pcb = P.ps("pcb", [128, 128], F32); b_pcb = P.buf()
    P.dma("sp", stg[:], XBC[512:640, :], writes=[b_stg])
    P.op("pool", lambda e: e.tensor_copy(out=Cf[:], in_=stg[:]), reads=[b_stg], writes=[b_Cf])
    P.dma("sp", stg[:], XBC[384:512, :], writes=[b_stg])
    P.op("dve", lambda e: e.tensor_copy(out=Bf[:], in_=stg[:]), reads=[b_stg], writes=[b_Bf])
    for c in range(NCH):
        P.op("pe", lambda e, c=c: e.transpose(out=pcb[:], in_=stg[:, c * 128:(c + 1) * 128], identity=idf[:]),
             reads=[b_stg, b_idf], writes=[b_pcb])
        P.op("act", lambda e, c=c: e.copy(out=Bt[:, c, :], in_=pcb[:]), reads=[b_pcb], djw=[b_Bt])
    dtsp = P.sb("dtsp", [128, 2, NCH, 6], F32); dta = P.sb("dta", [128, 2, NCH, 6], F32); b_dt = P.buf()
    P.op("dve", lambda e: e.tensor_add(out=dtsp[:], in0=dts[:], in1=prs[:, 0]), reads=[b_dts, b_prs], writes=[b_dt])
    P.op("act", lambda e: e.activation(out=dtsp[:], in_=dtsp[:], func=AF.Exp), reads=[b_dt], writes=[b_dt])
    P.op("dve", lambda e: e.tensor_scalar_add(out=dtsp[:], in0=dtsp[:], scalar1=1.0), reads=[b_dt], writes=[b_dt])
    P.op("act", lambda e: e.activation(out=dtsp[:], in_=dtsp[:], func=AF.Ln), reads=[b_dt], writes=[b_dt])
    P.op("act", lambda e: e.activation(out=dta[:], in_=prs[:, 1], func=AF.Exp), reads=[b_prs, b_dt], writes=[b_dt])
    P.op("dve", lambda e: e.scalar_tensor_tensor(out=dta[:], in0=dta[:], scalar=-1.0, in1=dtsp[:], op0=ALU.mult, op1=ALU.mult),
         reads=[b_dt], writes=[b_dt])
    acs = P.sb("acs", [128, 2, NCH, 6], F32); eacs = P.sb("eacs", [128, 2, NCH, 6], F32)
    tend = P.sb("tend", [128, 2, NCH, 6], F32); cdec = P.sb("cdec", [128, 2, NCH, 6], F32); b_ac = P.buf()
    pb = [P.ps(f"pb{i}", [128, NC6], F32) for i in range(2)]; b_pb = [P.buf() for _ in range(2)]
    for d in range(2):
        P.op("pe", lambda e, d=d: e.matmul(pb[0][:], lhsT=tri[:, d, :], rhs=dta[:, d].rearrange("p c h -> p (c h)"), start=True, stop=True),
             reads=[b_tri, b_dt], writes=[b_pb[0]])
        P.op("pe", lambda e, d=d: e.matmul(pb[1][:], lhsT=tri[:, 2, :], rhs=dta[:, d].rearrange("p c h -> p (c h)"), start=True, stop=True),
             reads=[b_tri, b_dt], writes=[b_pb[1]])
        av = lambda t, d=d: t[:, d].rearrange("p c h -> p (c h)")
        P.op("dve", lambda e, av=av: e.tensor_copy(out=av(acs), in_=pb[0][:]), reads=[b_pb[0]], writes=[b_ac])
        P.op("act", lambda e, av=av: e.activation(out=av(eacs), in_=pb[0][:], func=AF.Exp), reads=[b_pb[0]], writes=[b_ac])
        P.op("act", lambda e, av=av: e.activation(out=av(cdec), in_=pb[1][:], func=AF.Exp), reads=[b_pb[1]], writes=[b_ac])
        P.op("dve", lambda e, av=av: e.tensor_sub(out=av(tend), in0=pb[1][:], in1=av(acs)), reads=[b_pb[1], b_ac], writes=[b_ac])
        P.op("act", lambda e, av=av: e.activation(out=av(tend), in_=av(tend), func=AF.Exp), reads=[b_ac], writes=[b_ac])
    dtE = P.sb("dtE", [128, 2, NCH, 6], F32)
    P.op("dve", lambda e: e.tensor_mul(out=dtE[:], in0=dtsp[:], in1=tend[:]), reads=[b_dt, b_ac], writes=[b_ac])
    bc = lambda ap, n: ap.unsqueeze(2).to_broadcast([128, 6, n])
    xf = [P.sb(f"xf{i}", [128, 3, 128], F32) for i in range(2)]; b_xf = [P.buf() for _ in range(2)]
    xt = [P.sb(f"xt{i}", [128, 6, 64], F32) for i in range(2)]; b_xt = [P.buf() for _ in range(2)]
    Dm = [P.sb(f"Dm{i}", [128, 6, 128], F32) for i in range(2)]; b_Dm = [P.buf() for _ in range(2)]
    pR = P.ps("pR", [128, 6, 128], F32); b_pR = P.buf()
    cbU = [P.sb(f"cbU{i}", [128, 128], F32) for i in range(2)]; b_cbU = [P.buf() for _ in range(2)]
    arg = [P.sb(f"arg{i}", [128, 6, 128], F32) for i in range(2)]; b_arg = [P.buf() for _ in range(2)]
    Wh = [P.sb(f"Wh{i}", [128, 6, 128], BF16) for i in range(2)]; b_Wh = [P.buf() for _ in range(2)]
    xdt = [P.sb(f"xdt{i}", [128, 6, 64], BF16) for i in range(2)]; b_xdt = [P.buf() for _ in range(2)]
    xdtE = [P.sb(f"xdtE{i}", [128, 6, 64], BF16) for i in range(2)]; b_xdtE = [P.buf() for _ in range(2)]
    py = P.ps("py", [128, 6, 64], F32); b_py = P.buf()
    pyo = P.ps("pyo", [128, 6, 64], F32); b_pyo = P.buf()
    pst = P.ps("pst", [128, 6, 64], F32); b_pst = P.buf()
    tA = [P.sb(f"tA{i}", [128, 6, 64], F32) for i in range(2)]; b_tA = [P.buf() for _ in range(2)]
    tB = [P.sb(f"tB{i}", [128, 6, 64], F32) for i in range(2)]; b_tB = [P.buf() for _ in range(2)]
    yo = [P.sb(f"yo{i}", [128, 384], F32) for i in range(2)]; b_yo = [P.buf() for _ in range(2)]
    yfl = [P.sb(f"yfl{i}", [128, 384], F32) for i in range(2)]; b_yfl = [P.buf() for _ in range(2)]
    zt = [P.sb(f"zt{i}", [128, 384], F32) for i in range(2)]; b_zt = [P.buf() for _ in range(2)]
    ynb = [P.sb(f"ynb{i}", [128, 384], F32) for i in range(2)]; b_ynb = [P.buf() for _ in range(2)]
    ynT = [P.sb(f"ynT{i}", [128, 3, 128], BF16) for i in range(2)]; b_ynT = [P.buf() for _ in range(2)]
    junk = P.sb("junk", [128, 384], F32); b_junk = P.buf()
    ss = P.sb("ss", [128, NCH], F32); b_ss0 = P.buf(); b_ss = [P.buf() for _ in range(NCH)]
    P.op("pool", lambda e: e.memset(ss[:], 0.0), writes=[b_ss0])
    S = P.sb("S", [128, 6, 64], F32); Sb = P.sb("Sb", [128, 6, 64], BF16); b_S = P.buf(); b_Sb = P.buf()
    b_YF = [P.buf() for _ in range(NCH)]
    b_m0 = [P.buf() for _ in range(9)]; n_m0 = [0] * 9
    f2 = lambda t: t[:].rearrange("p h e -> p (h e)")
    n = 0
    for d in range(2):
        ctx_order = list(range(NCTX)) if d == 0 else list(range(NCTX - 1, -1, -1))
        lat_order = list(range(NCTX, NCH)) if d == 0 else list(range(NCH - 1, NCTX - 1, -1))
        P.op("pool", lambda e: e.memset(S[:], 0.0), writes=[b_S])
        P.op("pool", lambda e: e.memset(Sb[:], 0.0), writes=[b_Sb])
        for c in ctx_order + lat_order:
            i = n % 2; n += 1
            csl = slice(c * 128, (c + 1) * 128)
            P.dma("sp", xf[i][:], XBC[0:384, csl].rearrange("(j p) t -> p j t", p=128), writes=[b_xf[i]])
            if d == 1:
                P.dma("sp", yfl[i][:], YF[csl, :], reads=[b_YF[c]], writes=[b_yfl[i]])
                P.dma("sp", zt[i][:], ZDT[csl, 0:384], writes=[b_zt[i]])
            for j3 in range(3):
                P.op("pe", lambda e, i=i, j3=j3: e.transpose(out=pb[0][:, j3 * 128:(j3 + 1) * 128], in_=xf[i][:, j3, :], identity=idf[:]),
                     reads=[b_xf[i], b_idf], writes=[b_pb[0]])
            P.op("act", lambda e, i=i: e.copy(out=f2(xt[i]), in_=pb[0][:, 0:384]), reads=[b_pb[0]], writes=[b_xt[i]])
            P.op("pool", lambda e, i=i, c=c, d=d: e.tensor_mul(out=Dm[i][:], in0=tri[:, d, :].unsqueeze(1).to_broadcast([128, 6, 128]),
                                                               in1=bc(dta[:, d, c, :], 128)), reads=[b_tri, b_dt], writes=[b_Dm[i]])
            P.op("pe", lambda e, i=i: e.matmul(pR[:, 0:4, :].rearrange("p h l -> p (h l)"), lhsT=tri[:, 2, :],
                                               rhs=Dm[i][:, 0:4, :].rearrange("p h l -> p (h l)"), start=True, stop=True),
                 reads=[b_tri, b_Dm[i]], writes=[b_pR])
            P.op("pe", lambda e, i=i: e.matmul(pR[:, 4:6, :].rearrange("p h l -> p (h l)"), lhsT=tri[:, 2, :],
                                               rhs=Dm[i][:, 4:6, :].rearrange("p h l -> p (h l)"), start=True, stop=True),
                 reads=[b_tri, b_Dm[i]], writes=[b_pR])
            P.op("pe", lambda e, csl=csl: e.matmul(pcb[:], lhsT=Bf[:, csl], rhs=Cf[:, csl], start=True, stop=True),
                 reads=[b_Bf, b_Cf], writes=[b_pcb])
            P.op("dve", lambda e, i=i, d=d: e.tensor_mul(out=cbU[i][:], in0=pcb[:], in1=tri[:, d, :]), reads=[b_pcb, b_tri], writes=[b_cbU[i]])
            P.op("dve", lambda e, i=i, c=c, d=d: e.tensor_sub(out=arg[i][:], in0=pR[:], in1=bc(acs[:, d, c, :], 128)),
                 reads=[b_pR, b_ac], writes=[b_arg[i]])
            P.op("pool", lambda e, i=i: e.tensor_scalar_min(out=arg[i][:], in0=arg[i][:], scalar1=0.0), reads=[b_arg[i]], writes=[b_arg[i]])
            P.op("act", lambda e, i=i: e.activation(out=arg[i][:], in_=arg[i][:], func=AF.Exp), reads=[b_arg[i]], writes=[b_arg[i]])
            P.op("pool", lambda e, i=i: e.tensor_mul(out=Wh[i][:], in0=arg[i][:], in1=cbU[i][:].unsqueeze(1).to_broadcast([128, 6, 128])),
                 reads=[b_arg[i], b_cbU[i]], writes=[b_Wh[i]])
            P.op("dve", lambda e, i=i, c=c, d=d: e.tensor_mul(out=xdt[i][:], in0=xt[i][:], in1=bc(dtsp[:, d, c, :], 64)),
                 reads=[b_xt[i], b_dt], writes=[b_xdt[i]])
            P.op("pool", lambda e, i=i, c=c, d=d: e.tensor_mul(out=xdtE[i][:], in0=xt[i][:], in1=bc(dtE[:, d, c, :], 64)),
                 reads=[b_xt[i], b_ac], writes=[b_xdtE[i]])
            for h in range(6):
                P.op("pe", lambda e, h=h, i=i: e.matmul(py[:, h, :], lhsT=Wh[i][:, h, :], rhs=xdt[i][:, h, :], start=True, stop=True),
                     reads=[b_Wh[i], b_xdt[i]], writes=[b_py])
            P.op("pe", lambda e, csl=csl: e.matmul(f2(pyo), lhsT=Cf[:, csl], rhs=f2(Sb), start=True, stop=True),
                 reads=[b_Cf, b_Sb], writes=[b_pyo])
            P.op("pe", lambda e, c=c, i=i: e.matmul(f2(pst), lhsT=Bt[:, c, :], rhs=f2(xdtE[i]), start=True, stop=True),
                 reads=[b_Bt, b_xdtE[i]], writes=[b_pst])
            P.op("dve", lambda e, c=c, d=d: e.tensor_mul(out=S[:], in0=S[:], in1=bc(cdec[:, d, c, :], 64)), reads=[b_S, b_ac, b_pyo], writes=[b_S])
            P.op("dve", lambda e: e.tensor_add(out=S[:], in0=S[:], in1=pst[:]), reads=[b_S, b_pst], writes=[b_S])
            P.op("act", lambda e: e.copy(out=Sb[:], in_=S[:]), reads=[b_S, b_pyo], writes=[b_Sb])
            P.op("dve", lambda e, i=i, c=c, d=d: e.tensor_mul(out=tA[i][:], in0=pyo[:], in1=bc(eacs[:, d, c, :], 64)), reads=[b_pyo, b_ac], writes=[b_tA[i]])
            P.op("pool", lambda e, i=i, c=c, d=d: e.tensor_mul(out=tB[i][:], in0=xt[i][:], in1=bc(prs[:, 2, d, c, :], 64)), reads=[b_xt[i], b_prs], writes=[b_tB[i]])
            P.op("pool", lambda e, i=i: e.tensor_add(out=tA[i][:], in0=tA[i][:], in1=tB[i][:]), reads=[b_tA[i], b_tB[i]], writes=[b_tA[i]])
            P.op("dve", lambda e, i=i: e.tensor_add(out=yo[i][:], in0=f2(tA[i]), in1=f2(py)), reads=[b_tA[i], b_py], writes=[b_yo[i]])
            if d == 0:
                P.dma("sp", YF[csl, :], yo[i][:], reads=[b_yo[i]], writes=[b_YF[c]])
            else:
                P.op("act", lambda e, i=i: e.activation(out=zt[i][:], in_=zt[i][:], func=AF.Silu), reads=[b_zt[i]], writes=[b_zt[i]])
                P.op("pool", lambda e, i=i: e.tensor_add(out=yo[i][:], in0=yo[i][:], in1=yfl[i][:]), reads=[b_yo[i], b_yfl[i]], writes=[b_yo[i]])
                P.op("pool", lambda e, i=i: e.tensor_mul(out=yo[i][:], in0=yo[i][:], in1=zt[i][:]), reads=[b_yo[i], b_zt[i]], writes=[b_yo[i]])
                P.op("act", lambda e, i=i, c=c: e.activation(out=junk[:], in_=yo[i][:], func=AF.Square, accum_out=ss[:, c:c + 1]),
                     reads=[b_yo[i], b_ss0], writes=[b_junk, b_ss[c]])
                P.op("dve", lambda e, c=c: e.tensor_scalar(out=ss[:, c:c + 1], in0=ss[:, c:c + 1], scalar1=1.0 / 384, scalar2=EPS, op0=ALU.mult, op1=ALU.add),
                     reads=[b_ss[c]], writes=[b_ss[c]])
                P.op("act", lambda e, c=c: e.activation(out=ss[:, c:c + 1], in_=ss[:, c:c + 1], func=AF.Sqrt), reads=[b_ss[c]], writes=[b_ss[c]])
                P.op("dve", lambda e, c=c: e.reciprocal(out=ss[:, c:c + 1], in_=ss[:, c:c + 1]), reads=[b_ss[c]], writes=[b_ss[c]])
                P.op("dve", lambda e, i=i, c=c: e.scalar_tensor_tensor(out=ynb[i][:], in0=yo[i][:], scalar=ss[:, c:c + 1], in1=gn[:], op0=ALU.mult, op1=ALU.mult),
                     reads=[b_yo[i], b_ss[c], b_gn], writes=[b_ynb[i]])
                for j3 in range(3):
                    P.op("pe", lambda e, i=i, j3=j3: e.transpose(out=pb[1][:, j3 * 128:(j3 + 1) * 128], in_=ynb[i][:, j3 * 128:(j3 + 1) * 128], identity=idf[:]),
                         reads=[b_ynb[i], b_idf], writes=[b_pb[1]])
                P.op("act", lambda e, i=i: e.copy(out=ynT[i][:].rearrange("p j t -> p (j t)"), in_=pb[1][:, 0:384]), reads=[b_pb[1]], writes=[b_ynT[i]])
                if c < NCTX:
                    mci = 0
                    mdst = MIX0[0, 128:512, c * 128:(c + 1) * 128]
                else:
                    lt = c - NCTX
                    mci = 1 + lt // 8
                    mdst = MIX0[mci, 128:512, (lt % 8) * 128:(lt % 8 + 1) * 128]
                P.dma("sp", mdst.rearrange("(j p) t -> p j t", p=128), ynT[i][:], reads=[b_ynT[i]], djw=[b_m0[mci]])
                n_m0[mci] += 1
                if GMIX0 is not None and n_m0[mci] == (NCTX if mci == 0 else 8):
                    P.cc("AllGather", MIX0[mci].opt(), GMIX0[mci].opt(), G4, reads=[b_m0[mci]])
    P.barrier()
    P.pop_scope()


def phase_outproj(P, CM, NT, mt_srcs, h_src, w, gR, gate_dl, HMID):
    P.push_scope()
    nk = CM // 128
    g_sb = P.sb("g_sb", [128, 1024], F32); b_g = P.buf()
    P.dma("sp", g_sb[:], gR, writes=[b_g])
    GG = P.sb("GG", [128, 2, 1024], F32); b_gg = P.buf()
    for cls in range(2):
        P.dma("sp", GG[:, cls, :], gate_dl[cls, 0], writes=[b_gg])
    for cls in range(2):
        P.op("dve", lambda e, cls=cls: e.tensor_mul(out=GG[:, cls, :], in0=GG[:, cls, :], in1=g_sb[:]), reads=[b_g, b_gg], writes=[b_gg])
    w_sb = P.sb("w_sb", [128, nk, 1024], BF16); b_w = P.buf()
    stage = [P.sb(f"stage{i}", [128, 1024], F32) for i in range(2)]; b_stage = [P.buf() for _ in range(2)]
    load_weight_bf16(P, w, w_sb, b_w, nk, 1024, stage, b_stage)
    rn = ResNorm(P, "r_", NT)
    mT = [P.sb(f"mT{i}", [128, nk, 128], BF16) for i in range(2)]; b_mT = [P.buf() for _ in range(2)]
    xt = [P.sb(f"xt{i}", [128, 1024], F32) for i in range(2)]; b_xt = [P.buf() for _ in range(2)]
    po = [P.ps(f"po{i}", [128, 1024], F32) for i in range(2)]; b_po = [P.buf() for _ in range(2)]
    for i in range(2):
        P.op("pool", lambda e, i=i: e.memset(mT[i][:], 0.0), writes=[b_mT[i]])
    for t in range(NT):
        i = t % 2
        cls = 0 if t < 16 else 1
        dst_fn, src_fn = mt_srcs(t)
        P.dma("sp", dst_fn(mT[i]), src_fn, writes=[b_mT[i]])
        P.dma("sp", xt[i][:], h_src[t * 128:(t + 1) * 128, :], writes=[b_xt[i]])
        for cb in range(2):
            for k in range(nk):
                P.op("pe", lambda e, i=i, k=k, cb=cb: e.matmul(
                    po[i][:, cb * 512:(cb + 1) * 512], lhsT=mT[i][:, k, :], rhs=w_sb[:, k, cb * 512:(cb + 1) * 512],
                    start=(k == 0), stop=(k == nk - 1)), reads=[b_mT[i], b_w], writes=[b_po[i]])
        o_t, b_o = rn.run(po[i][:], b_po[i], t, xt[i][:], b_xt[i], GG[:, cls, :], b_gg)
        P.dma("sp", HMID[t * 128:(t + 1) * 128, :], o_t[:], reads=[b_o])
    P.barrier()
    P.pop_scope()


def phase_ffn(P, HMID, NT, wg, wu, wd, modT_dl, gT, gR, gate_dl, identd, OUT, ctx_tiles, GOUT=None):
    P.push_scope()
    FH = 2816
    NJ = FH // 128
    ident, b_ident = load_ident(P, identd, BF16)
    modsb = P.sb("modsb", [128, 2, 6, 8], F32); b_mod = P.buf()
    P.dma("sp", modsb[:], modT_dl, writes=[b_mod])
    gsb = P.sb("gsb", [128, 8], F32); b_g = P.buf()
    P.dma("sp", gsb[:], gT, writes=[b_g])
    Gs = P.sb("Gs", [128, 2, 8], F32); Sh = P.sb("Sh", [128, 2, 8], F32); b_gs = P.buf()
    for cls in range(2):
        P.op("dve", lambda e, cls=cls: e.scalar_tensor_tensor(out=Gs[:, cls, :], in0=modsb[:, cls, 4, :], scalar=1.0,
                                                               in1=gsb[:], op0=ALU.add, op1=ALU.mult), reads=[b_mod, b_g], writes=[b_gs])
        P.op("dve", lambda e, cls=cls: e.tensor_copy(out=Sh[:, cls, :], in_=modsb[:, cls, 3, :]), reads=[b_mod], writes=[b_gs])
    g_sb = P.sb("g_sb", [128, 1024], F32); b_g3 = P.buf()
    P.dma("sp", g_sb[:], gR, writes=[b_g3])
    GG = P.sb("GG", [128, 2, 1024], F32); b_gg = P.buf()
    for cls in range(2):
        P.dma("sp", GG[:, cls, :], gate_dl[cls, 1], writes=[b_gg])
    for cls in range(2):
        P.op("dve", lambda e, cls=cls: e.tensor_mul(out=GG[:, cls, :], in0=GG[:, cls, :], in1=g_sb[:]), reads=[b_g3, b_gg], writes=[b_gg])
    wg_sb = P.sb("wg_sb", [128, 8, FH], BF16); b_wg = P.buf()
    wu_sb = P.sb("wu_sb", [128, 8, FH], BF16); b_wu = P.buf()
    wd_sb = P.sb("wd_sb", [128, NJ, 1024], BF16); b_wd = P.buf()
    stage = [P.sb(f"stage{i}", [128, FH], F32) for i in range(2)]; b_stage = [P.buf() for _ in range(2)]
    load_weight_bf16(P, wg, wg_sb, b_wg, 8, FH, stage, b_stage, cast_engs=("pool", "dve"))
    load_weight_bf16(P, wu, wu_sb, b_wu, 8, FH, stage, b_stage, cast_engs=("pool", "dve"))
    load_weight_bf16(P, wd, wd_sb, b_wd, NJ, 1024, stage, b_stage, cast_engs=("pool", "dve"))
    nt = NormT(P, "n_", ident, b_ident, NT)
    rn = ResNorm(P, "r_", NT)
    ST = 2
    xt = [stage[0][:, i * 1024:(i + 1) * 1024] for i in range(2)]; b_xt = [P.alias(b_stage[0]) for _ in range(2)]
    xr = [stage[1][:, i * 1024:(i + 1) * 1024] for i in range(2)]; b_xr = [P.alias(b_stage[1]) for _ in range(2)]
    aT = P.sb("aT", [128, 8, ST * 128], BF16); b_aT = P.buf()
    hidT = P.sb("hidT", [128, NJ, ST * 128], BF16); b_hid = P.buf()
    sg = [P.sb(f"sg{i}", [128, ST * 128], F32) for i in range(2)]; b_sg = [P.buf() for _ in range(2)]
    psg = [P.ps(f"psg{i}", [128, 512], F32) for i in range(2)]; b_psg = [P.buf() for _ in range(2)]
    psu = [P.ps(f"psu{i}", [128, 512], F32) for i in range(2)]; b_psu = [P.buf() for _ in range(2)]
    po = P.ps("po", [128, 1024], F32); b_po = P.buf()
    outs = []
    nx = 0
    nr = 0
    for s0 in range(0, NT, ST):
        tiles = list(range(s0, min(NT, s0 + ST)))
        N = len(tiles) * 128
        for ti, t in enumerate(tiles):
            i = nx % 2; nx += 1
            cls = 1 if t in ctx_tiles else 0
            P.dma("sp", xt[i], HMID[t * 128:(t + 1) * 128, :], writes=[b_xt[i]])
            nt.run(xt[i], b_xt[i], t, aT[:, :, ti * 128:(ti + 1) * 128], b_aT, Gs, Sh, b_gs, cls)
        for j in range(NJ):
            pi = j % 2
            for k in range(8):
                P.op("pe", lambda e, pi=pi, j=j, k=k, N=N: e.matmul(
                    psg[pi][:, 0:N], lhsT=wg_sb[:, k, j * 128:(j + 1) * 128], rhs=aT[:, k, 0:N],
                    start=(k == 0), stop=(k == 7)), reads=[b_wg, b_aT], writes=[b_psg[pi]])
            for k in range(8):
                P.op("pe", lambda e, pi=pi, j=j, k=k, N=N: e.matmul(
                    psu[pi][:, 0:N], lhsT=wu_sb[:, k, j * 128:(j + 1) * 128], rhs=aT[:, k, 0:N],
                    start=(k == 0), stop=(k == 7)), reads=[b_wu, b_aT], writes=[b_psu[pi]])
            P.op("act", lambda e, pi=pi, N=N: e.activation(out=sg[pi][:, 0:N], in_=psg[pi][:, 0:N], func=AF.Silu),
                 reads=[b_psg[pi]], writes=[b_sg[pi]])
            P.op("dve", lambda e, pi=pi, j=j, N=N: e.tensor_mul(out=hidT[:, j, 0:N], in0=sg[pi][:, 0:N], in1=psu[pi][:, 0:N]),
                 reads=[b_sg[pi], b_psu[pi]], djw=[b_hid])
        for ti, t in enumerate(tiles):
            i = nr % 2; nr += 1
            cls = 1 if t in ctx_tiles else 0
            P.dma("sp", xr[i], HMID[t * 128:(t + 1) * 128, :], writes=[b_xr[i]])
            for cb in range(2):
                for j in range(NJ):
                    P.op("pe", lambda e, j=j, cb=cb, ti=ti: e.matmul(
                        po[:, cb * 512:(cb + 1) * 512], lhsT=hidT[:, j, ti * 128:(ti + 1) * 128],
                        rhs=wd_sb[:, j, cb * 512:(cb + 1) * 512], start=(j == 0), stop=(j == NJ - 1)),
                        reads=[b_hid, b_wd], writes=[b_po])
            o_t, b_o = rn.run(po[:], b_po, t, xr[i], b_xr[i], GG[:, cls, :], b_gg)
            b_chunk = P.buf() if (t % 2 == 0) else b_chunk
            outs.append(P.dma("sp", OUT[t * 128:(t + 1) * 128, :], o_t[:], reads=[b_o], djw=[b_chunk]))
            if GOUT is not None and (t % 2 == 1 or t == NT - 1):
                ci = t // 2
                P.cc("AllGather", OUT[ci * 256:(ci + 1) * 256, :].opt(), GOUT[ci].opt(), G4, reads=[b_chunk])
    P.barrier()
    P.pop_scope()
    return outs


def allgather(P, srcs, dsts):
    for a, b in zip(srcs, dsts):
        P.cc("AllGather", a.opt(), b.opt(), G4)
    P.barrier()


def phase_attn(P, QKV, csd, snd, lamRd, subCd, identd, lambda_init, MIX1, NQB=None):
    P.push_scope()
    NU = 2
    NKT = TA // 128
    NCT = TC // 128
    NLT = TL // 128
    if NQB is None:
        NQB = TL // 512
    ident, b_ident = load_ident(P, identd, BF16)
    lam = P.sb("lam", [128, 4, 64], F32); b_lam = P.buf()
    P.dma("sp", lam[:], lamRd, writes=[b_lam])
    GS = P.sb("GS", [128, 1], F32); b_gs = P.buf()
    P.dma("sp", GS[:], subCd, writes=[b_gs])
    P.op("dve", lambda e: e.tensor_scalar_mul(out=GS[:], in0=GS[:], scalar1=float(1.0 - lambda_init)), reads=[b_gs], writes=[b_gs])
    lp = P.sb("lp", [128, 2, 64], F32); ls = P.sb("ls", [128, 4], F32); b_ls = P.buf()
    P.op("dve", lambda e: e.tensor_mul(out=lp[:, 0, :], in0=lam[:, 0, :], in1=lam[:, 1, :]), reads=[b_lam], writes=[b_ls])
    P.op("dve", lambda e: e.tensor_mul(out=lp[:, 1, :], in0=lam[:, 2, :], in1=lam[:, 3, :]), reads=[b_lam, b_ls], writes=[b_ls])
    P.op("dve", lambda e: e.reduce_sum(out=ls[:, 0:2], in_=lp[:], axis=AX.X), reads=[b_ls], writes=[b_ls])
    P.op("act", lambda e: e.activation(out=ls[:, 0:2], in_=ls[:, 0:2], func=AF.Exp), reads=[b_ls], writes=[b_ls])
    P.op("dve", lambda e: e.tensor_sub(out=ls[:, 2:3], in0=ls[:, 1:2], in1=ls[:, 0:1]), reads=[b_ls], writes=[b_ls])
    P.op("dve", lambda e: e.tensor_scalar_add(out=ls[:, 3:4], in0=ls[:, 2:3], scalar1=float(-lambda_init)), reads=[b_ls], writes=[b_ls])
    neglam = ls[:, 3:4]
    kT = P.sb("kT", [128, TA], BF16); b_kT = P.buf()
    NB = TL // 256
    qTz = P.sb("qTz", [128, NB, 2, 256], BF16); b_qT = P.buf()
    P.op("pool", lambda e: e.memset(qTz[:], 0.0), writes=[b_qT])
    vaug = P.sb("vaug", [128, NKT, 129], BF16); b_va = P.buf()
    P.op("pool", lambda e: e.memset(vaug[:, :, 128:129], 1.0), writes=[b_va])
    ld = {n: [P.sb(f"ld_{n}{i}", [128, 128], F32) for i in range(2)] for n in ("q", "k", "v", "cs", "sn")}
    b_ld = {n: [P.buf() for _ in range(2)] for n in ld}
    t1 = {n: [P.sb(f"t1_{n}{i}", [128, 128], F32) for i in range(2)] for n in ("q", "k")}
    t2 = {n: [P.sb(f"t2_{n}{i}", [128, 128], F32) for i in range(2)] for n in ("q", "k")}
    b_t1 = {n: [P.buf() for _ in range(2)] for n in t1}
    b_t2 = {n: [P.buf() for _ in range(2)] for n in t1}
    rb = {n: [P.sb(f"rb_{n}{i}", [128, 128], BF16) for i in range(2)] for n in ("q", "k")}
    b_rb = {n: [P.buf() for _ in range(2)] for n in rb}
    psT = [P.ps(f"psT{i}", [128, 128], BF16) for i in range(2)]; b_psT = [P.buf() for _ in range(2)]
    NPS = 4
    ps_s = [P.ps(f"ps_s{i}", [128, 512], F32) for i in range(NPS)]; b_ps_s = [P.buf() for _ in range(NPS)]
    accT = [P.ps("accT0", [128, 512], F32)] * 2; b_accT = [P.buf()] * 2
    pden = P.ps("pden", [128, 512], F32); b_pden = P.buf()
    E = [P.sb(f"E{i}", [128, 512], BF16) for i in range(4)]; b_E = [P.buf() for _ in range(4)]
    NES = 4
    esum = [P.sb(f"esum{i}", [128, 512], F32) for i in range(NES)]; b_esum = [P.buf() for _ in range(NES)]
    onesf = P.sb("onesf", [128, 2, 128], F32); b_onesf = P.buf()
    P.op("pool", lambda e: e.memset(onesf[:, 0, :], 1.0), writes=[b_onesf])
    P.op("pool", lambda e: e.memset(onesf[:, 1, :], 1.0 / 128), reads=[b_onesf], writes=[b_onesf])
    onesb = P.sb("onesb", [128, 128], BF16)
    P.op("pool", lambda e: e.memset(onesb[:], 1.0), reads=[b_onesf], writes=[b_onesf])
    rden = P.sb("rden", [128, 512], F32); b_rden = P.buf()
    att0T = P.sb("att0T", [128, 512], F32); b_att0T = P.buf()
    attT = P.sb("attT", [128, 512], F32); b_attT = P.buf()
    sqT = P.sb("sqT", [128, 512], F32); b_sqT = P.buf()
    oT = [P.sb(f"oT{i}", [128, 512], BF16) for i in range(2)]; b_oT = [P.buf() for _ in range(2)]
    v5 = lambda ap: ap.rearrange("p (c a h f) -> p c a h f", c=2, a=2, h=2, f=16)
    npt = [0]

    def transpose_to(src_bf, b_src, dst_ap, b_dst):
        pi = npt[0] % 2; npt[0] += 1
        P.op("pe", lambda e: e.transpose(out=psT[pi][:], in_=src_bf, identity=ident[:]), reads=[b_src, b_ident], writes=[b_psT[pi]])
        P.op("act", lambda e: e.copy(out=dst_ap, in_=psT[pi][:]), reads=[b_psT[pi]], djw=[b_dst])

    no = 0
    for u in range(NU):
        qc = slice(u * 128, (u + 1) * 128)
        kc_ = slice(256 + u * 128, 256 + (u + 1) * 128)
        vc_ = slice(512 + u * 128, 512 + (u + 1) * 128)
        for j in range(NCT):
            i = j % 2
            rs = slice(j * 128, (j + 1) * 128)
            P.dma("sp", ld["k"][i][:], QKV[rs, kc_], writes=[b_ld["k"][i]])
            P.dma("sp", ld["v"][i][:], QKV[rs, vc_], writes=[b_ld["v"][i]])
            P.op("dve", lambda e, i=i: e.tensor_copy(out=rb["k"][i][:], in_=ld["k"][i][:]), reads=[b_ld["k"][i]], writes=[b_rb["k"][i]])
            transpose_to(rb["k"][i][:], b_rb["k"][i], kT[:, j * 128:(j + 1) * 128], b_kT)
            P.op("pool", lambda e, i=i, j=j: e.tensor_copy(out=vaug[:, j, 0:128], in_=ld["v"][i][:]), reads=[b_ld["v"][i]], djw=[b_va])
        for j in range(NLT):
            i = j % 2
            sl = slice(j * 128, (j + 1) * 128)
            rs = slice(TC + j * 128, TC + (j + 1) * 128)
            P.dma("sp", ld["q"][i][:], QKV[rs, qc], writes=[b_ld["q"][i]])
            P.dma("sp", ld["k"][i][:], QKV[rs, kc_], writes=[b_ld["k"][i]])
            P.dma("sp", ld["v"][i][:], QKV[rs, vc_], writes=[b_ld["v"][i]])
            P.dma("sp", ld["cs"][i][:], csd[sl, :], writes=[b_ld["cs"][i]])
            P.dma("sp", ld["sn"][i][:], snd[sl, :], writes=[b_ld["sn"][i]])
            for n in ("q", "k"):
                x = ld[n][i]; a1 = t1[n][i]; a2 = t2[n][i]
                P.op("dve", lambda e, x=x, a1=a1, i=i: e.tensor_mul(out=a1[:], in0=x[:], in1=ld["cs"][i][:]),
                     reads=[b_ld[n][i], b_ld["cs"][i]], writes=[b_t1[n][i]])
                P.op("pool", lambda e, x=x, a2=a2, i=i: e.tensor_mul(out=v5(a2[:])[:, :, :, 0, :], in0=v5(x[:])[:, :, :, 1, :],
                                                                     in1=v5(ld["sn"][i][:])[:, :, :, 0, :]),
                     reads=[b_ld[n][i], b_ld["sn"][i]], writes=[b_t2[n][i]])
                P.op("pool", lambda e, x=x, a2=a2, i=i: e.tensor_mul(out=v5(a2[:])[:, :, :, 1, :], in0=v5(x[:])[:, :, :, 0, :],
                                                                     in1=v5(ld["sn"][i][:])[:, :, :, 1, :]),
                     reads=[b_ld[n][i], b_ld["sn"][i]], writes=[b_t2[n][i]])
                P.op("dve", lambda e, a1=a1, a2=a2, n=n, i=i: e.tensor_add(out=rb[n][i][:], in0=a1[:], in1=a2[:]),
                     reads=[b_t1[n][i], b_t2[n][i]], writes=[b_rb[n][i]])
            pi_ = npt[0] % 2; npt[0] += 1
            P.op("pe", lambda e, i=i, pi_=pi_: e.transpose(out=psT[pi_][:], in_=rb["q"][i][:], identity=ident[:]),
                 reads=[b_rb["q"][i], b_ident], writes=[b_psT[pi_]])
            qb_, qo_ = j // 2, (j % 2) * 128
            P.op("act", lambda e, pi_=pi_, qb_=qb_, qo_=qo_: e.copy(out=qTz[0:64, qb_, 0, qo_:qo_ + 128], in_=psT[pi_][0:64, :]),
                 reads=[b_psT[pi_]], djw=[b_qT])
            P.op("act", lambda e, pi_=pi_, qb_=qb_, qo_=qo_: e.copy(out=qTz[64:128, qb_, 1, qo_:qo_ + 128], in_=psT[pi_][64:128, :]),
                 reads=[b_psT[pi_]], djw=[b_qT])
            transpose_to(rb["k"][i][:], b_rb["k"][i], kT[:, TC + j * 128:TC + (j + 1) * 128], b_kT)
            P.op("pool", lambda e, i=i, j=j: e.tensor_copy(out=vaug[:, NCT + j, 0:128], in_=ld["v"][i][:]), reads=[b_ld["v"][i]], djw=[b_va])
        for qb in range(NQB * 2):
            oi = no % 2; no += 1
            ac = accT[0]; b_ac_ = b_accT[0]
            qrhs = qTz[:, qb].rearrange("p c t -> p (c t)")

            def score(kt, qrhs=qrhs):
                pi = kt % NPS
                P.op("pe", lambda e, kt=kt, pi=pi, qrhs=qrhs: e.matmul(ps_s[pi][:], lhsT=kT[:, kt * 128:(kt + 1) * 128], rhs=qrhs,
                                                                        start=True, stop=True), reads=[b_kT, b_qT], writes=[b_ps_s[pi]])
            for k0 in range(NPS - 1):
                score(k0)
            for kt in range(NKT):
                pi = kt % NPS
                ei = kt % 4
                if kt + NPS - 1 < NKT:
                    score(kt + NPS - 1)
                P.op("act", lambda e, pi=pi, ei=ei: e.activation(out=E[ei][:], in_=ps_s[pi][:], func=AF.Exp, scale=0.125),
                     reads=[b_ps_s[pi]], writes=[b_E[ei]])
                P.op("pe", lambda e, ei=ei, kt=kt, ac=ac: e.matmul(ac[:], lhsT=vaug[:, kt, 0:128], rhs=E[ei][:], start=(kt == 0), stop=(kt == NKT - 1)),
                     reads=[b_E[ei], b_va], writes=[b_ac_])
                si = kt % NES
                if kt < NES:
                    P.op("dve", lambda e, ei=ei, si=si: e.tensor_copy(out=esum[si][:], in_=E[ei][:]), reads=[b_E[ei]], writes=[b_esum[si]])
                else:
                    P.op("dve", lambda e, ei=ei, si=si: e.tensor_add(out=esum[si][:], in0=esum[si][:], in1=E[ei][:]), reads=[b_E[ei], b_esum[si]], writes=[b_esum[si]])
            for si in range(NES):
                P.op("pe", lambda e, si=si: e.matmul(pden[:], lhsT=onesf[:, 0, :], rhs=esum[si][:], start=(si == 0), stop=(si == NES - 1)),
                     reads=[b_onesf, b_esum[si]], writes=[b_pden])
            P.op("dve", lambda e: e.reciprocal(out=rden[:], in_=pden[:]), reads=[b_pden], writes=[b_rden])
            P.op("dve", lambda e, ac=ac: e.tensor_mul(out=att0T[:], in0=ac[:], in1=rden[:]), reads=[b_ac_, b_rden], writes=[b_att0T])
            P.op("dve", lambda e: e.scalar_tensor_tensor(out=attT[:, 0:256], in0=att0T[:, 256:512], scalar=neglam, in1=att0T[:, 0:256], op0=ALU.mult, op1=ALU.add),
                 reads=[b_att0T, b_ls], writes=[b_attT])
            P.op("act", lambda e: e.activation(out=sqT[:, 0:256], in_=attT[:, 0:256], func=AF.Square), reads=[b_attT], writes=[b_sqT])
            P.op("pe", lambda e: e.matmul(pden[:, 0:256], lhsT=onesf[:, 1, :], rhs=sqT[:, 0:256], start=True, stop=True), reads=[b_onesf, b_sqT, b_rden], writes=[b_pden])
            P.op("dve", lambda e: e.tensor_scalar_add(out=rden[:, 0:256], in0=pden[:, 0:256], scalar1=EPS), reads=[b_pden, b_att0T], writes=[b_rden])
            P.op("act", lambda e: e.activation(out=rden[:, 0:256], in_=rden[:, 0:256], func=AF.Sqrt), reads=[b_rden], writes=[b_rden])
            P.op("dve", lambda e: e.reciprocal(out=rden[:, 0:256], in_=rden[:, 0:256]), reads=[b_rden], writes=[b_rden])
            P.op("dve", lambda e: e.tensor_mul(out=attT[:, 0:256], in0=attT[:, 0:256], in1=rden[:, 0:256]), reads=[b_attT, b_rden], writes=[b_attT])
            P.op("act", lambda e, oi=oi: e.activation(out=oT[oi][:, 0:256], in_=attT[:, 0:256], func=AF.Identity, scale=GS[:, 0:1]), reads=[b_attT, b_gs], writes=[b_oT[oi]])
            P.dma("sp", MIX1[1 + qb // 4, u * 128:(u + 1) * 128, (qb % 4) * 256:(qb % 4 + 1) * 256], oT[oi][:, 0:256], reads=[b_oT[oi]])
    P.barrier()
    P.pop_scope()


def phase_conformer(P, UT, cw, cb, lng, lnb, selm, ST, GST, MIX1, GMIX1=None):
    P.push_scope()
    K = 31
    TTp = TL + K - 1
    wsb = P.sb("wsb", [128, 1, K], F32); bsb = P.sb("bsb", [128, 1], F32); b_w = P.buf()
    gsb = P.sb("gsb", [128, 1], F32); lbsb = P.sb("lbsb", [128, 1], F32)
    P.dma("sp", wsb[:], cw, writes=[b_w]); P.dma("sp", bsb[:], cb, writes=[b_w])
    P.dma("sp", gsb[:], lng, writes=[b_w]); P.dma("sp", lbsb[:], lnb, writes=[b_w])
    ones = P.sb("ones", [128, 128], F32); b_ones = P.buf()
    P.op("pool", lambda e: e.memset(ones[:], 1.0), writes=[b_ones])
    a_sb = P.sb("a_sb", [128, TTp], F32); b_a = P.buf()
    g_sb = P.sb("g_sb", [128, TTp], F32); b_g = P.buf()
    P.dma("sp", a_sb[:, 15:15 + TL], UT[0:128, 15:15 + TL], writes=[b_a])
    P.dma("sp", g_sb[:, 15:15 + TL], UT[128:256, 15:15 + TL], writes=[b_g])
    for (c0, cn) in ((0, 15), (15 + TL, 15)):
        P.op("pool", lambda e, c0=c0, cn=cn: e.memset(a_sb[:, c0:c0 + cn], 0.0), djw=[b_a])
        P.op("pool", lambda e, c0=c0, cn=cn: e.memset(g_sb[:, c0:c0 + cn], 0.0), djw=[b_g])
    P.op("act", lambda e: e.activation(out=g_sb[:], in_=g_sb[:], func=AF.Sigmoid), reads=[b_g], writes=[b_g])
    P.op("pool", lambda e: e.tensor_mul(out=a_sb[:], in0=a_sb[:], in1=g_sb[:]), reads=[b_a, b_g], writes=[b_a])
    cv = g_sb
    b_cv = [P.alias(b_g) for _ in range(4)]
    for blk in range(4):
        conv_fm(P, "dve", a_sb, b_a, wsb, 0, bsb[:, 0:1], b_w, cv[:, blk * 2048:(blk + 1) * 2048], b_cv[blk], K, 2048, t0=blk * 2048)
    stt = P.sb("stt", [1, 2, TL], F32); b_stt = P.buf()
    sq = [P.sb(f"sq{i}", [128, 512], F32) for i in range(2)]; b_sq = [P.buf() for _ in range(2)]
    pm = [P.ps(f"pm{i}", [128, 512], F32) for i in range(2)]; b_pm = [P.buf() for _ in range(2)]
    pq = [P.ps(f"pq{i}", [128, 512], F32) for i in range(2)]; b_pq = [P.buf() for _ in range(2)]
    for tb in range(TL // 512):
        sl = slice(tb * 512, (tb + 1) * 512)
        pi = tb % 2
        bc = b_cv[tb // 4]
        P.op("act", lambda e, sl=sl, pi=pi: e.activation(out=sq[pi][:], in_=cv[:, sl], func=AF.Square), reads=[bc], writes=[b_sq[pi]])
        P.op("pe", lambda e, sl=sl, pi=pi: e.matmul(pm[pi][:], lhsT=ones[:], rhs=cv[:, sl], start=True, stop=True), reads=[b_ones, bc], writes=[b_pm[pi]])
        P.op("pe", lambda e, pi=pi: e.matmul(pq[pi][:], lhsT=ones[:], rhs=sq[pi][:], start=True, stop=True), reads=[b_ones, b_sq[pi]], writes=[b_pq[pi]])
        P.op("dve", lambda e, sl=sl, pi=pi: e.tensor_copy(out=stt[0:1, 0, sl], in_=pm[pi][0:1, :]), reads=[b_pm[pi]], writes=[b_stt])
        P.op("dve", lambda e, sl=sl, pi=pi: e.tensor_copy(out=stt[0:1, 1, sl], in_=pq[pi][0:1, :]), reads=[b_pq[pi]], writes=[b_stt])
    b_ST = P.buf()
    P.dma("sp", ST.rearrange("(o s) t -> o s t", o=1), stt[:], reads=[b_stt], writes=[b_ST])
    P.barrier()
    P.cc("AllGather", ST.opt(), GST.opt(), G4)
    P.barrier()
    gst = P.sb("gst", [8, TL], F32); b_gst = P.buf()
    P.dma("sp", gst[:], GST, writes=[b_gst])
    sm = P.sb("sm", [8, 2, 128], F32); b_sm = P.buf()
    P.dma("sp", sm[:], selm, writes=[b_sm])
    rstd = P.sb("rstd", [128, 512], F32); b_rstd = P.buf()
    msq = P.sb("msq", [128, 512], F32); b_msq = P.buf()
    xc = [P.sb(f"xc{i}", [128, 512], F32) for i in range(2)]; b_xc = [P.buf() for _ in range(2)]
    xo = [P.sb(f"xo{i}", [128, 512], BF16) for i in range(2)]; b_xo = [P.buf() for _ in range(2)]
    for tb in range(TL // 512):
        sl = slice(tb * 512, (tb + 1) * 512)
        pi = tb % 2
        bc = b_cv[tb // 4]
        P.op("pe", lambda e, sl=sl, pi=pi: e.matmul(pm[pi][:], lhsT=sm[:, 0, :], rhs=gst[:, sl], start=True, stop=True), reads=[b_sm, b_gst], writes=[b_pm[pi]])
        P.op("pe", lambda e, sl=sl, pi=pi: e.matmul(pq[pi][:], lhsT=sm[:, 1, :], rhs=gst[:, sl], start=True, stop=True), reads=[b_sm, b_gst], writes=[b_pq[pi]])
        P.op("act", lambda e, pi=pi: e.activation(out=msq[:], in_=pm[pi][:], func=AF.Square), reads=[b_pm[pi]], writes=[b_msq])
        P.op("dve", lambda e, pi=pi: e.scalar_tensor_tensor(out=rstd[:], in0=pq[pi][:], scalar=EPS, in1=msq[:], op0=ALU.add, op1=ALU.subtract),
             reads=[b_pq[pi], b_msq], writes=[b_rstd])
        P.op("act", lambda e: e.activation(out=rstd[:], in_=rstd[:], func=AF.Sqrt), reads=[b_rstd], writes=[b_rstd])
        P.op("dve", lambda e: e.reciprocal(out=rstd[:], in_=rstd[:]), reads=[b_rstd], writes=[b_rstd])
        P.op("dve", lambda e, sl=sl, pi=pi: e.tensor_sub(out=xc[pi][:], in0=cv[:, sl], in1=pm[pi][:]), reads=[bc, b_pm[pi]], writes=[b_xc[pi]])
        P.op("pool", lambda e, pi=pi: e.tensor_mul(out=xc[pi][:], in0=xc[pi][:], in1=rstd[:]), reads=[b_xc[pi], b_rstd], writes=[b_xc[pi]])
        P.op("act", lambda e, pi=pi: e.activation(out=xo[pi][:], in_=xc[pi][:], func=AF.Silu, scale=gsb[:, 0:1], bias=lbsb[:, 0:1]),
             reads=[b_xc[pi], b_w], writes=[b_xo[pi]])
        b_m1 = P.buf() if (tb % 2 == 0) else b_m1
        P.dma("sp", MIX1[1 + tb // 2, 256:384, (tb % 2) * 512:(tb % 2 + 1) * 512], xo[pi][:], reads=[b_xo[pi]], djw=[b_m1])
        if GMIX1 is not None and tb % 2 == 1:
            P.cc("AllGather", MIX1[1 + tb // 2].opt(), GMIX1[1 + tb // 2].opt(), G4, reads=[b_m1])
    P.barrier()
    P.pop_scope()


def build_fused(stop_after=None, debug=(), NQB=None):
    nc = new_nc()
    lambda_init = 0.8 - 0.6 * math.exp(-0.3 * 1)
    xall = din(nc, "xall", [TA, 1024]); xown = din(nc, "xown", [17 * 128, 1024])
    cT2 = din(nc, "cT2", [128, 8, 2]); mw = din(nc, "mw", [2, 1024, 6144]); mb = din(nc, "mb", [1, 2, 6144])
    selc = din(nc, "selc", [2, 258])
    identb = din(nc, "identb", [128, 128], BF16); identf = din(nc, "identf", [128, 128])
    gT = din(nc, "gT", [2, 4, 128, 8]); gR = din(nc, "gR", [2, 4, 128, 1024])
    w_in0 = din(nc, "w_in0", [1024, 1164]); cw0 = din(nc, "cw0", [128, 5, 5]); cb0 = din(nc, "cb0", [128, 5])
    ccsc = din(nc, "ccsc", [128, 256]); CT = din(nc, "CT", [TL, TL], BF16); STt = din(nc, "STt", [TL, TL], BF16)
    CTc = din(nc, "CTc", [TC, TC], BF16); STc = din(nc, "STc", [TC, TC], BF16)
    prm = din(nc, "prm", [128, 3, 2, NTA, 6]); tri = din(nc, "tri", [128, 3, 128]); gnR = din(nc, "gnR", [128, 384])
    w_out0 = din(nc, "w_out0", [2048, 1024])
    wg = din(nc, "wg", [2, 1024, 2816]); wu = din(nc, "wu", [2, 1024, 2816]); wd = din(nc, "wd", [2, 2816, 1024])
    w_in1 = din(nc, "w_in1", [1024, 1024]); cs = din(nc, "cs", [TL, 128]); sn = din(nc, "sn", [TL, 128])
    lamR = din(nc, "lamR", [128, 4, 64]); subC = din(nc, "subC", [128, 1])
    cw1 = din(nc, "cw1", [128, 1, 31]); cb1 = din(nc, "cb1", [128, 1]); lng = din(nc, "lng", [128, 1]); lnb = din(nc, "lnb", [128, 1])
    selm = din(nc, "selm", [8, 2, 128]); w_out1 = din(nc, "w_out1", [1536, 1024])
    out = nc.dram_tensor("out", [2048, 1024], F32, kind="ExternalOutput").ap()
    modT_d = dscr(nc, "modT_d", [2, 128, 2, 6, 8]); gate_d = dscr(nc, "gate_d", [2, 2, 2, 128, 1024])
    FM0 = dscr(nc, "FM0", [768, FMW]); ZDT = dscr(nc, "ZDT", [TA, 396]); XBC = dscr(nc, "XBC", [640, TA])
    YF = dscr(nc, "YF", [TA, 384]); MIX0 = dscr(nc, "MIX0", [9, 512, 1024], BF16); GMIX0 = dscr(nc, "GMIX0", [9, 2048, 1024], BF16)
    HMID = dscr(nc, "HMID", [17 * 128, 1024]); H1 = dscr(nc, "H1", [18 * 128, 1024]); GH1 = dscr(nc, "GH1", [9, 1024, 1024])
    QKV = dscr(nc, "QKV", [TA, 768]); UT = dscr(nc, "UT", [256, TL + 30])
    MIX1 = dscr(nc, "MIX1", [9, 384, 1024], BF16); GMIX1 = dscr(nc, "GMIX1", [9, 1536, 1024], BF16)
    OWN0 = dscr(nc, "OWN0", [2, 2048, 1024], BF16); OWNC = dscr(nc, "OWNC", [2048, 64], BF16); OWN1 = dscr(nc, "OWN1", [2, 1536, 1024], BF16)
    STs = dscr(nc, "STs", [2, TL]); GST = dscr(nc, "GST", [8, TL]); HMID2 = dscr(nc, "HMID2", [2048, 1024])
    scr = dict(modT_d=modT_d, gate_d=gate_d, FM0=FM0, ZDT=ZDT, XBC=XBC, YF=YF, MIX0=MIX0, GMIX0=GMIX0, HMID=HMID, H1=H1, GH1=GH1,
               QKV=QKV, UT=UT, MIX1=MIX1, GMIX1=GMIX1, GST=GST, HMID2=HMID2)
    dbg_out = {}
    for name in debug:
        a = scr[name]
        dbg_out[name] = nc.dram_tensor("dbg_" + name, list(a.shape), a.dtype, kind="ExternalOutput").ap()
    with ExitStack() as st:
        P = Prog(nc, st)
        dyn = {}

        def setup(e):
            pid = nc.partition_id([mybir.EngineType.SP])
            q = pid % 4
            dyn["qrow0"] = e.snap(q * (2 * 2048), min_val=0, max_val=3 * 2 * 2048)
            dyn["qrow1"] = e.snap(q * (2 * 1536), min_val=0, max_val=3 * 2 * 1536)
            dyn["ctx0"] = e.snap(q * 64, min_val=0, max_val=192)
        P.raw("sp", setup)

        def finish():
            for name in debug:
                a = scr[name]
                if len(a.shape) > 2:
                    continue
            toks = []
            for name in debug:
                a, o = scr[name], dbg_out[name]
                if len(a.shape) == 3:
                    a = a.rearrange("c r t -> (c r) t"); o = o.rearrange("c r t -> (c r) t")
                toks.append(P.dma("sp", o, a))
            P.barrier()
            P.emit()
            return nc

        stages = ["mods", "inproj0", "conv0", "fourier", "ssd", "ag0", "outproj0", "ffn0", "ag1", "inproj1", "attn", "conf", "ag2", "outproj1", "ffn1"]
        last = stages.index(stop_after) if stop_after else len(stages) - 1

        def want(name):
            return stages.index(name) <= last

        phase_mods(P, cT2, mw, mb, selc, modT_d, gate_d)
        if not want("inproj0"):
            return finish()
        tile_srcs = [[(slice(0, 128), xall[t * 128:(t + 1) * 128, :])] for t in range(NTA)]
        tile_cls = [1 if t < 2 else 0 for t in range(NTA)]
        groups = [[0, 1]] + [[2 + 4 * g + i for i in range(4)] for g in range(16)]

        def fm_dst0(c6, gi):
            if gi == 0:
                return FM0[c6 * 128:(c6 + 1) * 128, 2:2 + TC]
            return FM0[c6 * 128:(c6 + 1) * 128, LAT0 + (gi - 1) * 512:LAT0 + gi * 512]
        phase_inproj(P, tile_srcs, tile_cls, groups, w_in0, 6, 396, modT_d[0], gT[0, 0], identb, fm_dst0,
                     lambda t: ZDT[t * 128:(t + 1) * 128, :])
        if not want("conv0"):
            return finish()
        phase_conv0(P, FM0, cw0, cb0, XBC)
        if not want("fourier"):
            return finish()
        phase_fourier(P, FM0, ccsc, CT, STt, CTc, STc, MIX0)
        if not want("ssd"):
            return finish()
        phase_ssd(P, XBC, ZDT, prm, tri, gnR, identf, YF, MIX0, GMIX0)
        if not want("ag0"):
            return finish()
        if not want("outproj0"):
            return finish()
        g0f = GMIX0.rearrange("c r t -> (c r) t")
        P.dma("sp", OWN0.rearrange("c r t -> (c r) t"), lambda: g0f[2048:, :][bass.ds(dyn["qrow0"], 2 * 2048), :])
        P.dma("sp", OWNC, lambda: GMIX0[0][:, bass.ds(dyn["ctx0"], 64)])
        P.barrier()
        o0v = OWN0.rearrange("c (k p) t -> p c k t", p=128)
        ocv = OWNC.rearrange("(k p) t -> p k t", p=128)

        def mt0(t):
            if t < 16:
                return (lambda m: m[:], o0v[:, t // 8, :, (t % 8) * 128:(t % 8 + 1) * 128])
            return (lambda m: m[:, :, 0:64], ocv)
        phase_outproj(P, 2048, 17, mt0, xown, w_out0, gR[0, 1], gate_d[0], HMID)
        if not want("ffn0"):
            return finish()
        phase_ffn(P, HMID, 17, wg[0], wu[0], wd[0], modT_d[0], gT[0, 2], gR[0, 3], gate_d[0], identb, H1, (16,), GOUT=GH1)
        if not want("ag1"):
            return finish()
        if not want("inproj1"):
            return finish()
        tile_srcs = [[(slice(0, 64), GH1[8, 0:64, :]), (slice(64, 128), GH1[8, 256:320, :])],
                     [(slice(0, 64), GH1[8, 512:576, :]), (slice(64, 128), GH1[8, 768:832, :])]]
        for j in range(64):
            r, t = j // 16, j % 16
            r0 = r * 256 + (t % 2) * 128
            tile_srcs.append([(slice(0, 128), GH1[t // 2, r0:r0 + 128, :])])
        phase_inproj(P, tile_srcs, tile_cls, groups, w_in1, 2, 768, modT_d[1], gT[1, 0], identb,
                     lambda c2, gi: UT[c2 * 128:(c2 + 1) * 128, 15 + (gi - 1) * 512:15 + gi * 512],
                     lambda t: QKV[t * 128:(t + 1) * 128, :], fm_groups=set(range(1, 17)))
        if not want("attn"):
            return finish()
        phase_attn(P, QKV, cs, sn, lamR, subC, identb, lambda_init, MIX1, NQB=NQB)
        if not want("conf"):
            return finish()
        phase_conformer(P, UT, cw1, cb1, lng, lnb, selm, STs, GST, MIX1, GMIX1)
        if not want("ag2"):
            return finish()
        if not want("outproj1"):
            return finish()
        g1f = GMIX1.rearrange("c r t -> (c r) t")
        P.dma("sp", OWN1.rearrange("c r t -> (c r) t"), lambda: g1f[1536:, :][bass.ds(dyn["qrow1"], 2 * 1536), :])
        P.barrier()
        o1v = OWN1.rearrange("c (k p) t -> p c k t", p=128)

        def mt1(t):
            return (lambda m: m[:], o1v[:, t // 8, :, (t % 8) * 128:(t % 8 + 1) * 128])
        phase_outproj(P, 1536, 16, mt1, H1, w_out1, gR[1, 1], gate_d[1], HMID2)
        if not want("ffn1"):
            return finish()
        phase_ffn(P, HMID2, 16, wg[1], wu[1], wd[1], modT_d[1], gT[1, 2], gR[1, 3], gate_d[1], identb, out, ())
        return finish()


import math
import ml_dtypes

NCORES = 8
CORES = list(range(NCORES))
_NC_CACHE = {}


def featT(v):
    n = v.shape[0] // 128
    return np.ascontiguousarray(v.reshape(n, 128).T)


def rep(v):
    return np.ascontiguousarray(np.broadcast_to(v, (128,) + v.shape))


def dft_tabs(n):
    tab = np.arange(n, dtype=np.float64) * (2 * np.pi / n)
    idx = (np.arange(n, dtype=np.int64)[:, None] * np.arange(n, dtype=np.int64)[None, :]) % n
    c = (np.cos(tab) / math.sqrt(n)).astype(np.float32).astype(ml_dtypes.bfloat16)
    s = (np.sin(tab) / math.sqrt(n)).astype(np.float32).astype(ml_dtypes.bfloat16)
    return c[idx], s[idx]


def rope_tables():
    t = 8192
    row = np.repeat(np.arange(t // 64, dtype=np.float32), 64)
    col = np.tile(np.arange(64, dtype=np.float32), t // 64)
    inv = (10000.0 ** (-np.arange(16, dtype=np.float32) * 2.0 / 32)).astype(np.float32)
    ang = np.stack([row, col], -1)[:, :, None] * inv
    cos, sin = np.cos(ang).astype(np.float32), np.sin(ang).astype(np.float32)
    cs = np.broadcast_to(cos[:, None, :, None, :], (t, 2, 2, 2, 16)).reshape(t, 128)
    sg = np.array([-1.0, 1.0], np.float32)[None, None, None, :, None]
    sn = (np.broadcast_to(sin[:, None, :, None, :], (t, 2, 2, 2, 16)) * sg).reshape(t, 128)
    return np.ascontiguousarray(cs), np.ascontiguousarray(sn)


def prep_inputs(x, c, ctx, c_ctx, mod_w, mod_b, norm_g, ffn_w_gate, ffn_w_up, ffn_w_down,
                ev_w_in, ev_conv_w, ev_conv_b, ev_dt_bias, ev_a_log, ev_d_skip, ev_gnorm_g, ev_w_out,
                od_w_in, od_lambda, od_subln_g, od_conv_w, od_conv_b, od_cnorm_g, od_cnorm_b, od_w_out):
    identf = np.eye(128, dtype=np.float32)
    identb = identf.astype(ml_dtypes.bfloat16)
    selc = np.zeros((2, 258), np.float32)
    selc[0, 0] = 1; selc[1, 1] = 1; selc[0, 2:130] = 1; selc[1, 130:258] = 1
    gT = np.stack([np.stack([featT(norm_g[l, j]) for j in range(4)]) for l in range(2)])
    gR = np.stack([np.stack([rep(norm_g[l, j]) for j in range(4)]) for l in range(2)])
    CT, ST = dft_tabs(8192)
    CTc, STc = dft_tabs(256)
    kk = np.arange(128)
    ang = 2 * np.pi * np.outer(kk, kk) / 128
    ccsc = (np.concatenate([np.cos(ang), -np.sin(ang)], 1) / math.sqrt(128)).astype(np.float32)
    tri = np.zeros((128, 3, 128), np.float32)
    s_, l_ = np.meshgrid(np.arange(128), np.arange(128), indexing="ij")
    tri[:, 0] = (s_ <= l_); tri[:, 1] = (s_ >= l_); tri[:, 2] = 1.0
    cs, sn = rope_tables()
    selm = np.zeros((8, 2, 128), np.float32)
    selm[0::2, 0, :] = 1.0 / 512
    selm[1::2, 1, :] = 1.0 / 512
    p0 = np.concatenate([np.concatenate([np.arange(r * 128, (r + 1) * 128), 512 + np.arange(r * 384, (r + 1) * 384)]) for r in range(4)])
    p1 = np.concatenate([np.concatenate([np.arange(r * 256, (r + 1) * 256), 1024 + np.arange(r * 128, (r + 1) * 128)]) for r in range(4)])
    w_out0 = np.ascontiguousarray(ev_w_out[0][p0])
    w_out1 = np.ascontiguousarray(od_w_out[0][p1])
    shared = dict(mw=mod_w, mb=np.ascontiguousarray(mod_b[None]), selc=selc, identb=identb, identf=identf, gT=gT, gR=gR,
                  ccsc=ccsc, CT=CT, STt=ST, CTc=CTc, STc=STc, tri=tri, w_out0=w_out0, wg=ffn_w_gate, wu=ffn_w_up, wd=ffn_w_down,
                  cs=cs, sn=sn, lamR=rep(od_lambda[0]), subC=np.ascontiguousarray(od_subln_g[0].reshape(128, 1)), selm=selm, w_out1=w_out1)
    maps = []
    for core in CORES:
        b, g = core // 4, core % 4
        q = g
        m = dict(shared)
        m["xall"] = np.ascontiguousarray(np.concatenate([ctx[b], x[b]], 0))
        xo = np.zeros((17 * 128, 1024), np.float32)
        xo[:2048] = x[b, q * 2048:(q + 1) * 2048]; xo[2048:2112] = ctx[b, q * 64:(q + 1) * 64]
        m["xown"] = xo
        cv = np.stack([c[b], c_ctx], 0)
        m["cT2"] = np.ascontiguousarray(cv.reshape(2, 8, 128).transpose(2, 1, 0))
        dtcols = np.array([4608 + d * 24 + g * 6 + h for d in range(2) for h in range(6)])
        cols0 = np.concatenate([np.arange(g * 128, (g + 1) * 128), 2048 + np.arange(g * 384, (g + 1) * 384),
                                2048 + 1536 + np.arange(g * 128, (g + 1) * 128), 2048 + 2048 + np.arange(g * 128, (g + 1) * 128),
                                512 + np.arange(g * 384, (g + 1) * 384), dtcols])
        m["w_in0"] = np.ascontiguousarray(ev_w_in[0][:, cols0])
        ch = np.concatenate([np.arange(g * 384, (g + 1) * 384), 1536 + np.arange(g * 128, (g + 1) * 128), 2048 + np.arange(g * 128, (g + 1) * 128)])
        m["cw0"] = np.ascontiguousarray(ev_conv_w[0][:, ch].T.reshape(5, 128, 5).transpose(1, 0, 2))
        m["cb0"] = np.ascontiguousarray(ev_conv_b[0][ch].reshape(5, 128).T)
        prm = np.zeros((128, 3, 2, 66, 6), np.float32)
        for k_, a_ in enumerate((ev_dt_bias, ev_a_log, ev_d_skip)):
            prm[:, k_] = a_[0].reshape(2, 4, 6)[:, g, :][None, :, None, :]
        m["prm"] = prm
        m["gnR"] = rep(ev_gnorm_g[0][g * 384:(g + 1) * 384])
        hd0 = 2 * q
        cols1 = np.concatenate([3072 + np.arange(q * 128, (q + 1) * 128), 3072 + 512 + np.arange(q * 128, (q + 1) * 128),
                                np.arange(hd0 * 128, (hd0 + 2) * 128), 1024 + np.arange(hd0 * 128, (hd0 + 2) * 128),
                                2048 + np.arange(hd0 * 128, (hd0 + 2) * 128)])
        m["w_in1"] = np.ascontiguousarray(od_w_in[0][:, cols1])
        cq = slice(q * 128, (q + 1) * 128)
        m["cw1"] = np.ascontiguousarray(od_conv_w[0][:, cq].T.reshape(128, 1, 31))
        m["cb1"] = np.ascontiguousarray(od_conv_b[0][cq].reshape(128, 1))
        m["lng"] = np.ascontiguousarray(od_cnorm_g[0][cq].reshape(128, 1))
        m["lnb"] = np.ascontiguousarray(od_cnorm_b[0][cq].reshape(128, 1))
        maps.append(m)
    return maps


def kernel(**inputs):
    inputs = {k: np.ascontiguousarray(np.asarray(v, dtype=np.float32)) for k, v in inputs.items()}
    maps = prep_inputs(**inputs)
    if "fused" not in _NC_CACHE:
        _NC_CACHE["fused"] = build_fused()
    res = run_bass_kernel_spmd(_NC_CACHE["fused"], maps, core_ids=CORES)
    out = np.zeros((2, 8192, 1024), np.float32)
    for core in CORES:
        b, q = core // 4, core % 4
        out[b, q * 2048:(q + 1) * 2048] = res.results[core]["out"]
    return out
```
